# Optimizing a Trainium2 kernel written in Bass

```python
import jax, jax.numpy as jnp
from jax import lax
import numpy as np

D_MODEL = 1024
BATCH = 8
SEQ = 2048
DEPTH = 1

DILATED_GROUPS = ((128, 1), (512, 4), (2048, 16))
N_ATT_GROUPS = 3
ATT_HEADS_PER_GROUP = 4
ATT_HEAD_DIM = 128
ATT_WIDTH = N_ATT_GROUPS * ATT_HEADS_PER_GROUP * ATT_HEAD_DIM
ATT_OUT_WIDTH = ATT_HEADS_PER_GROUP * ATT_HEAD_DIM
ATT_Q_BLOCK = 128
ROPE_THETA = 500000.0
ROPE_DIMS = ATT_HEAD_DIM // 4
MLSTM_HEADS = 4
MLSTM_WIDTH = D_MODEL
MLSTM_HEAD_DIM = MLSTM_WIDTH // MLSTM_HEADS
MLSTM_CHUNK = 64
CONV_WIDTH = 4
D_FF = ((8 * D_MODEL // 3 + 127) // 128) * 128
RMS_EPS = 1e-6
COL_SIZES = (ATT_WIDTH, ATT_WIDTH, ATT_WIDTH, 2 * MLSTM_WIDTH, MLSTM_WIDTH, MLSTM_WIDTH,
             MLSTM_HEADS, MLSTM_HEADS, D_MODEL, D_MODEL)
IN_WIDTH = 3 * ATT_WIDTH + 4 * MLSTM_WIDTH + 2 * MLSTM_HEADS + 2 * D_MODEL

kernel_name = 'hybrid_dilated_attn_mlstm_macaron_block'


def rmsnorm(x, g):
    xf = x.astype(jnp.float32)
    y = xf * lax.rsqrt(jnp.mean(xf * xf, axis=-1, keepdims=True) + RMS_EPS)
    return (y * g.astype(jnp.float32)).astype(x.dtype)


def swiglu(x, w_gate, w_up, w_down):
    return (jax.nn.silu(x @ w_gate) * (x @ w_up)) @ w_down


def partial_rope(x, positions):
    half = ROPE_DIMS // 2
    inv_freq = jnp.power(ROPE_THETA, -(jnp.arange(half, dtype=jnp.float32) * 2.0 / ROPE_DIMS))
    ang = positions.astype(jnp.float32)[:, None] * inv_freq[None, :]
    cos, sin = jnp.cos(ang)[:, None, :], jnp.sin(ang)[:, None, :]
    xf = x.astype(jnp.float32)
    x1, x2, xp = xf[..., :half], xf[..., half:ROPE_DIMS], xf[..., ROPE_DIMS:]
    out = jnp.concatenate([x1 * cos - x2 * sin, x2 * cos + x1 * sin, xp], axis=-1)
    return out.astype(x.dtype)


def dilated_window_attention(q, k, v, window, dilation):
    B, S, H, dh = q.shape
    L = S // dilation
    span = window // dilation
    bq = ATT_Q_BLOCK
    nblk = -(-L // bq)
    lp = nblk * bq

    def classes(t, front):
        t = t.reshape(B, L, dilation, H, dh)
        return jnp.pad(t, ((0, 0), (front, lp - L), (0, 0), (0, 0), (0, 0)))

    qb = classes(q, 0).reshape(B, nblk, bq, dilation, H, dh)
    idx = np.arange(nblk)[:, None] * bq + np.arange(bq + span)[None, :]
    kw = classes(k, span)[:, idx]
    vw = classes(v, span)[:, idx]
    s = jnp.einsum('bnirhc,bnjrhc->bnrhij', qb, kw).astype(jnp.float32) * (dh ** -0.5)
    qpos = np.arange(nblk)[:, None, None] * bq + np.arange(bq)[None, :, None]
    kpos = np.arange(nblk)[:, None, None] * bq - span + np.arange(bq + span)[None, None, :]
    dist = qpos - kpos
    valid = (dist >= 0) & (dist <= span) & (kpos >= 0)
    s = jnp.where(valid[None, :, None, None], s, -jnp.inf)
    m = jnp.max(s, axis=-1)
    p = jnp.exp(s - m[..., None])
    l = jnp.sum(p, axis=-1)
    o = jnp.einsum('bnrhij,bnjrhc->bnirhc', p, vw.astype(jnp.float32))
    o = o / jnp.moveaxis(l, -1, 2)[..., None]
    lse = jnp.moveaxis(m + jnp.log(l), -1, 2)
    o = o.reshape(B, lp, dilation, H, dh)[:, :L].reshape(B, S, H, dh)
    lse = lse.reshape(B, lp, dilation, H)[:, :L].reshape(B, S, H)
    return o, lse


def causal_depthwise_conv(x, w, b):
    S, K = x.shape[1], w.shape[0]
    xp = jnp.pad(x, ((0, 0), (K - 1, 0), (0, 0)))
    y = b + w[0] * xp[:, 0:S]
    for j in range(1, K):
        y = y + w[j] * xp[:, j:j + S]
    return y


def mlstm_chunkwise(q, k, v, i_pre, log_f):
    B, S, H, dh = q.shape
    lc = MLSTM_CHUNK
    nc = S // lc

    def chunks(t):
        return t.astype(jnp.float32).reshape(B, nc, lc, H, dh).transpose(1, 0, 3, 2, 4)

    def gchunks(t):
        return t.reshape(B, nc, lc, H).transpose(1, 0, 3, 2)

    qc, kc, vc = chunks(q), chunks(k) * (dh ** -0.5), chunks(v)
    ic, fc = gchunks(i_pre), gchunks(log_f)
    causal = np.tril(np.ones((lc, lc), dtype=bool))

    def step(carry, inp):
        C, n, m = carry
        qt, kt, vt, it, ft = inp
        b = jnp.cumsum(ft, axis=-1)
        D = jnp.where(causal, b[..., :, None] - b[..., None, :] + it[..., None, :], -jnp.inf)
        m_inter = b + m[..., None]
        m_t = jnp.maximum(m_inter, jnp.max(D, axis=-1))
        w = jnp.exp(D - m_t[..., None]) * jnp.einsum('bhtk,bhsk->bhts', qt, kt)
        inter = jnp.exp(m_inter - m_t)
        num = jnp.einsum('bhts,bhsv->bhtv', w, vt) + inter[..., None] * jnp.einsum('bhvk,bhtk->bhtv', C, qt)
        den = jnp.sum(w, axis=-1) + inter * jnp.einsum('bhk,bhtk->bht', n, qt)
        h = num / jnp.maximum(jnp.abs(den), jnp.exp(-m_t))[..., None]
        bl = b[..., -1]
        g = bl[..., None] - b + it
        m_new = jnp.maximum(bl + m, jnp.max(g, axis=-1))
        a = jnp.exp(g - m_new[..., None])
        decay = jnp.exp(bl + m - m_new)
        C_new = decay[..., None, None] * C + jnp.einsum('bhsv,bhsk->bhvk', a[..., None] * vt, kt)
        n_new = decay[..., None] * n + jnp.einsum('bhs,bhsk->bhk', a, kt)
        return (C_new, n_new, m_new), h

    init = (jnp.zeros((B, H, dh, dh), jnp.float32), jnp.zeros((B, H, dh), jnp.float32),
            jnp.zeros((B, H), jnp.float32))
    _, hs = lax.scan(step, init, (qc, kc, vc, ic, fc))
    return hs.transpose(1, 0, 3, 2, 4).reshape(B, S, H, dh)


def hybrid_mixer(h, positions, w_in, conv_w, conv_b, i_bias, f_bias, head_g,
                 w_att_branch, w_mlstm_branch, w_out):
    B, S, _ = h.shape
    proj = h @ w_in
    points = np.cumsum(np.array(COL_SIZES))[:-1].tolist()
    q_a, k_a, v_a, qk_m, v_m, o_m, i_pre, f_pre, g_a, g_m = jnp.split(proj, points, axis=-1)

    nh = N_ATT_GROUPS * ATT_HEADS_PER_GROUP
    q_a = partial_rope(q_a.reshape(B, S, nh, ATT_HEAD_DIM), positions)
    k_a = partial_rope(k_a.reshape(B, S, nh, ATT_HEAD_DIM), positions)
    q_a = q_a.reshape(B, S, N_ATT_GROUPS, ATT_HEADS_PER_GROUP, ATT_HEAD_DIM)
    k_a = k_a.reshape(B, S, N_ATT_GROUPS, ATT_HEADS_PER_GROUP, ATT_HEAD_DIM)
    v_a = v_a.reshape(B, S, N_ATT_GROUPS, ATT_HEADS_PER_GROUP, ATT_HEAD_DIM)
    outs, lses = [], []
    for g, (window, dilation) in enumerate(DILATED_GROUPS):
        o, lse = dilated_window_attention(q_a[:, :, g], k_a[:, :, g], v_a[:, :, g], window, dilation)
        outs.append(o)
        lses.append(lse)
    alpha = jax.nn.softmax(jnp.stack(lses, axis=0), axis=0)
    att = jnp.sum(alpha[..., None] * jnp.stack(outs, axis=0), axis=0)
    att = att.reshape(B, S, ATT_OUT_WIDTH).astype(h.dtype)

    qk = jax.nn.silu(causal_depthwise_conv(qk_m, conv_w, conv_b))
    q_m, k_m = jnp.split(qk, 2, axis=-1)
    shp = (B, S, MLSTM_HEADS, MLSTM_HEAD_DIM)
    ig = i_pre.astype(jnp.float32) + i_bias.astype(jnp.float32)
    lf = jax.nn.log_sigmoid(f_pre.astype(jnp.float32) + f_bias.astype(jnp.float32))
    hm = mlstm_chunkwise(q_m.reshape(shp), k_m.reshape(shp), v_m.reshape(shp), ig, lf)
    hm = hm * lax.rsqrt(jnp.mean(hm * hm, axis=-1, keepdims=True) + RMS_EPS)
    hm = hm * head_g.astype(jnp.float32).reshape(MLSTM_HEADS, MLSTM_HEAD_DIM)
    ml = (jax.nn.sigmoid(o_m.astype(jnp.float32)) * hm.reshape(B, S, MLSTM_WIDTH)).astype(h.dtype)

    merged = jax.nn.sigmoid(g_a) * (att @ w_att_branch) + jax.nn.sigmoid(g_m) * (ml @ w_mlstm_branch)
    return merged @ w_out


def setup_inputs(seed: int = 0) -> dict:
    key = jax.random.key(seed)
    ks = jax.random.split(key, 24)
    f32 = jnp.float32

    def dense(k, fan_in, shape):
        return jax.random.normal(k, (DEPTH,) + shape, f32) * (fan_in ** -0.5)

    def gain(k, n):
        return 1.0 + 0.05 * jax.random.normal(k, (DEPTH, n), f32)

    return {
        'x': jax.random.normal(ks[0], (BATCH, SEQ, D_MODEL), f32),
        'ffn1_pre_g': gain(ks[1], D_MODEL),
        'ffn1_w_gate': dense(ks[2], D_MODEL, (D_MODEL, D_FF)),
        'ffn1_w_up': dense(ks[3], D_MODEL, (D_MODEL, D_FF)),
        'ffn1_w_down': dense(ks[4], D_FF, (D_FF, D_MODEL)),
        'ffn1_post_g': gain(ks[5], D_MODEL),
        'mix_pre_g': gain(ks[6], D_MODEL),
        'w_in': dense(ks[7], D_MODEL, (D_MODEL, IN_WIDTH)),
        'conv_w': dense(ks[8], CONV_WIDTH, (CONV_WIDTH, 2 * MLSTM_WIDTH)),
        'conv_b': 0.01 * jax.random.normal(ks[9], (DEPTH, 2 * MLSTM_WIDTH), f32),
        'mlstm_i_bias': 0.1 * jax.random.normal(ks[10], (DEPTH, MLSTM_HEADS), f32),
        'mlstm_f_bias': jnp.linspace(3.0, 6.0, MLSTM_HEADS, dtype=f32)[None, :]
                        + 0.1 * jax.random.normal(ks[11], (DEPTH, MLSTM_HEADS), f32),
        'mlstm_head_g': gain(ks[12], MLSTM_WIDTH),
        'w_att_branch': dense(ks[13], ATT_OUT_WIDTH, (ATT_OUT_WIDTH, D_MODEL)),
        'w_mlstm_branch': dense(ks[14], MLSTM_WIDTH, (MLSTM_WIDTH, D_MODEL)),
        'w_out': dense(ks[15], D_MODEL, (D_MODEL, D_MODEL)),
        'mix_post_g': gain(ks[16], D_MODEL),
        'ffn2_pre_g': gain(ks[17], D_MODEL),
        'ffn2_w_gate': dense(ks[18], D_MODEL, (D_MODEL, D_FF)),
        'ffn2_w_up': dense(ks[19], D_MODEL, (D_MODEL, D_FF)),
        'ffn2_w_down': dense(ks[20], D_FF, (D_FF, D_MODEL)),
        'ffn2_post_g': gain(ks[21], D_MODEL),
    }


def reference(x, ffn1_pre_g, ffn1_w_gate, ffn1_w_up, ffn1_w_down, ffn1_post_g,
              mix_pre_g, w_in, conv_w, conv_b, mlstm_i_bias, mlstm_f_bias, mlstm_head_g,
              w_att_branch, w_mlstm_branch, w_out, mix_post_g,
              ffn2_pre_g, ffn2_w_gate, ffn2_w_up, ffn2_w_down, ffn2_post_g):
    positions = jnp.arange(x.shape[1], dtype=jnp.int32)
    for l in range(DEPTH):
        f = swiglu(rmsnorm(x, ffn1_pre_g[l]), ffn1_w_gate[l], ffn1_w_up[l], ffn1_w_down[l])
        x = x + 0.5 * rmsnorm(f, ffn1_post_g[l])
        y = hybrid_mixer(rmsnorm(x, mix_pre_g[l]), positions, w_in[l], conv_w[l], conv_b[l],
                         mlstm_i_bias[l], mlstm_f_bias[l], mlstm_head_g[l],
                         w_att_branch[l], w_mlstm_branch[l], w_out[l])
        x = x + rmsnorm(y, mix_post_g[l])
        f = swiglu(rmsnorm(x, ffn2_pre_g[l]), ffn2_w_gate[l], ffn2_w_up[l], ffn2_w_down[l])
        x = x + 0.5 * rmsnorm(f, ffn2_post_g[l])
    return x
```

```python
from contextlib import ExitStack
import math
import os

import numpy as np
import ml_dtypes
import concourse.bass as bass
import concourse.mybir as mybir
from concourse.bass_utils import run_bass_kernel_spmd

F32 = mybir.dt.float32
BF16 = mybir.dt.bfloat16
AF = mybir.ActivationFunctionType
ALU = mybir.AluOpType
AX = mybir.AxisListType

S = 2048
D = 1024
FF = 2816
NT = S // 128
KC = D // 128
FC = FF // 128
IN_W = 10760
EPS = 1e-6
ENGS = ("pe", "act", "dve", "pool", "sp")


class Reg:
    __slots__ = ("name", "w", "rs", "rd", "excl")

    def __init__(self, name="", excl=False):
        self.name = name
        self.excl = excl
        self.w = None
        self.rs = {}
        self.rd = []


class Ins:
    __slots__ = ("eng", "fn", "deps", "signal", "val", "dma", "key")

    def __init__(self, eng, fn, dma=False, key=None):
        self.eng = eng
        self.fn = fn
        self.deps = ()
        self.signal = dma
        self.val = 0
        self.dma = dma
        self.key = key


class Prog:
    def __init__(self, nc):
        self.nc = nc
        self.engs = {e: [] for e in ENGS}
        self.dma_tot = {}
        self.dma_last = {}

    def _add(self, ins, reads, writes):
        eng = ins.eng
        deps = {}
        for r in reads:
            d = r.w
            if d is not None:
                deps[id(d)] = d
            if r.excl:
                for e2, x in r.rs.items():
                    if e2 != eng:
                        deps[id(x)] = x
        for w in writes:
            d = w.w
            if d is not None:
                deps[id(d)] = d
            for e2, x in w.rs.items():
                if (not ins.dma) and e2 == eng:
                    continue
                deps[id(x)] = x
            for x in w.rd:
                deps[id(x)] = x
        out = []
        for d in deps.values():
            if d is ins:
                continue
            if (not d.dma) and (not ins.dma) and d.eng == "pe" and eng == "pe":
                continue
            d.signal = True
            out.append(d)
        ins.deps = out
        for r in reads:
            if ins.dma:
                r.rd.append(ins)
            else:
                r.rs[eng] = ins
        for w in writes:
            w.w = ins
            w.rs = {}
            w.rd = []
        self.engs[eng].append(ins)
        return ins

    def op(self, eng, fn, reads=(), writes=()):
        return self._add(Ins(eng, fn), reads, writes)

    def dma(self, queue, out, in_, key, reads=(), writes=(), **kw):
        ins = Ins(queue, lambda e: e.dma_start(out=out, in_=in_, **kw), dma=True, key=key)
        self.dma_tot[key] = self.dma_tot.get(key, 0) + 16
        ins.val = self.dma_tot[key]
        self.dma_last[key] = ins
        return self._add(ins, reads, writes)

    def barrier(self):
        lasts = []
        for e in ENGS:
            for ins in reversed(self.engs[e]):
                if ins.fn is not None and not ins.dma:
                    ins.signal = True
                    lasts.append(ins)
                    break
        lasts += list(self.dma_last.values())
        for e in ENGS:
            ins = Ins(e, None)
            ins.deps = [d for d in lasts if d.dma or d.eng != e]
            self.engs[e].append(ins)

    def finish(self):
        ins = Ins("sp", None)
        ins.deps = list(self.dma_last.values())
        self.engs["sp"].append(ins)

    def emit(self):
        nc = self.nc
        for e in ENGS:
            c = 0
            for ins in self.engs[e]:
                if ins.dma:
                    continue
                if ins.signal and ins.fn is not None:
                    c += 1
                    ins.val = c
        with ExitStack() as st:
            sems = {e: st.enter_context(nc.semaphore(f"s_{e}")) for e in ENGS}
            dsem = {k: st.enter_context(nc.semaphore(f"d_{k}")) for k in self.dma_tot}
            block = st.enter_context(nc.Block())
            bname = {"pe": "tensor", "act": "scalar", "dve": "vector", "pool": "gpsimd", "sp": "sync"}
            stats = {}
            self.dump = {}
            for e in ENGS:
                def body(engine, e=e):
                    seen = {}
                    nw = 0
                    for ins in self.engs[e]:
                        need = {}
                        for d in ins.deps:
                            s = ("d", d.key) if d.dma else ("c", d.eng)
                            if need.get(s, 0) < d.val:
                                need[s] = d.val
                        for s, v in need.items():
                            if seen.get(s, 0) < v:
                                seen[s] = v
                                sh = dsem[s[1]] if s[0] == "d" else sems[s[1]]
                                engine.wait_ge(sh, v)
                                nw += 1
                        if os.environ.get("DUMP"):
                            self.dump.setdefault(e, []).append((sorted((k, v) for k, v in need.items()), ins.fn is not None, ins.dma, ins.key, ins.signal, ins.val))
                        if ins.fn is not None:
                            bi = ins.fn(engine)
                            if ins.dma:
                                bi.then_inc(dsem[ins.key], 16)
                            elif ins.signal:
                                bi.then_inc(sems[e], 1)
                    stats[e] = (len(self.engs[e]), nw)
                getattr(block, bname[e])(body)
            self.stats = stats


class Arena:
    def __init__(self, ap, start, end):
        self.ap = ap
        self.start = start
        self.off = start
        self.end = end

    def alloc(self, free_shape, dt, parts=128):
        n = 1
        for v in free_shape:
            n *= v
        esz = 4 if dt == F32 else 2
        nbytes = n * esz
        st = (self.off + 63) // 64 * 64
        assert st + nbytes <= self.end, f"arena overflow: need {st + nbytes} > {self.end}"
        self.off = st + nbytes
        Arena.last = (st, tuple(free_shape), dt)
        v = self.ap[:parts, st // 2:(st + nbytes) // 2]
        if dt == F32:
            v = v.bitcast(F32)
        if len(free_shape) == 2:
            v = v.rearrange("p (a b) -> p a b", a=free_shape[0])
        elif len(free_shape) == 3:
            v = v.rearrange("p (a b c) -> p a b c", a=free_shape[0], b=free_shape[1])
        return v

    def sub(self, nbytes):
        st = (self.off + 63) // 64 * 64
        assert st + nbytes <= self.end, f"arena overflow(sub): need {st + nbytes} > {self.end}"
        self.off = st + nbytes
        return Arena(self.ap, st, st + nbytes)

    def child(self):
        return Arena(self.ap, self.start, self.end)


class Ctx:
    pass


DBG = {}


NST = 3


def load_cast(C, dst, src, dst_reg):
    P = C.P
    s = C.stage_i % NST
    C.stage_i += 1
    sh = src.shape
    n = 1
    for v in sh[1:]:
        n *= v
    assert n <= 1024
    stg = C.stage[s][:, :n]
    if len(sh) == 3:
        stg = stg.rearrange("p (a b) -> p a b", a=sh[1])
    P.dma("sp", stg, src, f"st{s}", writes=[C.stage_r[s]])
    P.op("pool", lambda e: e.tensor_copy(out=dst, in_=stg), reads=[C.stage_r[s]], writes=[dst_reg])


def norm_transpose(C, A_stage, x_src, g_pre_dram, hT):
    P, PS, psr = C.P, C.PS, C.psr
    gb = A_stage.alloc((D,), F32)
    gb_r = Reg()
    P.dma("sp", gb, g_pre_dram.partition_broadcast(128), "g", writes=[gb_r])
    ss = A_stage.alloc((NT,), F32)
    rstd = A_stage.alloc((NT,), F32)
    junk = A_stage.alloc((D,), BF16)
    junk_r = Reg()
    hb = [A_stage.alloc((D,), BF16) for _ in range(2)]
    hb_r = [Reg() for _ in range(2)]
    xs = [A_stage.alloc((D,), F32) for _ in range(NT)]
    xs_r = [Reg() for _ in range(NT)]
    ss_r = [Reg() for _ in range(NT)]
    rstd_r = Reg()
    for i in range(NT):
        P.dma("sp", xs[i], x_src[i * 128:(i + 1) * 128, :], f"xs{i}", writes=[xs_r[i]])
    for i in range(NT):
        P.op("act", lambda e, i=i: e.activation(out=junk, in_=xs[i], func=AF.Square, accum_out=ss[:, i:i + 1]),
             reads=[xs_r[i]], writes=[ss_r[i], junk_r])
    P.op("act", lambda e: e.activation(out=rstd, in_=ss, func=AF.Ln, scale=1.0 / D, bias=EPS),
         reads=ss_r, writes=[rstd_r])
    P.op("act", lambda e: e.activation(out=rstd, in_=rstd, func=AF.Exp, scale=-0.5),
         reads=[rstd_r], writes=[rstd_r])
    for i in range(NT):
        s = i % 2
        P.op("dve", lambda e, i=i, s=s: e.scalar_tensor_tensor(out=hb[s], in0=xs[i], scalar=rstd[:, i:i + 1], in1=gb,
                                                              op0=ALU.mult, op1=ALU.mult),
             reads=[xs_r[i], rstd_r, gb_r], writes=[hb_r[s]])
        b = i % 2
        psb = PS[:, 512 * b:512 * (b + 1)].bitcast(BF16)
        for k in range(KC):
            P.op("pe", lambda e, k=k, s=s, psb=psb: e.transpose(out=psb[:, k * 128:(k + 1) * 128],
                                                                in_=hb[s][:, k * 128:(k + 1) * 128], identity=C.ident),
                 reads=[hb_r[s], C.ident_r], writes=[psr[b]])
        P.op("act", lambda e, i=i, psb=psb: e.activation(out=hT[:, :, i * 128:(i + 1) * 128],
                                                         in_=psb.rearrange("p (k n) -> p k n", k=KC), func=AF.Copy),
             reads=[psr[b]], writes=[])
    P.barrier()


def ffn_phase(C, x_src, x_dst, g_pre, wg, wu, wd, g_post, stop_after=None):
    P, PS, psr = C.P, C.PS, C.psr
    A = C.A.child()
    hT_ar = A.sub(KC * S * 2)
    actT_ar = A.sub(FC * S * 2)
    hT = hT_ar.child().alloc((KC, S), BF16)
    actT = actT_ar.child().alloc((FC, S), BF16)
    norm_transpose(C, actT_ar.child(), x_src, g_pre, hT)

    NSL = 2
    wg_s = [A.alloc((KC, 256), BF16) for _ in range(NSL)]
    wu_s = [A.alloc((KC, 256), BF16) for _ in range(NSL)]
    wg_r = [[Reg(), Reg()] for _ in range(NSL)]
    wu_r = [[Reg(), Reg()] for _ in range(NSL)]
    wd_h = [A.alloc((FC, 512), BF16) for _ in range(2)]
    wd_r = [[Reg() for _ in range(FC // 2)] for _ in range(2)]
    sg = [A.alloc((512,), BF16) for _ in range(2)]
    sg_r = [Reg() for _ in range(2)]
    gb = A.alloc((D,), F32)
    gb_r = Reg()
    ss2 = A.alloc((NT,), F32)
    r2 = A.alloc((NT,), F32)
    junk = A.alloc((D,), BF16)
    junk_r = Reg()
    P.dma("sp", gb, g_post.partition_broadcast(128), "g", writes=[gb_r])

    wd_jobs = [(h, c2) for h in range(2) for c2 in range(FC // 2)]

    def load_wd_piece():
        if not wd_jobs:
            return
        h, c2 = wd_jobs.pop(0)
        load_cast(C, wd_h[h][:, 2 * c2:2 * c2 + 2, :],
                  wd[c2 * 256:(c2 + 1) * 256, h * 512:(h + 1) * 512].rearrange("(c p) n -> p c n", p=128),
                  wd_r[h][c2])

    j = 0
    for cb in range(FC // 2):
        s = cb % NSL
        for part in range(2):
            load_cast(C, wg_s[s][:, 4 * part:4 * part + 4, :],
                      wg[part * 512:(part + 1) * 512, cb * 256:(cb + 1) * 256].rearrange("(k p) n -> p k n", p=128),
                      wg_r[s][part])
            load_cast(C, wu_s[s][:, 4 * part:4 * part + 4, :],
                      wu[part * 512:(part + 1) * 512, cb * 256:(cb + 1) * 256].rearrange("(k p) n -> p k n", p=128),
                      wu_r[s][part])
        if cb >= 1:
            for _ in range(3):
                load_wd_piece()
        for sub in range(2):
            ffc = cb * 2 + sub
            for tg in range(4):
                bG = 2 * (j % 4)
                bU = bG + 1
                q = j % 2
                j += 1
                for k in range(KC):
                    P.op("pe", lambda e, k=k, s=s, sub=sub, tg=tg, bG=bG: e.matmul(
                        PS[:, 512 * bG:512 * (bG + 1)], lhsT=wg_s[s][:, k, sub * 128:(sub + 1) * 128],
                        rhs=hT[:, k, tg * 512:(tg + 1) * 512], start=(k == 0), stop=(k == KC - 1)),
                        reads=[wg_r[s][k // 4]], writes=[psr[bG]])
                for k in range(KC):
                    P.op("pe", lambda e, k=k, s=s, sub=sub, tg=tg, bU=bU: e.matmul(
                        PS[:, 512 * bU:512 * (bU + 1)], lhsT=wu_s[s][:, k, sub * 128:(sub + 1) * 128],
                        rhs=hT[:, k, tg * 512:(tg + 1) * 512], start=(k == 0), stop=(k == KC - 1)),
                        reads=[wu_r[s][k // 4]], writes=[psr[bU]])
                P.op("act", lambda e, q=q, bG=bG: e.activation(out=sg[q], in_=PS[:, 512 * bG:512 * (bG + 1)], func=AF.Silu),
                     reads=[psr[bG]], writes=[sg_r[q]])
                P.op("dve", lambda e, q=q, bU=bU, ffc=ffc, tg=tg: e.tensor_tensor(
                    out=actT[:, ffc, tg * 512:(tg + 1) * 512], in0=PS[:, 512 * bU:512 * (bU + 1)], in1=sg[q], op=ALU.mult),
                    reads=[psr[bU], sg_r[q]], writes=[])
    while wd_jobs:
        load_wd_piece()
    P.barrier()
    if stop_after == "B":
        ov = x_dst.rearrange("(a b) d -> a (b d)", a=1024).rearrange("(k p) t -> p k t", p=128)
        for k in range(KC):
            for hh in range(2):
                P.dma("pool", ov[:, k, hh * 1024:(hh + 1) * 1024], actT[:, k + 14, hh * 1024:(hh + 1) * 1024], "dbg")
        return

    Ah = hT_ar.child()
    NSC = int(os.environ.get('NSC', 2))
    NSX = int(os.environ.get('NSX', NSC))
    xc = [Ah.alloc((D,), F32) for _ in range(NSX)]
    tt = [Ah.alloc((D,), F32) for _ in range(NSC)]
    xc_r = [Reg() for _ in range(NSX)]
    tt_r = [Reg() for _ in range(NSC)]
    tth_r = [[Reg(), Reg()] for _ in range(NSC)]
    r2_r = [Reg() for _ in range(NT)]
    for i in range(int(os.environ.get("CT", NT))):
        s = i % NSC
        sx = i % NSX
        P.dma("sp", xc[sx], x_src[i * 128:(i + 1) * 128, :], f"xc{sx}", writes=[xc_r[sx]])
        pb = (i % 4) * 2
        psf = PS[:, 512 * pb:512 * (pb + 2)]
        for h in range(2):
            for ffc in range(FC):
                P.op("pe", lambda e, h=h, ffc=ffc, i=i, pb=pb: e.matmul(
                    PS[:, 512 * (pb + h):512 * (pb + h + 1)], lhsT=actT[:, ffc, i * 128:(i + 1) * 128],
                    rhs=wd_h[h][:, ffc, :], start=(ffc == 0), stop=(ffc == FC - 1)),
                    reads=[wd_r[h][ffc // 2]], writes=[psr[pb + h]])
        P.op("act", lambda e, i=i, psf=psf: e.activation(out=junk, in_=psf, func=AF.Square, accum_out=ss2[:, i:i + 1]),
             reads=[psr[pb], psr[pb + 1]], writes=[r2_r[i], junk_r])
        cstop = int(os.environ.get("CSTOP", 9))
        if cstop == 1:
            P.dma("sp", x_dst[i * 128:(i + 1) * 128, :], xc[sx], f"xo{s}", reads=[xc_r[sx], r2_r[i]])
            continue
        P.op("act", lambda e, i=i: e.activation(out=r2[:, i:i + 1], in_=ss2[:, i:i + 1], func=AF.Ln, scale=1.0 / D, bias=EPS),
             reads=[r2_r[i]], writes=[r2_r[i]])
        P.op("act", lambda e, i=i: e.activation(out=r2[:, i:i + 1], in_=r2[:, i:i + 1], func=AF.Exp, scale=-0.5,
                                                bias=math.log(0.5)),
             reads=[r2_r[i]], writes=[r2_r[i]])
        if cstop == 2:
            P.dma("sp", x_dst[i * 128:(i + 1) * 128, :], xc[sx], f"xo{s}", reads=[xc_r[sx], r2_r[i]])
            continue
        for h in range(2):
            if os.environ.get("DVEVAR") == "copy":
                P.op("dve", lambda e, s=s, h=h, pb=pb: e.tensor_copy(
                    out=tt[s][:, 512 * h:512 * (h + 1)], in_=PS[:, 512 * (pb + h):512 * (pb + h + 1)]),
                    reads=[psr[pb + h], gb_r], writes=[tth_r[s][h]])
                continue
            if os.environ.get("DVEVAR") == "sbuf":
                P.op("dve", lambda e, s=s, h=h, pb=pb: e.tensor_tensor(
                    out=tt[s][:, 512 * h:512 * (h + 1)], in0=xc[sx][:, 512 * h:512 * (h + 1)],
                    in1=gb[:, 512 * h:512 * (h + 1)], op=ALU.mult),
                    reads=[psr[pb + h], gb_r, xc_r[sx]], writes=[tth_r[s][h]])
                continue
            P.op("dve", lambda e, s=s, h=h, pb=pb: e.tensor_tensor(
                out=tt[s][:, 512 * h:512 * (h + 1)], in0=PS[:, 512 * (pb + h):512 * (pb + h + 1)],
                in1=gb[:, 512 * h:512 * (h + 1)], op=ALU.mult),
                reads=[psr[pb + h], gb_r, r2_r[i]], writes=[tth_r[s][h]])
        if cstop == 3:
            P.dma("sp", x_dst[i * 128:(i + 1) * 128, :], tt[s], f"xo{s}", reads=[xc_r[sx], r2_r[i], tth_r[s][0], tth_r[s][1]], writes=[tth_r[s][0], tth_r[s][1]])
            continue
        P.op("dve", lambda e, s=s, sx=sx, i=i: e.scalar_tensor_tensor(out=xc[sx], in0=tt[s], scalar=r2[:, i:i + 1], in1=xc[sx],
                                                              op0=ALU.mult, op1=ALU.add),
             reads=[tth_r[s][0], tth_r[s][1], r2_r[i], xc_r[sx]], writes=[xc_r[sx]])
        P.dma("sp", x_dst[i * 128:(i + 1) * 128, :], xc[sx], f"xo{sx}", reads=[xc_r[sx]])
    P.barrier()


def tok_slice(start, step):
    return slice(start, start + step * 127 + 1, step) if step > 1 else slice(start, start + 128)


def attention_phase(C, A, hT, attT, w_in):
    P, PS, psr = C.P, C.PS, C.psr
    cos = A.alloc((S,), F32)[:32]
    sin = A.alloc((S,), F32)[:32]
    cs_r = Reg()
    P.dma("sp", cos, C.dram["c_cos"], "cst", writes=[cs_r])
    P.dma("sp", sin, C.dram["c_sin"], "cst", writes=[cs_r])
    wsl = [[A.alloc((KC, 128), BF16) for _ in range(3)] for _ in range(2)]
    wsl_r = [[Reg() for _ in range(3)] for _ in range(2)]
    DBG.clear()
    qk = [[None, None], [None, None]]
    for a_ in range(2):
        for b_ in range(2):
            qk[a_][b_] = A.alloc((S,), BF16)
            DBG[f"qk{a_}{b_}"] = Arena.last
    qk_r = [[[Reg() for _ in range(4)] for _ in range(2)] for _ in range(2)]
    vt = [A.alloc((NT, 128), BF16) for _ in range(2)]
    vt_r = [[Reg() for _ in range(4)] for _ in range(2)]
    acc_n = A.alloc((S,), F32)
    acc_d = A.alloc((S,), F32)
    accn_r = [Reg() for _ in range(4)]
    accd_r = [Reg() for _ in range(4)]
    acc_all = Reg()
    pT = [A.alloc((512,), BF16) for _ in range(4)]
    pT_r = [Reg() for _ in range(4)]
    t1 = [A.alloc((512,), F32)[:32] for _ in range(2)]
    t2 = [A.alloc((512,), F32)[:32] for _ in range(2)]
    t1_r = [Reg() for _ in range(2)]
    t2_r = [Reg() for _ in range(2)]
    rcp = [A.alloc((512,), F32) for _ in range(2)]
    rcp_r = [Reg() for _ in range(2)]
    SCALE = 128.0 ** -0.5

    def bank(b):
        return PS[:, 512 * b:512 * (b + 1)]

    cnt = {"proj": 0, "rot": 0, "s": 0, "p": 0, "nd": 0, "t": 0, "it": 0}
    for hs in range(4):
        for g in range(3):
            sl = cnt["it"] % 2
            cnt["it"] += 1
            head = g * 4 + hs
            dil = (1, 4, 16)[g]
            for m, off in enumerate((0, 1536, 3072)):
                c0 = off + head * 128
                load_cast(C, wsl[sl][m], w_in[:, c0:c0 + 128].rearrange("(k p) n -> p k n", p=128), wsl_r[sl][m])
            for m in range(2):
                dst = qk[sl][m]
                for tg in range(4):
                    pb = cnt["proj"] % 2
                    cnt["proj"] += 1
                    for k in range(KC):
                        P.op("pe", lambda e, k=k, m=m, tg=tg, pb=pb, sl=sl: e.matmul(
                            bank(pb), lhsT=wsl[sl][m][:, k, :], rhs=hT[:, k, tg * 512:(tg + 1) * 512],
                            start=(k == 0), stop=(k == KC - 1)), reads=[wsl_r[sl][m]], writes=[psr[pb]])
                    dcol = dst[:, tg * 512:(tg + 1) * 512]
                    dr = qk_r[sl][m][tg]
                    P.op("act", lambda e, dcol=dcol, pb=pb: e.activation(out=dcol, in_=bank(pb), func=AF.Copy),
                         reads=[psr[pb]], writes=[dr])
                    rb = 2 + cnt["rot"] % 2
                    ts = cnt["rot"] % 2
                    cnt["rot"] += 1
                    P.op("pe", lambda e, dcol=dcol, rb=rb: e.matmul(bank(rb)[:32, :], lhsT=C.rm, rhs=dcol[:32, :],
                                                                   start=True, stop=True),
                         reads=[dr, C.cst_r], writes=[psr[rb]])
                    P.op("dve", lambda e, rb=rb, ts=ts, tg=tg: e.tensor_tensor(
                        out=t1[ts], in0=bank(rb)[:32, :], in1=sin[:, tg * 512:(tg + 1) * 512], op=ALU.mult),
                        reads=[psr[rb], cs_r], writes=[t1_r[ts]])
                    P.op("dve", lambda e, dcol=dcol, ts=ts, tg=tg: e.tensor_tensor(
                        out=t2[ts], in0=dcol[:32, :], in1=cos[:, tg * 512:(tg + 1) * 512], op=ALU.mult),
                        reads=[dr, cs_r], writes=[t2_r[ts]])
                    P.op("dve", lambda e, dcol=dcol, ts=ts: e.tensor_tensor(out=dcol[:32, :], in0=t1[ts], in1=t2[ts], op=ALU.add),
                         reads=[t1_r[ts], t2_r[ts]], writes=[dr])
            blocks = []
            if g == 0:
                for b in range(16):
                    blocks.append((tok_slice(128 * b, 1), tok_slice(128 * (b - 1), 1) if b > 0 else None))
            elif g == 1:
                for r in range(4):
                    for b in range(4):
                        blocks.append((tok_slice(4 * 128 * b + r, 4), tok_slice(4 * 128 * (b - 1) + r, 4) if b > 0 else None))
            else:
                for r in range(16):
                    blocks.append((tok_slice(r, 16), None))
            for j in range(4):
                pb = cnt["proj"] % 2
                cnt["proj"] += 1
                for bi in range(4):
                    qs = blocks[4 * j + bi][0]
                    for k in range(KC):
                        P.op("pe", lambda e, k=k, qs=qs, pb=pb, bi=bi, sl=sl: e.matmul(
                            bank(pb)[:, bi * 128:(bi + 1) * 128], lhsT=hT[:, k, qs], rhs=wsl[sl][2][:, k, :],
                            start=(k == 0), stop=(k == KC - 1)), reads=[wsl_r[sl][2]], writes=[psr[pb]])
                P.op("act", lambda e, j=j, pb=pb, sl=sl: e.activation(
                    out=vt[sl][:, 4 * j:4 * j + 4, :], in_=bank(pb).rearrange("p (a b) -> p a b", a=4), func=AF.Copy),
                    reads=[psr[pb]], writes=[vt_r[sl][j]])
            qT, kT = qk[sl][0], qk[sl][1]
            qr = qk_r[sl][0] + qk_r[sl][1]
            pairs = [(2 * i, 2 * i + 1) for i in range(8)]

            def do_qk(pair):
                sb = 4 + cnt["s"] % 2
                cnt["s"] += 1
                for ti, blk in enumerate(pair):
                    qs, ps_ = blocks[blk]
                    for ci, ks in enumerate((ps_, qs)):
                        o = bank(sb)[:, (2 * ti + ci) * 128:(2 * ti + ci + 1) * 128]
                        if ks is None:
                            P.op("pe", lambda e, o=o: e.matmul(o, lhsT=C.ident, rhs=C.maskn, start=True, stop=True),
                                 reads=[C.cst_r], writes=[psr[sb]])
                            continue
                        P.op("pe", lambda e, o=o, ks=ks, qs=qs, kT=kT, qT=qT: e.matmul(o, lhsT=kT[:, ks], rhs=qT[:, qs], start=True, stop=False),
                             reads=qr, writes=[psr[sb]])
                        mk = C.maskc if ci == 1 else C.maskp
                        P.op("pe", lambda e, o=o, mk=mk: e.matmul(o, lhsT=C.ident, rhs=mk, start=False, stop=True),
                             reads=[C.cst_r], writes=[psr[sb]])
                pslot = cnt["p"] % 4
                cnt["p"] += 1
                P.op("act", lambda e, sb=sb, pslot=pslot: e.activation(out=pT[pslot], in_=bank(sb), func=AF.Exp, scale=SCALE),
                     reads=[psr[sb]], writes=[pT_r[pslot]])
                return pslot

            def do_pv(pair, pslot):
                for ti, blk in enumerate(pair):
                    qs, ps_ = blocks[blk]
                    col = (blk % 4) * 128
                    for (bnk, is_num) in ((6, True), (7, False)):
                        o = bank(bnk)[:, col:col + 128]
                        for ci in range(2):
                            kblk = blk if (ci == 1 or ps_ is None) else blk - 1
                            lhs = vt[sl][:, kblk, :] if is_num else C.ones_bf
                            P.op("pe", lambda e, o=o, lhs=lhs, pslot=pslot, ti=ti, ci=ci: e.matmul(
                                o, lhsT=lhs, rhs=pT[pslot][:, (2 * ti + ci) * 128:(2 * ti + ci + 1) * 128],
                                start=(ci == 0), stop=(ci == 1)),
                                reads=[pT_r[pslot], vt_r[sl][kblk // 4], C.cst_r], writes=[psr[bnk]])
                if pair[1] % 4 == 3:
                    j = pair[1] // 4
                    if g == 0:
                        dn = acc_n[:, 512 * j:512 * (j + 1)]
                        dd = acc_d[:, 512 * j:512 * (j + 1)]
                        sn, sd = bank(6), bank(7)
                    elif g == 1:
                        dn = acc_n[:, j::4]
                        dd = acc_d[:, j::4]
                        sn, sd = bank(6), bank(7)
                    else:
                        dn = acc_n.rearrange("p (n r) -> p r n", r=16)[:, 4 * j:4 * j + 4, :]
                        dd = acc_d.rearrange("p (n r) -> p r n", r=16)[:, 4 * j:4 * j + 4, :]
                        sn = bank(6).rearrange("p (a b) -> p a b", a=4)
                        sd = bank(7).rearrange("p (a b) -> p a b", a=4)
                    if g == 0:
                        P.op("dve", lambda e, dn=dn, sn=sn: e.tensor_copy(out=dn, in_=sn), reads=[psr[6]], writes=[acc_all])
                        P.op("dve", lambda e, dd=dd, sd=sd: e.tensor_copy(out=dd, in_=sd), reads=[psr[7]], writes=[acc_all])
                    else:
                        P.op("dve", lambda e, dn=dn, sn=sn: e.tensor_tensor(out=dn, in0=sn, in1=dn, op=ALU.add),
                             reads=[psr[6], acc_all], writes=[acc_all])
                        P.op("dve", lambda e, dd=dd, sd=sd: e.tensor_tensor(out=dd, in0=sd, in1=dd, op=ALU.add),
                             reads=[psr[7], acc_all], writes=[acc_all])

            prev = None
            for pair in pairs:
                pslot = do_qk(pair)
                if prev is not None:
                    do_pv(*prev)
                prev = (pair, pslot)
            do_pv(*prev)
        for tg in range(4):
            rs = tg % 2
            P.op("dve", lambda e, tg=tg, rs=rs: e.reciprocal(out=rcp[rs], in_=acc_d[:, tg * 512:(tg + 1) * 512]),
                 reads=[acc_all], writes=[rcp_r[rs]])
            P.op("dve", lambda e, tg=tg, rs=rs, hs=hs: e.tensor_tensor(
                out=attT[:, hs, tg * 512:(tg + 1) * 512], in0=acc_n[:, tg * 512:(tg + 1) * 512], in1=rcp[rs], op=ALU.mult),
                reads=[acc_all, rcp_r[rs]], writes=[C.attT_r])


def mlstm_phase(C, A, hT, mlT, w_in, conv_w, conv_b, i_bias, f_bias, head_g):
    P, PS, psr = C.P, C.PS, C.psr
    OQ, OK_, OV, OO, OI = 4608, 5632, 6656, 7680, 8704

    def bank(b):
        return PS[:, 512 * b:512 * (b + 1)]

    def R():
        return Reg()

    cwb = A.alloc((2048,), F32)[:5]
    cwb_r = R()
    P.dma("sp", cwb[0:4, :], conv_w, "mc", writes=[cwb_r])
    P.dma("sp", cwb[4:5, :], conv_b.rearrange("(o n) -> o n", o=1), "mc", writes=[cwb_r])
    cwT = A.alloc((16, 8), F32)
    ncb = A.alloc((16,), F32)
    cwT_r = R()
    for c in range(16):
        P.op("pe", lambda e, c=c: e.matmul(bank(6)[:, c * 8:c * 8 + 5], lhsT=cwb[:, c * 128:(c + 1) * 128],
                                           rhs=C.identf[:5, :5], start=True, stop=True),
             reads=[cwb_r, C.cst_r], writes=[psr[6]])
    P.op("dve", lambda e: e.tensor_copy(out=cwT[:, :, 0:5], in_=bank(6)[:, 0:128].rearrange("p (c j) -> p c j", j=8)[:, :, 0:5]),
         reads=[psr[6]], writes=[cwT_r])
    P.op("dve", lambda e: e.tensor_scalar(out=ncb, in0=cwT[:, :, 4], scalar1=-1.0, scalar2=None, op0=ALU.mult),
         reads=[cwT_r], writes=[cwT_r])
    hg = A.alloc((1024,), F32)
    hg_r = R()
    P.dma("sp", hg, head_g.partition_broadcast(128), "mc", writes=[hg_r])
    bias8 = A.alloc((8,), F32)
    b8_r = R()
    P.dma("sp", bias8[:, 0:4], i_bias.partition_broadcast(128), "mc", writes=[b8_r])
    P.dma("sp", bias8[:, 4:8], f_bias.partition_broadcast(128), "mc", writes=[b8_r])

    wif = A.alloc((KC, 8), BF16)
    wif_r = R()
    load_cast(C, wif, w_in[:, OI:OI + 8].rearrange("(k p) n -> p k n", p=128), wif_r)
    for c in range(NT):
        for k in range(KC):
            P.op("pe", lambda e, c=c, k=k: e.matmul(bank(7)[:, c * 8:(c + 1) * 8], lhsT=hT[:, k, c * 128:(c + 1) * 128],
                                                    rhs=wif[:, k, :], start=(k == 0), stop=(k == KC - 1)),
                 reads=[wif_r], writes=[psr[7]])
    gi = A.alloc((NT, 4), F32)
    lg = A.alloc((NT, 4), F32)
    g_r = R()
    pre3 = bank(7)[:, 0:128].rearrange("p (c j) -> p c j", j=8)
    P.op("dve", lambda e: e.tensor_tensor(out=gi, in0=pre3[:, :, 0:4],
                                          in1=bias8[:, 0:4].unsqueeze(1).to_broadcast([128, NT, 4]), op=ALU.add),
         reads=[psr[7], b8_r], writes=[g_r])
    P.op("dve", lambda e: e.tensor_tensor(out=lg, in0=pre3[:, :, 4:8],
                                          in1=bias8[:, 4:8].unsqueeze(1).to_broadcast([128, NT, 4]), op=ALU.add),
         reads=[psr[7], b8_r, g_r], writes=[g_r])
    P.op("act", lambda e: e.activation(out=lg, in_=lg, func=AF.Exp, scale=-1.0), reads=[g_r], writes=[g_r])
    P.op("act", lambda e: e.activation(out=lg, in_=lg, func=AF.Ln, bias=1.0), reads=[g_r], writes=[g_r])
    lg2 = lg.rearrange("p c h -> p (c h)")
    gi2 = gi.rearrange("p c h -> p (c h)")
    P.op("pe", lambda e: e.matmul(bank(6)[:, 0:64], lhsT=C.tri, rhs=lg2, start=True, stop=True),
         reads=[g_r, C.cst_r], writes=[psr[6]])
    P.op("pe", lambda e: e.matmul(bank(6)[:, 64:128], lhsT=C.ones_f, rhs=lg2, start=True, stop=True),
         reads=[g_r, C.cst_r], writes=[psr[6]])
    e_in = A.alloc((64,), F32)
    e_out = A.alloc((64,), F32)
    e_L = A.alloc((64,), F32)
    e_v = A.alloc((64,), F32)
    ee_r = R()
    P.op("dve", lambda e: e.tensor_tensor(out=e_in, in0=bank(6)[:, 0:64], in1=gi2, op=ALU.add),
         reads=[psr[6], g_r], writes=[ee_r])
    P.op("act", lambda e: e.activation(out=e_in, in_=e_in, func=AF.Exp), reads=[ee_r], writes=[ee_r])
    P.op("act", lambda e: e.activation(out=e_out, in_=bank(6)[:, 0:64], func=AF.Exp, scale=-1.0),
         reads=[psr[6], ee_r], writes=[ee_r])
    P.op("act", lambda e: e.activation(out=e_L, in_=bank(6)[:, 64:128], func=AF.Exp, scale=-1.0),
         reads=[psr[6], ee_r], writes=[ee_r])
    P.op("dve", lambda e: e.tensor_scalar(out=e_in, in0=e_in, scalar1=1.0 / 16.0, scalar2=None, op0=ALU.mult),
         reads=[ee_r], writes=[ee_r])
    P.op("dve", lambda e: e.tensor_tensor(out=e_v, in0=e_in, in1=e_L, op=ALU.mult), reads=[ee_r], writes=[ee_r])

    wq = [A.alloc((KC, 256), BF16) for _ in range(4)]
    wq_r = [[R(), R()] for _ in range(4)]
    raw = [A.alloc((S + 3,), F32) for _ in range(2)]
    raw_r = [R() for _ in range(2)]
    acc = [A.alloc((S,), F32) for _ in range(2)]
    acc_r = [R() for _ in range(2)]
    for i in range(2):
        P.op("dve", lambda e, i=i: e.memset(raw[i][:, 0:3], 0.0), writes=[raw_r[i]])
    qT = A.alloc((2, S), BF16)
    kT = A.alloc((2, S), BF16)
    qk_r = [[R(), R()], [R(), R()]]
    ktok = A.alloc((256,), BF16)
    ktok_r = R()
    vaug = A.alloc((258,), BF16)
    vaug_r = R()
    P.op("dve", lambda e: e.memset(vaug[:, 256:257], 1.0), writes=[vaug_r])
    P.op("dve", lambda e: e.memset(vaug[:, 257:258], 0.0), writes=[vaug_r])
    vp = A.alloc((258,), BF16)
    vp_r = R()
    eo = A.alloc((256,), F32)
    eo_r = R()
    wT = A.alloc((128,), BF16)
    wT_r = R()
    Cf = A.alloc((2, 258), F32)
    Cf_r = R()
    Cbf = A.alloc((2, 258), BF16)
    Cbf_r = R()
    hu = A.alloc((256,), F32)
    hu_r = R()
    sm = A.alloc((8,), F32)
    sm_r = R()
    junk = A.alloc((256,), BF16)
    mlb = A.alloc((256,), BF16)
    mlb_r = R()
    cj = 0
    for hd in range(4):
        for m, off in enumerate((OQ, OK_, OV, OO)):
            c0 = off + hd * 256
            for part in range(2):
                load_cast(C, wq[m][:, 4 * part:4 * part + 4, :],
                          w_in[part * 512:(part + 1) * 512, c0:c0 + 256].rearrange("(k p) n -> p k n", p=128), wq_r[m][part])
        for m, dstT in ((0, qT), (1, kT)):
            for cc in range(2):
                ch = m * 8 + hd * 2 + cc
                bi = cj % 2
                cj += 1
                for tg in range(4):
                    pb = tg % 2
                    for k in range(KC):
                        P.op("pe", lambda e, k=k, m=m, cc=cc, tg=tg, pb=pb: e.matmul(
                            bank(pb), lhsT=wq[m][:, k, cc * 128:(cc + 1) * 128], rhs=hT[:, k, tg * 512:(tg + 1) * 512],
                            start=(k == 0), stop=(k == KC - 1)), reads=[wq_r[m][k // 4]], writes=[psr[pb]])
                    P.op("act", lambda e, bi=bi, tg=tg, pb=pb: e.activation(
                        out=raw[bi][:, 3 + tg * 512:3 + (tg + 1) * 512], in_=bank(pb), func=AF.Copy),
                        reads=[psr[pb]], writes=[raw_r[bi]])
                P.op("dve", lambda e, bi=bi, ch=ch: e.tensor_scalar(out=acc[bi], in0=raw[bi][:, 3:3 + S], scalar1=cwT[:, ch, 3:4],
                                                                     scalar2=None, op0=ALU.mult),
                     reads=[raw_r[bi], cwT_r], writes=[acc_r[bi]])
                for j in (2, 1, 0):
                    P.op("dve", lambda e, bi=bi, ch=ch, j=j: e.scalar_tensor_tensor(
                        out=acc[bi], in0=raw[bi][:, j:j + S], scalar=cwT[:, ch, j:j + 1], in1=acc[bi], op0=ALU.mult, op1=ALU.add),
                        reads=[raw_r[bi], cwT_r, acc_r[bi]], writes=[acc_r[bi]])
                ex = raw[bi][:, 3:3 + S]
                P.op("act", lambda e, ex=ex, bi=bi, ch=ch: e.activation(out=ex, in_=acc[bi], func=AF.Exp, scale=-1.0,
                                                                        bias=ncb[:, ch:ch + 1]),
                     reads=[acc_r[bi], cwT_r], writes=[raw_r[bi]])
                P.op("dve", lambda e, ex=ex: e.tensor_scalar(out=ex, in0=ex, scalar1=1.0, scalar2=None, op0=ALU.add),
                     reads=[raw_r[bi]], writes=[raw_r[bi]])
                P.op("dve", lambda e, ex=ex: e.reciprocal(out=ex, in_=ex), reads=[raw_r[bi]], writes=[raw_r[bi]])
                P.op("dve", lambda e, ex=ex, bi=bi, ch=ch, dstT=dstT, cc=cc: e.scalar_tensor_tensor(
                    out=dstT[:, cc, :], in0=acc[bi], scalar=cwT[:, ch, 4:5], in1=ex, op0=ALU.add, op1=ALU.mult),
                    reads=[acc_r[bi], raw_r[bi], cwT_r], writes=[qk_r[m][cc]])
        qkr = [qk_r[0][0], qk_r[0][1], qk_r[1][0], qk_r[1][1]]
        for c in range(NT):
            cs = slice(c * 128, (c + 1) * 128)
            col = c * 4 + hd
            kb = bank(2).bitcast(BF16)
            for dk in range(2):
                P.op("pe", lambda e, dk=dk, cs=cs, kb=kb: e.transpose(out=kb[:, dk * 128:(dk + 1) * 128], in_=kT[:, dk, cs],
                                                                      identity=C.ident),
                     reads=[qk_r[1][dk], C.cst_r], writes=[psr[2]])
            P.op("act", lambda e, kb=kb: e.activation(out=ktok, in_=kb[:, 0:256], func=AF.Copy),
                 reads=[psr[2]], writes=[ktok_r])
            for k in range(KC):
                P.op("pe", lambda e, k=k, cs=cs: e.matmul(bank(0)[:, 0:256], lhsT=hT[:, k, cs], rhs=wq[2][:, k, :],
                                                          start=(k == 0), stop=(k == KC - 1)),
                     reads=[wq_r[2][k // 4]], writes=[psr[0]])
            P.op("act", lambda e: e.activation(out=vaug[:, 0:256], in_=bank(0)[:, 0:256], func=AF.Copy),
                 reads=[psr[0]], writes=[vaug_r])
            for k in range(KC):
                P.op("pe", lambda e, k=k, cs=cs: e.matmul(bank(1)[:, 0:256], lhsT=hT[:, k, cs], rhs=wq[3][:, k, :],
                                                          start=(k == 0), stop=(k == KC - 1)),
                     reads=[wq_r[3][k // 4]], writes=[psr[1]])
            P.op("act", lambda e: e.activation(out=eo, in_=bank(1)[:, 0:256], func=AF.Exp, scale=-1.0),
                 reads=[psr[1]], writes=[eo_r])
            P.op("dve", lambda e: e.tensor_scalar(out=eo, in0=eo, scalar1=1.0, scalar2=None, op0=ALU.add),
                 reads=[eo_r], writes=[eo_r])
            P.op("dve", lambda e: e.reciprocal(out=eo, in_=eo), reads=[eo_r], writes=[eo_r])
            for dk in range(2):
                P.op("pe", lambda e, dk=dk, cs=cs: e.matmul(bank(3)[:, 0:128], lhsT=kT[:, dk, cs], rhs=qT[:, dk, cs],
                                                            start=(dk == 0), stop=(dk == 1)),
                     reads=qkr, writes=[psr[3]])
            P.op("dve", lambda e, col=col: e.scalar_tensor_tensor(out=wT, in0=bank(3)[:, 0:128], scalar=e_in[:, col:col + 1],
                                                                  in1=C.tri, op0=ALU.mult, op1=ALU.mult),
                 reads=[psr[3], ee_r, C.cst_r], writes=[wT_r])
            P.op("pe", lambda e, c=c: e.matmul(bank(4)[:, 0:258], lhsT=wT, rhs=vaug, start=True, stop=(c == 0)),
                 reads=[wT_r, vaug_r], writes=[psr[4]])
            if c > 0:
                for dk in range(2):
                    P.op("pe", lambda e, dk=dk, cs=cs: e.matmul(bank(4)[:, 0:258], lhsT=qT[:, dk, cs], rhs=Cbf[:, dk, :],
                                                                start=False, stop=(dk == 1)),
                         reads=qkr + [Cbf_r], writes=[psr[4]])
            if c < NT - 1:
                P.op("dve", lambda e, col=col: e.tensor_scalar(out=vp, in0=vaug, scalar1=e_v[:, col:col + 1], scalar2=None,
                                                               op0=ALU.mult),
                     reads=[vaug_r, ee_r], writes=[vp_r])
                for dk in range(2):
                    P.op("pe", lambda e, dk=dk: e.matmul(bank(5 + dk)[:, 0:258], lhsT=ktok[:, dk * 128:(dk + 1) * 128], rhs=vp,
                                                         start=True, stop=True),
                         reads=[ktok_r, vp_r], writes=[psr[5 + dk]])
                    if c == 0:
                        P.op("dve", lambda e, dk=dk: e.tensor_copy(out=Cf[:, dk, :], in_=bank(5 + dk)[:, 0:258]),
                             reads=[psr[5 + dk]], writes=[Cf_r])
                    else:
                        P.op("dve", lambda e, dk=dk, col=col: e.scalar_tensor_tensor(
                            out=Cf[:, dk, :], in0=Cf[:, dk, :], scalar=e_L[:, col:col + 1], in1=bank(5 + dk)[:, 0:258],
                            op0=ALU.mult, op1=ALU.add), reads=[psr[5 + dk], Cf_r, ee_r], writes=[Cf_r])
                P.op("act", lambda e: e.activation(out=Cbf, in_=Cf, func=AF.Copy), reads=[Cf_r], writes=[Cbf_r])
            P.op("dve", lambda e, col=col: e.tensor_tensor(out=sm[:, 0:1], in0=bank(4)[:, 256:257], in1=e_out[:, col:col + 1],
                                                           op=ALU.mult), reads=[psr[4], ee_r], writes=[sm_r])
            P.op("dve", lambda e: e.scalar_tensor_tensor(out=sm[:, 1:2], in0=sm[:, 0:1], scalar=-1.0, in1=sm[:, 0:1],
                                                         op0=ALU.mult, op1=ALU.max), reads=[sm_r], writes=[sm_r])
            P.op("dve", lambda e: e.tensor_scalar(out=sm[:, 1:2], in0=sm[:, 1:2], scalar1=1.0, scalar2=None, op0=ALU.max),
                 reads=[sm_r], writes=[sm_r])
            P.op("dve", lambda e: e.reciprocal(out=sm[:, 2:3], in_=sm[:, 1:2]), reads=[sm_r], writes=[sm_r])
            P.op("dve", lambda e, col=col: e.tensor_tensor(out=sm[:, 3:4], in0=sm[:, 2:3], in1=e_out[:, col:col + 1], op=ALU.mult),
                 reads=[sm_r, ee_r], writes=[sm_r])
            P.op("dve", lambda e: e.tensor_scalar(out=hu, in0=bank(4)[:, 0:256], scalar1=sm[:, 3:4], scalar2=None, op0=ALU.mult),
                 reads=[psr[4], sm_r], writes=[hu_r])
            P.op("act", lambda e: e.activation(out=junk, in_=hu, func=AF.Square, accum_out=sm[:, 4:5]),
                 reads=[hu_r, sm_r], writes=[sm_r])
            P.op("act", lambda e: e.activation(out=sm[:, 5:6], in_=sm[:, 4:5], func=AF.Ln, scale=1.0 / 256, bias=EPS),
                 reads=[sm_r], writes=[sm_r])
            P.op("act", lambda e: e.activation(out=sm[:, 5:6], in_=sm[:, 5:6], func=AF.Exp, scale=-0.5),
                 reads=[sm_r], writes=[sm_r])
            P.op("dve", lambda e, hd=hd: e.scalar_tensor_tensor(out=hu, in0=hu, scalar=sm[:, 5:6], in1=hg[:, hd * 256:(hd + 1) * 256],
                                                                op0=ALU.mult, op1=ALU.mult),
                 reads=[hu_r, sm_r, hg_r], writes=[hu_r])
            P.op("dve", lambda e: e.tensor_tensor(out=mlb, in0=hu, in1=eo, op=ALU.mult), reads=[hu_r, eo_r], writes=[mlb_r])
            mb = bank(7).bitcast(BF16)
            for j in range(2):
                P.op("pe", lambda e, j=j, mb=mb: e.transpose(out=mb[:, j * 128:(j + 1) * 128], in_=mlb[:, j * 128:(j + 1) * 128],
                                                             identity=C.ident),
                     reads=[mlb_r, C.cst_r], writes=[psr[7]])
            P.op("act", lambda e, hd=hd, cs=cs, mb=mb: e.activation(
                out=mlT[:, 2 * hd:2 * hd + 2, cs], in_=mb[:, 0:256].rearrange("p (j n) -> p j n", j=2), func=AF.Copy),
                reads=[psr[7]], writes=[C.mlT_r])


def merge_phase(C, A, hT, attT, mlT, w_in, w_a, w_m, w_out, g_post, x_src, x_dst):
    P, PS, psr = C.P, C.PS, C.psr
    OGA, OGM = 8712, 9736

    def bank(b):
        return PS[:, 512 * b:512 * (b + 1)]

    mgT = A.alloc((KC, S), BF16)
    wa = A.alloc((4, 1024), BF16)
    wa_r = [Reg() for _ in range(4)]
    wm = A.alloc((KC, 1024), BF16)
    wm_r = [Reg() for _ in range(KC)]
    wo = A.alloc((KC, 1024), BF16)
    wo_r = [Reg() for _ in range(KC)]
    for k in range(4):
        load_cast(C, wa[:, k, :], w_a[k * 128:(k + 1) * 128, :], wa_r[k])
    for k in range(KC):
        load_cast(C, wm[:, k, :], w_m[k * 128:(k + 1) * 128, :], wm_r[k])
    wg = [[A.alloc((KC, 128), BF16) for _ in range(2)] for _ in range(2)]
    wg_r = [[Reg() for _ in range(2)] for _ in range(2)]
    ga = [A.alloc((512,), F32) for _ in range(2)]
    ga_r = [Reg() for _ in range(2)]
    gm = [A.alloc((512,), F32) for _ in range(2)]
    gm_r = [Reg() for _ in range(2)]
    ta = [A.alloc((512,), F32) for _ in range(2)]
    ta_r = [Reg() for _ in range(2)]
    gb = A.alloc((D,), F32)
    gb_r = Reg()
    P.dma("sp", gb, g_post.partition_broadcast(128), "g", writes=[gb_r])
    j = 0
    for mc in range(KC):
        sl = mc % 2
        load_cast(C, wg[sl][0], w_in[:, OGA + mc * 128:OGA + (mc + 1) * 128].rearrange("(k p) n -> p k n", p=128), wg_r[sl][0])
        load_cast(C, wg[sl][1], w_in[:, OGM + mc * 128:OGM + (mc + 1) * 128].rearrange("(k p) n -> p k n", p=128), wg_r[sl][1])
        if mc == 0:
            for k in range(KC):
                load_cast(C, wo[:, k, :], w_out[k * 128:(k + 1) * 128, :], wo_r[k])
        for tg in range(4):
            q = j % 2
            j += 1
            ts = slice(tg * 512, (tg + 1) * 512)
            b0 = 4 * q
            for gi_, (gbuf, gr) in enumerate(((ga, ga_r), (gm, gm_r))):
                for k in range(KC):
                    P.op("pe", lambda e, k=k, gi_=gi_, sl=sl, ts=ts, b0=b0: e.matmul(
                        bank(b0 + gi_), lhsT=wg[sl][gi_][:, k, :], rhs=hT[:, k, ts], start=(k == 0), stop=(k == KC - 1)),
                        reads=[wg_r[sl][gi_]], writes=[psr[b0 + gi_]])
                P.op("act", lambda e, gbuf=gbuf, q=q, gi_=gi_, b0=b0: e.activation(out=gbuf[q], in_=bank(b0 + gi_), func=AF.Sigmoid),
                     reads=[psr[b0 + gi_]], writes=[gr[q]])
            for k in range(4):
                P.op("pe", lambda e, k=k, mc=mc, ts=ts, b0=b0: e.matmul(
                    bank(b0 + 2), lhsT=wa[:, k, mc * 128:(mc + 1) * 128], rhs=attT[:, k, ts], start=(k == 0), stop=(k == 3)),
                    reads=[wa_r[k], C.attT_r], writes=[psr[b0 + 2]])
            for k in range(KC):
                P.op("pe", lambda e, k=k, mc=mc, ts=ts, b0=b0: e.matmul(
                    bank(b0 + 3), lhsT=wm[:, k, mc * 128:(mc + 1) * 128], rhs=mlT[:, k, ts], start=(k == 0), stop=(k == KC - 1)),
                    reads=[wm_r[k], C.mlT_r], writes=[psr[b0 + 3]])
            P.op("dve", lambda e, q=q, b0=b0: e.tensor_tensor(out=ta[q], in0=bank(b0 + 2), in1=ga[q], op=ALU.mult),
                 reads=[psr[b0 + 2], ga_r[q]], writes=[ta_r[q]])
            P.op("dve", lambda e, q=q, b0=b0: e.tensor_tensor(out=gm[q], in0=bank(b0 + 3), in1=gm[q], op=ALU.mult),
                 reads=[psr[b0 + 3], gm_r[q]], writes=[gm_r[q]])
            P.op("dve", lambda e, q=q, mc=mc, ts=ts: e.tensor_tensor(out=mgT[:, mc, ts], in0=ta[q], in1=gm[q], op=ALU.add),
                 reads=[ta_r[q], gm_r[q]], writes=[C.mg_r])
    xc = [A.alloc((D,), F32) for _ in range(2)]
    tt = [A.alloc((D,), F32) for _ in range(2)]
    xc_r = [Reg() for _ in range(2)]
    tth_r = [[Reg(), Reg()] for _ in range(2)]
    ss2 = A.alloc((NT,), F32)
    r2 = A.alloc((NT,), F32)
    junk = ga[0].bitcast(BF16)
    junk_r = ga_r[0]
    r2_r = [Reg() for _ in range(NT)]
    for i in range(NT):
        s = i % 2
        P.dma("sp", xc[s], x_src[i * 128:(i + 1) * 128, :], f"xc{s}", writes=[xc_r[s]])
        pb = (i % 4) * 2
        psf = PS[:, 512 * pb:512 * (pb + 2)]
        for h in range(2):
            for k in range(KC):
                P.op("pe", lambda e, h=h, k=k, i=i, pb=pb: e.matmul(
                    bank(pb + h), lhsT=mgT[:, k, i * 128:(i + 1) * 128], rhs=wo[:, k, h * 512:(h + 1) * 512],
                    start=(k == 0), stop=(k == KC - 1)), reads=[wo_r[k], C.mg_r], writes=[psr[pb + h]])
        P.op("act", lambda e, i=i, psf=psf: e.activation(out=junk, in_=psf, func=AF.Square, accum_out=ss2[:, i:i + 1]),
             reads=[psr[pb], psr[pb + 1]], writes=[r2_r[i], junk_r])
        P.op("act", lambda e, i=i: e.activation(out=r2[:, i:i + 1], in_=ss2[:, i:i + 1], func=AF.Ln, scale=1.0 / D, bias=EPS),
             reads=[r2_r[i]], writes=[r2_r[i]])
        P.op("act", lambda e, i=i: e.activation(out=r2[:, i:i + 1], in_=r2[:, i:i + 1], func=AF.Exp, scale=-0.5),
             reads=[r2_r[i]], writes=[r2_r[i]])
        for h in range(2):
            P.op("dve", lambda e, s=s, h=h, pb=pb: e.tensor_tensor(
                out=tt[s][:, 512 * h:512 * (h + 1)], in0=bank(pb + h), in1=gb[:, 512 * h:512 * (h + 1)], op=ALU.mult),
                reads=[psr[pb + h], gb_r, r2_r[i]], writes=[tth_r[s][h]])
        P.op("dve", lambda e, s=s, i=i: e.scalar_tensor_tensor(out=xc[s], in0=tt[s], scalar=r2[:, i:i + 1], in1=xc[s],
                                                              op0=ALU.mult, op1=ALU.add),
             reads=[tth_r[s][0], tth_r[s][1], r2_r[i], xc_r[s]], writes=[xc_r[s]])
        P.dma("sp", x_dst[i * 128:(i + 1) * 128, :], xc[s], f"xo{s}", reads=[xc_r[s]])


def mixer_phase(C, x_src, x_dst, W):
    A = C.A.child()
    hT = A.alloc((KC, S), BF16)
    attT = A.alloc((4, S), BF16)
    mlT = A.alloc((8, S), BF16)
    C.attT_r = Reg()
    C.mlT_r = Reg()
    C.mg_r = Reg()
    base = A.off
    end = A.end
    norm_transpose(C, Arena(A.ap, base, end), x_src, W["mix_pre_g"][0], hT)
    attention_phase(C, Arena(A.ap, base, end), hT, attT, W["w_in"][0])
    C.P.barrier()
    mlstm_phase(C, Arena(A.ap, base, end), hT, mlT, W["w_in"][0], W["conv_w"][0], W["conv_b"][0],
                W["mlstm_i_bias"][0], W["mlstm_f_bias"][0], W["mlstm_head_g"][0])
    C.P.barrier()
    merge_phase(C, Arena(A.ap, base, end), hT, attT, mlT, W["w_in"][0], W["w_att_branch"][0], W["w_mlstm_branch"][0],
                W["w_out"][0], W["mix_post_g"][0], x_src, x_dst)
    C.P.barrier()

def host_consts():
    c = {}
    bf = ml_dtypes.bfloat16
    c["ident"] = np.eye(128, dtype=np.float32).astype(bf)
    half = 16
    inv_freq = np.power(np.float32(500000.0), -(np.arange(half, dtype=np.float32) * 2.0 / 32)).astype(np.float32)
    ang = np.arange(S, dtype=np.float32)[None, :] * inv_freq[:, None]
    c["cos"] = np.concatenate([np.cos(ang), np.cos(ang)], 0).astype(np.float32)
    c["sin"] = np.concatenate([np.sin(ang), np.sin(ang)], 0).astype(np.float32)
    rm = np.zeros((32, 32), np.float32)
    for j in range(16):
        rm[16 + j, j] = -1.0
        rm[j, 16 + j] = 1.0
    c["rm"] = rm.astype(bf)
    jj = np.arange(128)[:, None]
    ii = np.arange(128)[None, :]
    NEG = -30000.0
    c["maskc"] = np.where(jj <= ii, 0.0, NEG).astype(bf)
    c["maskp"] = np.where(jj >= ii, 0.0, NEG).astype(bf)
    c["maskn"] = np.full((128, 128), NEG, np.float32).astype(bf)
    c["tri"] = (jj <= ii).astype(np.float32)
    c["ones_f"] = np.ones((128, 128), np.float32)
    c["identf"] = np.eye(128, dtype=np.float32)
    c["ones_bf"] = np.ones((128, 128), np.float32).astype(bf)
    return c


def build(stage="full"):
    nc = bass.Bass("TRN2", target_bir_lowering=False)

    def din(name, shape, dt=F32):
        return nc.dram_tensor(name, list(shape), dt, kind="ExternalInput").ap()

    x = din("x", [S, D])
    W = {}
    for name, shape in [("ffn1_pre_g", [1, D]), ("ffn1_w_gate", [1, D, FF]), ("ffn1_w_up", [1, D, FF]),
                        ("ffn1_w_down", [1, FF, D]), ("ffn1_post_g", [1, D]), ("mix_pre_g", [1, D]),
                        ("w_in", [1, D, IN_W]), ("conv_w", [1, 4, 2048]), ("conv_b", [1, 2048]),
                        ("mlstm_i_bias", [1, 4]), ("mlstm_f_bias", [1, 4]), ("mlstm_head_g", [1, 1024]),
                        ("w_att_branch", [1, 512, D]), ("w_mlstm_branch", [1, D, D]), ("w_out", [1, D, D]),
                        ("mix_post_g", [1, D]), ("ffn2_pre_g", [1, D]), ("ffn2_w_gate", [1, D, FF]),
                        ("ffn2_w_up", [1, D, FF]), ("ffn2_w_down", [1, FF, D]), ("ffn2_post_g", [1, D])]:
        W[name] = din(name, shape)
    CD = {}
    for name, shape, dt in [("c_ident", [128, 128], BF16), ("c_cos", [32, S], F32), ("c_sin", [32, S], F32),
                            ("c_rm", [32, 32], BF16), ("c_maskc", [128, 128], BF16), ("c_maskp", [128, 128], BF16),
                            ("c_maskn", [128, 128], BF16), ("c_tri", [128, 128], F32), ("c_ones_f", [128, 128], F32), ("c_identf", [128, 128], F32),
                            ("c_ones_bf", [128, 128], BF16)]:
        CD[name] = din(name, shape, dt)
    c_ident = CD["c_ident"]
    out = nc.dram_tensor("out", [S, D], F32, kind="ExternalOutput").ap()
    x1 = nc.dram_tensor("x1", [S, D], F32, kind="Internal").ap()
    x2 = nc.dram_tensor("x2", [S, D], F32, kind="Internal").ap()

    with ExitStack() as st:
        ARENA_BYTES = 212480
        arena = st.enter_context(nc.sbuf_tensor("arena", [128, ARENA_BYTES // 2], BF16))
        PS = st.enter_context(nc.psum_tensor("ps", [128, 4096], F32))
        C = Ctx()
        C.nc = nc
        C.P = P = Prog(nc)
        C.PS = PS
        C.psr = [Reg(f"ps{i}", excl=True) for i in range(8)]
        top = Arena(arena, 0, ARENA_BYTES)
        C.ident = top.alloc((128,), BF16)
        C.ident_r = Reg()
        P.dma("sp", C.ident, c_ident, "cst", writes=[C.ident_r])
        C.dram = CD
        C.cst_r = C.ident_r
        for nm, shp, dt in [("rm", (32,), BF16), ("maskc", (128,), BF16), ("maskp", (128,), BF16), ("maskn", (128,), BF16),
                            ("tri", (128,), F32), ("ones_f", (128,), F32), ("identf", (128,), F32), ("ones_bf", (128,), BF16)]:
            v = top.alloc(shp, dt)
            if nm == "rm":
                v = v[:32]
            setattr(C, nm, v)
            P.dma("sp", v, CD["c_" + nm], "cst", writes=[C.cst_r])
        C.stage = [top.alloc((1024,), F32) for _ in range(NST)]
        C.stage_r = [Reg() for _ in range(NST)]
        C.stage_i = 0
        C.A = Arena(arena, top.off, ARENA_BYTES)

        if stage == "attn":
            A = C.A.child()
            hT = A.alloc((KC, S), BF16)
            attT = A.alloc((4, S), BF16)
            C.attT_r = Reg()
            mk = Arena(arena, A.off, ARENA_BYTES)
            norm_transpose(C, mk, x, W["mix_pre_g"][0], hT)
            attention_phase(C, Arena(arena, A.off, ARENA_BYTES), hT, attT, W["w_in"][0])
            P.barrier()
            ov = out.rearrange("(a b) d -> a (b d)", a=1024).rearrange("(k p) t -> p k t", p=128)
            for k in range(4):
                for hh in range(2):
                    P.dma("pool", ov[:, k, hh * 1024:(hh + 1) * 1024], attT[:, k, hh * 1024:(hh + 1) * 1024], "dbg")
        if stage == "full":
            ffn_phase(C, x, x1, W["ffn1_pre_g"][0], W["ffn1_w_gate"][0], W["ffn1_w_up"][0], W["ffn1_w_down"][0],
                      W["ffn1_post_g"][0])
            mixer_phase(C, x1, x2, W)
            ffn_phase(C, x2, out, W["ffn2_pre_g"][0], W["ffn2_w_gate"][0], W["ffn2_w_up"][0], W["ffn2_w_down"][0],
                      W["ffn2_post_g"][0])
        if stage == "mix":
            mixer_phase(C, x, out, W)
        if stage == "ml":
            A = C.A.child()
            hT = A.alloc((KC, S), BF16)
            mlT = A.alloc((8, S), BF16)
            C.mlT_r = Reg()
            mk = Arena(arena, A.off, ARENA_BYTES)
            norm_transpose(C, mk, x, W["mix_pre_g"][0], hT)
            mlstm_phase(C, Arena(arena, A.off, ARENA_BYTES), hT, mlT, W["w_in"][0], W["conv_w"][0], W["conv_b"][0],
                        W["mlstm_i_bias"][0], W["mlstm_f_bias"][0], W["mlstm_head_g"][0])
            P.barrier()
            ov = out.rearrange("(a b) d -> a (b d)", a=1024).rearrange("(k p) t -> p k t", p=128)
            for k in range(8):
                for hh in range(2):
                    P.dma("pool", ov[:, k, hh * 1024:(hh + 1) * 1024], mlT[:, k, hh * 1024:(hh + 1) * 1024], "dbg")
        if stage == "ffn1a":
            A = C.A.child()
            hT_ar = A.sub(KC * S * 2)
            actT_ar = A.sub(FC * S * 2)
            hT = hT_ar.child().alloc((KC, S), BF16)
            norm_transpose(C, actT_ar.child(), x, W["ffn1_pre_g"][0], hT)
            ov = out.rearrange("(a b) d -> a (b d)", a=1024).rearrange("(k p) t -> p k t", p=128)
            for k in range(KC):
                for hh in range(2):
                    P.dma("pool", ov[:, k, hh * 1024:(hh + 1) * 1024], hT[:, k, hh * 1024:(hh + 1) * 1024], "dbg")
        if stage == "ffn1b":
            ffn_phase(C, x, out, W["ffn1_pre_g"][0], W["ffn1_w_gate"][0], W["ffn1_w_up"][0], W["ffn1_w_down"][0],
                      W["ffn1_post_g"][0], stop_after="B")
        if stage == "ffn1":
            ffn_phase(C, x, out, W["ffn1_pre_g"][0], W["ffn1_w_gate"][0], W["ffn1_w_up"][0], W["ffn1_w_down"][0],
                      W["ffn1_post_g"][0])
        P.finish()
        P.emit()
        print("prog stats", P.stats, "sems", len(P.dma_tot) + 5)
        if os.environ.get("DUMP"):
            for e in ("sp", "dve", "act"):
                print("====", e)
                for r in P.dump[e][-int(os.environ["DUMP"]):]:
                    print(r)
    return nc


_NC_CACHE = {}


def kernel(**inputs):
    stage = inputs.pop("_stage", "full")
    if stage not in _NC_CACHE:
        _NC_CACHE[stage] = build(stage)
    nc = _NC_CACHE[stage]
    consts = host_consts()
    xfull = np.ascontiguousarray(inputs["x"], dtype=np.float32)
    shared = {k: np.ascontiguousarray(v, dtype=np.float32) for k, v in inputs.items() if k != "x"}
    for k, v in consts.items():
        shared["c_" + k] = v
    in_maps = []
    import os
    ncores = int(os.environ.get("NCORES", 8))
    for b in range(ncores):
        m = dict(shared)
        m["x"] = xfull[b]
        in_maps.append(m)
    res = run_bass_kernel_spmd(nc, in_maps, core_ids=list(range(ncores)))
    return np.stack([r["out"] for r in res.results], axis=0)
```

```python
from contextlib import ExitStack
import math
import os

import numpy as np
import ml_dtypes
import concourse.bass as bass
import concourse.mybir as mybir
from concourse.bass_utils import run_bass_kernel_spmd

F32 = mybir.dt.float32
BF16 = mybir.dt.bfloat16
AF = mybir.ActivationFunctionType
ALU = mybir.AluOpType
AX = mybir.AxisListType

S = 2048
D = 1024
FF = 2816
NT = S // 128
KC = D // 128
FC = FF // 128
IN_W = 10760
EPS = 1e-6
ENGS = ("pe", "act", "dve", "pool", "sp")


class Reg:
    __slots__ = ("name", "w", "rs", "rd", "excl")

    def __init__(self, name="", excl=False):
        self.name = name
        self.excl = excl
        self.w = None
        self.rs = {}
        self.rd = []


class Ins:
    __slots__ = ("eng", "fn", "deps", "signal", "val", "dma", "key")

    def __init__(self, eng, fn, dma=False, key=None):
        self.eng = eng
        self.fn = fn
        self.deps = ()
        self.signal = dma
        self.val = 0
        self.dma = dma
        self.key = key


class Prog:
    def __init__(self, nc):
        self.nc = nc
        self.engs = {e: [] for e in ENGS}
        self.dma_tot = {}
        self.dma_last = {}

    def _add(self, ins, reads, writes):
        eng = ins.eng
        deps = {}
        for r in reads:
            d = r.w
            if d is not None:
                deps[id(d)] = d
            if r.excl:
                for e2, x in r.rs.items():
                    if e2 != eng:
                        deps[id(x)] = x
        for w in writes:
            d = w.w
            if d is not None:
                deps[id(d)] = d
            for e2, x in w.rs.items():
                if (not ins.dma) and e2 == eng:
                    continue
                deps[id(x)] = x
            for x in w.rd:
                deps[id(x)] = x
        out = []
        for d in deps.values():
            if d is ins:
                continue
            if (not d.dma) and (not ins.dma) and d.eng == "pe" and eng == "pe":
                continue
            d.signal = True
            out.append(d)
        ins.deps = out
        for r in reads:
            if ins.dma:
                r.rd.append(ins)
            else:
                r.rs[eng] = ins
        for w in writes:
            w.w = ins
            w.rs = {}
            w.rd = []
        self.engs[eng].append(ins)
        return ins

    def op(self, eng, fn, reads=(), writes=()):
        return self._add(Ins(eng, fn), reads, writes)

    def dma(self, queue, out, in_, key, reads=(), writes=(), **kw):
        ins = Ins(queue, lambda e: e.dma_start(out=out, in_=in_, **kw), dma=True, key=key)
        self.dma_tot[key] = self.dma_tot.get(key, 0) + 16
        ins.val = self.dma_tot[key]
        self.dma_last[key] = ins
        return self._add(ins, reads, writes)

    def barrier(self):
        lasts = []
        for e in ENGS:
            for ins in reversed(self.engs[e]):
                if ins.fn is not None and not ins.dma:
                    ins.signal = True
                    lasts.append(ins)
                    break
        lasts += list(self.dma_last.values())
        for e in ENGS:
            ins = Ins(e, None)
            ins.deps = [d for d in lasts if d.dma or d.eng != e]
            self.engs[e].append(ins)

    def finish(self):
        ins = Ins("sp", None)
        ins.deps = list(self.dma_last.values())
        self.engs["sp"].append(ins)

    def emit(self):
        nc = self.nc
        for e in ENGS:
            c = 0
            for ins in self.engs[e]:
                if ins.dma:
                    continue
                if ins.signal and ins.fn is not None:
                    c += 1
                    ins.val = c
        with ExitStack() as st:
            sems = {e: st.enter_context(nc.semaphore(f"s_{e}")) for e in ENGS}
            dsem = {k: st.enter_context(nc.semaphore(f"d_{k}")) for k in self.dma_tot}
            block = st.enter_context(nc.Block())
            bname = {"pe": "tensor", "act": "scalar", "dve": "vector", "pool": "gpsimd", "sp": "sync"}
            stats = {}
            self.dump = {}
            for e in ENGS:
                def body(engine, e=e):
                    seen = {}
                    nw = 0
                    for ins in self.engs[e]:
                        need = {}
                        for d in ins.deps:
                            s = ("d", d.key) if d.dma else ("c", d.eng)
                            if need.get(s, 0) < d.val:
                                need[s] = d.val
                        for s, v in need.items():
                            if seen.get(s, 0) < v:
                                seen[s] = v
                                sh = dsem[s[1]] if s[0] == "d" else sems[s[1]]
                                engine.wait_ge(sh, v)
                                nw += 1
                        if os.environ.get("DUMP"):
                            self.dump.setdefault(e, []).append((sorted((k, v) for k, v in need.items()), ins.fn is not None, ins.dma, ins.key, ins.signal, ins.val))
                        if ins.fn is not None:
                            bi = ins.fn(engine)
                            if ins.dma:
                                bi.then_inc(dsem[ins.key], 16)
                            elif ins.signal:
                                bi.then_inc(sems[e], 1)
                    stats[e] = (len(self.engs[e]), nw)
                getattr(block, bname[e])(body)
            self.stats = stats


class Arena:
    def __init__(self, ap, start, end):
        self.ap = ap
        self.start = start
        self.off = start
        self.end = end

    def alloc(self, free_shape, dt, parts=128):
        n = 1
        for v in free_shape:
            n *= v
        esz = 4 if dt == F32 else 2
        nbytes = n * esz
        st = (self.off + 63) // 64 * 64
        assert st + nbytes <= self.end, f"arena overflow: need {st + nbytes} > {self.end}"
        self.off = st + nbytes
        Arena.last = (st, tuple(free_shape), dt)
        v = self.ap[:parts, st // 2:(st + nbytes) // 2]
        if dt == F32:
            v = v.bitcast(F32)
        if len(free_shape) == 2:
            v = v.rearrange("p (a b) -> p a b", a=free_shape[0])
        elif len(free_shape) == 3:
            v = v.rearrange("p (a b c) -> p a b c", a=free_shape[0], b=free_shape[1])
        return v

    def sub(self, nbytes):
        st = (self.off + 63) // 64 * 64
        assert st + nbytes <= self.end, f"arena overflow(sub): need {st + nbytes} > {self.end}"
        self.off = st + nbytes
        return Arena(self.ap, st, st + nbytes)

    def child(self):
        return Arena(self.ap, self.start, self.end)


class Ctx:
    pass


DBG = {}


NST = 3


def load_cast(C, dst, src, dst_reg):
    P = C.P
    s = C.stage_i % NST
    C.stage_i += 1
    sh = src.shape
    n = 1
    for v in sh[1:]:
        n *= v
    assert n <= 1024
    stg = C.stage[s][:, :n]
    if len(sh) == 3:
        stg = stg.rearrange("p (a b) -> p a b", a=sh[1])
    P.dma("sp", stg, src, f"st{s}", writes=[C.stage_r[s]])
    P.op("pool", lambda e: e.tensor_copy(out=dst, in_=stg), reads=[C.stage_r[s]], writes=[dst_reg])


def norm_transpose(C, A_stage, x_src, g_pre_dram, hT):
    P, PS, psr = C.P, C.PS, C.psr
    gb = A_stage.alloc((D,), F32)
    gb_r = Reg()
    P.dma("sp", gb, g_pre_dram.partition_broadcast(128), "g", writes=[gb_r])
    ss = A_stage.alloc((NT,), F32)
    rstd = A_stage.alloc((NT,), F32)
    junk = A_stage.alloc((D,), BF16)
    junk_r = Reg()
    hb = [A_stage.alloc((D,), BF16) for _ in range(2)]
    hb_r = [Reg() for _ in range(2)]
    xs = [A_stage.alloc((D,), F32) for _ in range(NT)]
    xs_r = [Reg() for _ in range(NT)]
    ss_r = [Reg() for _ in range(NT)]
    rstd_r = Reg()
    for i in range(NT):
        P.dma("sp", xs[i], x_src[i * 128:(i + 1) * 128, :], f"xs{i}", writes=[xs_r[i]])
    for i in range(NT):
        P.op("act", lambda e, i=i: e.activation(out=junk, in_=xs[i], func=AF.Square, accum_out=ss[:, i:i + 1]),
             reads=[xs_r[i]], writes=[ss_r[i], junk_r])
    P.op("act", lambda e: e.activation(out=rstd, in_=ss, func=AF.Ln, scale=1.0 / D, bias=EPS),
         reads=ss_r, writes=[rstd_r])
    P.op("act", lambda e: e.activation(out=rstd, in_=rstd, func=AF.Exp, scale=-0.5),
         reads=[rstd_r], writes=[rstd_r])
    for i in range(NT):
        s = i % 2
        P.op("dve", lambda e, i=i, s=s: e.scalar_tensor_tensor(out=hb[s], in0=xs[i], scalar=rstd[:, i:i + 1], in1=gb,
                                                              op0=ALU.mult, op1=ALU.mult),
             reads=[xs_r[i], rstd_r, gb_r], writes=[hb_r[s]])
        b = i % 2
        psb = PS[:, 512 * b:512 * (b + 1)].bitcast(BF16)
        for k in range(KC):
            P.op("pe", lambda e, k=k, s=s, psb=psb: e.transpose(out=psb[:, k * 128:(k + 1) * 128],
                                                                in_=hb[s][:, k * 128:(k + 1) * 128], identity=C.ident),
                 reads=[hb_r[s], C.ident_r], writes=[psr[b]])
        P.op("act", lambda e, i=i, psb=psb: e.activation(out=hT[:, :, i * 128:(i + 1) * 128],
                                                         in_=psb.rearrange("p (k n) -> p k n", k=KC), func=AF.Copy),
             reads=[psr[b]], writes=[])
    P.barrier()


def ffn_phase(C, x_src, x_dst, g_pre, wg, wu, wd, g_post, stop_after=None):
    P, PS, psr = C.P, C.PS, C.psr
    A = C.A.child()
    hT_ar = A.sub(KC * S * 2)
    actT_ar = A.sub(FC * S * 2)
    hT = hT_ar.child().alloc((KC, S), BF16)
    actT = actT_ar.child().alloc((FC, S), BF16)
    norm_transpose(C, actT_ar.child(), x_src, g_pre, hT)

    NSL = 2
    wg_s = [A.alloc((KC, 256), BF16) for _ in range(NSL)]
    wu_s = [A.alloc((KC, 256), BF16) for _ in range(NSL)]
    wg_r = [[Reg(), Reg()] for _ in range(NSL)]
    wu_r = [[Reg(), Reg()] for _ in range(NSL)]
    wd_h = [A.alloc((FC, 512), BF16) for _ in range(2)]
    wd_r = [[Reg() for _ in range(FC // 2)] for _ in range(2)]
    sg = [A.alloc((512,), BF16) for _ in range(2)]
    sg_r = [Reg() for _ in range(2)]
    gb = A.alloc((D,), F32)
    gb_r = Reg()
    ss2 = A.alloc((NT,), F32)
    r2 = A.alloc((NT,), F32)
    junk = A.alloc((D,), BF16)
    junk_r = Reg()
    P.dma("sp", gb, g_post.partition_broadcast(128), "g", writes=[gb_r])

    wd_jobs = [(h, c2) for h in range(2) for c2 in range(FC // 2)]

    def load_wd_piece():
        if not wd_jobs:
            return
        h, c2 = wd_jobs.pop(0)
        load_cast(C, wd_h[h][:, 2 * c2:2 * c2 + 2, :],
                  wd[c2 * 256:(c2 + 1) * 256, h * 512:(h + 1) * 512].rearrange("(c p) n -> p c n", p=128),
                  wd_r[h][c2])

    j = 0
    for cb in range(FC // 2):
        s = cb % NSL
        for part in range(2):
            load_cast(C, wg_s[s][:, 4 * part:4 * part + 4, :],
                      wg[part * 512:(part + 1) * 512, cb * 256:(cb + 1) * 256].rearrange("(k p) n -> p k n", p=128),
                      wg_r[s][part])
            load_cast(C, wu_s[s][:, 4 * part:4 * part + 4, :],
                      wu[part * 512:(part + 1) * 512, cb * 256:(cb + 1) * 256].rearrange("(k p) n -> p k n", p=128),
                      wu_r[s][part])
        if cb >= 1:
            for _ in range(3):
                load_wd_piece()
        for sub in range(2):
            ffc = cb * 2 + sub
            for tg in range(4):
                bG = 2 * (j % 4)
                bU = bG + 1
                q = j % 2
                j += 1
                for k in range(KC):
                    P.op("pe", lambda e, k=k, s=s, sub=sub, tg=tg, bG=bG: e.matmul(
                        PS[:, 512 * bG:512 * (bG + 1)], lhsT=wg_s[s][:, k, sub * 128:(sub + 1) * 128],
                        rhs=hT[:, k, tg * 512:(tg + 1) * 512], start=(k == 0), stop=(k == KC - 1)),
                        reads=[wg_r[s][k // 4]], writes=[psr[bG]])
                for k in range(KC):
                    P.op("pe", lambda e, k=k, s=s, sub=sub, tg=tg, bU=bU: e.matmul(
                        PS[:, 512 * bU:512 * (bU + 1)], lhsT=wu_s[s][:, k, sub * 128:(sub + 1) * 128],
                        rhs=hT[:, k, tg * 512:(tg + 1) * 512], start=(k == 0), stop=(k == KC - 1)),
                        reads=[wu_r[s][k // 4]], writes=[psr[bU]])
                P.op("act", lambda e, q=q, bG=bG: e.activation(out=sg[q], in_=PS[:, 512 * bG:512 * (bG + 1)], func=AF.Silu),
                     reads=[psr[bG]], writes=[sg_r[q]])
                P.op("dve", lambda e, q=q, bU=bU, ffc=ffc, tg=tg: e.tensor_tensor(
                    out=actT[:, ffc, tg * 512:(tg + 1) * 512], in0=PS[:, 512 * bU:512 * (bU + 1)], in1=sg[q], op=ALU.mult),
                    reads=[psr[bU], sg_r[q]], writes=[])
    while wd_jobs:
        load_wd_piece()
    P.barrier()
    if stop_after == "B":
        ov = x_dst.rearrange("(a b) d -> a (b d)", a=1024).rearrange("(k p) t -> p k t", p=128)
        for k in range(KC):
            for hh in range(2):
                P.dma("pool", ov[:, k, hh * 1024:(hh + 1) * 1024], actT[:, k + 14, hh * 1024:(hh + 1) * 1024], "dbg")
        return

    Ah = hT_ar.child()
    NSC = int(os.environ.get('NSC', 2))
    NSX = int(os.environ.get('NSX', NSC))
    xc = [Ah.alloc((D,), F32) for _ in range(NSX)]
    tt = [Ah.alloc((D,), F32) for _ in range(NSC)]
    xc_r = [Reg() for _ in range(NSX)]
    tt_r = [Reg() for _ in range(NSC)]
    tth_r = [[Reg(), Reg()] for _ in range(NSC)]
    r2_r = [Reg() for _ in range(NT)]
    for i in range(int(os.environ.get("CT", NT))):
        s = i % NSC
        sx = i % NSX
        P.dma("sp", xc[sx], x_src[i * 128:(i + 1) * 128, :], f"xc{sx}", writes=[xc_r[sx]])
        pb = (i % 4) * 2
        psf = PS[:, 512 * pb:512 * (pb + 2)]
        for h in range(2):
            for ffc in range(FC):
                P.op("pe", lambda e, h=h, ffc=ffc, i=i, pb=pb: e.matmul(
                    PS[:, 512 * (pb + h):512 * (pb + h + 1)], lhsT=actT[:, ffc, i * 128:(i + 1) * 128],
                    rhs=wd_h[h][:, ffc, :], start=(ffc == 0), stop=(ffc == FC - 1)),
                    reads=[wd_r[h][ffc // 2]], writes=[psr[pb + h]])
        P.op("act", lambda e, i=i, psf=psf: e.activation(out=junk, in_=psf, func=AF.Square, accum_out=ss2[:, i:i + 1]),
             reads=[psr[pb], psr[pb + 1]], writes=[r2_r[i], junk_r])
        cstop = int(os.environ.get("CSTOP", 9))
        if cstop == 1:
            P.dma("sp", x_dst[i * 128:(i + 1) * 128, :], xc[sx], f"xo{s}", reads=[xc_r[sx], r2_r[i]])
            continue
        P.op("act", lambda e, i=i: e.activation(out=r2[:, i:i + 1], in_=ss2[:, i:i + 1], func=AF.Ln, scale=1.0 / D, bias=EPS),
             reads=[r2_r[i]], writes=[r2_r[i]])
        P.op("act", lambda e, i=i: e.activation(out=r2[:, i:i + 1], in_=r2[:, i:i + 1], func=AF.Exp, scale=-0.5,
                                                bias=math.log(0.5)),
             reads=[r2_r[i]], writes=[r2_r[i]])
        if cstop == 2:
            P.dma("sp", x_dst[i * 128:(i + 1) * 128, :], xc[sx], f"xo{s}", reads=[xc_r[sx], r2_r[i]])
            continue
        for h in range(2):
            if os.environ.get("DVEVAR") == "copy":
                P.op("dve", lambda e, s=s, h=h, pb=pb: e.tensor_copy(
                    out=tt[s][:, 512 * h:512 * (h + 1)], in_=PS[:, 512 * (pb + h):512 * (pb + h + 1)]),
                    reads=[psr[pb + h], gb_r], writes=[tth_r[s][h]])
                continue
            if os.environ.get("DVEVAR") == "sbuf":
                P.op("dve", lambda e, s=s, h=h, pb=pb: e.tensor_tensor(
                    out=tt[s][:, 512 * h:512 * (h + 1)], in0=xc[sx][:, 512 * h:512 * (h + 1)],
                    in1=gb[:, 512 * h:512 * (h + 1)], op=ALU.mult),
                    reads=[psr[pb + h], gb_r, xc_r[sx]], writes=[tth_r[s][h]])
                continue
            P.op("dve", lambda e, s=s, h=h, pb=pb: e.tensor_tensor(
                out=tt[s][:, 512 * h:512 * (h + 1)], in0=PS[:, 512 * (pb + h):512 * (pb + h + 1)],
                in1=gb[:, 512 * h:512 * (h + 1)], op=ALU.mult),
                reads=[psr[pb + h], gb_r, r2_r[i]], writes=[tth_r[s][h]])
        if cstop == 3:
            P.dma("sp", x_dst[i * 128:(i + 1) * 128, :], tt[s], f"xo{s}", reads=[xc_r[sx], r2_r[i], tth_r[s][0], tth_r[s][1]], writes=[tth_r[s][0], tth_r[s][1]])
            continue
        P.op("dve", lambda e, s=s, sx=sx, i=i: e.scalar_tensor_tensor(out=xc[sx], in0=tt[s], scalar=r2[:, i:i + 1], in1=xc[sx],
                                                              op0=ALU.mult, op1=ALU.add),
             reads=[tth_r[s][0], tth_r[s][1], r2_r[i], xc_r[sx]], writes=[xc_r[sx]])
        P.dma("sp", x_dst[i * 128:(i + 1) * 128, :], xc[sx], f"xo{sx}", reads=[xc_r[sx]])
    P.barrier()


def tok_slice(start, step):
    return slice(start, start + step * 127 + 1, step) if step > 1 else slice(start, start + 128)


def attention_phase(C, A, hT, attT, w_in):
    P, PS, psr = C.P, C.PS, C.psr
    cos = A.alloc((S,), F32)[:32]
    sin = A.alloc((S,), F32)[:32]
    cs_r = Reg()
    cs2_r = Reg()
    P.dma("sp", cos, C.dram["c_cos"], "cs", writes=[cs_r])
    P.dma("sp", sin, C.dram["c_sin"], "cs", writes=[cs2_r])
    wsl = [[A.alloc((KC, 128), BF16) for _ in range(3)] for _ in range(2)]
    wsl_r = [[Reg() for _ in range(3)] for _ in range(2)]
    DBG.clear()
    qk = [[None, None], [None, None]]
    for a_ in range(2):
        for b_ in range(2):
            qk[a_][b_] = A.alloc((S,), BF16)
            DBG[f"qk{a_}{b_}"] = Arena.last
    qk_r = [[[Reg() for _ in range(4)] for _ in range(2)] for _ in range(2)]
    vt = [A.alloc((NT, 128), BF16) for _ in range(2)]
    vt_r = [[Reg() for _ in range(4)] for _ in range(2)]
    acc_n = A.alloc((S,), F32)
    acc_d = A.alloc((S,), F32)
    accn_r = [Reg() for _ in range(4)]
    accd_r = [Reg() for _ in range(4)]
    acc_all = Reg()
    pT = [A.alloc((512,), BF16) for _ in range(4)]
    pT_r = [Reg() for _ in range(4)]
    t1 = [A.alloc((512,), F32)[:32] for _ in range(2)]
    t2 = [A.alloc((512,), F32)[:32] for _ in range(2)]
    t1_r = [Reg() for _ in range(2)]
    t2_r = [Reg() for _ in range(2)]
    rcp = [A.alloc((512,), F32) for _ in range(2)]
    rcp_r = [Reg() for _ in range(2)]
    SCALE = 128.0 ** -0.5

    def bank(b):
        return PS[:, 512 * b:512 * (b + 1)]

    cnt = {"proj": 0, "rot": 0, "s": 0, "p": 0, "nd": 0, "t": 0, "it": 0}
    for hs in range(4):
        for g in range(3):
            sl = cnt["it"] % 2
            cnt["it"] += 1
            head = g * 4 + hs
            dil = (1, 4, 16)[g]
            for m, off in enumerate((0, 1536, 3072)):
                c0 = off + head * 128
                load_cast(C, wsl[sl][m], w_in[:, c0:c0 + 128].rearrange("(k p) n -> p k n", p=128), wsl_r[sl][m])
            for m in range(2):
                dst = qk[sl][m]
                for tg in range(4):
                    pb = cnt["proj"] % 2
                    cnt["proj"] += 1
                    for k in range(KC):
                        P.op("pe", lambda e, k=k, m=m, tg=tg, pb=pb, sl=sl: e.matmul(
                            bank(pb), lhsT=wsl[sl][m][:, k, :], rhs=hT[:, k, tg * 512:(tg + 1) * 512],
                            start=(k == 0), stop=(k == KC - 1)), reads=[wsl_r[sl][m]], writes=[psr[pb]])
                    dcol = dst[:, tg * 512:(tg + 1) * 512]
                    dr = qk_r[sl][m][tg]
                    P.op("act", lambda e, dcol=dcol, pb=pb: e.activation(out=dcol, in_=bank(pb), func=AF.Copy),
                         reads=[psr[pb]], writes=[dr])
                    rb = 2 + cnt["rot"] % 2
                    ts = cnt["rot"] % 2
                    cnt["rot"] += 1
                    P.op("pe", lambda e, dcol=dcol, rb=rb: e.matmul(bank(rb)[:32, :], lhsT=C.rm, rhs=dcol[:32, :],
                                                                   start=True, stop=True),
                         reads=[dr, C.cst_r], writes=[psr[rb]])
                    P.op("dve", lambda e, rb=rb, ts=ts, tg=tg: e.tensor_tensor(
                        out=t1[ts], in0=bank(rb)[:32, :], in1=sin[:, tg * 512:(tg + 1) * 512], op=ALU.mult),
                        reads=[psr[rb], cs_r, cs2_r], writes=[t1_r[ts]])
                    P.op("dve", lambda e, dcol=dcol, ts=ts, tg=tg: e.tensor_tensor(
                        out=t2[ts], in0=dcol[:32, :], in1=cos[:, tg * 512:(tg + 1) * 512], op=ALU.mult),
                        reads=[dr, cs_r], writes=[t2_r[ts]])
                    P.op("dve", lambda e, dcol=dcol, ts=ts: e.tensor_tensor(out=dcol[:32, :], in0=t1[ts], in1=t2[ts], op=ALU.add),
                         reads=[t1_r[ts], t2_r[ts]], writes=[dr])
            blocks = []
            if g == 0:
                for b in range(16):
                    blocks.append((tok_slice(128 * b, 1), tok_slice(128 * (b - 1), 1) if b > 0 else None))
            elif g == 1:
                for r in range(4):
                    for b in range(4):
                        blocks.append((tok_slice(4 * 128 * b + r, 4), tok_slice(4 * 128 * (b - 1) + r, 4) if b > 0 else None))
            else:
                for r in range(16):
                    blocks.append((tok_slice(r, 16), None))
            for j in range(4):
                pb = cnt["proj"] % 2
                cnt["proj"] += 1
                for bi in range(4):
                    qs = blocks[4 * j + bi][0]
                    for k in range(KC):
                        P.op("pe", lambda e, k=k, qs=qs, pb=pb, bi=bi, sl=sl: e.matmul(
                            bank(pb)[:, bi * 128:(bi + 1) * 128], lhsT=hT[:, k, qs], rhs=wsl[sl][2][:, k, :],
                            start=(k == 0), stop=(k == KC - 1)), reads=[wsl_r[sl][2]], writes=[psr[pb]])
                P.op("act", lambda e, j=j, pb=pb, sl=sl: e.activation(
                    out=vt[sl][:, 4 * j:4 * j + 4, :], in_=bank(pb).rearrange("p (a b) -> p a b", a=4), func=AF.Copy),
                    reads=[psr[pb]], writes=[vt_r[sl][j]])
            qT, kT = qk[sl][0], qk[sl][1]
            qr = qk_r[sl][0] + qk_r[sl][1]
            pairs = [(2 * i, 2 * i + 1) for i in range(8)]

            def do_qk(pair):
                sb = 4 + cnt["s"] % 2
                cnt["s"] += 1
                for ti, blk in enumerate(pair):
                    qs, ps_ = blocks[blk]
                    for ci, ks in enumerate((ps_, qs)):
                        o = bank(sb)[:, (2 * ti + ci) * 128:(2 * ti + ci + 1) * 128]
                        if ks is None:
                            P.op("pe", lambda e, o=o: e.matmul(o, lhsT=C.ident, rhs=C.maskn, start=True, stop=True),
                                 reads=[C.cst_r], writes=[psr[sb]])
                            continue
                        P.op("pe", lambda e, o=o, ks=ks, qs=qs, kT=kT, qT=qT: e.matmul(o, lhsT=kT[:, ks], rhs=qT[:, qs], start=True, stop=False),
                             reads=qr, writes=[psr[sb]])
                        mk = C.maskc if ci == 1 else C.maskp
                        P.op("pe", lambda e, o=o, mk=mk: e.matmul(o, lhsT=C.ident, rhs=mk, start=False, stop=True),
                             reads=[C.cst_r], writes=[psr[sb]])
                pslot = cnt["p"] % 4
                cnt["p"] += 1
                P.op("act", lambda e, sb=sb, pslot=pslot: e.activation(out=pT[pslot], in_=bank(sb), func=AF.Exp, scale=SCALE),
                     reads=[psr[sb]], writes=[pT_r[pslot]])
                return pslot

            def do_pv(pair, pslot):
                nb = 6 + cnt["nd"] % 2
                cnt["nd"] += 1
                for ti, blk in enumerate(pair):
                    qs, ps_ = blocks[blk]
                    for is_num in (True, False):
                        col = ti * 128 + (0 if is_num else 256)
                        o = bank(nb)[:, col:col + 128]
                        for ci in range(2):
                            kblk = blk if (ci == 1 or ps_ is None) else blk - 1
                            lhs = vt[sl][:, kblk, :] if is_num else C.ones_bf
                            P.op("pe", lambda e, o=o, lhs=lhs, pslot=pslot, ti=ti, ci=ci: e.matmul(
                                o, lhsT=lhs, rhs=pT[pslot][:, (2 * ti + ci) * 128:(2 * ti + ci + 1) * 128],
                                start=(ci == 0), stop=(ci == 1)),
                                reads=[pT_r[pslot], vt_r[sl][kblk // 4], C.cst_r], writes=[psr[nb]])
                pi = pair[0] // 2
                if g == 0:
                    sel = slice(256 * pi, 256 * pi + 256)
                    dn, dd = acc_n[:, sel], acc_d[:, sel]
                    sn, sd = bank(nb)[:, 0:256], bank(nb)[:, 256:512]
                elif g == 1:
                    r_, b_ = pair[0] // 4, pair[0] % 4
                    st_ = r_ + 512 * b_
                    sel = slice(st_, st_ + 4 * 255 + 1, 4)
                    dn, dd = acc_n[:, sel], acc_d[:, sel]
                    sn, sd = bank(nb)[:, 0:256], bank(nb)[:, 256:512]
                else:
                    r_ = pair[0]
                    dn = acc_n.rearrange("p (n r) -> p r n", r=16)[:, r_:r_ + 2, :]
                    dd = acc_d.rearrange("p (n r) -> p r n", r=16)[:, r_:r_ + 2, :]
                    sn = bank(nb)[:, 0:256].rearrange("p (a b) -> p a b", a=2)
                    sd = bank(nb)[:, 256:512].rearrange("p (a b) -> p a b", a=2)
                if g == 0:
                    P.op("dve", lambda e, dn=dn, sn=sn: e.tensor_copy(out=dn, in_=sn), reads=[psr[nb]], writes=[acc_all])
                    P.op("dve", lambda e, dd=dd, sd=sd: e.tensor_copy(out=dd, in_=sd), reads=[psr[nb]], writes=[acc_all])
                else:
                    P.op("dve", lambda e, dn=dn, sn=sn: e.tensor_tensor(out=dn, in0=sn, in1=dn, op=ALU.add),
                         reads=[psr[nb], acc_all], writes=[acc_all])
                    P.op("dve", lambda e, dd=dd, sd=sd: e.tensor_tensor(out=dd, in0=sd, in1=dd, op=ALU.add),
                         reads=[psr[nb], acc_all], writes=[acc_all])

            prev = None
            for pair in pairs:
                pslot = do_qk(pair)
                if prev is not None:
                    do_pv(*prev)
                prev = (pair, pslot)
            do_pv(*prev)
        for tg in range(4):
            rs = tg % 2
            P.op("dve", lambda e, tg=tg, rs=rs: e.reciprocal(out=rcp[rs], in_=acc_d[:, tg * 512:(tg + 1) * 512]),
                 reads=[acc_all], writes=[rcp_r[rs]])
            P.op("dve", lambda e, tg=tg, rs=rs, hs=hs: e.tensor_tensor(
                out=attT[:, hs, tg * 512:(tg + 1) * 512], in0=acc_n[:, tg * 512:(tg + 1) * 512], in1=rcp[rs], op=ALU.mult),
                reads=[acc_all, rcp_r[rs]], writes=[C.attT_r])


def mlstm_phase(C, A, hT, mlT, w_in, conv_w, conv_b, i_bias, f_bias, head_g):
    P, PS, psr = C.P, C.PS, C.psr
    OQ, OK_, OV, OO, OI = 4608, 5632, 6656, 7680, 8704

    def bank(b):
        return PS[:, 512 * b:512 * (b + 1)]

    def R():
        return Reg()

    cwb = A.alloc((2048,), F32)[:5]
    cwb_r = R()
    cwb2_r = R()
    P.dma("sp", cwb[0:4, :], conv_w, "mca", writes=[cwb_r])
    P.dma("sp", cwb[4:5, :], conv_b.rearrange("(o n) -> o n", o=1), "mca", writes=[cwb2_r])
    cwT = A.alloc((16, 8), F32)
    ncb = A.alloc((16,), F32)
    cwT_r = R()
    for c in range(16):
        P.op("pe", lambda e, c=c: e.matmul(bank(6)[:, c * 8:c * 8 + 5], lhsT=cwb[:, c * 128:(c + 1) * 128],
                                           rhs=C.identf[:5, :5], start=True, stop=True),
             reads=[cwb_r, cwb2_r, C.cst_r], writes=[psr[6]])
    P.op("dve", lambda e: e.tensor_copy(out=cwT[:, :, 0:5], in_=bank(6)[:, 0:128].rearrange("p (c j) -> p c j", j=8)[:, :, 0:5]),
         reads=[psr[6]], writes=[cwT_r])
    P.op("dve", lambda e: e.tensor_scalar(out=ncb, in0=cwT[:, :, 4], scalar1=-1.0, scalar2=None, op0=ALU.mult),
         reads=[cwT_r], writes=[cwT_r])
    hg = A.alloc((1024,), F32)
    hg_r = R()
    P.dma("sp", hg, head_g.partition_broadcast(128), "mch", writes=[hg_r])
    bias8 = A.alloc((8,), F32)
    b8_r = R()
    b8b_r = R()
    P.dma("sp", bias8[:, 0:4], i_bias.partition_broadcast(128), "mcb", writes=[b8_r])
    P.dma("sp", bias8[:, 4:8], f_bias.partition_broadcast(128), "mcb", writes=[b8b_r])

    wif = A.alloc((KC, 8), BF16)
    wif_r = R()
    load_cast(C, wif, w_in[:, OI:OI + 8].rearrange("(k p) n -> p k n", p=128), wif_r)
    for c in range(NT):
        for k in range(KC):
            P.op("pe", lambda e, c=c, k=k: e.matmul(bank(7)[:, c * 8:(c + 1) * 8], lhsT=hT[:, k, c * 128:(c + 1) * 128],
                                                    rhs=wif[:, k, :], start=(k == 0), stop=(k == KC - 1)),
                 reads=[wif_r], writes=[psr[7]])
    gi = A.alloc((NT, 4), F32)
    lg = A.alloc((NT, 4), F32)
    g_r = R()
    pre3 = bank(7)[:, 0:128].rearrange("p (c j) -> p c j", j=8)
    P.op("dve", lambda e: e.tensor_tensor(out=gi, in0=pre3[:, :, 0:4],
                                          in1=bias8[:, 0:4].unsqueeze(1).to_broadcast([128, NT, 4]), op=ALU.add),
         reads=[psr[7], b8_r, b8b_r], writes=[g_r])
    P.op("dve", lambda e: e.tensor_tensor(out=lg, in0=pre3[:, :, 4:8],
                                          in1=bias8[:, 4:8].unsqueeze(1).to_broadcast([128, NT, 4]), op=ALU.add),
         reads=[psr[7], b8_r, b8b_r, g_r], writes=[g_r])
    P.op("act", lambda e: e.activation(out=lg, in_=lg, func=AF.Exp, scale=-1.0), reads=[g_r], writes=[g_r])
    P.op("act", lambda e: e.activation(out=lg, in_=lg, func=AF.Ln, bias=1.0), reads=[g_r], writes=[g_r])
    lg2 = lg.rearrange("p c h -> p (c h)")
    gi2 = gi.rearrange("p c h -> p (c h)")
    P.op("pe", lambda e: e.matmul(bank(6)[:, 0:64], lhsT=C.tri, rhs=lg2, start=True, stop=True),
         reads=[g_r, C.cst_r], writes=[psr[6]])
    P.op("pe", lambda e: e.matmul(bank(6)[:, 64:128], lhsT=C.ones_f, rhs=lg2, start=True, stop=True),
         reads=[g_r, C.cst_r], writes=[psr[6]])
    e_in = A.alloc((64,), F32)
    e_out = A.alloc((64,), F32)
    e_L = A.alloc((64,), F32)
    e_v = A.alloc((64,), F32)
    ee_r = R()
    P.op("dve", lambda e: e.tensor_tensor(out=e_in, in0=bank(6)[:, 0:64], in1=gi2, op=ALU.add),
         reads=[psr[6], g_r], writes=[ee_r])
    P.op("act", lambda e: e.activation(out=e_in, in_=e_in, func=AF.Exp), reads=[ee_r], writes=[ee_r])
    P.op("act", lambda e: e.activation(out=e_out, in_=bank(6)[:, 0:64], func=AF.Exp, scale=-1.0),
         reads=[psr[6], ee_r], writes=[ee_r])
    P.op("act", lambda e: e.activation(out=e_L, in_=bank(6)[:, 64:128], func=AF.Exp, scale=-1.0),
         reads=[psr[6], ee_r], writes=[ee_r])
    P.op("dve", lambda e: e.tensor_scalar(out=e_in, in0=e_in, scalar1=1.0 / 16.0, scalar2=None, op0=ALU.mult),
         reads=[ee_r], writes=[ee_r])
    P.op("dve", lambda e: e.tensor_tensor(out=e_v, in0=e_in, in1=e_L, op=ALU.mult), reads=[ee_r], writes=[ee_r])

    wq = [A.alloc((KC, 256), BF16) for _ in range(4)]
    wq_r = [[R(), R()] for _ in range(4)]
    raw = [A.alloc((S + 3,), F32) for _ in range(2)]
    raw_r = [R() for _ in range(2)]
    acc = [A.alloc((S,), F32) for _ in range(2)]
    acc_r = [R() for _ in range(2)]
    for i in range(2):
        P.op("dve", lambda e, i=i: e.memset(raw[i][:, 0:3], 0.0), writes=[raw_r[i]])
    qT = A.alloc((2, S), BF16)
    kT = A.alloc((2, S), BF16)
    qk_r = [[R(), R()], [R(), R()]]
    ktok2 = [A.alloc((256,), BF16) for _ in range(2)]
    ktok2_r = [R(), R()]
    vaug2 = [A.alloc((258,), BF16) for _ in range(2)]
    vaug2_r = [R(), R()]
    for i in range(2):
        P.op("dve", lambda e, i=i: e.memset(vaug2[i][:, 256:257], 1.0), writes=[vaug2_r[i]])
        P.op("dve", lambda e, i=i: e.memset(vaug2[i][:, 257:258], 0.0), writes=[vaug2_r[i]])
    vp2 = [A.alloc((258,), BF16) for _ in range(2)]
    vp2_r = [R(), R()]
    sigo = A.alloc((NT, 256), BF16)
    sigo_r = [R() for _ in range(NT)]
    wT2 = [A.alloc((128,), BF16) for _ in range(2)]
    wT2_r = [R(), R()]
    Cf = A.alloc((2, 258), F32)
    Cf_r = R()
    Cbf = A.alloc((2, 258), BF16)
    Cbf_r = R()
    hu2 = [A.alloc((256,), F32) for _ in range(2)]
    hu2_r = [R(), R()]
    sm2 = [A.alloc((8,), F32) for _ in range(2)]
    sm2_r = [R(), R()]
    junk2 = [A.alloc((256,), BF16) for _ in range(2)]
    mlb2 = [A.alloc((256,), BF16) for _ in range(2)]
    mlb2_r = [R(), R()]
    cj = 0
    for hd in range(4):
        for m, off in enumerate((OQ, OK_, OV, OO)):
            c0 = off + hd * 256
            for part in range(2):
                load_cast(C, wq[m][:, 4 * part:4 * part + 4, :],
                          w_in[part * 512:(part + 1) * 512, c0:c0 + 256].rearrange("(k p) n -> p k n", p=128), wq_r[m][part])
        for m, dstT in ((0, qT), (1, kT)):
            for cc in range(2):
                ch = m * 8 + hd * 2 + cc
                bi = cj % 2
                cj += 1
                for tg in range(4):
                    pb = tg % 2
                    for k in range(KC):
                        P.op("pe", lambda e, k=k, m=m, cc=cc, tg=tg, pb=pb: e.matmul(
                            bank(pb), lhsT=wq[m][:, k, cc * 128:(cc + 1) * 128], rhs=hT[:, k, tg * 512:(tg + 1) * 512],
                            start=(k == 0), stop=(k == KC - 1)), reads=[wq_r[m][k // 4]], writes=[psr[pb]])
                    P.op("act", lambda e, bi=bi, tg=tg, pb=pb: e.activation(
                        out=raw[bi][:, 3 + tg * 512:3 + (tg + 1) * 512], in_=bank(pb), func=AF.Copy),
                        reads=[psr[pb]], writes=[raw_r[bi]])
                P.op("dve", lambda e, bi=bi, ch=ch: e.tensor_scalar(out=acc[bi], in0=raw[bi][:, 3:3 + S], scalar1=cwT[:, ch, 3:4],
                                                                     scalar2=None, op0=ALU.mult),
                     reads=[raw_r[bi], cwT_r], writes=[acc_r[bi]])
                for j in (2, 1, 0):
                    P.op("dve", lambda e, bi=bi, ch=ch, j=j: e.scalar_tensor_tensor(
                        out=acc[bi], in0=raw[bi][:, j:j + S], scalar=cwT[:, ch, j:j + 1], in1=acc[bi], op0=ALU.mult, op1=ALU.add),
                        reads=[raw_r[bi], cwT_r, acc_r[bi]], writes=[acc_r[bi]])
                P.op("act", lambda e, bi=bi, ch=ch, dstT=dstT, cc=cc: e.activation(
                    out=dstT[:, cc, :], in_=acc[bi], func=AF.Silu, bias=cwT[:, ch, 4:5]),
                    reads=[acc_r[bi], cwT_r], writes=[qk_r[m][cc]])
        qkr = [qk_r[0][0], qk_r[0][1], qk_r[1][0], qk_r[1][1]]
        for c in range(NT):
            cs = slice(c * 128, (c + 1) * 128)
            ob = 6 + c % 2
            for k in range(KC):
                P.op("pe", lambda e, k=k, cs=cs, ob=ob: e.matmul(bank(ob)[:, 0:256], lhsT=hT[:, k, cs], rhs=wq[3][:, k, :],
                                                               start=(k == 0), stop=(k == KC - 1)),
                     reads=[wq_r[3][k // 4]], writes=[psr[ob]])
            P.op("act", lambda e, c=c, ob=ob: e.activation(out=sigo[:, c, :], in_=bank(ob)[:, 0:256], func=AF.Sigmoid),
                 reads=[psr[ob]], writes=[sigo_r[c]])
        for c in range(NT):
            cs = slice(c * 128, (c + 1) * 128)
            col = c * 4 + hd
            p = c % 2
            ktok, ktok_r, vaug, vaug_r, vp, vp_r = ktok2[p], ktok2_r[p], vaug2[p], vaug2_r[p], vp2[p], vp2_r[p]
            wT, wT_r, hu, hu_r, sm, sm_r, junk, mlb, mlb_r = wT2[p], wT2_r[p], hu2[p], hu2_r[p], sm2[p], sm2_r[p], junk2[p], mlb2[p], mlb2_r[p]
            ab, db = p, 2 + p
            abf = bank(ab).bitcast(BF16)
            for k in range(KC):
                P.op("pe", lambda e, k=k, cs=cs, ab=ab: e.matmul(bank(ab)[:, 0:256], lhsT=hT[:, k, cs], rhs=wq[2][:, k, :],
                                                               start=(k == 0), stop=(k == KC - 1)),
                     reads=[wq_r[2][k // 4]], writes=[psr[ab]])
            for dk in range(2):
                P.op("pe", lambda e, dk=dk, cs=cs, abf=abf: e.transpose(out=abf[:, 512 + dk * 128:512 + (dk + 1) * 128],
                                                                        in_=kT[:, dk, cs], identity=C.ident),
                     reads=[qk_r[1][dk], C.cst_r], writes=[psr[ab]])
            P.op("act", lambda e, vaug=vaug, ab=ab: e.activation(out=vaug[:, 0:256], in_=bank(ab)[:, 0:256], func=AF.Copy),
                 reads=[psr[ab]], writes=[vaug_r])
            P.op("act", lambda e, ktok=ktok, abf=abf: e.activation(out=ktok, in_=abf[:, 512:768], func=AF.Copy),
                 reads=[psr[ab]], writes=[ktok_r])
            for dk in range(2):
                P.op("pe", lambda e, dk=dk, cs=cs, db=db: e.matmul(bank(db)[:, 384:512], lhsT=kT[:, dk, cs], rhs=qT[:, dk, cs],
                                                                 start=(dk == 0), stop=(dk == 1)),
                     reads=qkr, writes=[psr[db]])
            P.op("dve", lambda e, col=col, wT=wT, db=db: e.scalar_tensor_tensor(
                out=wT, in0=bank(db)[:, 384:512], scalar=e_in[:, col:col + 1], in1=C.tri, op0=ALU.mult, op1=ALU.mult),
                reads=[psr[db], ee_r, C.cst_r], writes=[wT_r])
            P.op("pe", lambda e, c=c, wT=wT, vaug=vaug, db=db: e.matmul(bank(db)[:, 0:258], lhsT=wT, rhs=vaug, start=True, stop=(c == 0)),
                 reads=[wT_r, vaug_r], writes=[psr[db]])
            if c > 0:
                for dk in range(2):
                    P.op("pe", lambda e, dk=dk, cs=cs, db=db: e.matmul(bank(db)[:, 0:258], lhsT=qT[:, dk, cs], rhs=Cbf[:, dk, :],
                                                                     start=False, stop=(dk == 1)),
                         reads=qkr + [Cbf_r], writes=[psr[db]])
            if c < NT - 1:
                P.op("dve", lambda e, col=col, vp=vp, vaug=vaug: e.tensor_scalar(out=vp, in0=vaug, scalar1=e_v[:, col:col + 1],
                                                                                 scalar2=None, op0=ALU.mult),
                     reads=[vaug_r, ee_r], writes=[vp_r])
                for dk in range(2):
                    P.op("pe", lambda e, dk=dk, ktok=ktok, vp=vp: e.matmul(bank(4 + dk)[:, 0:258], lhsT=ktok[:, dk * 128:(dk + 1) * 128],
                                                                         rhs=vp, start=True, stop=True),
                         reads=[ktok_r, vp_r], writes=[psr[4 + dk]])
                    if c == 0:
                        P.op("dve", lambda e, dk=dk: e.tensor_copy(out=Cf[:, dk, :], in_=bank(4 + dk)[:, 0:258]),
                             reads=[psr[4 + dk]], writes=[Cf_r])
                    else:
                        P.op("dve", lambda e, dk=dk, col=col: e.scalar_tensor_tensor(
                            out=Cf[:, dk, :], in0=Cf[:, dk, :], scalar=e_L[:, col:col + 1], in1=bank(4 + dk)[:, 0:258],
                            op0=ALU.mult, op1=ALU.add), reads=[psr[4 + dk], Cf_r, ee_r], writes=[Cf_r])
                P.op("act", lambda e: e.activation(out=Cbf, in_=Cf, func=AF.Copy), reads=[Cf_r], writes=[Cbf_r])
            P.op("dve", lambda e, col=col, sm=sm, db=db: e.tensor_tensor(out=sm[:, 0:1], in0=bank(db)[:, 256:257],
                                                                       in1=e_out[:, col:col + 1], op=ALU.mult),
                 reads=[psr[db], ee_r], writes=[sm_r])
            P.op("dve", lambda e, sm=sm: e.scalar_tensor_tensor(out=sm[:, 1:2], in0=sm[:, 0:1], scalar=-1.0, in1=sm[:, 0:1],
                                                                op0=ALU.mult, op1=ALU.max), reads=[sm_r], writes=[sm_r])
            P.op("dve", lambda e, sm=sm: e.tensor_scalar(out=sm[:, 1:2], in0=sm[:, 1:2], scalar1=1.0, scalar2=None, op0=ALU.max),
                 reads=[sm_r], writes=[sm_r])
            P.op("dve", lambda e, sm=sm: e.reciprocal(out=sm[:, 2:3], in_=sm[:, 1:2]), reads=[sm_r], writes=[sm_r])
            P.op("dve", lambda e, col=col, sm=sm: e.tensor_tensor(out=sm[:, 3:4], in0=sm[:, 2:3], in1=e_out[:, col:col + 1], op=ALU.mult),
                 reads=[sm_r, ee_r], writes=[sm_r])
            P.op("dve", lambda e, sm=sm, hu=hu, db=db: e.tensor_scalar(out=hu, in0=bank(db)[:, 0:256], scalar1=sm[:, 3:4], scalar2=None,
                                                                     op0=ALU.mult),
                 reads=[psr[db], sm_r], writes=[hu_r])
            P.op("act", lambda e, sm=sm, hu=hu, junk=junk: e.activation(out=junk, in_=hu, func=AF.Square, accum_out=sm[:, 4:5]),
                 reads=[hu_r, sm_r], writes=[sm_r])
            P.op("act", lambda e, sm=sm: e.activation(out=sm[:, 5:6], in_=sm[:, 4:5], func=AF.Ln, scale=1.0 / 256, bias=EPS),
                 reads=[sm_r], writes=[sm_r])
            P.op("act", lambda e, sm=sm: e.activation(out=sm[:, 5:6], in_=sm[:, 5:6], func=AF.Exp, scale=-0.5),
                 reads=[sm_r], writes=[sm_r])
            P.op("dve", lambda e, hd=hd, sm=sm, hu=hu: e.scalar_tensor_tensor(out=hu, in0=hu, scalar=sm[:, 5:6],
                                                                              in1=hg[:, hd * 256:(hd + 1) * 256],
                                                                              op0=ALU.mult, op1=ALU.mult),
                 reads=[hu_r, sm_r, hg_r], writes=[hu_r])
            P.op("dve", lambda e, c=c, hu=hu, mlb=mlb: e.tensor_tensor(out=mlb, in0=hu, in1=sigo[:, c, :], op=ALU.mult),
                 reads=[hu_r, sigo_r[c]], writes=[mlb_r])
            for j in range(2):
                P.op("pe", lambda e, j=j, abf=abf, mlb=mlb: e.transpose(out=abf[:, 768 + j * 128:768 + (j + 1) * 128],
                                                                       in_=mlb[:, j * 128:(j + 1) * 128], identity=C.ident),
                     reads=[mlb_r, C.cst_r], writes=[psr[ab]])
            P.op("act", lambda e, hd=hd, cs=cs, abf=abf: e.activation(
                out=mlT[:, 2 * hd:2 * hd + 2, cs], in_=abf[:, 768:1024].rearrange("p (j n) -> p j n", j=2), func=AF.Copy),
                reads=[psr[ab]], writes=[C.mlT_r])

def merge_phase(C, A, hT, attT, mlT, w_in, w_a, w_m, w_out, g_post, x_src, x_dst):
    P, PS, psr = C.P, C.PS, C.psr
    OGA, OGM = 8712, 9736

    def bank(b):
        return PS[:, 512 * b:512 * (b + 1)]

    mgT = A.alloc((KC, S), BF16)
    wa = A.alloc((4, 1024), BF16)
    wa_r = [Reg() for _ in range(4)]
    wm = A.alloc((KC, 1024), BF16)
    wm_r = [Reg() for _ in range(KC)]
    wo = A.alloc((KC, 1024), BF16)
    wo_r = [Reg() for _ in range(KC)]
    wg = [[A.alloc((KC, 128), BF16) for _ in range(2)] for _ in range(2)]
    wg_r = [[Reg() for _ in range(2)] for _ in range(2)]
    load_cast(C, wg[0][0], w_in[:, OGA:OGA + 128].rearrange("(k p) n -> p k n", p=128), wg_r[0][0])
    load_cast(C, wg[0][1], w_in[:, OGM:OGM + 128].rearrange("(k p) n -> p k n", p=128), wg_r[0][1])
    for k in range(4):
        load_cast(C, wa[:, k, :], w_a[k * 128:(k + 1) * 128, :], wa_r[k])
    for k in range(KC):
        load_cast(C, wm[:, k, :], w_m[k * 128:(k + 1) * 128, :], wm_r[k])
    ga = [A.alloc((512,), F32) for _ in range(2)]
    ga_r = [Reg() for _ in range(2)]
    gm = [A.alloc((512,), F32) for _ in range(2)]
    gm_r = [Reg() for _ in range(2)]
    ta = [A.alloc((512,), F32) for _ in range(2)]
    ta_r = [Reg() for _ in range(2)]
    gb = A.alloc((D,), F32)
    gb_r = Reg()
    P.dma("sp", gb, g_post.partition_broadcast(128), "g", writes=[gb_r])
    j = 0
    for mc in range(KC):
        sl = mc % 2
        if mc > 0:
            load_cast(C, wg[sl][0], w_in[:, OGA + mc * 128:OGA + (mc + 1) * 128].rearrange("(k p) n -> p k n", p=128), wg_r[sl][0])
            load_cast(C, wg[sl][1], w_in[:, OGM + mc * 128:OGM + (mc + 1) * 128].rearrange("(k p) n -> p k n", p=128), wg_r[sl][1])
        if mc == 0:
            for k in range(KC):
                load_cast(C, wo[:, k, :], w_out[k * 128:(k + 1) * 128, :], wo_r[k])
        for tg in range(4):
            q = j % 2
            j += 1
            ts = slice(tg * 512, (tg + 1) * 512)
            b0 = 4 * q
            for gi_, (gbuf, gr) in enumerate(((ga, ga_r), (gm, gm_r))):
                for k in range(KC):
                    P.op("pe", lambda e, k=k, gi_=gi_, sl=sl, ts=ts, b0=b0: e.matmul(
                        bank(b0 + gi_), lhsT=wg[sl][gi_][:, k, :], rhs=hT[:, k, ts], start=(k == 0), stop=(k == KC - 1)),
                        reads=[wg_r[sl][gi_]], writes=[psr[b0 + gi_]])
                P.op("act", lambda e, gbuf=gbuf, q=q, gi_=gi_, b0=b0: e.activation(out=gbuf[q], in_=bank(b0 + gi_), func=AF.Sigmoid),
                     reads=[psr[b0 + gi_]], writes=[gr[q]])
            for k in range(4):
                P.op("pe", lambda e, k=k, mc=mc, ts=ts, b0=b0: e.matmul(
                    bank(b0 + 2), lhsT=wa[:, k, mc * 128:(mc + 1) * 128], rhs=attT[:, k, ts], start=(k == 0), stop=(k == 3)),
                    reads=[wa_r[k], C.attT_r], writes=[psr[b0 + 2]])
            for k in range(KC):
                P.op("pe", lambda e, k=k, mc=mc, ts=ts, b0=b0: e.matmul(
                    bank(b0 + 3), lhsT=wm[:, k, mc * 128:(mc + 1) * 128], rhs=mlT[:, k, ts], start=(k == 0), stop=(k == KC - 1)),
                    reads=[wm_r[k], C.mlT_r], writes=[psr[b0 + 3]])
            P.op("dve", lambda e, q=q, b0=b0: e.tensor_tensor(out=ta[q], in0=bank(b0 + 2), in1=ga[q], op=ALU.mult),
                 reads=[psr[b0 + 2], ga_r[q]], writes=[ta_r[q]])
            P.op("dve", lambda e, q=q, b0=b0: e.tensor_tensor(out=gm[q], in0=bank(b0 + 3), in1=gm[q], op=ALU.mult),
                 reads=[psr[b0 + 3], gm_r[q]], writes=[gm_r[q]])
            P.op("dve", lambda e, q=q, mc=mc, ts=ts: e.tensor_tensor(out=mgT[:, mc, ts], in0=ta[q], in1=gm[q], op=ALU.add),
                 reads=[ta_r[q], gm_r[q]], writes=[C.mg_r])
    xc = [A.alloc((D,), F32) for _ in range(2)]
    tt = [A.alloc((D,), F32) for _ in range(2)]
    xc_r = [Reg() for _ in range(2)]
    tth_r = [[Reg(), Reg()] for _ in range(2)]
    ss2 = A.alloc((NT,), F32)
    r2 = A.alloc((NT,), F32)
    junk = ga[0].bitcast(BF16)
    junk_r = ga_r[0]
    r2_r = [Reg() for _ in range(NT)]
    for i in range(NT):
        s = i % 2
        P.dma("sp", xc[s], x_src[i * 128:(i + 1) * 128, :], f"xc{s}", writes=[xc_r[s]])
        pb = (i % 4) * 2
        psf = PS[:, 512 * pb:512 * (pb + 2)]
        for h in range(2):
            for k in range(KC):
                P.op("pe", lambda e, h=h, k=k, i=i, pb=pb: e.matmul(
                    bank(pb + h), lhsT=mgT[:, k, i * 128:(i + 1) * 128], rhs=wo[:, k, h * 512:(h + 1) * 512],
                    start=(k == 0), stop=(k == KC - 1)), reads=[wo_r[k], C.mg_r], writes=[psr[pb + h]])
        P.op("act", lambda e, i=i, psf=psf: e.activation(out=junk, in_=psf, func=AF.Square, accum_out=ss2[:, i:i + 1]),
             reads=[psr[pb], psr[pb + 1]], writes=[r2_r[i], junk_r])
        P.op("act", lambda e, i=i: e.activation(out=r2[:, i:i + 1], in_=ss2[:, i:i + 1], func=AF.Ln, scale=1.0 / D, bias=EPS),
             reads=[r2_r[i]], writes=[r2_r[i]])
        P.op("act", lambda e, i=i: e.activation(out=r2[:, i:i + 1], in_=r2[:, i:i + 1], func=AF.Exp, scale=-0.5),
             reads=[r2_r[i]], writes=[r2_r[i]])
        for h in range(2):
            P.op("dve", lambda e, s=s, h=h, pb=pb: e.tensor_tensor(
                out=tt[s][:, 512 * h:512 * (h + 1)], in0=bank(pb + h), in1=gb[:, 512 * h:512 * (h + 1)], op=ALU.mult),
                reads=[psr[pb + h], gb_r, r2_r[i]], writes=[tth_r[s][h]])
        P.op("dve", lambda e, s=s, i=i: e.scalar_tensor_tensor(out=xc[s], in0=tt[s], scalar=r2[:, i:i + 1], in1=xc[s],
                                                              op0=ALU.mult, op1=ALU.add),
             reads=[tth_r[s][0], tth_r[s][1], r2_r[i], xc_r[s]], writes=[xc_r[s]])
        P.dma("sp", x_dst[i * 128:(i + 1) * 128, :], xc[s], f"xo{s}", reads=[xc_r[s]])


def mixer_phase(C, x_src, x_dst, W):
    A = C.A.child()
    hT = A.alloc((KC, S), BF16)
    attT = A.alloc((4, S), BF16)
    mlT = A.alloc((8, S), BF16)
    C.attT_r = Reg()
    C.mlT_r = Reg()
    C.mg_r = Reg()
    base = A.off
    end = A.end
    norm_transpose(C, Arena(A.ap, base, end), x_src, W["mix_pre_g"][0], hT)
    attention_phase(C, Arena(A.ap, base, end), hT, attT, W["w_in"][0])
    C.P.barrier()
    mlstm_phase(C, Arena(A.ap, base, end), hT, mlT, W["w_in"][0], W["conv_w"][0], W["conv_b"][0],
                W["mlstm_i_bias"][0], W["mlstm_f_bias"][0], W["mlstm_head_g"][0])
    C.P.barrier()
    merge_phase(C, Arena(A.ap, base, end), hT, attT, mlT, W["w_in"][0], W["w_att_branch"][0], W["w_mlstm_branch"][0],
                W["w_out"][0], W["mix_post_g"][0], x_src, x_dst)
    C.P.barrier()

def host_consts():
    c = {}
    bf = ml_dtypes.bfloat16
    c["ident"] = np.eye(128, dtype=np.float32).astype(bf)
    half = 16
    inv_freq = np.power(np.float32(500000.0), -(np.arange(half, dtype=np.float32) * 2.0 / 32)).astype(np.float32)
    ang = np.arange(S, dtype=np.float32)[None, :] * inv_freq[:, None]
    c["cos"] = np.concatenate([np.cos(ang), np.cos(ang)], 0).astype(np.float32)
    c["sin"] = np.concatenate([np.sin(ang), np.sin(ang)], 0).astype(np.float32)
    rm = np.zeros((32, 32), np.float32)
    for j in range(16):
        rm[16 + j, j] = -1.0
        rm[j, 16 + j] = 1.0
    c["rm"] = rm.astype(bf)
    jj = np.arange(128)[:, None]
    ii = np.arange(128)[None, :]
    NEG = -30000.0
    c["maskc"] = np.where(jj <= ii, 0.0, NEG).astype(bf)
    c["maskp"] = np.where(jj >= ii, 0.0, NEG).astype(bf)
    c["maskn"] = np.full((128, 128), NEG, np.float32).astype(bf)
    c["tri"] = (jj <= ii).astype(np.float32)
    c["ones_f"] = np.ones((128, 128), np.float32)
    c["identf"] = np.eye(128, dtype=np.float32)
    c["ones_bf"] = np.ones((128, 128), np.float32).astype(bf)
    return c


def build(stage="full"):
    nc = bass.Bass("TRN2", target_bir_lowering=False)

    def din(name, shape, dt=F32):
        return nc.dram_tensor(name, list(shape), dt, kind="ExternalInput").ap()

    x = din("x", [S, D])
    W = {}
    for name, shape in [("ffn1_pre_g", [1, D]), ("ffn1_w_gate", [1, D, FF]), ("ffn1_w_up", [1, D, FF]),
                        ("ffn1_w_down", [1, FF, D]), ("ffn1_post_g", [1, D]), ("mix_pre_g", [1, D]),
                        ("w_in", [1, D, IN_W]), ("conv_w", [1, 4, 2048]), ("conv_b", [1, 2048]),
                        ("mlstm_i_bias", [1, 4]), ("mlstm_f_bias", [1, 4]), ("mlstm_head_g", [1, 1024]),
                        ("w_att_branch", [1, 512, D]), ("w_mlstm_branch", [1, D, D]), ("w_out", [1, D, D]),
                        ("mix_post_g", [1, D]), ("ffn2_pre_g", [1, D]), ("ffn2_w_gate", [1, D, FF]),
                        ("ffn2_w_up", [1, D, FF]), ("ffn2_w_down", [1, FF, D]), ("ffn2_post_g", [1, D])]:
        W[name] = din(name, shape)
    CD = {}
    for name, shape, dt in [("c_ident", [128, 128], BF16), ("c_cos", [32, S], F32), ("c_sin", [32, S], F32),
                            ("c_rm", [32, 32], BF16), ("c_maskc", [128, 128], BF16), ("c_maskp", [128, 128], BF16),
                            ("c_maskn", [128, 128], BF16), ("c_tri", [128, 128], F32), ("c_ones_f", [128, 128], F32), ("c_identf", [128, 128], F32),
                            ("c_ones_bf", [128, 128], BF16)]:
        CD[name] = din(name, shape, dt)
    c_ident = CD["c_ident"]
    out = nc.dram_tensor("out", [S, D], F32, kind="ExternalOutput").ap()
    x1 = nc.dram_tensor("x1", [S, D], F32, kind="Internal").ap()
    x2 = nc.dram_tensor("x2", [S, D], F32, kind="Internal").ap()

    with ExitStack() as st:
        ARENA_BYTES = 212480
        arena = st.enter_context(nc.sbuf_tensor("arena", [128, ARENA_BYTES // 2], BF16))
        PS = st.enter_context(nc.psum_tensor("ps", [128, 4096], F32))
        C = Ctx()
        C.nc = nc
        C.P = P = Prog(nc)
        C.PS = PS
        C.psr = [Reg(f"ps{i}", excl=True) for i in range(8)]
        top = Arena(arena, 0, ARENA_BYTES)
        C.ident = top.alloc((128,), BF16)
        C.ident_r = Reg()
        P.dma("sp", C.ident, c_ident, "cst", writes=[C.ident_r])
        C.dram = CD
        C.cst_r = C.ident_r
        for nm, shp, dt in [("rm", (32,), BF16), ("maskc", (128,), BF16), ("maskp", (128,), BF16), ("maskn", (128,), BF16),
                            ("tri", (128,), F32), ("ones_f", (128,), F32), ("identf", (128,), F32), ("ones_bf", (128,), BF16)]:
            v = top.alloc(shp, dt)
            if nm == "rm":
                v = v[:32]
            setattr(C, nm, v)
            P.dma("sp", v, CD["c_" + nm], "cst", writes=[C.cst_r])
        C.stage = [top.alloc((1024,), F32) for _ in range(NST)]
        C.stage_r = [Reg() for _ in range(NST)]
        C.stage_i = 0
        C.A = Arena(arena, top.off, ARENA_BYTES)

        if stage == "attn":
            A = C.A.child()
            hT = A.alloc((KC, S), BF16)
            attT = A.alloc((4, S), BF16)
            C.attT_r = Reg()
            mk = Arena(arena, A.off, ARENA_BYTES)
            norm_transpose(C, mk, x, W["mix_pre_g"][0], hT)
            attention_phase(C, Arena(arena, A.off, ARENA_BYTES), hT, attT, W["w_in"][0])
            P.barrier()
            ov = out.rearrange("(a b) d -> a (b d)", a=1024).rearrange("(k p) t -> p k t", p=128)
            for k in range(4):
                for hh in range(2):
                    P.dma("pool", ov[:, k, hh * 1024:(hh + 1) * 1024], attT[:, k, hh * 1024:(hh + 1) * 1024], "dbg")
        if stage == "full":
            ffn_phase(C, x, x1, W["ffn1_pre_g"][0], W["ffn1_w_gate"][0], W["ffn1_w_up"][0], W["ffn1_w_down"][0],
                      W["ffn1_post_g"][0])
            mixer_phase(C, x1, x2, W)
            ffn_phase(C, x2, out, W["ffn2_pre_g"][0], W["ffn2_w_gate"][0], W["ffn2_w_up"][0], W["ffn2_w_down"][0],
                      W["ffn2_post_g"][0])
        if stage == "mix":
            mixer_phase(C, x, out, W)
        if stage == "ml":
            A = C.A.child()
            hT = A.alloc((KC, S), BF16)
            mlT = A.alloc((8, S), BF16)
            C.mlT_r = Reg()
            mk = Arena(arena, A.off, ARENA_BYTES)
            norm_transpose(C, mk, x, W["mix_pre_g"][0], hT)
            mlstm_phase(C, Arena(arena, A.off, ARENA_BYTES), hT, mlT, W["w_in"][0], W["conv_w"][0], W["conv_b"][0],
                        W["mlstm_i_bias"][0], W["mlstm_f_bias"][0], W["mlstm_head_g"][0])
            P.barrier()
            ov = out.rearrange("(a b) d -> a (b d)", a=1024).rearrange("(k p) t -> p k t", p=128)
            for k in range(8):
                for hh in range(2):
                    P.dma("pool", ov[:, k, hh * 1024:(hh + 1) * 1024], mlT[:, k, hh * 1024:(hh + 1) * 1024], "dbg")
        if stage == "ffn1a":
            A = C.A.child()
            hT_ar = A.sub(KC * S * 2)
            actT_ar = A.sub(FC * S * 2)
            hT = hT_ar.child().alloc((KC, S), BF16)
            norm_transpose(C, actT_ar.child(), x, W["ffn1_pre_g"][0], hT)
            ov = out.rearrange("(a b) d -> a (b d)", a=1024).rearrange("(k p) t -> p k t", p=128)
            for k in range(KC):
                for hh in range(2):
                    P.dma("pool", ov[:, k, hh * 1024:(hh + 1) * 1024], hT[:, k, hh * 1024:(hh + 1) * 1024], "dbg")
        if stage == "ffn1b":
            ffn_phase(C, x, out, W["ffn1_pre_g"][0], W["ffn1_w_gate"][0], W["ffn1_w_up"][0], W["ffn1_w_down"][0],
                      W["ffn1_post_g"][0], stop_after="B")
        if stage == "ffn1":
            ffn_phase(C, x, out, W["ffn1_pre_g"][0], W["ffn1_w_gate"][0], W["ffn1_w_up"][0], W["ffn1_w_down"][0],
                      W["ffn1_post_g"][0])
        P.finish()
        P.emit()
        print("prog stats", P.stats, "sems", len(P.dma_tot) + 5)
        if os.environ.get("DUMP"):
            for e in ("sp", "dve", "act"):
                print("====", e)
                for r in P.dump[e][-int(os.environ["DUMP"]):]:
                    print(r)
    return nc


_NC_CACHE = {}


def kernel(**inputs):
    stage = inputs.pop("_stage", os.environ.get("KSTAGE", "full"))
    if stage not in _NC_CACHE:
        _NC_CACHE[stage] = build(stage)
    nc = _NC_CACHE[stage]
    consts = host_consts()
    xfull = np.ascontiguousarray(inputs["x"], dtype=np.float32)
    shared = {k: np.ascontiguousarray(v, dtype=np.float32) for k, v in inputs.items() if k != "x"}
    for k, v in consts.items():
        shared["c_" + k] = v
    in_maps = []
    ncores = int(os.environ.get("NCORES", 8))
    for b in range(ncores):
        m = dict(shared)
        m["x"] = xfull[b]
        in_maps.append(m)
    res = run_bass_kernel_spmd(nc, in_maps, core_ids=list(range(ncores)))
    return np.stack([r["out"] for r in res.results], axis=0)
```

```python
from contextlib import ExitStack
import math
import os

import numpy as np
import ml_dtypes
import concourse.bass as bass
import concourse.mybir as mybir
from concourse.bass_utils import run_bass_kernel_spmd

F32 = mybir.dt.float32
BF16 = mybir.dt.bfloat16
AF = mybir.ActivationFunctionType
ALU = mybir.AluOpType
AX = mybir.AxisListType

S = 2048
D = 1024
FF = 2816
NT = S // 128
KC = D // 128
FC = FF // 128
IN_W = 10760
EPS = 1e-6
ENGS = ("pe", "act", "dve", "pool", "sp")


class Reg:
    __slots__ = ("name", "w", "rs", "rd", "excl")

    def __init__(self, name="", excl=False):
        self.name = name
        self.excl = excl
        self.w = None
        self.rs = {}
        self.rd = []


class Ins:
    __slots__ = ("eng", "fn", "deps", "signal", "val", "dma", "key")

    def __init__(self, eng, fn, dma=False, key=None):
        self.eng = eng
        self.fn = fn
        self.deps = ()
        self.signal = dma
        self.val = 0
        self.dma = dma
        self.key = key


class Prog:
    def __init__(self, nc):
        self.nc = nc
        self.engs = {e: [] for e in ENGS}
        self.dma_tot = {}
        self.dma_last = {}

    def _add(self, ins, reads, writes):
        eng = ins.eng
        deps = {}
        for r in reads:
            d = r.w
            if d is not None:
                deps[id(d)] = d
            if r.excl:
                for e2, x in r.rs.items():
                    if e2 != eng:
                        deps[id(x)] = x
        for w in writes:
            d = w.w
            if d is not None:
                deps[id(d)] = d
            for e2, x in w.rs.items():
                if (not ins.dma) and e2 == eng:
                    continue
                deps[id(x)] = x
            for x in w.rd:
                deps[id(x)] = x
        out = []
        for d in deps.values():
            if d is ins:
                continue
            if (not d.dma) and (not ins.dma) and d.eng == "pe" and eng == "pe":
                continue
            d.signal = True
            out.append(d)
        ins.deps = out
        for r in reads:
            if ins.dma:
                r.rd.append(ins)
            else:
                r.rs[eng] = ins
        for w in writes:
            w.w = ins
            w.rs = {}
            w.rd = []
        self.engs[eng].append(ins)
        return ins

    def op(self, eng, fn, reads=(), writes=()):
        return self._add(Ins(eng, fn), reads, writes)

    def dma(self, queue, out, in_, key, reads=(), writes=(), **kw):
        ins = Ins(queue, lambda e: e.dma_start(out=out, in_=in_, **kw), dma=True, key=key)
        self.dma_tot[key] = self.dma_tot.get(key, 0) + 16
        ins.val = self.dma_tot[key]
        self.dma_last[key] = ins
        return self._add(ins, reads, writes)

    def barrier(self):
        lasts = []
        for e in ENGS:
            for ins in reversed(self.engs[e]):
                if ins.fn is not None and not ins.dma:
                    ins.signal = True
                    lasts.append(ins)
                    break
        lasts += list(self.dma_last.values())
        for e in ENGS:
            ins = Ins(e, None)
            ins.deps = [d for d in lasts if d.dma or d.eng != e]
            self.engs[e].append(ins)

    def finish(self):
        ins = Ins("sp", None)
        ins.deps = list(self.dma_last.values())
        self.engs["sp"].append(ins)

    def emit(self):
        nc = self.nc
        for e in ENGS:
            c = 0
            for ins in self.engs[e]:
                if ins.dma:
                    continue
                if ins.signal and ins.fn is not None:
                    c += 1
                    ins.val = c
        with ExitStack() as st:
            sems = {e: st.enter_context(nc.semaphore(f"s_{e}")) for e in ENGS}
            dsem = {k: st.enter_context(nc.semaphore(f"d_{k}")) for k in self.dma_tot}
            block = st.enter_context(nc.Block())
            bname = {"pe": "tensor", "act": "scalar", "dve": "vector", "pool": "gpsimd", "sp": "sync"}
            stats = {}
            self.dump = {}
            for e in ENGS:
                def body(engine, e=e):
                    seen = {}
                    nw = 0
                    for ins in self.engs[e]:
                        need = {}
                        for d in ins.deps:
                            s = ("d", d.key) if d.dma else ("c", d.eng)
                            if need.get(s, 0) < d.val:
                                need[s] = d.val
                        for s, v in need.items():
                            if seen.get(s, 0) < v:
                                seen[s] = v
                                sh = dsem[s[1]] if s[0] == "d" else sems[s[1]]
                                engine.wait_ge(sh, v)
                                nw += 1
                        if os.environ.get("DUMP"):
                            self.dump.setdefault(e, []).append((sorted((k, v) for k, v in need.items()), ins.fn is not None, ins.dma, ins.key, ins.signal, ins.val))
                        if ins.fn is not None:
                            bi = ins.fn(engine)
                            if ins.dma:
                                bi.then_inc(dsem[ins.key], 16)
                            elif ins.signal:
                                bi.then_inc(sems[e], 1)
                    stats[e] = (len(self.engs[e]), nw)
                getattr(block, bname[e])(body)
            self.stats = stats


class Arena:
    def __init__(self, ap, start, end):
        self.ap = ap
        self.start = start
        self.off = start
        self.end = end

    def alloc(self, free_shape, dt, parts=128):
        n = 1
        for v in free_shape:
            n *= v
        esz = 4 if dt == F32 else 2
        nbytes = n * esz
        st = (self.off + 63) // 64 * 64
        assert st + nbytes <= self.end, f"arena overflow: need {st + nbytes} > {self.end}"
        self.off = st + nbytes
        Arena.last = (st, tuple(free_shape), dt)
        v = self.ap[:parts, st // 2:(st + nbytes) // 2]
        if dt == F32:
            v = v.bitcast(F32)
        if len(free_shape) == 2:
            v = v.rearrange("p (a b) -> p a b", a=free_shape[0])
        elif len(free_shape) == 3:
            v = v.rearrange("p (a b c) -> p a b c", a=free_shape[0], b=free_shape[1])
        return v

    def sub(self, nbytes):
        st = (self.off + 63) // 64 * 64
        assert st + nbytes <= self.end, f"arena overflow(sub): need {st + nbytes} > {self.end}"
        self.off = st + nbytes
        return Arena(self.ap, st, st + nbytes)

    def child(self):
        return Arena(self.ap, self.start, self.end)


class Ctx:
    pass


DBG = {}


NST = 3


def load_cast(C, dst, src, dst_reg):
    P = C.P
    s = C.stage_i % NST
    C.stage_i += 1
    sh = src.shape
    n = 1
    for v in sh[1:]:
        n *= v
    assert n <= 1024
    stg = C.stage[s][:, :n]
    if len(sh) == 3:
        stg = stg.rearrange("p (a b) -> p a b", a=sh[1])
    P.dma("sp", stg, src, f"st{s}", writes=[C.stage_r[s]])
    P.op("pool", lambda e: e.tensor_copy(out=dst, in_=stg), reads=[C.stage_r[s]], writes=[dst_reg])


def norm_transpose(C, A_stage, x_src, g_pre_dram, hT):
    P, PS, psr = C.P, C.PS, C.psr
    gb = A_stage.alloc((D,), F32)
    gb_r = Reg()
    P.dma("sp", gb, g_pre_dram.partition_broadcast(128), "g", writes=[gb_r])
    ss = A_stage.alloc((NT,), F32)
    rstd = A_stage.alloc((NT,), F32)
    junk = A_stage.alloc((D,), BF16)
    junk_r = Reg()
    hb = [A_stage.alloc((D,), BF16) for _ in range(2)]
    hb_r = [Reg() for _ in range(2)]
    xs = [A_stage.alloc((D,), F32) for _ in range(NT)]
    xs_r = [Reg() for _ in range(NT)]
    ss_r = [Reg() for _ in range(NT)]
    rstd_r = Reg()
    for i in range(NT):
        P.dma("sp", xs[i], x_src[i * 128:(i + 1) * 128, :], f"xs{i}", writes=[xs_r[i]])
    for i in range(NT):
        P.op("act", lambda e, i=i: e.activation(out=junk, in_=xs[i], func=AF.Square, accum_out=ss[:, i:i + 1]),
             reads=[xs_r[i]], writes=[ss_r[i], junk_r])
    P.op("act", lambda e: e.activation(out=rstd, in_=ss, func=AF.Ln, scale=1.0 / D, bias=EPS),
         reads=ss_r, writes=[rstd_r])
    P.op("act", lambda e: e.activation(out=rstd, in_=rstd, func=AF.Exp, scale=-0.5),
         reads=[rstd_r], writes=[rstd_r])
    for i in range(NT):
        s = i % 2
        P.op("dve", lambda e, i=i, s=s: e.scalar_tensor_tensor(out=hb[s], in0=xs[i], scalar=rstd[:, i:i + 1], in1=gb,
                                                              op0=ALU.mult, op1=ALU.mult),
             reads=[xs_r[i], rstd_r, gb_r], writes=[hb_r[s]])
        b = i % 2
        psb = PS[:, 512 * b:512 * (b + 1)].bitcast(BF16)
        for k in range(KC):
            P.op("pe", lambda e, k=k, s=s, psb=psb: e.transpose(out=psb[:, k * 128:(k + 1) * 128],
                                                                in_=hb[s][:, k * 128:(k + 1) * 128], identity=C.ident),
                 reads=[hb_r[s], C.ident_r], writes=[psr[b]])
        P.op("act", lambda e, i=i, psb=psb: e.activation(out=hT[:, :, i * 128:(i + 1) * 128],
                                                         in_=psb.rearrange("p (k n) -> p k n", k=KC), func=AF.Copy),
             reads=[psr[b]], writes=[])
    P.barrier()


def ffn_phase(C, x_src, x_dst, g_pre, wg, wu, wd, g_post, stop_after=None):
    P, PS, psr = C.P, C.PS, C.psr
    A = C.A.child()
    hT_ar = A.sub(KC * S * 2)
    actT_ar = A.sub(FC * S * 2)
    hT = hT_ar.child().alloc((KC, S), BF16)
    actT = actT_ar.child().alloc((FC, S), BF16)
    norm_transpose(C, actT_ar.child(), x_src, g_pre, hT)

    NSL = 2
    wg_s = [A.alloc((KC, 256), BF16) for _ in range(NSL)]
    wu_s = [A.alloc((KC, 256), BF16) for _ in range(NSL)]
    wg_r = [[Reg(), Reg()] for _ in range(NSL)]
    wu_r = [[Reg(), Reg()] for _ in range(NSL)]
    wd_h = [A.alloc((FC, 512), BF16) for _ in range(2)]
    wd_r = [[Reg() for _ in range(FC // 2)] for _ in range(2)]
    sg = [A.alloc((512,), BF16) for _ in range(2)]
    sg_r = [Reg() for _ in range(2)]
    gb = A.alloc((D,), F32)
    gb_r = Reg()
    ss2 = A.alloc((NT,), F32)
    r2 = A.alloc((NT,), F32)
    junk = A.alloc((D,), BF16)
    junk_r = Reg()
    P.dma("sp", gb, g_post.partition_broadcast(128), "g", writes=[gb_r])

    wd_jobs = [(h, c2) for h in range(2) for c2 in range(FC // 2)]

    def load_wd_piece():
        if not wd_jobs:
            return
        h, c2 = wd_jobs.pop(0)
        load_cast(C, wd_h[h][:, 2 * c2:2 * c2 + 2, :],
                  wd[c2 * 256:(c2 + 1) * 256, h * 512:(h + 1) * 512].rearrange("(c p) n -> p c n", p=128),
                  wd_r[h][c2])

    j = 0
    for cb in range(FC // 2):
        s = cb % NSL
        for part in range(2):
            load_cast(C, wg_s[s][:, 4 * part:4 * part + 4, :],
                      wg[part * 512:(part + 1) * 512, cb * 256:(cb + 1) * 256].rearrange("(k p) n -> p k n", p=128),
                      wg_r[s][part])
            load_cast(C, wu_s[s][:, 4 * part:4 * part + 4, :],
                      wu[part * 512:(part + 1) * 512, cb * 256:(cb + 1) * 256].rearrange("(k p) n -> p k n", p=128),
                      wu_r[s][part])
        if cb >= 1:
            for _ in range(3):
                load_wd_piece()
        for sub in range(2):
            ffc = cb * 2 + sub
            for tg in range(4):
                bG = 2 * (j % 4)
                bU = bG + 1
                q = j % 2
                j += 1
                for k in range(KC):
                    P.op("pe", lambda e, k=k, s=s, sub=sub, tg=tg, bG=bG: e.matmul(
                        PS[:, 512 * bG:512 * (bG + 1)], lhsT=wg_s[s][:, k, sub * 128:(sub + 1) * 128],
                        rhs=hT[:, k, tg * 512:(tg + 1) * 512], start=(k == 0), stop=(k == KC - 1)),
                        reads=[wg_r[s][k // 4]], writes=[psr[bG]])
                for k in range(KC):
                    P.op("pe", lambda e, k=k, s=s, sub=sub, tg=tg, bU=bU: e.matmul(
                        PS[:, 512 * bU:512 * (bU + 1)], lhsT=wu_s[s][:, k, sub * 128:(sub + 1) * 128],
                        rhs=hT[:, k, tg * 512:(tg + 1) * 512], start=(k == 0), stop=(k == KC - 1)),
                        reads=[wu_r[s][k // 4]], writes=[psr[bU]])
                P.op("act", lambda e, q=q, bG=bG: e.activation(out=sg[q], in_=PS[:, 512 * bG:512 * (bG + 1)], func=AF.Silu),
                     reads=[psr[bG]], writes=[sg_r[q]])
                P.op("dve", lambda e, q=q, bU=bU, ffc=ffc, tg=tg: e.tensor_tensor(
                    out=actT[:, ffc, tg * 512:(tg + 1) * 512], in0=PS[:, 512 * bU:512 * (bU + 1)], in1=sg[q], op=ALU.mult),
                    reads=[psr[bU], sg_r[q]], writes=[])
    while wd_jobs:
        load_wd_piece()
    P.barrier()
    if stop_after == "B":
        ov = x_dst.rearrange("(a b) d -> a (b d)", a=1024).rearrange("(k p) t -> p k t", p=128)
        for k in range(KC):
            for hh in range(2):
                P.dma("pool", ov[:, k, hh * 1024:(hh + 1) * 1024], actT[:, k + 14, hh * 1024:(hh + 1) * 1024], "dbg")
        return

    Ah = hT_ar.child()
    NSC = int(os.environ.get('NSC', 2))
    NSX = int(os.environ.get('NSX', NSC))
    xc = [Ah.alloc((D,), F32) for _ in range(NSX)]
    tt = [Ah.alloc((D,), F32) for _ in range(NSC)]
    xc_r = [Reg() for _ in range(NSX)]
    tt_r = [Reg() for _ in range(NSC)]
    tth_r = [[Reg(), Reg()] for _ in range(NSC)]
    r2_r = [Reg() for _ in range(NT)]
    for i in range(int(os.environ.get("CT", NT))):
        s = i % NSC
        sx = i % NSX
        P.dma("sp", xc[sx], x_src[i * 128:(i + 1) * 128, :], f"xc{sx}", writes=[xc_r[sx]])
        pb = (i % 4) * 2
        psf = PS[:, 512 * pb:512 * (pb + 2)]
        for h in range(2):
            for ffc in range(FC):
                P.op("pe", lambda e, h=h, ffc=ffc, i=i, pb=pb: e.matmul(
                    PS[:, 512 * (pb + h):512 * (pb + h + 1)], lhsT=actT[:, ffc, i * 128:(i + 1) * 128],
                    rhs=wd_h[h][:, ffc, :], start=(ffc == 0), stop=(ffc == FC - 1)),
                    reads=[wd_r[h][ffc // 2]], writes=[psr[pb + h]])
        P.op("act", lambda e, i=i, psf=psf: e.activation(out=junk, in_=psf, func=AF.Square, accum_out=ss2[:, i:i + 1]),
             reads=[psr[pb], psr[pb + 1]], writes=[r2_r[i], junk_r])
        cstop = int(os.environ.get("CSTOP", 9))
        if cstop == 1:
            P.dma("sp", x_dst[i * 128:(i + 1) * 128, :], xc[sx], f"xo{s}", reads=[xc_r[sx], r2_r[i]])
            continue
        P.op("act", lambda e, i=i: e.activation(out=r2[:, i:i + 1], in_=ss2[:, i:i + 1], func=AF.Ln, scale=1.0 / D, bias=EPS),
             reads=[r2_r[i]], writes=[r2_r[i]])
        P.op("act", lambda e, i=i: e.activation(out=r2[:, i:i + 1], in_=r2[:, i:i + 1], func=AF.Exp, scale=-0.5,
                                                bias=math.log(0.5)),
             reads=[r2_r[i]], writes=[r2_r[i]])
        if cstop == 2:
            P.dma("sp", x_dst[i * 128:(i + 1) * 128, :], xc[sx], f"xo{s}", reads=[xc_r[sx], r2_r[i]])
            continue
        for h in range(2):
            if os.environ.get("DVEVAR") == "copy":
                P.op("dve", lambda e, s=s, h=h, pb=pb: e.tensor_copy(
                    out=tt[s][:, 512 * h:512 * (h + 1)], in_=PS[:, 512 * (pb + h):512 * (pb + h + 1)]),
                    reads=[psr[pb + h], gb_r], writes=[tth_r[s][h]])
                continue
            if os.environ.get("DVEVAR") == "sbuf":
                P.op("dve", lambda e, s=s, h=h, pb=pb: e.tensor_tensor(
                    out=tt[s][:, 512 * h:512 * (h + 1)], in0=xc[sx][:, 512 * h:512 * (h + 1)],
                    in1=gb[:, 512 * h:512 * (h + 1)], op=ALU.mult),
                    reads=[psr[pb + h], gb_r, xc_r[sx]], writes=[tth_r[s][h]])
                continue
            P.op("dve", lambda e, s=s, h=h, pb=pb: e.tensor_tensor(
                out=tt[s][:, 512 * h:512 * (h + 1)], in0=PS[:, 512 * (pb + h):512 * (pb + h + 1)],
                in1=gb[:, 512 * h:512 * (h + 1)], op=ALU.mult),
                reads=[psr[pb + h], gb_r, r2_r[i]], writes=[tth_r[s][h]])
        if cstop == 3:
            P.dma("sp", x_dst[i * 128:(i + 1) * 128, :], tt[s], f"xo{s}", reads=[xc_r[sx], r2_r[i], tth_r[s][0], tth_r[s][1]], writes=[tth_r[s][0], tth_r[s][1]])
            continue
        P.op("dve", lambda e, s=s, sx=sx, i=i: e.scalar_tensor_tensor(out=xc[sx], in0=tt[s], scalar=r2[:, i:i + 1], in1=xc[sx],
                                                              op0=ALU.mult, op1=ALU.add),
             reads=[tth_r[s][0], tth_r[s][1], r2_r[i], xc_r[sx]], writes=[xc_r[sx]])
        P.dma("sp", x_dst[i * 128:(i + 1) * 128, :], xc[sx], f"xo{sx}", reads=[xc_r[sx]])
    P.barrier()


def tok_slice(start, step):
    return slice(start, start + step * 127 + 1, step) if step > 1 else slice(start, start + 128)


def attention_phase(C, A, hT, attT, w_in):
    P, PS, psr = C.P, C.PS, C.psr
    cos = A.alloc((S,), F32)[:32]
    sin = A.alloc((S,), F32)[:32]
    cs_r = Reg()
    cs2_r = Reg()
    P.dma("sp", cos, C.dram["c_cos"], "cs", writes=[cs_r])
    P.dma("sp", sin, C.dram["c_sin"], "cs", writes=[cs2_r])
    wsl = [[A.alloc((KC, 128), BF16) for _ in range(3)] for _ in range(2)]
    wsl_r = [[Reg() for _ in range(3)] for _ in range(2)]
    DBG.clear()
    qk = [[None, None], [None, None]]
    for a_ in range(2):
        for b_ in range(2):
            qk[a_][b_] = A.alloc((S,), BF16)
            DBG[f"qk{a_}{b_}"] = Arena.last
    qk_r = [[[Reg() for _ in range(4)] for _ in range(2)] for _ in range(2)]
    vt = [A.alloc((NT, 128), BF16) for _ in range(2)]
    vt_r = [[Reg() for _ in range(4)] for _ in range(2)]
    acc_n = A.alloc((S,), F32)
    acc_d = A.alloc((S,), F32)
    accn_r = [Reg() for _ in range(4)]
    accd_r = [Reg() for _ in range(4)]
    acc_all = Reg()
    pT = [A.alloc((512,), BF16) for _ in range(4)]
    pT_r = [Reg() for _ in range(4)]
    t1 = [A.alloc((512,), F32)[:32] for _ in range(2)]
    t2 = [A.alloc((512,), F32)[:32] for _ in range(2)]
    t1_r = [Reg() for _ in range(2)]
    t2_r = [Reg() for _ in range(2)]
    rcp = [A.alloc((512,), F32) for _ in range(2)]
    rcp_r = [Reg() for _ in range(2)]
    SCALE = 128.0 ** -0.5

    def bank(b):
        return PS[:, 512 * b:512 * (b + 1)]

    cnt = {"proj": 0, "rot": 0, "s": 0, "p": 0, "nd": 0, "t": 0, "it": 0}
    for hs in range(4):
        for g in range(3):
            sl = cnt["it"] % 2
            cnt["it"] += 1
            head = g * 4 + hs
            dil = (1, 4, 16)[g]
            for m, off in enumerate((0, 1536, 3072)):
                c0 = off + head * 128
                load_cast(C, wsl[sl][m], w_in[:, c0:c0 + 128].rearrange("(k p) n -> p k n", p=128), wsl_r[sl][m])
            for m in range(2):
                dst = qk[sl][m]
                for tg in range(4):
                    pb = cnt["proj"] % 2
                    cnt["proj"] += 1
                    for k in range(KC):
                        P.op("pe", lambda e, k=k, m=m, tg=tg, pb=pb, sl=sl: e.matmul(
                            bank(pb), lhsT=wsl[sl][m][:, k, :], rhs=hT[:, k, tg * 512:(tg + 1) * 512],
                            start=(k == 0), stop=(k == KC - 1)), reads=[wsl_r[sl][m]], writes=[psr[pb]])
                    dcol = dst[:, tg * 512:(tg + 1) * 512]
                    dr = qk_r[sl][m][tg]
                    P.op("act", lambda e, dcol=dcol, pb=pb: e.activation(out=dcol, in_=bank(pb), func=AF.Copy),
                         reads=[psr[pb]], writes=[dr])
                    rb = 2 + cnt["rot"] % 2
                    ts = cnt["rot"] % 2
                    cnt["rot"] += 1
                    P.op("pe", lambda e, dcol=dcol, rb=rb: e.matmul(bank(rb)[:32, :], lhsT=C.rm, rhs=dcol[:32, :],
                                                                   start=True, stop=True),
                         reads=[dr, C.cst_r], writes=[psr[rb]])
                    P.op("dve", lambda e, rb=rb, ts=ts, tg=tg: e.tensor_tensor(
                        out=t1[ts], in0=bank(rb)[:32, :], in1=sin[:, tg * 512:(tg + 1) * 512], op=ALU.mult),
                        reads=[psr[rb], cs_r, cs2_r], writes=[t1_r[ts]])
                    P.op("dve", lambda e, dcol=dcol, ts=ts, tg=tg: e.tensor_tensor(
                        out=t2[ts], in0=dcol[:32, :], in1=cos[:, tg * 512:(tg + 1) * 512], op=ALU.mult),
                        reads=[dr, cs_r], writes=[t2_r[ts]])
                    P.op("dve", lambda e, dcol=dcol, ts=ts: e.tensor_tensor(out=dcol[:32, :], in0=t1[ts], in1=t2[ts], op=ALU.add),
                         reads=[t1_r[ts], t2_r[ts]], writes=[dr])
            blocks = []
            if g == 0:
                for b in range(16):
                    blocks.append((tok_slice(128 * b, 1), tok_slice(128 * (b - 1), 1) if b > 0 else None))
            elif g == 1:
                for r in range(4):
                    for b in range(4):
                        blocks.append((tok_slice(4 * 128 * b + r, 4), tok_slice(4 * 128 * (b - 1) + r, 4) if b > 0 else None))
            else:
                for r in range(16):
                    blocks.append((tok_slice(r, 16), None))
            for j in range(4):
                pb = cnt["proj"] % 2
                cnt["proj"] += 1
                for bi in range(4):
                    qs = blocks[4 * j + bi][0]
                    for k in range(KC):
                        P.op("pe", lambda e, k=k, qs=qs, pb=pb, bi=bi, sl=sl: e.matmul(
                            bank(pb)[:, bi * 128:(bi + 1) * 128], lhsT=hT[:, k, qs], rhs=wsl[sl][2][:, k, :],
                            start=(k == 0), stop=(k == KC - 1)), reads=[wsl_r[sl][2]], writes=[psr[pb]])
                P.op("act", lambda e, j=j, pb=pb, sl=sl: e.activation(
                    out=vt[sl][:, 4 * j:4 * j + 4, :], in_=bank(pb).rearrange("p (a b) -> p a b", a=4), func=AF.Copy),
                    reads=[psr[pb]], writes=[vt_r[sl][j]])
            qT, kT = qk[sl][0], qk[sl][1]
            qr = qk_r[sl][0] + qk_r[sl][1]
            pairs = [(2 * i, 2 * i + 1) for i in range(8)]

            def do_qk(pair):
                sb = 4 + cnt["s"] % 2
                cnt["s"] += 1
                for ti, blk in enumerate(pair):
                    qs, ps_ = blocks[blk]
                    for ci, ks in enumerate((ps_, qs)):
                        o = bank(sb)[:, (2 * ti + ci) * 128:(2 * ti + ci + 1) * 128]
                        if ks is None:
                            P.op("pe", lambda e, o=o: e.matmul(o, lhsT=C.ident, rhs=C.maskn, start=True, stop=True),
                                 reads=[C.cst_r], writes=[psr[sb]])
                            continue
                        P.op("pe", lambda e, o=o, ks=ks, qs=qs, kT=kT, qT=qT: e.matmul(o, lhsT=kT[:, ks], rhs=qT[:, qs], start=True, stop=False),
                             reads=qr, writes=[psr[sb]])
                        mk = C.maskc if ci == 1 else C.maskp
                        P.op("pe", lambda e, o=o, mk=mk: e.matmul(o, lhsT=C.ident, rhs=mk, start=False, stop=True),
                             reads=[C.cst_r], writes=[psr[sb]])
                pslot = cnt["p"] % 4
                cnt["p"] += 1
                P.op("act", lambda e, sb=sb, pslot=pslot: e.activation(out=pT[pslot], in_=bank(sb), func=AF.Exp, scale=SCALE),
                     reads=[psr[sb]], writes=[pT_r[pslot]])
                return pslot

            def do_pv(pair, pslot):
                nb = 6 + cnt["nd"] % 2
                cnt["nd"] += 1
                for ti, blk in enumerate(pair):
                    qs, ps_ = blocks[blk]
                    for is_num in (True, False):
                        col = ti * 128 + (0 if is_num else 256)
                        o = bank(nb)[:, col:col + 128]
                        for ci in range(2):
                            kblk = blk if (ci == 1 or ps_ is None) else blk - 1
                            lhs = vt[sl][:, kblk, :] if is_num else C.ones_bf
                            P.op("pe", lambda e, o=o, lhs=lhs, pslot=pslot, ti=ti, ci=ci: e.matmul(
                                o, lhsT=lhs, rhs=pT[pslot][:, (2 * ti + ci) * 128:(2 * ti + ci + 1) * 128],
                                start=(ci == 0), stop=(ci == 1)),
                                reads=[pT_r[pslot], vt_r[sl][kblk // 4], C.cst_r], writes=[psr[nb]])
                pi = pair[0] // 2
                if g == 0:
                    sel = slice(256 * pi, 256 * pi + 256)
                    dn, dd = acc_n[:, sel], acc_d[:, sel]
                    sn, sd = bank(nb)[:, 0:256], bank(nb)[:, 256:512]
                elif g == 1:
                    r_, b_ = pair[0] // 4, pair[0] % 4
                    st_ = r_ + 512 * b_
                    sel = slice(st_, st_ + 4 * 255 + 1, 4)
                    dn, dd = acc_n[:, sel], acc_d[:, sel]
                    sn, sd = bank(nb)[:, 0:256], bank(nb)[:, 256:512]
                else:
                    r_ = pair[0]
                    dn = acc_n.rearrange("p (n r) -> p r n", r=16)[:, r_:r_ + 2, :]
                    dd = acc_d.rearrange("p (n r) -> p r n", r=16)[:, r_:r_ + 2, :]
                    sn = bank(nb)[:, 0:256].rearrange("p (a b) -> p a b", a=2)
                    sd = bank(nb)[:, 256:512].rearrange("p (a b) -> p a b", a=2)
                if g == 0:
                    P.op("dve", lambda e, dn=dn, sn=sn: e.tensor_copy(out=dn, in_=sn), reads=[psr[nb]], writes=[acc_all])
                    P.op("dve", lambda e, dd=dd, sd=sd: e.tensor_copy(out=dd, in_=sd), reads=[psr[nb]], writes=[acc_all])
                else:
                    P.op("dve", lambda e, dn=dn, sn=sn: e.tensor_tensor(out=dn, in0=sn, in1=dn, op=ALU.add),
                         reads=[psr[nb], acc_all], writes=[acc_all])
                    P.op("dve", lambda e, dd=dd, sd=sd: e.tensor_tensor(out=dd, in0=sd, in1=dd, op=ALU.add),
                         reads=[psr[nb], acc_all], writes=[acc_all])

            prev = None
            for pair in pairs:
                pslot = do_qk(pair)
                if prev is not None:
                    do_pv(*prev)
                prev = (pair, pslot)
            do_pv(*prev)
        for tg in range(4):
            rs = tg % 2
            P.op("dve", lambda e, tg=tg, rs=rs: e.reciprocal(out=rcp[rs], in_=acc_d[:, tg * 512:(tg + 1) * 512]),
                 reads=[acc_all], writes=[rcp_r[rs]])
            P.op("dve", lambda e, tg=tg, rs=rs, hs=hs: e.tensor_tensor(
                out=attT[:, hs, tg * 512:(tg + 1) * 512], in0=acc_n[:, tg * 512:(tg + 1) * 512], in1=rcp[rs], op=ALU.mult),
                reads=[acc_all, rcp_r[rs]], writes=[C.attT_r])


def mlstm_phase(C, A, hT, mlT, w_in, conv_w, conv_b, i_bias, f_bias, head_g):
    P, PS, psr = C.P, C.PS, C.psr
    OQ, OK_, OV, OO, OI = 4608, 5632, 6656, 7680, 8704

    def bank(b):
        return PS[:, 512 * b:512 * (b + 1)]

    def R():
        return Reg()

    cwb = A.alloc((2048,), F32)[:5]
    cwb_r = R()
    cwb2_r = R()
    P.dma("sp", cwb[0:4, :], conv_w, "mca", writes=[cwb_r])
    P.dma("sp", cwb[4:5, :], conv_b.rearrange("(o n) -> o n", o=1), "mca", writes=[cwb2_r])
    cwT = A.alloc((16, 8), F32)
    ncb = A.alloc((16,), F32)
    cwT_r = R()
    for c in range(16):
        P.op("pe", lambda e, c=c: e.matmul(bank(6)[:, c * 8:c * 8 + 5], lhsT=cwb[:, c * 128:(c + 1) * 128],
                                           rhs=C.identf[:5, :5], start=True, stop=True),
             reads=[cwb_r, cwb2_r, C.cst_r], writes=[psr[6]])
    P.op("dve", lambda e: e.tensor_copy(out=cwT[:, :, 0:5], in_=bank(6)[:, 0:128].rearrange("p (c j) -> p c j", j=8)[:, :, 0:5]),
         reads=[psr[6]], writes=[cwT_r])
    P.op("dve", lambda e: e.tensor_scalar(out=ncb, in0=cwT[:, :, 4], scalar1=-1.0, scalar2=None, op0=ALU.mult),
         reads=[cwT_r], writes=[cwT_r])
    hg = A.alloc((1024,), F32)
    hg_r = R()
    P.dma("sp", hg, head_g.partition_broadcast(128), "mch", writes=[hg_r])
    bias8 = A.alloc((8,), F32)
    b8_r = R()
    b8b_r = R()
    P.dma("sp", bias8[:, 0:4], i_bias.partition_broadcast(128), "mcb", writes=[b8_r])
    P.dma("sp", bias8[:, 4:8], f_bias.partition_broadcast(128), "mcb", writes=[b8b_r])

    wif = A.alloc((KC, 8), BF16)
    wif_r = R()
    load_cast(C, wif, w_in[:, OI:OI + 8].rearrange("(k p) n -> p k n", p=128), wif_r)
    for c in range(NT):
        for k in range(KC):
            P.op("pe", lambda e, c=c, k=k: e.matmul(bank(7)[:, c * 8:(c + 1) * 8], lhsT=hT[:, k, c * 128:(c + 1) * 128],
                                                    rhs=wif[:, k, :], start=(k == 0), stop=(k == KC - 1)),
                 reads=[wif_r], writes=[psr[7]])
    gi = A.alloc((NT, 4), F32)
    lg = A.alloc((NT, 4), F32)
    g_r = R()
    pre3 = bank(7)[:, 0:128].rearrange("p (c j) -> p c j", j=8)
    P.op("dve", lambda e: e.tensor_tensor(out=gi, in0=pre3[:, :, 0:4],
                                          in1=bias8[:, 0:4].unsqueeze(1).to_broadcast([128, NT, 4]), op=ALU.add),
         reads=[psr[7], b8_r, b8b_r], writes=[g_r])
    P.op("dve", lambda e: e.tensor_tensor(out=lg, in0=pre3[:, :, 4:8],
                                          in1=bias8[:, 4:8].unsqueeze(1).to_broadcast([128, NT, 4]), op=ALU.add),
         reads=[psr[7], b8_r, b8b_r, g_r], writes=[g_r])
    P.op("act", lambda e: e.activation(out=lg, in_=lg, func=AF.Exp, scale=-1.0), reads=[g_r], writes=[g_r])
    P.op("act", lambda e: e.activation(out=lg, in_=lg, func=AF.Ln, bias=1.0), reads=[g_r], writes=[g_r])
    lg2 = lg.rearrange("p c h -> p (c h)")
    gi2 = gi.rearrange("p c h -> p (c h)")
    P.op("pe", lambda e: e.matmul(bank(6)[:, 0:64], lhsT=C.tri, rhs=lg2, start=True, stop=True),
         reads=[g_r, C.cst_r], writes=[psr[6]])
    P.op("pe", lambda e: e.matmul(bank(6)[:, 64:128], lhsT=C.ones_f, rhs=lg2, start=True, stop=True),
         reads=[g_r, C.cst_r], writes=[psr[6]])
    e_in = A.alloc((64,), F32)
    e_out = A.alloc((64,), F32)
    e_L = A.alloc((64,), F32)
    e_v = A.alloc((64,), F32)
    ee_r = R()
    P.op("dve", lambda e: e.tensor_tensor(out=e_in, in0=bank(6)[:, 0:64], in1=gi2, op=ALU.add),
         reads=[psr[6], g_r], writes=[ee_r])
    P.op("act", lambda e: e.activation(out=e_in, in_=e_in, func=AF.Exp), reads=[ee_r], writes=[ee_r])
    P.op("act", lambda e: e.activation(out=e_out, in_=bank(6)[:, 0:64], func=AF.Exp, scale=-1.0),
         reads=[psr[6], ee_r], writes=[ee_r])
    P.op("act", lambda e: e.activation(out=e_L, in_=bank(6)[:, 64:128], func=AF.Exp, scale=-1.0),
         reads=[psr[6], ee_r], writes=[ee_r])
    P.op("dve", lambda e: e.tensor_scalar(out=e_in, in0=e_in, scalar1=1.0 / 16.0, scalar2=None, op0=ALU.mult),
         reads=[ee_r], writes=[ee_r])
    P.op("dve", lambda e: e.tensor_tensor(out=e_v, in0=e_in, in1=e_L, op=ALU.mult), reads=[ee_r], writes=[ee_r])

    wq = [A.alloc((KC, 256), BF16) for _ in range(4)]
    wq_r = [[R(), R()] for _ in range(4)]
    raw = [A.alloc((S + 3,), F32) for _ in range(2)]
    raw_r = [R() for _ in range(2)]
    acc = [A.alloc((S,), F32) for _ in range(2)]
    acc_r = [R() for _ in range(2)]
    for i in range(2):
        P.op("dve", lambda e, i=i: e.memset(raw[i][:, 0:3], 0.0), writes=[raw_r[i]])
    qT = A.alloc((2, S), BF16)
    kT = A.alloc((2, S), BF16)
    qk_r = [[R(), R()], [R(), R()]]
    ktok2 = [A.alloc((256,), BF16) for _ in range(2)]
    ktok2_r = [R(), R()]
    vaug2 = [A.alloc((258,), BF16) for _ in range(2)]
    vaug2_r = [R(), R()]
    for i in range(2):
        P.op("dve", lambda e, i=i: e.memset(vaug2[i][:, 256:257], 1.0), writes=[vaug2_r[i]])
        P.op("dve", lambda e, i=i: e.memset(vaug2[i][:, 257:258], 0.0), writes=[vaug2_r[i]])
    vp2 = [A.alloc((258,), BF16) for _ in range(2)]
    vp2_r = [R(), R()]
    sigo = A.alloc((NT, 256), BF16)
    sigo_r = [R() for _ in range(NT)]
    wT2 = [A.alloc((128,), BF16) for _ in range(2)]
    wT2_r = [R(), R()]
    Cf = A.alloc((2, 258), F32)
    Cf_r = R()
    Cbf = A.alloc((2, 258), BF16)
    Cbf_r = R()
    hu3 = [A.alloc((256,), F32) for _ in range(3)]
    hu3_r = [R(), R(), R()]
    sm3 = [A.alloc((8,), F32) for _ in range(3)]
    sm3_r = [R(), R(), R()]
    junk3 = [A.alloc((256,), BF16) for _ in range(3)]
    mlb3 = [A.alloc((256,), BF16) for _ in range(3)]
    mlb3_r = [R(), R(), R()]
    cj = 0
    for hd in range(4):
        for m, off in enumerate((OQ, OK_, OV, OO)):
            c0 = off + hd * 256
            for part in range(2):
                load_cast(C, wq[m][:, 4 * part:4 * part + 4, :],
                          w_in[part * 512:(part + 1) * 512, c0:c0 + 256].rearrange("(k p) n -> p k n", p=128), wq_r[m][part])
        for m, dstT in ((0, qT), (1, kT)):
            for cc in range(2):
                ch = m * 8 + hd * 2 + cc
                bi = cj % 2
                cj += 1
                for tg in range(4):
                    pb = tg % 2
                    for k in range(KC):
                        P.op("pe", lambda e, k=k, m=m, cc=cc, tg=tg, pb=pb: e.matmul(
                            bank(pb), lhsT=wq[m][:, k, cc * 128:(cc + 1) * 128], rhs=hT[:, k, tg * 512:(tg + 1) * 512],
                            start=(k == 0), stop=(k == KC - 1)), reads=[wq_r[m][k // 4]], writes=[psr[pb]])
                    P.op("act", lambda e, bi=bi, tg=tg, pb=pb: e.activation(
                        out=raw[bi][:, 3 + tg * 512:3 + (tg + 1) * 512], in_=bank(pb), func=AF.Copy),
                        reads=[psr[pb]], writes=[raw_r[bi]])
                P.op("dve", lambda e, bi=bi, ch=ch: e.tensor_scalar(out=acc[bi], in0=raw[bi][:, 3:3 + S], scalar1=cwT[:, ch, 3:4],
                                                                     scalar2=None, op0=ALU.mult),
                     reads=[raw_r[bi], cwT_r], writes=[acc_r[bi]])
                for j in (2, 1, 0):
                    P.op("dve", lambda e, bi=bi, ch=ch, j=j: e.scalar_tensor_tensor(
                        out=acc[bi], in0=raw[bi][:, j:j + S], scalar=cwT[:, ch, j:j + 1], in1=acc[bi], op0=ALU.mult, op1=ALU.add),
                        reads=[raw_r[bi], cwT_r, acc_r[bi]], writes=[acc_r[bi]])
                P.op("act", lambda e, bi=bi, ch=ch, dstT=dstT, cc=cc: e.activation(
                    out=dstT[:, cc, :], in_=acc[bi], func=AF.Silu, bias=cwT[:, ch, 4:5]),
                    reads=[acc_r[bi], cwT_r], writes=[qk_r[m][cc]])
        qkr = [qk_r[0][0], qk_r[0][1], qk_r[1][0], qk_r[1][1]]
        for c in range(NT):
            cs = slice(c * 128, (c + 1) * 128)
            ob = 6 + c % 2
            for k in range(KC):
                P.op("pe", lambda e, k=k, cs=cs, ob=ob: e.matmul(bank(ob)[:, 0:256], lhsT=hT[:, k, cs], rhs=wq[3][:, k, :],
                                                               start=(k == 0), stop=(k == KC - 1)),
                     reads=[wq_r[3][k // 4]], writes=[psr[ob]])
            P.op("act", lambda e, c=c, ob=ob: e.activation(out=sigo[:, c, :], in_=bank(ob)[:, 0:256], func=AF.Sigmoid),
                 reads=[psr[ob]], writes=[sigo_r[c]])
        def chunk_vars(c):
            p = c % 2
            q3 = c % 3
            return dict(cs=slice(c * 128, (c + 1) * 128), col=c * 4 + hd, ktok=ktok2[p], ktok_r=ktok2_r[p], vaug=vaug2[p],
                        vaug_r=vaug2_r[p], vp=vp2[p], vp_r=vp2_r[p], wT=wT2[p], wT_r=wT2_r[p], hu=hu3[q3], hu_r=hu3_r[q3],
                        sm=sm3[q3], sm_r=sm3_r[q3], junk=junk3[q3], mlb=mlb3[q3], mlb_r=mlb3_r[q3], ab=p, db=2 + p, tb=6 + p)

        def stageA(c):
            v_ = chunk_vars(c)
            cs, col, ktok, ktok_r, vaug, vaug_r, vp, vp_r, wT, wT_r, ab, db = (v_[k] for k in (
                "cs", "col", "ktok", "ktok_r", "vaug", "vaug_r", "vp", "vp_r", "wT", "wT_r", "ab", "db"))
            abf = bank(ab).bitcast(BF16)
            for k in range(KC):
                P.op("pe", lambda e, k=k, cs=cs, ab=ab: e.matmul(bank(ab)[:, 0:256], lhsT=hT[:, k, cs], rhs=wq[2][:, k, :],
                                                               start=(k == 0), stop=(k == KC - 1)),
                     reads=[wq_r[2][k // 4]], writes=[psr[ab]])
            for dk in range(2):
                P.op("pe", lambda e, dk=dk, cs=cs, abf=abf: e.transpose(out=abf[:, 512 + dk * 128:512 + (dk + 1) * 128],
                                                                        in_=kT[:, dk, cs], identity=C.ident),
                     reads=[qk_r[1][dk], C.cst_r], writes=[psr[ab]])
            P.op("act", lambda e, vaug=vaug, ab=ab: e.activation(out=vaug[:, 0:256], in_=bank(ab)[:, 0:256], func=AF.Copy),
                 reads=[psr[ab]], writes=[vaug_r])
            P.op("act", lambda e, ktok=ktok, abf=abf: e.activation(out=ktok, in_=abf[:, 512:768], func=AF.Copy),
                 reads=[psr[ab]], writes=[ktok_r])
            for dk in range(2):
                P.op("pe", lambda e, dk=dk, cs=cs, db=db: e.matmul(bank(db)[:, 384:512], lhsT=kT[:, dk, cs], rhs=qT[:, dk, cs],
                                                                 start=(dk == 0), stop=(dk == 1)),
                     reads=qkr, writes=[psr[db]])
            P.op("dve", lambda e, col=col, wT=wT, db=db: e.scalar_tensor_tensor(
                out=wT, in0=bank(db)[:, 384:512], scalar=e_in[:, col:col + 1], in1=C.tri, op0=ALU.mult, op1=ALU.mult),
                reads=[psr[db], ee_r, C.cst_r], writes=[wT_r])
            P.op("pe", lambda e, c=c, wT=wT, vaug=vaug, db=db: e.matmul(bank(db)[:, 0:258], lhsT=wT, rhs=vaug, start=True, stop=(c == 0)),
                 reads=[wT_r, vaug_r], writes=[psr[db]])
            if c > 0:
                for dk in range(2):
                    P.op("pe", lambda e, dk=dk, cs=cs, db=db: e.matmul(bank(db)[:, 0:258], lhsT=qT[:, dk, cs], rhs=Cbf[:, dk, :],
                                                                     start=False, stop=(dk == 1)),
                         reads=qkr + [Cbf_r], writes=[psr[db]])
            if c < NT - 1:
                P.op("dve", lambda e, col=col, vp=vp, vaug=vaug: e.tensor_scalar(out=vp, in0=vaug, scalar1=e_v[:, col:col + 1],
                                                                                 scalar2=None, op0=ALU.mult),
                     reads=[vaug_r, ee_r], writes=[vp_r])
                for dk in range(2):
                    P.op("pe", lambda e, dk=dk, ktok=ktok, vp=vp: e.matmul(bank(4 + dk)[:, 0:258], lhsT=ktok[:, dk * 128:(dk + 1) * 128],
                                                                         rhs=vp, start=True, stop=True),
                         reads=[ktok_r, vp_r], writes=[psr[4 + dk]])
                    if c == 0:
                        P.op("dve", lambda e, dk=dk: e.tensor_copy(out=Cf[:, dk, :], in_=bank(4 + dk)[:, 0:258]),
                             reads=[psr[4 + dk]], writes=[Cf_r])
                    else:
                        P.op("dve", lambda e, dk=dk, col=col: e.scalar_tensor_tensor(
                            out=Cf[:, dk, :], in0=Cf[:, dk, :], scalar=e_L[:, col:col + 1], in1=bank(4 + dk)[:, 0:258],
                            op0=ALU.mult, op1=ALU.add), reads=[psr[4 + dk], Cf_r, ee_r], writes=[Cf_r])
                P.op("act", lambda e: e.activation(out=Cbf, in_=Cf, func=AF.Copy), reads=[Cf_r], writes=[Cbf_r])

        def stageB(c):
            v_ = chunk_vars(c)
            col, hu, hu_r, sm, sm_r, junk, db = (v_[k] for k in ("col", "hu", "hu_r", "sm", "sm_r", "junk", "db"))
            P.op("dve", lambda e, col=col, sm=sm, db=db: e.tensor_tensor(out=sm[:, 0:1], in0=bank(db)[:, 256:257],
                                                                       in1=e_out[:, col:col + 1], op=ALU.mult),
                 reads=[psr[db], ee_r], writes=[sm_r])
            P.op("dve", lambda e, sm=sm: e.scalar_tensor_tensor(out=sm[:, 1:2], in0=sm[:, 0:1], scalar=-1.0, in1=sm[:, 0:1],
                                                                op0=ALU.mult, op1=ALU.max), reads=[sm_r], writes=[sm_r])
            P.op("dve", lambda e, sm=sm: e.tensor_scalar(out=sm[:, 1:2], in0=sm[:, 1:2], scalar1=1.0, scalar2=None, op0=ALU.max),
                 reads=[sm_r], writes=[sm_r])
            P.op("dve", lambda e, sm=sm: e.reciprocal(out=sm[:, 2:3], in_=sm[:, 1:2]), reads=[sm_r], writes=[sm_r])
            P.op("dve", lambda e, col=col, sm=sm: e.tensor_tensor(out=sm[:, 3:4], in0=sm[:, 2:3], in1=e_out[:, col:col + 1], op=ALU.mult),
                 reads=[sm_r, ee_r], writes=[sm_r])
            P.op("dve", lambda e, sm=sm, hu=hu, db=db: e.tensor_scalar(out=hu, in0=bank(db)[:, 0:256], scalar1=sm[:, 3:4], scalar2=None,
                                                                     op0=ALU.mult),
                 reads=[psr[db], sm_r], writes=[hu_r])
            P.op("act", lambda e, sm=sm, hu=hu, junk=junk: e.activation(out=junk, in_=hu, func=AF.Square, accum_out=sm[:, 4:5]),
                 reads=[hu_r, sm_r], writes=[sm_r])
            P.op("act", lambda e, sm=sm: e.activation(out=sm[:, 5:6], in_=sm[:, 4:5], func=AF.Ln, scale=1.0 / 256, bias=EPS),
                 reads=[sm_r], writes=[sm_r])
            P.op("act", lambda e, sm=sm: e.activation(out=sm[:, 5:6], in_=sm[:, 5:6], func=AF.Exp, scale=-0.5),
                 reads=[sm_r], writes=[sm_r])

        def stageC(c):
            v_ = chunk_vars(c)
            cs, hu, hu_r, sm, sm_r, mlb, mlb_r, tb = (v_[k] for k in ("cs", "hu", "hu_r", "sm", "sm_r", "mlb", "mlb_r", "tb"))
            abf = bank(tb).bitcast(BF16)
            ab = tb
            P.op("dve", lambda e, hd=hd, sm=sm, hu=hu: e.scalar_tensor_tensor(out=hu, in0=hu, scalar=sm[:, 5:6],
                                                                              in1=hg[:, hd * 256:(hd + 1) * 256],
                                                                              op0=ALU.mult, op1=ALU.mult),
                 reads=[hu_r, sm_r, hg_r], writes=[hu_r])
            P.op("dve", lambda e, c=c, hu=hu, mlb=mlb: e.tensor_tensor(out=mlb, in0=hu, in1=sigo[:, c, :], op=ALU.mult),
                 reads=[hu_r, sigo_r[c]], writes=[mlb_r])
            for j in range(2):
                P.op("pe", lambda e, j=j, abf=abf, mlb=mlb: e.transpose(out=abf[:, j * 128:(j + 1) * 128],
                                                                       in_=mlb[:, j * 128:(j + 1) * 128], identity=C.ident),
                     reads=[mlb_r, C.cst_r], writes=[psr[ab]])
            P.op("act", lambda e, hd=hd, cs=cs, abf=abf: e.activation(
                out=mlT[:, 2 * hd:2 * hd + 2, cs], in_=abf[:, 0:256].rearrange("p (j n) -> p j n", j=2), func=AF.Copy),
                reads=[psr[ab]], writes=[C.mlT_r])


        for step in range(NT + 2):
            if step < NT:
                stageA(step)
            if 0 <= step - 1 < NT:
                stageB(step - 1)
            if 0 <= step - 2 < NT:
                stageC(step - 2)

def merge_phase(C, A, hT, attT, mlT, w_in, w_a, w_m, w_out, g_post, x_src, x_dst):
    P, PS, psr = C.P, C.PS, C.psr
    OGA, OGM = 8712, 9736

    def bank(b):
        return PS[:, 512 * b:512 * (b + 1)]

    mgT = A.alloc((KC, S), BF16)
    wa = A.alloc((4, 1024), BF16)
    wa_r = [Reg() for _ in range(4)]
    wm = A.alloc((KC, 1024), BF16)
    wm_r = [Reg() for _ in range(KC)]
    wo = A.alloc((KC, 1024), BF16)
    wo_r = [Reg() for _ in range(KC)]
    wg = [[A.alloc((KC, 128), BF16) for _ in range(2)] for _ in range(2)]
    wg_r = [[Reg() for _ in range(2)] for _ in range(2)]
    load_cast(C, wg[0][0], w_in[:, OGA:OGA + 128].rearrange("(k p) n -> p k n", p=128), wg_r[0][0])
    load_cast(C, wg[0][1], w_in[:, OGM:OGM + 128].rearrange("(k p) n -> p k n", p=128), wg_r[0][1])
    for k in range(4):
        load_cast(C, wa[:, k, :], w_a[k * 128:(k + 1) * 128, :], wa_r[k])
    for k in range(KC):
        load_cast(C, wm[:, k, :], w_m[k * 128:(k + 1) * 128, :], wm_r[k])
    ga = [A.alloc((512,), F32) for _ in range(2)]
    ga_r = [Reg() for _ in range(2)]
    gm = [A.alloc((512,), F32) for _ in range(2)]
    gm_r = [Reg() for _ in range(2)]
    ta = [A.alloc((512,), F32) for _ in range(2)]
    ta_r = [Reg() for _ in range(2)]
    gb = A.alloc((D,), F32)
    gb_r = Reg()
    P.dma("sp", gb, g_post.partition_broadcast(128), "g", writes=[gb_r])
    j = 0
    for mc in range(KC):
        sl = mc % 2
        if mc > 0:
            load_cast(C, wg[sl][0], w_in[:, OGA + mc * 128:OGA + (mc + 1) * 128].rearrange("(k p) n -> p k n", p=128), wg_r[sl][0])
            load_cast(C, wg[sl][1], w_in[:, OGM + mc * 128:OGM + (mc + 1) * 128].rearrange("(k p) n -> p k n", p=128), wg_r[sl][1])
        if mc == 0:
            for k in range(KC):
                load_cast(C, wo[:, k, :], w_out[k * 128:(k + 1) * 128, :], wo_r[k])
        for tg in range(4):
            q = j % 2
            j += 1
            ts = slice(tg * 512, (tg + 1) * 512)
            b0 = 4 * q
            for gi_, (gbuf, gr) in enumerate(((ga, ga_r), (gm, gm_r))):
                for k in range(KC):
                    P.op("pe", lambda e, k=k, gi_=gi_, sl=sl, ts=ts, b0=b0: e.matmul(
                        bank(b0 + gi_), lhsT=wg[sl][gi_][:, k, :], rhs=hT[:, k, ts], start=(k == 0), stop=(k == KC - 1)),
                        reads=[wg_r[sl][gi_]], writes=[psr[b0 + gi_]])
                P.op("act", lambda e, gbuf=gbuf, q=q, gi_=gi_, b0=b0: e.activation(out=gbuf[q], in_=bank(b0 + gi_), func=AF.Sigmoid),
                     reads=[psr[b0 + gi_]], writes=[gr[q]])
            for k in range(4):
                P.op("pe", lambda e, k=k, mc=mc, ts=ts, b0=b0: e.matmul(
                    bank(b0 + 2), lhsT=wa[:, k, mc * 128:(mc + 1) * 128], rhs=attT[:, k, ts], start=(k == 0), stop=(k == 3)),
                    reads=[wa_r[k], C.attT_r], writes=[psr[b0 + 2]])
            for k in range(KC):
                P.op("pe", lambda e, k=k, mc=mc, ts=ts, b0=b0: e.matmul(
                    bank(b0 + 3), lhsT=wm[:, k, mc * 128:(mc + 1) * 128], rhs=mlT[:, k, ts], start=(k == 0), stop=(k == KC - 1)),
                    reads=[wm_r[k], C.mlT_r], writes=[psr[b0 + 3]])
            P.op("dve", lambda e, q=q, b0=b0: e.tensor_tensor(out=ta[q], in0=bank(b0 + 2), in1=ga[q], op=ALU.mult),
                 reads=[psr[b0 + 2], ga_r[q]], writes=[ta_r[q]])
            P.op("dve", lambda e, q=q, b0=b0: e.tensor_tensor(out=gm[q], in0=bank(b0 + 3), in1=gm[q], op=ALU.mult),
                 reads=[psr[b0 + 3], gm_r[q]], writes=[gm_r[q]])
            P.op("dve", lambda e, q=q, mc=mc, ts=ts: e.tensor_tensor(out=mgT[:, mc, ts], in0=ta[q], in1=gm[q], op=ALU.add),
                 reads=[ta_r[q], gm_r[q]], writes=[C.mg_r])
    xc = [A.alloc((D,), F32) for _ in range(2)]
    tt = [A.alloc((D,), F32) for _ in range(2)]
    xc_r = [Reg() for _ in range(2)]
    tth_r = [[Reg(), Reg()] for _ in range(2)]
    ss2 = A.alloc((NT,), F32)
    r2 = A.alloc((NT,), F32)
    junk = ga[0].bitcast(BF16)
    junk_r = ga_r[0]
    r2_r = [Reg() for _ in range(NT)]
    for i in range(NT):
        s = i % 2
        P.dma("sp", xc[s], x_src[i * 128:(i + 1) * 128, :], f"xc{s}", writes=[xc_r[s]])
        pb = (i % 4) * 2
        psf = PS[:, 512 * pb:512 * (pb + 2)]
        for h in range(2):
            for k in range(KC):
                P.op("pe", lambda e, h=h, k=k, i=i, pb=pb: e.matmul(
                    bank(pb + h), lhsT=mgT[:, k, i * 128:(i + 1) * 128], rhs=wo[:, k, h * 512:(h + 1) * 512],
                    start=(k == 0), stop=(k == KC - 1)), reads=[wo_r[k], C.mg_r], writes=[psr[pb + h]])
        P.op("act", lambda e, i=i, psf=psf: e.activation(out=junk, in_=psf, func=AF.Square, accum_out=ss2[:, i:i + 1]),
             reads=[psr[pb], psr[pb + 1]], writes=[r2_r[i], junk_r])
        P.op("act", lambda e, i=i: e.activation(out=r2[:, i:i + 1], in_=ss2[:, i:i + 1], func=AF.Ln, scale=1.0 / D, bias=EPS),
             reads=[r2_r[i]], writes=[r2_r[i]])
        P.op("act", lambda e, i=i: e.activation(out=r2[:, i:i + 1], in_=r2[:, i:i + 1], func=AF.Exp, scale=-0.5),
             reads=[r2_r[i]], writes=[r2_r[i]])
        for h in range(2):
            P.op("dve", lambda e, s=s, h=h, pb=pb: e.tensor_tensor(
                out=tt[s][:, 512 * h:512 * (h + 1)], in0=bank(pb + h), in1=gb[:, 512 * h:512 * (h + 1)], op=ALU.mult),
                reads=[psr[pb + h], gb_r, r2_r[i]], writes=[tth_r[s][h]])
        P.op("dve", lambda e, s=s, i=i: e.scalar_tensor_tensor(out=xc[s], in0=tt[s], scalar=r2[:, i:i + 1], in1=xc[s],
                                                              op0=ALU.mult, op1=ALU.add),
             reads=[tth_r[s][0], tth_r[s][1], r2_r[i], xc_r[s]], writes=[xc_r[s]])
        P.dma("sp", x_dst[i * 128:(i + 1) * 128, :], xc[s], f"xo{s}", reads=[xc_r[s]])


def mixer_phase(C, x_src, x_dst, W):
    A = C.A.child()
    hT = A.alloc((KC, S), BF16)
    attT = A.alloc((4, S), BF16)
    mlT = A.alloc((8, S), BF16)
    C.attT_r = Reg()
    C.mlT_r = Reg()
    C.mg_r = Reg()
    base = A.off
    end = A.end
    norm_transpose(C, Arena(A.ap, base, end), x_src, W["mix_pre_g"][0], hT)
    attention_phase(C, Arena(A.ap, base, end), hT, attT, W["w_in"][0])
    C.P.barrier()
    mlstm_phase(C, Arena(A.ap, base, end), hT, mlT, W["w_in"][0], W["conv_w"][0], W["conv_b"][0],
                W["mlstm_i_bias"][0], W["mlstm_f_bias"][0], W["mlstm_head_g"][0])
    C.P.barrier()
    merge_phase(C, Arena(A.ap, base, end), hT, attT, mlT, W["w_in"][0], W["w_att_branch"][0], W["w_mlstm_branch"][0],
                W["w_out"][0], W["mix_post_g"][0], x_src, x_dst)
    C.P.barrier()

def host_consts():
    c = {}
    bf = ml_dtypes.bfloat16
    c["ident"] = np.eye(128, dtype=np.float32).astype(bf)
    half = 16
    inv_freq = np.power(np.float32(500000.0), -(np.arange(half, dtype=np.float32) * 2.0 / 32)).astype(np.float32)
    ang = np.arange(S, dtype=np.float32)[None, :] * inv_freq[:, None]
    c["cos"] = np.concatenate([np.cos(ang), np.cos(ang)], 0).astype(np.float32)
    c["sin"] = np.concatenate([np.sin(ang), np.sin(ang)], 0).astype(np.float32)
    rm = np.zeros((32, 32), np.float32)
    for j in range(16):
        rm[16 + j, j] = -1.0
        rm[j, 16 + j] = 1.0
    c["rm"] = rm.astype(bf)
    jj = np.arange(128)[:, None]
    ii = np.arange(128)[None, :]
    NEG = -30000.0
    c["maskc"] = np.where(jj <= ii, 0.0, NEG).astype(bf)
    c["maskp"] = np.where(jj >= ii, 0.0, NEG).astype(bf)
    c["maskn"] = np.full((128, 128), NEG, np.float32).astype(bf)
    c["tri"] = (jj <= ii).astype(np.float32)
    c["ones_f"] = np.ones((128, 128), np.float32)
    c["identf"] = np.eye(128, dtype=np.float32)
    c["ones_bf"] = np.ones((128, 128), np.float32).astype(bf)
    return c


def build(stage="full"):
    nc = bass.Bass("TRN2", target_bir_lowering=False)

    def din(name, shape, dt=F32):
        return nc.dram_tensor(name, list(shape), dt, kind="ExternalInput").ap()

    x = din("x", [S, D])
    W = {}
    for name, shape in [("ffn1_pre_g", [1, D]), ("ffn1_w_gate", [1, D, FF]), ("ffn1_w_up", [1, D, FF]),
                        ("ffn1_w_down", [1, FF, D]), ("ffn1_post_g", [1, D]), ("mix_pre_g", [1, D]),
                        ("w_in", [1, D, IN_W]), ("conv_w", [1, 4, 2048]), ("conv_b", [1, 2048]),
                        ("mlstm_i_bias", [1, 4]), ("mlstm_f_bias", [1, 4]), ("mlstm_head_g", [1, 1024]),
                        ("w_att_branch", [1, 512, D]), ("w_mlstm_branch", [1, D, D]), ("w_out", [1, D, D]),
                        ("mix_post_g", [1, D]), ("ffn2_pre_g", [1, D]), ("ffn2_w_gate", [1, D, FF]),
                        ("ffn2_w_up", [1, D, FF]), ("ffn2_w_down", [1, FF, D]), ("ffn2_post_g", [1, D])]:
        W[name] = din(name, shape)
    CD = {}
    for name, shape, dt in [("c_ident", [128, 128], BF16), ("c_cos", [32, S], F32), ("c_sin", [32, S], F32),
                            ("c_rm", [32, 32], BF16), ("c_maskc", [128, 128], BF16), ("c_maskp", [128, 128], BF16),
                            ("c_maskn", [128, 128], BF16), ("c_tri", [128, 128], F32), ("c_ones_f", [128, 128], F32), ("c_identf", [128, 128], F32),
                            ("c_ones_bf", [128, 128], BF16)]:
        CD[name] = din(name, shape, dt)
    c_ident = CD["c_ident"]
    out = nc.dram_tensor("out", [S, D], F32, kind="ExternalOutput").ap()
    x1 = nc.dram_tensor("x1", [S, D], F32, kind="Internal").ap()
    x2 = nc.dram_tensor("x2", [S, D], F32, kind="Internal").ap()

    with ExitStack() as st:
        ARENA_BYTES = 212480
        arena = st.enter_context(nc.sbuf_tensor("arena", [128, ARENA_BYTES // 2], BF16))
        PS = st.enter_context(nc.psum_tensor("ps", [128, 4096], F32))
        C = Ctx()
        C.nc = nc
        C.P = P = Prog(nc)
        C.PS = PS
        C.psr = [Reg(f"ps{i}", excl=True) for i in range(8)]
        top = Arena(arena, 0, ARENA_BYTES)
        C.ident = top.alloc((128,), BF16)
        C.ident_r = Reg()
        P.dma("sp", C.ident, c_ident, "cst", writes=[C.ident_r])
        C.dram = CD
        C.cst_r = C.ident_r
        for nm, shp, dt in [("rm", (32,), BF16), ("maskc", (128,), BF16), ("maskp", (128,), BF16), ("maskn", (128,), BF16),
                            ("tri", (128,), F32), ("ones_f", (128,), F32), ("identf", (128,), F32), ("ones_bf", (128,), BF16)]:
            v = top.alloc(shp, dt)
            if nm == "rm":
                v = v[:32]
            setattr(C, nm, v)
            P.dma("sp", v, CD["c_" + nm], "cst", writes=[C.cst_r])
        C.stage = [top.alloc((1024,), F32) for _ in range(NST)]
        C.stage_r = [Reg() for _ in range(NST)]
        C.stage_i = 0
        C.A = Arena(arena, top.off, ARENA_BYTES)

        if stage == "attn":
            A = C.A.child()
            hT = A.alloc((KC, S), BF16)
            attT = A.alloc((4, S), BF16)
            C.attT_r = Reg()
            mk = Arena(arena, A.off, ARENA_BYTES)
            norm_transpose(C, mk, x, W["mix_pre_g"][0], hT)
            attention_phase(C, Arena(arena, A.off, ARENA_BYTES), hT, attT, W["w_in"][0])
            P.barrier()
            ov = out.rearrange("(a b) d -> a (b d)", a=1024).rearrange("(k p) t -> p k t", p=128)
            for k in range(4):
                for hh in range(2):
                    P.dma("pool", ov[:, k, hh * 1024:(hh + 1) * 1024], attT[:, k, hh * 1024:(hh + 1) * 1024], "dbg")
        if stage == "full":
            ffn_phase(C, x, x1, W["ffn1_pre_g"][0], W["ffn1_w_gate"][0], W["ffn1_w_up"][0], W["ffn1_w_down"][0],
                      W["ffn1_post_g"][0])
            mixer_phase(C, x1, x2, W)
            ffn_phase(C, x2, out, W["ffn2_pre_g"][0], W["ffn2_w_gate"][0], W["ffn2_w_up"][0], W["ffn2_w_down"][0],
                      W["ffn2_post_g"][0])
        if stage == "mix":
            mixer_phase(C, x, out, W)
        if stage == "ml":
            A = C.A.child()
            hT = A.alloc((KC, S), BF16)
            mlT = A.alloc((8, S), BF16)
            C.mlT_r = Reg()
            mk = Arena(arena, A.off, ARENA_BYTES)
            norm_transpose(C, mk, x, W["mix_pre_g"][0], hT)
            mlstm_phase(C, Arena(arena, A.off, ARENA_BYTES), hT, mlT, W["w_in"][0], W["conv_w"][0], W["conv_b"][0],
                        W["mlstm_i_bias"][0], W["mlstm_f_bias"][0], W["mlstm_head_g"][0])
            P.barrier()
            ov = out.rearrange("(a b) d -> a (b d)", a=1024).rearrange("(k p) t -> p k t", p=128)
            for k in range(8):
                for hh in range(2):
                    P.dma("pool", ov[:, k, hh * 1024:(hh + 1) * 1024], mlT[:, k, hh * 1024:(hh + 1) * 1024], "dbg")
        if stage == "ffn1a":
            A = C.A.child()
            hT_ar = A.sub(KC * S * 2)
            actT_ar = A.sub(FC * S * 2)
            hT = hT_ar.child().alloc((KC, S), BF16)
            norm_transpose(C, actT_ar.child(), x, W["ffn1_pre_g"][0], hT)
            ov = out.rearrange("(a b) d -> a (b d)", a=1024).rearrange("(k p) t -> p k t", p=128)
            for k in range(KC):
                for hh in range(2):
                    P.dma("pool", ov[:, k, hh * 1024:(hh + 1) * 1024], hT[:, k, hh * 1024:(hh + 1) * 1024], "dbg")
        if stage == "ffn1b":
            ffn_phase(C, x, out, W["ffn1_pre_g"][0], W["ffn1_w_gate"][0], W["ffn1_w_up"][0], W["ffn1_w_down"][0],
                      W["ffn1_post_g"][0], stop_after="B")
        if stage == "ffn1":
            ffn_phase(C, x, out, W["ffn1_pre_g"][0], W["ffn1_w_gate"][0], W["ffn1_w_up"][0], W["ffn1_w_down"][0],
                      W["ffn1_post_g"][0])
        P.finish()
        P.emit()
        print("prog stats", P.stats, "sems", len(P.dma_tot) + 5)
        if os.environ.get("DUMP"):
            for e in ("sp", "dve", "act"):
                print("====", e)
                for r in P.dump[e][-int(os.environ["DUMP"]):]:
                    print(r)
    return nc


_NC_CACHE = {}


def kernel(**inputs):
    stage = inputs.pop("_stage", os.environ.get("KSTAGE", "full"))
    if stage not in _NC_CACHE:
        _NC_CACHE[stage] = build(stage)
    nc = _NC_CACHE[stage]
    consts = host_consts()
    xfull = np.ascontiguousarray(inputs["x"], dtype=np.float32)
    shared = {k: np.ascontiguousarray(v, dtype=np.float32) for k, v in inputs.items() if k != "x"}
    for k, v in consts.items():
        shared["c_" + k] = v
    in_maps = []
    ncores = int(os.environ.get("NCORES", 8))
    for b in range(ncores):
        m = dict(shared)
        m["x"] = xfull[b]
        in_maps.append(m)
    res = run_bass_kernel_spmd(nc, in_maps, core_ids=list(range(ncores)))
    return np.stack([r["out"] for r in res.results], axis=0)
```

```python
from contextlib import ExitStack
import math
import os

import numpy as np
import ml_dtypes
import concourse.bass as bass
import concourse.mybir as mybir
from concourse.bass_utils import run_bass_kernel_spmd

F32 = mybir.dt.float32
BF16 = mybir.dt.bfloat16
AF = mybir.ActivationFunctionType
ALU = mybir.AluOpType
AX = mybir.AxisListType

S = 2048
D = 1024
FF = 2816
NT = S // 128
KC = D // 128
FC = FF // 128
IN_W = 10760
EPS = 1e-6
ENGS = ("pe", "act", "dve", "pool", "sp")


class Reg:
    __slots__ = ("name", "w", "rs", "rd", "excl")

    def __init__(self, name="", excl=False):
        self.name = name
        self.excl = excl
        self.w = None
        self.rs = {}
        self.rd = []


class Ins:
    __slots__ = ("eng", "fn", "deps", "signal", "val", "dma", "key")

    def __init__(self, eng, fn, dma=False, key=None):
        self.eng = eng
        self.fn = fn
        self.deps = ()
        self.signal = dma
        self.val = 0
        self.dma = dma
        self.key = key


class Prog:
    def __init__(self, nc):
        self.nc = nc
        self.engs = {e: [] for e in ENGS}
        self.dma_tot = {}
        self.dma_last = {}

    def _add(self, ins, reads, writes):
        eng = ins.eng
        deps = {}
        for r in reads:
            d = r.w
            if d is not None:
                deps[id(d)] = d
            if r.excl:
                for e2, x in r.rs.items():
                    if e2 != eng:
                        deps[id(x)] = x
        for w in writes:
            d = w.w
            if d is not None:
                deps[id(d)] = d
            for e2, x in w.rs.items():
                if (not ins.dma) and e2 == eng:
                    continue
                deps[id(x)] = x
            for x in w.rd:
                deps[id(x)] = x
        out = []
        for d in deps.values():
            if d is ins:
                continue
            if (not d.dma) and (not ins.dma) and d.eng == "pe" and eng == "pe":
                continue
            d.signal = True
            out.append(d)
        ins.deps = out
        for r in reads:
            if ins.dma:
                r.rd.append(ins)
            else:
                r.rs[eng] = ins
        for w in writes:
            w.w = ins
            w.rs = {}
            w.rd = []
        self.engs[eng].append(ins)
        return ins

    def op(self, eng, fn, reads=(), writes=()):
        return self._add(Ins(eng, fn), reads, writes)

    def dma(self, queue, out, in_, key, reads=(), writes=(), **kw):
        ins = Ins(queue, lambda e: e.dma_start(out=out, in_=in_, **kw), dma=True, key=key)
        self.dma_tot[key] = self.dma_tot.get(key, 0) + 16
        ins.val = self.dma_tot[key]
        self.dma_last[key] = ins
        return self._add(ins, reads, writes)

    def barrier(self):
        lasts = []
        for e in ENGS:
            for ins in reversed(self.engs[e]):
                if ins.fn is not None and not ins.dma:
                    ins.signal = True
                    lasts.append(ins)
                    break
        lasts += list(self.dma_last.values())
        for e in ENGS:
            ins = Ins(e, None)
            ins.deps = [d for d in lasts if d.dma or d.eng != e]
            self.engs[e].append(ins)

    def finish(self):
        ins = Ins("sp", None)
        ins.deps = list(self.dma_last.values())
        self.engs["sp"].append(ins)

    def emit(self):
        nc = self.nc
        for e in ENGS:
            c = 0
            for ins in self.engs[e]:
                if ins.dma:
                    continue
                if ins.signal and ins.fn is not None:
                    c += 1
                    ins.val = c
        with ExitStack() as st:
            sems = {e: st.enter_context(nc.semaphore(f"s_{e}")) for e in ENGS}
            dsem = {k: st.enter_context(nc.semaphore(f"d_{k}")) for k in self.dma_tot}
            block = st.enter_context(nc.Block())
            bname = {"pe": "tensor", "act": "scalar", "dve": "vector", "pool": "gpsimd", "sp": "sync"}
            stats = {}
            self.dump = {}
            for e in ENGS:
                def body(engine, e=e):
                    seen = {}
                    nw = 0
                    for ins in self.engs[e]:
                        need = {}
                        for d in ins.deps:
                            s = ("d", d.key) if d.dma else ("c", d.eng)
                            if need.get(s, 0) < d.val:
                                need[s] = d.val
                        for s, v in need.items():
                            if seen.get(s, 0) < v:
                                seen[s] = v
                                sh = dsem[s[1]] if s[0] == "d" else sems[s[1]]
                                engine.wait_ge(sh, v)
                                nw += 1
                        if os.environ.get("DUMP"):
                            self.dump.setdefault(e, []).append((sorted((k, v) for k, v in need.items()), ins.fn is not None, ins.dma, ins.key, ins.signal, ins.val))
                        if ins.fn is not None:
                            bi = ins.fn(engine)
                            if ins.dma:
                                bi.then_inc(dsem[ins.key], 16)
                            elif ins.signal:
                                bi.then_inc(sems[e], 1)
                    stats[e] = (len(self.engs[e]), nw)
                getattr(block, bname[e])(body)
            self.stats = stats


class Arena:
    def __init__(self, ap, start, end):
        self.ap = ap
        self.start = start
        self.off = start
        self.end = end

    def alloc(self, free_shape, dt, parts=128):
        n = 1
        for v in free_shape:
            n *= v
        esz = 4 if dt == F32 else 2
        nbytes = n * esz
        st = (self.off + 63) // 64 * 64
        assert st + nbytes <= self.end, f"arena overflow: need {st + nbytes} > {self.end}"
        self.off = st + nbytes
        Arena.last = (st, tuple(free_shape), dt)
        v = self.ap[:parts, st // 2:(st + nbytes) // 2]
        if dt == F32:
            v = v.bitcast(F32)
        if len(free_shape) == 2:
            v = v.rearrange("p (a b) -> p a b", a=free_shape[0])
        elif len(free_shape) == 3:
            v = v.rearrange("p (a b c) -> p a b c", a=free_shape[0], b=free_shape[1])
        return v

    def sub(self, nbytes):
        st = (self.off + 63) // 64 * 64
        assert st + nbytes <= self.end, f"arena overflow(sub): need {st + nbytes} > {self.end}"
        self.off = st + nbytes
        return Arena(self.ap, st, st + nbytes)

    def child(self):
        return Arena(self.ap, self.start, self.end)


class Ctx:
    pass


DBG = {}


NST = 3


def load_cast(C, dst, src, dst_reg):
    P = C.P
    s = C.stage_i % NST
    C.stage_i += 1
    sh = src.shape
    n = 1
    for v in sh[1:]:
        n *= v
    assert n <= 1024
    stg = C.stage[s][:, :n]
    if len(sh) == 3:
        stg = stg.rearrange("p (a b) -> p a b", a=sh[1])
    P.dma("sp", stg, src, f"st{s}", writes=[C.stage_r[s]])
    P.op("pool", lambda e: e.tensor_copy(out=dst, in_=stg), reads=[C.stage_r[s]], writes=[dst_reg])


def norm_transpose(C, A_stage, x_src, g_pre_dram, hT):
    P, PS, psr = C.P, C.PS, C.psr
    gb = A_stage.alloc((D,), F32)
    gb_r = Reg()
    P.dma("sp", gb, g_pre_dram.partition_broadcast(128), "g", writes=[gb_r])
    ss = A_stage.alloc((NT,), F32)
    rstd = A_stage.alloc((NT,), F32)
    junk = A_stage.alloc((D,), BF16)
    junk_r = Reg()
    hb = [A_stage.alloc((D,), BF16) for _ in range(2)]
    hb_r = [Reg() for _ in range(2)]
    xs = [A_stage.alloc((D,), F32) for _ in range(NT)]
    xs_r = [Reg() for _ in range(NT)]
    ss_r = [Reg() for _ in range(NT)]
    rstd_r = Reg()
    for i in range(NT):
        P.dma("sp", xs[i], x_src[i * 128:(i + 1) * 128, :], f"xs{i}", writes=[xs_r[i]])
    for i in range(NT):
        P.op("act", lambda e, i=i: e.activation(out=junk, in_=xs[i], func=AF.Square, accum_out=ss[:, i:i + 1]),
             reads=[xs_r[i]], writes=[ss_r[i], junk_r])
    P.op("act", lambda e: e.activation(out=rstd, in_=ss, func=AF.Ln, scale=1.0 / D, bias=EPS),
         reads=ss_r, writes=[rstd_r])
    P.op("act", lambda e: e.activation(out=rstd, in_=rstd, func=AF.Exp, scale=-0.5),
         reads=[rstd_r], writes=[rstd_r])
    for i in range(NT):
        s = i % 2
        P.op("dve", lambda e, i=i, s=s: e.scalar_tensor_tensor(out=hb[s], in0=xs[i], scalar=rstd[:, i:i + 1], in1=gb,
                                                              op0=ALU.mult, op1=ALU.mult),
             reads=[xs_r[i], rstd_r, gb_r], writes=[hb_r[s]])
        b = i % 2
        psb = PS[:, 512 * b:512 * (b + 1)].bitcast(BF16)
        for k in range(KC):
            P.op("pe", lambda e, k=k, s=s, psb=psb: e.transpose(out=psb[:, k * 128:(k + 1) * 128],
                                                                in_=hb[s][:, k * 128:(k + 1) * 128], identity=C.ident),
                 reads=[hb_r[s], C.ident_r], writes=[psr[b]])
        P.op("act", lambda e, i=i, psb=psb: e.activation(out=hT[:, :, i * 128:(i + 1) * 128],
                                                         in_=psb.rearrange("p (k n) -> p k n", k=KC), func=AF.Copy),
             reads=[psr[b]], writes=[])
    P.barrier()


def ffn_phase(C, x_src, x_dst, g_pre, wg, wu, wd, g_post, stop_after=None):
    P, PS, psr = C.P, C.PS, C.psr
    A = C.A.child()
    hT_ar = A.sub(KC * S * 2)
    actT_ar = A.sub(FC * S * 2)
    hT = hT_ar.child().alloc((KC, S), BF16)
    actT = actT_ar.child().alloc((FC, S), BF16)
    norm_transpose(C, actT_ar.child(), x_src, g_pre, hT)

    NSL = 2
    wg_s = [A.alloc((KC, 256), BF16) for _ in range(NSL)]
    wu_s = [A.alloc((KC, 256), BF16) for _ in range(NSL)]
    wg_r = [[Reg(), Reg()] for _ in range(NSL)]
    wu_r = [[Reg(), Reg()] for _ in range(NSL)]
    wd_h = [A.alloc((FC, 512), BF16) for _ in range(2)]
    wd_r = [[Reg() for _ in range(FC // 2)] for _ in range(2)]
    sg = [A.alloc((512,), BF16) for _ in range(2)]
    sg_r = [Reg() for _ in range(2)]
    gb = A.alloc((D,), F32)
    gb_r = Reg()
    ss2 = A.alloc((NT,), F32)
    r2 = A.alloc((NT,), F32)
    junk = A.alloc((D,), BF16)
    junk_r = Reg()
    P.dma("sp", gb, g_post.partition_broadcast(128), "g", writes=[gb_r])

    wd_jobs = [(h, c2) for h in range(2) for c2 in range(FC // 2)]

    def load_wd_piece():
        if not wd_jobs:
            return
        h, c2 = wd_jobs.pop(0)
        load_cast(C, wd_h[h][:, 2 * c2:2 * c2 + 2, :],
                  wd[c2 * 256:(c2 + 1) * 256, h * 512:(h + 1) * 512].rearrange("(c p) n -> p c n", p=128),
                  wd_r[h][c2])

    j = 0
    for cb in range(FC // 2):
        s = cb % NSL
        for part in range(2):
            load_cast(C, wg_s[s][:, 4 * part:4 * part + 4, :],
                      wg[part * 512:(part + 1) * 512, cb * 256:(cb + 1) * 256].rearrange("(k p) n -> p k n", p=128),
                      wg_r[s][part])
            load_cast(C, wu_s[s][:, 4 * part:4 * part + 4, :],
                      wu[part * 512:(part + 1) * 512, cb * 256:(cb + 1) * 256].rearrange("(k p) n -> p k n", p=128),
                      wu_r[s][part])
        if cb >= 1:
            for _ in range(3):
                load_wd_piece()
        for sub in range(2):
            ffc = cb * 2 + sub
            for tg in range(4):
                bG = 2 * (j % 4)
                bU = bG + 1
                q = j % 2
                j += 1
                for k in range(KC):
                    P.op("pe", lambda e, k=k, s=s, sub=sub, tg=tg, bG=bG: e.matmul(
                        PS[:, 512 * bG:512 * (bG + 1)], lhsT=wg_s[s][:, k, sub * 128:(sub + 1) * 128],
                        rhs=hT[:, k, tg * 512:(tg + 1) * 512], start=(k == 0), stop=(k == KC - 1)),
                        reads=[wg_r[s][k // 4]], writes=[psr[bG]])
                for k in range(KC):
                    P.op("pe", lambda e, k=k, s=s, sub=sub, tg=tg, bU=bU: e.matmul(
                        PS[:, 512 * bU:512 * (bU + 1)], lhsT=wu_s[s][:, k, sub * 128:(sub + 1) * 128],
                        rhs=hT[:, k, tg * 512:(tg + 1) * 512], start=(k == 0), stop=(k == KC - 1)),
                        reads=[wu_r[s][k // 4]], writes=[psr[bU]])
                P.op("act", lambda e, q=q, bG=bG: e.activation(out=sg[q], in_=PS[:, 512 * bG:512 * (bG + 1)], func=AF.Silu),
                     reads=[psr[bG]], writes=[sg_r[q]])
                P.op("dve", lambda e, q=q, bU=bU, ffc=ffc, tg=tg: e.tensor_tensor(
                    out=actT[:, ffc, tg * 512:(tg + 1) * 512], in0=PS[:, 512 * bU:512 * (bU + 1)], in1=sg[q], op=ALU.mult),
                    reads=[psr[bU], sg_r[q]], writes=[])
    while wd_jobs:
        load_wd_piece()
    P.barrier()
    if stop_after == "B":
        ov = x_dst.rearrange("(a b) d -> a (b d)", a=1024).rearrange("(k p) t -> p k t", p=128)
        for k in range(KC):
            for hh in range(2):
                P.dma("pool", ov[:, k, hh * 1024:(hh + 1) * 1024], actT[:, k + 14, hh * 1024:(hh + 1) * 1024], "dbg")
        return

    Ah = hT_ar.child()
    NSC = int(os.environ.get('NSC', 2))
    NSX = int(os.environ.get('NSX', NSC))
    xc = [Ah.alloc((D,), F32) for _ in range(NSX)]
    tt = [Ah.alloc((D,), F32) for _ in range(NSC)]
    xc_r = [Reg() for _ in range(NSX)]
    tt_r = [Reg() for _ in range(NSC)]
    tth_r = [[Reg(), Reg()] for _ in range(NSC)]
    r2_r = [Reg() for _ in range(NT)]
    for i in range(int(os.environ.get("CT", NT))):
        s = i % NSC
        sx = i % NSX
        P.dma("sp", xc[sx], x_src[i * 128:(i + 1) * 128, :], f"xc{sx}", writes=[xc_r[sx]])
        pb = (i % 4) * 2
        psf = PS[:, 512 * pb:512 * (pb + 2)]
        for h in range(2):
            for ffc in range(FC):
                P.op("pe", lambda e, h=h, ffc=ffc, i=i, pb=pb: e.matmul(
                    PS[:, 512 * (pb + h):512 * (pb + h + 1)], lhsT=actT[:, ffc, i * 128:(i + 1) * 128],
                    rhs=wd_h[h][:, ffc, :], start=(ffc == 0), stop=(ffc == FC - 1)),
                    reads=[wd_r[h][ffc // 2]], writes=[psr[pb + h]])
        P.op("act", lambda e, i=i, psf=psf: e.activation(out=junk, in_=psf, func=AF.Square, accum_out=ss2[:, i:i + 1]),
             reads=[psr[pb], psr[pb + 1]], writes=[r2_r[i], junk_r])
        cstop = int(os.environ.get("CSTOP", 9))
        if cstop == 1:
            P.dma("sp", x_dst[i * 128:(i + 1) * 128, :], xc[sx], f"xo{s}", reads=[xc_r[sx], r2_r[i]])
            continue
        P.op("act", lambda e, i=i: e.activation(out=r2[:, i:i + 1], in_=ss2[:, i:i + 1], func=AF.Ln, scale=1.0 / D, bias=EPS),
             reads=[r2_r[i]], writes=[r2_r[i]])
        P.op("act", lambda e, i=i: e.activation(out=r2[:, i:i + 1], in_=r2[:, i:i + 1], func=AF.Exp, scale=-0.5,
                                                bias=math.log(0.5)),
             reads=[r2_r[i]], writes=[r2_r[i]])
        if cstop == 2:
            P.dma("sp", x_dst[i * 128:(i + 1) * 128, :], xc[sx], f"xo{s}", reads=[xc_r[sx], r2_r[i]])
            continue
        for h in range(2):
            if os.environ.get("DVEVAR") == "copy":
                P.op("dve", lambda e, s=s, h=h, pb=pb: e.tensor_copy(
                    out=tt[s][:, 512 * h:512 * (h + 1)], in_=PS[:, 512 * (pb + h):512 * (pb + h + 1)]),
                    reads=[psr[pb + h], gb_r], writes=[tth_r[s][h]])
                continue
            if os.environ.get("DVEVAR") == "sbuf":
                P.op("dve", lambda e, s=s, h=h, pb=pb: e.tensor_tensor(
                    out=tt[s][:, 512 * h:512 * (h + 1)], in0=xc[sx][:, 512 * h:512 * (h + 1)],
                    in1=gb[:, 512 * h:512 * (h + 1)], op=ALU.mult),
                    reads=[psr[pb + h], gb_r, xc_r[sx]], writes=[tth_r[s][h]])
                continue
            P.op("dve", lambda e, s=s, h=h, pb=pb: e.tensor_tensor(
                out=tt[s][:, 512 * h:512 * (h + 1)], in0=PS[:, 512 * (pb + h):512 * (pb + h + 1)],
                in1=gb[:, 512 * h:512 * (h + 1)], op=ALU.mult),
                reads=[psr[pb + h], gb_r, r2_r[i]], writes=[tth_r[s][h]])
        if cstop == 3:
            P.dma("sp", x_dst[i * 128:(i + 1) * 128, :], tt[s], f"xo{s}", reads=[xc_r[sx], r2_r[i], tth_r[s][0], tth_r[s][1]], writes=[tth_r[s][0], tth_r[s][1]])
            continue
        P.op("dve", lambda e, s=s, sx=sx, i=i: e.scalar_tensor_tensor(out=xc[sx], in0=tt[s], scalar=r2[:, i:i + 1], in1=xc[sx],
                                                              op0=ALU.mult, op1=ALU.add),
             reads=[tth_r[s][0], tth_r[s][1], r2_r[i], xc_r[sx]], writes=[xc_r[sx]])
        P.dma("sp", x_dst[i * 128:(i + 1) * 128, :], xc[sx], f"xo{sx}", reads=[xc_r[sx]])
    P.barrier()


def tok_slice(start, step):
    return slice(start, start + step * 127 + 1, step) if step > 1 else slice(start, start + 128)


def attention_phase(C, A, hT, attT, w_in):
    P, PS, psr = C.P, C.PS, C.psr
    cos = A.alloc((S,), F32)[:32]
    sin = A.alloc((S,), F32)[:32]
    cs_r = Reg()
    cs2_r = Reg()
    P.dma("sp", cos, C.dram["c_cos"], "cs", writes=[cs_r])
    P.dma("sp", sin, C.dram["c_sin"], "cs", writes=[cs2_r])
    wsl = [[A.alloc((KC, 128), BF16) for _ in range(3)] for _ in range(2)]
    wsl_r = [[Reg() for _ in range(3)] for _ in range(2)]
    DBG.clear()
    qk = [[None, None], [None, None]]
    for a_ in range(2):
        for b_ in range(2):
            qk[a_][b_] = A.alloc((S,), BF16)
            DBG[f"qk{a_}{b_}"] = Arena.last
    qk_r = [[[Reg() for _ in range(4)] for _ in range(2)] for _ in range(2)]
    vt = [A.alloc((NT, 128), BF16) for _ in range(2)]
    vt_r = [[Reg() for _ in range(4)] for _ in range(2)]
    acc_n = A.alloc((S,), F32)
    acc_d = A.alloc((S,), F32)
    accn_r = [Reg() for _ in range(4)]
    accd_r = [Reg() for _ in range(4)]
    acc_all = Reg()
    pT = [A.alloc((512,), BF16) for _ in range(4)]
    pT_r = [Reg() for _ in range(4)]
    t1 = [A.alloc((512,), F32)[:32] for _ in range(2)]
    t2 = [A.alloc((512,), F32)[:32] for _ in range(2)]
    t1_r = [Reg() for _ in range(2)]
    t2_r = [Reg() for _ in range(2)]
    rcp = [A.alloc((512,), F32) for _ in range(2)]
    rcp_r = [Reg() for _ in range(2)]
    SCALE = 128.0 ** -0.5

    def bank(b):
        return PS[:, 512 * b:512 * (b + 1)]

    cnt = {"proj": 0, "rot": 0, "s": 0, "p": 0, "nd": 0, "t": 0, "it": 0}
    for hs in range(4):
        for g in range(3):
            sl = cnt["it"] % 2
            cnt["it"] += 1
            head = g * 4 + hs
            dil = (1, 4, 16)[g]
            for m, off in enumerate((0, 1536, 3072)):
                c0 = off + head * 128
                load_cast(C, wsl[sl][m], w_in[:, c0:c0 + 128].rearrange("(k p) n -> p k n", p=128), wsl_r[sl][m])
            for m in range(2):
                dst = qk[sl][m]
                for tg in range(4):
                    pb = cnt["proj"] % 2
                    cnt["proj"] += 1
                    for k in range(KC):
                        P.op("pe", lambda e, k=k, m=m, tg=tg, pb=pb, sl=sl: e.matmul(
                            bank(pb), lhsT=wsl[sl][m][:, k, :], rhs=hT[:, k, tg * 512:(tg + 1) * 512],
                            start=(k == 0), stop=(k == KC - 1)), reads=[wsl_r[sl][m]], writes=[psr[pb]])
                    dcol = dst[:, tg * 512:(tg + 1) * 512]
                    dr = qk_r[sl][m][tg]
                    P.op("act", lambda e, dcol=dcol, pb=pb: e.activation(out=dcol, in_=bank(pb), func=AF.Copy),
                         reads=[psr[pb]], writes=[dr])
                    rb = 2 + cnt["rot"] % 2
                    ts = cnt["rot"] % 2
                    cnt["rot"] += 1
                    P.op("pe", lambda e, dcol=dcol, rb=rb: e.matmul(bank(rb)[:32, :], lhsT=C.rm, rhs=dcol[:32, :],
                                                                   start=True, stop=True),
                         reads=[dr, C.cst_r], writes=[psr[rb]])
                    P.op("dve", lambda e, rb=rb, ts=ts, tg=tg: e.tensor_tensor(
                        out=t1[ts], in0=bank(rb)[:32, :], in1=sin[:, tg * 512:(tg + 1) * 512], op=ALU.mult),
                        reads=[psr[rb], cs_r, cs2_r], writes=[t1_r[ts]])
                    P.op("dve", lambda e, dcol=dcol, ts=ts, tg=tg: e.tensor_tensor(
                        out=t2[ts], in0=dcol[:32, :], in1=cos[:, tg * 512:(tg + 1) * 512], op=ALU.mult),
                        reads=[dr, cs_r], writes=[t2_r[ts]])
                    P.op("dve", lambda e, dcol=dcol, ts=ts: e.tensor_tensor(out=dcol[:32, :], in0=t1[ts], in1=t2[ts], op=ALU.add),
                         reads=[t1_r[ts], t2_r[ts]], writes=[dr])
            blocks = []
            if g == 0:
                for b in range(16):
                    blocks.append((tok_slice(128 * b, 1), tok_slice(128 * (b - 1), 1) if b > 0 else None))
            elif g == 1:
                for r in range(4):
                    for b in range(4):
                        blocks.append((tok_slice(4 * 128 * b + r, 4), tok_slice(4 * 128 * (b - 1) + r, 4) if b > 0 else None))
            else:
                for r in range(16):
                    blocks.append((tok_slice(r, 16), None))
            for j in range(4):
                pb = cnt["proj"] % 2
                cnt["proj"] += 1
                for bi in range(4):
                    qs = blocks[4 * j + bi][0]
                    for k in range(KC):
                        P.op("pe", lambda e, k=k, qs=qs, pb=pb, bi=bi, sl=sl: e.matmul(
                            bank(pb)[:, bi * 128:(bi + 1) * 128], lhsT=hT[:, k, qs], rhs=wsl[sl][2][:, k, :],
                            start=(k == 0), stop=(k == KC - 1)), reads=[wsl_r[sl][2]], writes=[psr[pb]])
                P.op("act", lambda e, j=j, pb=pb, sl=sl: e.activation(
                    out=vt[sl][:, 4 * j:4 * j + 4, :], in_=bank(pb).rearrange("p (a b) -> p a b", a=4), func=AF.Copy),
                    reads=[psr[pb]], writes=[vt_r[sl][j]])
            qT, kT = qk[sl][0], qk[sl][1]
            qr = qk_r[sl][0] + qk_r[sl][1]
            pairs = [(2 * i, 2 * i + 1) for i in range(8)]

            def do_qk(pair):
                sb = 4 + cnt["s"] % 2
                cnt["s"] += 1
                for ti, blk in enumerate(pair):
                    qs, ps_ = blocks[blk]
                    for ci, ks in enumerate((ps_, qs)):
                        o = bank(sb)[:, (2 * ti + ci) * 128:(2 * ti + ci + 1) * 128]
                        if ks is None:
                            P.op("pe", lambda e, o=o: e.matmul(o, lhsT=C.ident, rhs=C.maskn, start=True, stop=True),
                                 reads=[C.cst_r], writes=[psr[sb]])
                            continue
                        P.op("pe", lambda e, o=o, ks=ks, qs=qs, kT=kT, qT=qT: e.matmul(o, lhsT=kT[:, ks], rhs=qT[:, qs], start=True, stop=False),
                             reads=qr, writes=[psr[sb]])
                        mk = C.maskc if ci == 1 else C.maskp
                        P.op("pe", lambda e, o=o, mk=mk: e.matmul(o, lhsT=C.ident, rhs=mk, start=False, stop=True),
                             reads=[C.cst_r], writes=[psr[sb]])
                pslot = cnt["p"] % 4
                cnt["p"] += 1
                P.op("act", lambda e, sb=sb, pslot=pslot: e.activation(out=pT[pslot], in_=bank(sb), func=AF.Exp, scale=SCALE),
                     reads=[psr[sb]], writes=[pT_r[pslot]])
                return pslot

            def do_pv(pair, pslot):
                nb = 6 + cnt["nd"] % 2
                cnt["nd"] += 1
                for ti, blk in enumerate(pair):
                    qs, ps_ = blocks[blk]
                    for is_num in (True, False):
                        col = ti * 128 + (0 if is_num else 256)
                        o = bank(nb)[:, col:col + 128]
                        for ci in range(2):
                            kblk = blk if (ci == 1 or ps_ is None) else blk - 1
                            lhs = vt[sl][:, kblk, :] if is_num else C.ones_bf
                            P.op("pe", lambda e, o=o, lhs=lhs, pslot=pslot, ti=ti, ci=ci: e.matmul(
                                o, lhsT=lhs, rhs=pT[pslot][:, (2 * ti + ci) * 128:(2 * ti + ci + 1) * 128],
                                start=(ci == 0), stop=(ci == 1)),
                                reads=[pT_r[pslot], vt_r[sl][kblk // 4], C.cst_r], writes=[psr[nb]])
                pi = pair[0] // 2
                if g == 0:
                    sel = slice(256 * pi, 256 * pi + 256)
                    dn, dd = acc_n[:, sel], acc_d[:, sel]
                    sn, sd = bank(nb)[:, 0:256], bank(nb)[:, 256:512]
                elif g == 1:
                    r_, b_ = pair[0] // 4, pair[0] % 4
                    st_ = r_ + 512 * b_
                    sel = slice(st_, st_ + 4 * 255 + 1, 4)
                    dn, dd = acc_n[:, sel], acc_d[:, sel]
                    sn, sd = bank(nb)[:, 0:256], bank(nb)[:, 256:512]
                else:
                    r_ = pair[0]
                    dn = acc_n.rearrange("p (n r) -> p r n", r=16)[:, r_:r_ + 2, :]
                    dd = acc_d.rearrange("p (n r) -> p r n", r=16)[:, r_:r_ + 2, :]
                    sn = bank(nb)[:, 0:256].rearrange("p (a b) -> p a b", a=2)
                    sd = bank(nb)[:, 256:512].rearrange("p (a b) -> p a b", a=2)
                if g == 0:
                    P.op("dve", lambda e, dn=dn, sn=sn: e.tensor_copy(out=dn, in_=sn), reads=[psr[nb]], writes=[acc_all])
                    P.op("dve", lambda e, dd=dd, sd=sd: e.tensor_copy(out=dd, in_=sd), reads=[psr[nb]], writes=[acc_all])
                else:
                    P.op("dve", lambda e, dn=dn, sn=sn: e.tensor_tensor(out=dn, in0=sn, in1=dn, op=ALU.add),
                         reads=[psr[nb], acc_all], writes=[acc_all])
                    P.op("dve", lambda e, dd=dd, sd=sd: e.tensor_tensor(out=dd, in0=sd, in1=dd, op=ALU.add),
                         reads=[psr[nb], acc_all], writes=[acc_all])

            prev = None
            for pair in pairs:
                pslot = do_qk(pair)
                if prev is not None:
                    do_pv(*prev)
                prev = (pair, pslot)
            do_pv(*prev)
        for tg in range(4):
            rs = tg % 2
            P.op("dve", lambda e, tg=tg, rs=rs: e.reciprocal(out=rcp[rs], in_=acc_d[:, tg * 512:(tg + 1) * 512]),
                 reads=[acc_all], writes=[rcp_r[rs]])
            P.op("dve", lambda e, tg=tg, rs=rs, hs=hs: e.tensor_tensor(
                out=attT[:, hs, tg * 512:(tg + 1) * 512], in0=acc_n[:, tg * 512:(tg + 1) * 512], in1=rcp[rs], op=ALU.mult),
                reads=[acc_all, rcp_r[rs]], writes=[C.attT_r])


def mlstm_phase(C, A, hT, mlT, w_in, conv_w, conv_b, i_bias, f_bias, head_g):
    P, PS, psr = C.P, C.PS, C.psr
    OQ, OK_, OV, OO, OI = 4608, 5632, 6656, 7680, 8704

    def bank(b):
        return PS[:, 512 * b:512 * (b + 1)]

    def R():
        return Reg()

    cwb = A.alloc((2048,), F32)[:5]
    cwb_r = R()
    cwb2_r = R()
    P.dma("sp", cwb[0:4, :], conv_w, "mca", writes=[cwb_r])
    P.dma("sp", cwb[4:5, :], conv_b.rearrange("(o n) -> o n", o=1), "mca", writes=[cwb2_r])
    cwT = A.alloc((16, 8), F32)
    ncb = A.alloc((16,), F32)
    cwT_r = R()
    for c in range(16):
        P.op("pe", lambda e, c=c: e.matmul(bank(6)[:, c * 8:c * 8 + 5], lhsT=cwb[:, c * 128:(c + 1) * 128],
                                           rhs=C.identf[:5, :5], start=True, stop=True),
             reads=[cwb_r, cwb2_r, C.cst_r], writes=[psr[6]])
    P.op("dve", lambda e: e.tensor_copy(out=cwT[:, :, 0:5], in_=bank(6)[:, 0:128].rearrange("p (c j) -> p c j", j=8)[:, :, 0:5]),
         reads=[psr[6]], writes=[cwT_r])
    P.op("dve", lambda e: e.tensor_scalar(out=ncb, in0=cwT[:, :, 4], scalar1=-1.0, scalar2=None, op0=ALU.mult),
         reads=[cwT_r], writes=[cwT_r])
    hg = A.alloc((1024,), F32)
    hg_r = R()
    P.dma("sp", hg, head_g.partition_broadcast(128), "mch", writes=[hg_r])
    bias8 = A.alloc((8,), F32)
    b8_r = R()
    b8b_r = R()
    P.dma("sp", bias8[:, 0:4], i_bias.partition_broadcast(128), "mcb", writes=[b8_r])
    P.dma("sp", bias8[:, 4:8], f_bias.partition_broadcast(128), "mcb", writes=[b8b_r])

    wif = A.alloc((KC, 8), BF16)
    wif_r = R()
    load_cast(C, wif, w_in[:, OI:OI + 8].rearrange("(k p) n -> p k n", p=128), wif_r)
    for c in range(NT):
        for k in range(KC):
            P.op("pe", lambda e, c=c, k=k: e.matmul(bank(7)[:, c * 8:(c + 1) * 8], lhsT=hT[:, k, c * 128:(c + 1) * 128],
                                                    rhs=wif[:, k, :], start=(k == 0), stop=(k == KC - 1)),
                 reads=[wif_r], writes=[psr[7]])
    gi = A.alloc((NT, 4), F32)
    lg = A.alloc((NT, 4), F32)
    g_r = R()
    pre3 = bank(7)[:, 0:128].rearrange("p (c j) -> p c j", j=8)
    P.op("dve", lambda e: e.tensor_tensor(out=gi, in0=pre3[:, :, 0:4],
                                          in1=bias8[:, 0:4].unsqueeze(1).to_broadcast([128, NT, 4]), op=ALU.add),
         reads=[psr[7], b8_r, b8b_r], writes=[g_r])
    P.op("dve", lambda e: e.tensor_tensor(out=lg, in0=pre3[:, :, 4:8],
                                          in1=bias8[:, 4:8].unsqueeze(1).to_broadcast([128, NT, 4]), op=ALU.add),
         reads=[psr[7], b8_r, b8b_r, g_r], writes=[g_r])
    P.op("act", lambda e: e.activation(out=lg, in_=lg, func=AF.Exp, scale=-1.0), reads=[g_r], writes=[g_r])
    P.op("act", lambda e: e.activation(out=lg, in_=lg, func=AF.Ln, bias=1.0), reads=[g_r], writes=[g_r])
    lg2 = lg.rearrange("p c h -> p (c h)")
    gi2 = gi.rearrange("p c h -> p (c h)")
    P.op("pe", lambda e: e.matmul(bank(6)[:, 0:64], lhsT=C.tri, rhs=lg2, start=True, stop=True),
         reads=[g_r, C.cst_r], writes=[psr[6]])
    P.op("pe", lambda e: e.matmul(bank(6)[:, 64:128], lhsT=C.ones_f, rhs=lg2, start=True, stop=True),
         reads=[g_r, C.cst_r], writes=[psr[6]])
    e_in = A.alloc((64,), F32)
    e_out = A.alloc((64,), F32)
    e_L = A.alloc((64,), F32)
    e_v = A.alloc((64,), F32)
    ee_r = R()
    P.op("dve", lambda e: e.tensor_tensor(out=e_in, in0=bank(6)[:, 0:64], in1=gi2, op=ALU.add),
         reads=[psr[6], g_r], writes=[ee_r])
    P.op("act", lambda e: e.activation(out=e_in, in_=e_in, func=AF.Exp), reads=[ee_r], writes=[ee_r])
    P.op("act", lambda e: e.activation(out=e_out, in_=bank(6)[:, 0:64], func=AF.Exp, scale=-1.0),
         reads=[psr[6], ee_r], writes=[ee_r])
    P.op("act", lambda e: e.activation(out=e_L, in_=bank(6)[:, 64:128], func=AF.Exp, scale=-1.0),
         reads=[psr[6], ee_r], writes=[ee_r])
    P.op("dve", lambda e: e.tensor_scalar(out=e_in, in0=e_in, scalar1=1.0 / 16.0, scalar2=None, op0=ALU.mult),
         reads=[ee_r], writes=[ee_r])
    P.op("dve", lambda e: e.tensor_tensor(out=e_v, in0=e_in, in1=e_L, op=ALU.mult), reads=[ee_r], writes=[ee_r])

    wq = [A.alloc((KC, 256), BF16) for _ in range(5)]
    wq_r = [[R(), R()] for _ in range(5)]
    raw = [A.alloc((S + 4,), BF16) for _ in range(2)]
    raw_r = [R(), R()]
    for i in range(2):
        P.op("dve", lambda e, i=i: e.memset(raw[i][:, 0:3], 0.0), writes=[raw_r[i]])
    neghalf = A.alloc((1,), F32)
    nh_r = R()
    P.op("dve", lambda e: e.memset(neghalf, -0.5), writes=[nh_r])
    dg = [[A.alloc((128,), BF16) for _ in range(4)] for _ in range(2)]
    dg_r = [R(), R()]
    qT2 = [A.alloc((2, S), BF16) for _ in range(2)]
    kT2 = [A.alloc((2, S), BF16) for _ in range(2)]
    qk2_r = [[[R(), R()], [R(), R()]] for _ in range(2)]
    ktok2 = [A.alloc((256,), BF16) for _ in range(2)]
    ktok2_r = [R(), R()]
    vaug2 = [A.alloc((258,), BF16) for _ in range(2)]
    vaug2_r = [R(), R()]
    for i in range(2):
        P.op("dve", lambda e, i=i: e.memset(vaug2[i][:, 256:257], 1.0), writes=[vaug2_r[i]])
        P.op("dve", lambda e, i=i: e.memset(vaug2[i][:, 257:258], 0.0), writes=[vaug2_r[i]])
    vp2 = [A.alloc((258,), BF16) for _ in range(2)]
    vp2_r = [R(), R()]
    sigo2 = [A.alloc((NT, 256), BF16) for _ in range(2)]
    sigo2_r = [[R() for _ in range(NT)] for _ in range(2)]
    wT2 = [A.alloc((128,), BF16) for _ in range(2)]
    wT2_r = [R(), R()]
    Cf = A.alloc((2, 258), F32)
    Cf_r = R()
    Cbf2 = [A.alloc((2, 258), BF16) for _ in range(2)]
    Cbf2_r = [R(), R()]
    hu3 = [A.alloc((256,), F32) for _ in range(3)]
    hu3_r = [R(), R(), R()]
    sm3 = [A.alloc((8,), F32) for _ in range(3)]
    sm3_r = [R(), R(), R()]
    junk3 = [A.alloc((256,), BF16) for _ in range(3)]
    mlb3 = [A.alloc((256,), BF16) for _ in range(3)]
    mlb3_r = [R(), R(), R()]

    def pre_units(hd):
        sl = hd % 2
        qT, kT, qk_r, sigo, sigo_r = qT2[sl], kT2[sl], qk2_r[sl], sigo2[sl], sigo2_r[sl]
        wsel = (0, 1, 3 + sl, 2)
        units = []

        def u_load():
            for m, off in enumerate((OQ, OK_, OV, OO)):
                c0 = off + hd * 256
                for part in range(2):
                    load_cast(C, wq[wsel[m]][:, 4 * part:4 * part + 4, :],
                              w_in[part * 512:(part + 1) * 512, c0:c0 + 256].rearrange("(k p) n -> p k n", p=128),
                              wq_r[wsel[m]][part])
        units.append(u_load)
        for m, dstT in ((0, qT), (1, kT)):
            for cc in range(2):
                ch = m * 8 + hd * 2 + cc
                bi = (m * 2 + cc) % 2

                def u_diag(ch=ch, bi=bi):
                    for j in range(4):
                        P.op("dve", lambda e, j=j: e.tensor_scalar(out=dg[bi][j], in0=C.ident, scalar1=cwT[:, ch, j:j + 1],
                                                                   scalar2=None, op0=ALU.mult),
                             reads=[C.cst_r, cwT_r], writes=[dg_r[bi]])
                units.append(u_diag)
                for tg in range(4):
                    def u_proj(m=m, cc=cc, tg=tg, ch=ch, bi=bi, dstT=dstT):
                        for k in range(KC):
                            P.op("pe", lambda e, k=k: e.matmul(
                                bank(6), lhsT=wq[m][:, k, cc * 128:(cc + 1) * 128], rhs=hT[:, k, tg * 512:(tg + 1) * 512],
                                start=(k == 0), stop=(k == KC - 1)), reads=[wq_r[m][k // 4]], writes=[psr[6]])
                        P.op("act", lambda e: e.activation(
                            out=raw[bi][:, 3 + tg * 512:3 + (tg + 1) * 512], in_=bank(6), func=AF.Copy),
                            reads=[psr[6]], writes=[raw_r[bi]])
                        for j in range(4):
                            P.op("pe", lambda e, j=j: e.matmul(bank(7), lhsT=dg[bi][j], rhs=raw[bi][:, tg * 512 + j:tg * 512 + j + 512],
                                                               start=(j == 0), stop=(j == 3)),
                                 reads=[dg_r[bi], raw_r[bi]], writes=[psr[7]])
                        P.op("act", lambda e: e.activation(out=dstT[:, cc, tg * 512:(tg + 1) * 512], in_=bank(7), func=AF.Silu,
                                                           bias=cwT[:, ch, 4:5]),
                             reads=[psr[7], cwT_r], writes=[qk_r[m][cc]])
                    units.append(u_proj)
        for c in range(NT):
            def u_gate(c=c):
                cs = slice(c * 128, (c + 1) * 128)
                ob = 6 + c % 2
                for k in range(KC):
                    P.op("pe", lambda e, k=k: e.matmul(bank(ob)[:, 0:256], lhsT=hT[:, k, cs], rhs=wq[2][:, k, :],
                                                       start=(k == 0), stop=(k == KC - 1)),
                         reads=[wq_r[2][k // 4]], writes=[psr[ob]])
                P.op("act", lambda e: e.activation(out=sigo[:, c, :], in_=bank(ob)[:, 0:256], func=AF.Sigmoid),
                     reads=[psr[ob]], writes=[sigo_r[c]])
            units.append(u_gate)
        return units

    def loop_steps(hd):
        sl = hd % 2
        qT, kT, qk_r, sigo, sigo_r = qT2[sl], kT2[sl], qk2_r[sl], sigo2[sl], sigo2_r[sl]
        wv, wv_r = wq[3 + sl], wq_r[3 + sl]
        qkr = [qk_r[0][0], qk_r[0][1], qk_r[1][0], qk_r[1][1]]
        def chunk_vars(c):
            p = c % 2
            q3 = c % 3
            return dict(cs=slice(c * 128, (c + 1) * 128), col=c * 4 + hd, ktok=ktok2[p], ktok_r=ktok2_r[p], vaug=vaug2[p],
                        vaug_r=vaug2_r[p], vp=vp2[p], vp_r=vp2_r[p], wT=wT2[p], wT_r=wT2_r[p], hu=hu3[q3], hu_r=hu3_r[q3],
                        sm=sm3[q3], sm_r=sm3_r[q3], junk=junk3[q3], mlb=mlb3[q3], mlb_r=mlb3_r[q3], ab=p, db=2 + p, tb=p)

        def stageA1(c):
            v_ = chunk_vars(c)
            cs, col, ktok, ktok_r, vaug, vaug_r, vp, vp_r, wT, wT_r, ab, db = (v_[k] for k in (
                "cs", "col", "ktok", "ktok_r", "vaug", "vaug_r", "vp", "vp_r", "wT", "wT_r", "ab", "db"))
            abf = bank(ab).bitcast(BF16)
            for k in range(KC):
                P.op("pe", lambda e, k=k: e.matmul(bank(ab)[:, 0:256], lhsT=hT[:, k, cs], rhs=wv[:, k, :],
                                                   start=(k == 0), stop=(k == KC - 1)),
                     reads=[wv_r[k // 4]], writes=[psr[ab]])
            for dk in range(2):
                P.op("pe", lambda e, dk=dk: e.transpose(out=abf[:, 512 + dk * 128:512 + (dk + 1) * 128],
                                                        in_=kT[:, dk, cs], identity=C.ident),
                     reads=[qk_r[1][dk], C.cst_r], writes=[psr[ab]])
            P.op("act", lambda e: e.activation(out=vaug[:, 0:256], in_=bank(ab)[:, 0:256], func=AF.Copy),
                 reads=[psr[ab]], writes=[vaug_r])
            P.op("act", lambda e: e.activation(out=ktok, in_=abf[:, 512:768], func=AF.Copy),
                 reads=[psr[ab]], writes=[ktok_r])
            for dk in range(2):
                P.op("pe", lambda e, dk=dk: e.matmul(bank(db)[:, 384:512], lhsT=kT[:, dk, cs], rhs=qT[:, dk, cs],
                                                     start=(dk == 0), stop=(dk == 1)),
                     reads=qkr, writes=[psr[db]])
            P.op("dve", lambda e: e.scalar_tensor_tensor(
                out=wT, in0=bank(db)[:, 384:512], scalar=e_in[:, col:col + 1], in1=C.tri, op0=ALU.mult, op1=ALU.mult),
                reads=[psr[db], ee_r, C.cst_r], writes=[wT_r])
            if c < NT - 1:
                P.op("dve", lambda e: e.tensor_scalar(out=vp, in0=vaug, scalar1=e_v[:, col:col + 1], scalar2=None, op0=ALU.mult),
                     reads=[vaug_r, ee_r], writes=[vp_r])

        def stageA2(c):
            v_ = chunk_vars(c)
            cs, col, ktok, ktok_r, vaug, vaug_r, vp, vp_r, wT, wT_r, db = (v_[k] for k in (
                "cs", "col", "ktok", "ktok_r", "vaug", "vaug_r", "vp", "vp_r", "wT", "wT_r", "db"))
            if c < NT - 1:
                for dk in range(2):
                    P.op("pe", lambda e, dk=dk: e.matmul(bank(4 + dk)[:, 0:258], lhsT=ktok[:, dk * 128:(dk + 1) * 128], rhs=vp,
                                                         start=True, stop=True),
                         reads=[ktok_r, vp_r], writes=[psr[4 + dk]])
            P.op("pe", lambda e: e.matmul(bank(db)[:, 0:258], lhsT=wT, rhs=vaug, start=True, stop=(c == 0)),
                 reads=[wT_r, vaug_r], writes=[psr[db]])
            if c > 0:
                cprev, cprev_r = Cbf2[(c - 1) % 2], Cbf2_r[(c - 1) % 2]
                for dk in range(2):
                    P.op("pe", lambda e, dk=dk: e.matmul(bank(db)[:, 0:258], lhsT=qT[:, dk, cs], rhs=cprev[:, dk, :],
                                                         start=False, stop=(dk == 1)),
                         reads=qkr + [cprev_r], writes=[psr[db]])
            if c < NT - 1:
                for dk in range(2):
                    if c == 0:
                        P.op("dve", lambda e, dk=dk: e.tensor_copy(out=Cf[:, dk, :], in_=bank(4 + dk)[:, 0:258]),
                             reads=[psr[4 + dk]], writes=[Cf_r])
                    else:
                        P.op("dve", lambda e, dk=dk: e.scalar_tensor_tensor(
                            out=Cf[:, dk, :], in0=Cf[:, dk, :], scalar=e_L[:, col:col + 1], in1=bank(4 + dk)[:, 0:258],
                            op0=ALU.mult, op1=ALU.add), reads=[psr[4 + dk], Cf_r, ee_r], writes=[Cf_r])
                ccur, ccur_r = Cbf2[c % 2], Cbf2_r[c % 2]
                P.op("act", lambda e: e.activation(out=ccur, in_=Cf, func=AF.Copy), reads=[Cf_r], writes=[ccur_r])

        def stageB(c):
            v_ = chunk_vars(c)
            col, hu, hu_r, sm, sm_r, junk, db = (v_[k] for k in ("col", "hu", "hu_r", "sm", "sm_r", "junk", "db"))
            P.op("dve", lambda e, col=col, sm=sm, db=db: e.tensor_tensor(out=sm[:, 0:1], in0=bank(db)[:, 256:257],
                                                                       in1=e_out[:, col:col + 1], op=ALU.mult),
                 reads=[psr[db], ee_r], writes=[sm_r])
            P.op("dve", lambda e, sm=sm: e.scalar_tensor_tensor(out=sm[:, 1:2], in0=sm[:, 0:1], scalar=-1.0, in1=sm[:, 0:1],
                                                                op0=ALU.mult, op1=ALU.max), reads=[sm_r], writes=[sm_r])
            P.op("dve", lambda e, sm=sm: e.tensor_scalar(out=sm[:, 1:2], in0=sm[:, 1:2], scalar1=1.0, scalar2=None, op0=ALU.max),
                 reads=[sm_r], writes=[sm_r])
            P.op("dve", lambda e, sm=sm: e.reciprocal(out=sm[:, 2:3], in_=sm[:, 1:2]), reads=[sm_r], writes=[sm_r])
            P.op("dve", lambda e, col=col, sm=sm: e.tensor_tensor(out=sm[:, 3:4], in0=sm[:, 2:3], in1=e_out[:, col:col + 1], op=ALU.mult),
                 reads=[sm_r, ee_r], writes=[sm_r])
            P.op("dve", lambda e, sm=sm, hu=hu, db=db: e.tensor_scalar(out=hu, in0=bank(db)[:, 0:256], scalar1=sm[:, 3:4], scalar2=None,
                                                                     op0=ALU.mult),
                 reads=[psr[db], sm_r], writes=[hu_r])
            P.op("act", lambda e, sm=sm, hu=hu, junk=junk: e.activation(out=junk, in_=hu, func=AF.Square, accum_out=sm[:, 4:5]),
                 reads=[hu_r, sm_r], writes=[sm_r])
            P.op("pool", lambda e, sm=sm: e.tensor_scalar(out=sm[:, 5:6], in0=sm[:, 4:5], scalar1=1.0 / 256, scalar2=EPS,
                                                          op0=ALU.mult, op1=ALU.add), reads=[sm_r], writes=[sm_r])
            P.op("pool", lambda e, sm=sm: e.tensor_tensor(out=sm[:, 5:6], in0=sm[:, 5:6], in1=neghalf, op=ALU.pow),
                 reads=[sm_r, nh_r], writes=[sm_r])

        def stageC(c):
            v_ = chunk_vars(c)
            cs, hu, hu_r, sm, sm_r, mlb, mlb_r, tb = (v_[k] for k in ("cs", "hu", "hu_r", "sm", "sm_r", "mlb", "mlb_r", "tb"))
            abf = bank(tb).bitcast(BF16)
            ab = tb
            P.op("dve", lambda e, hd=hd, sm=sm, hu=hu: e.scalar_tensor_tensor(out=hu, in0=hu, scalar=sm[:, 5:6],
                                                                              in1=hg[:, hd * 256:(hd + 1) * 256],
                                                                              op0=ALU.mult, op1=ALU.mult),
                 reads=[hu_r, sm_r, hg_r], writes=[hu_r])
            P.op("dve", lambda e, c=c, hu=hu, mlb=mlb: e.tensor_tensor(out=mlb, in0=hu, in1=sigo[:, c, :], op=ALU.mult),
                 reads=[hu_r, sigo_r[c]], writes=[mlb_r])
            for j in range(2):
                P.op("pe", lambda e, j=j, abf=abf, mlb=mlb: e.transpose(out=abf[:, 768 + j * 128:768 + (j + 1) * 128],
                                                                       in_=mlb[:, j * 128:(j + 1) * 128], identity=C.ident),
                     reads=[mlb_r, C.cst_r], writes=[psr[ab]])
            P.op("act", lambda e, hd=hd, cs=cs, abf=abf: e.activation(
                out=mlT[:, 2 * hd:2 * hd + 2, cs], in_=abf[:, 768:1024].rearrange("p (j n) -> p j n", j=2), func=AF.Copy),
                reads=[psr[ab]], writes=[C.mlT_r])


        steps = []
        for step in range(NT + 2):
            def st(fill, step=step):
                if step == 0:
                    stageA1(0)
                if step + 1 < NT:
                    stageA1(step + 1)
                fill()
                if step < NT:
                    stageA2(step)
                fill()
                if 0 <= step - 1 < NT:
                    stageB(step - 1)
                fill()
                if 0 <= step - 2 < NT:
                    stageC(step - 2)
            steps.append(st)
        return steps

    for u in pre_units(0):
        u()
    NH = int(os.environ.get("NH", 4))
    for hd in range(NH):
        nxt = pre_units(hd + 1) if hd < NH - 1 else []
        steps = loop_steps(hd) if not os.environ.get("NOLOOP") else []
        if not steps:
            for u in nxt:
                u()
            continue
        nfill = 3 * len(steps)
        per = -(-len(nxt) // nfill) if nxt else 0
        pos = [0]

        def fill():
            for u in nxt[pos[0]:pos[0] + per]:
                u()
            pos[0] += per
        for st in steps:
            st(fill)
        for u in nxt[pos[0]:]:
            u()

def merge_phase(C, A, hT, attT, mlT, w_in, w_a, w_m, w_out, g_post, x_src, x_dst):
    P, PS, psr = C.P, C.PS, C.psr
    OGA, OGM = 8712, 9736

    def bank(b):
        return PS[:, 512 * b:512 * (b + 1)]

    mgT = A.alloc((KC, S), BF16)
    wa = A.alloc((4, 1024), BF16)
    wa_r = [Reg() for _ in range(4)]
    wm = A.alloc((KC, 1024), BF16)
    wm_r = [Reg() for _ in range(KC)]
    wo = A.alloc((KC, 1024), BF16)
    wo_r = [Reg() for _ in range(KC)]
    wg = [[A.alloc((KC, 128), BF16) for _ in range(2)] for _ in range(2)]
    wg_r = [[Reg() for _ in range(2)] for _ in range(2)]
    load_cast(C, wg[0][0], w_in[:, OGA:OGA + 128].rearrange("(k p) n -> p k n", p=128), wg_r[0][0])
    load_cast(C, wg[0][1], w_in[:, OGM:OGM + 128].rearrange("(k p) n -> p k n", p=128), wg_r[0][1])
    for k in range(4):
        load_cast(C, wa[:, k, :], w_a[k * 128:(k + 1) * 128, :], wa_r[k])
    for k in range(KC):
        load_cast(C, wm[:, k, :], w_m[k * 128:(k + 1) * 128, :], wm_r[k])
    ga = [A.alloc((512,), F32) for _ in range(2)]
    ga_r = [Reg() for _ in range(2)]
    gm = [A.alloc((512,), F32) for _ in range(2)]
    gm_r = [Reg() for _ in range(2)]
    ta = [A.alloc((512,), F32) for _ in range(2)]
    ta_r = [Reg() for _ in range(2)]
    gb = A.alloc((D,), F32)
    gb_r = Reg()
    P.dma("sp", gb, g_post.partition_broadcast(128), "g", writes=[gb_r])
    j = 0
    for mc in range(KC):
        sl = mc % 2
        if mc > 0:
            load_cast(C, wg[sl][0], w_in[:, OGA + mc * 128:OGA + (mc + 1) * 128].rearrange("(k p) n -> p k n", p=128), wg_r[sl][0])
            load_cast(C, wg[sl][1], w_in[:, OGM + mc * 128:OGM + (mc + 1) * 128].rearrange("(k p) n -> p k n", p=128), wg_r[sl][1])
        if mc == 0:
            for k in range(KC):
                load_cast(C, wo[:, k, :], w_out[k * 128:(k + 1) * 128, :], wo_r[k])
        for tg in range(4):
            q = j % 2
            j += 1
            ts = slice(tg * 512, (tg + 1) * 512)
            b0 = 4 * q
            for gi_, (gbuf, gr) in enumerate(((ga, ga_r), (gm, gm_r))):
                for k in range(KC):
                    P.op("pe", lambda e, k=k, gi_=gi_, sl=sl, ts=ts, b0=b0: e.matmul(
                        bank(b0 + gi_), lhsT=wg[sl][gi_][:, k, :], rhs=hT[:, k, ts], start=(k == 0), stop=(k == KC - 1)),
                        reads=[wg_r[sl][gi_]], writes=[psr[b0 + gi_]])
                P.op("act", lambda e, gbuf=gbuf, q=q, gi_=gi_, b0=b0: e.activation(out=gbuf[q], in_=bank(b0 + gi_), func=AF.Sigmoid),
                     reads=[psr[b0 + gi_]], writes=[gr[q]])
            for k in range(4):
                P.op("pe", lambda e, k=k, mc=mc, ts=ts, b0=b0: e.matmul(
                    bank(b0 + 2), lhsT=wa[:, k, mc * 128:(mc + 1) * 128], rhs=attT[:, k, ts], start=(k == 0), stop=(k == 3)),
                    reads=[wa_r[k], C.attT_r], writes=[psr[b0 + 2]])
            for k in range(KC):
                P.op("pe", lambda e, k=k, mc=mc, ts=ts, b0=b0: e.matmul(
                    bank(b0 + 3), lhsT=wm[:, k, mc * 128:(mc + 1) * 128], rhs=mlT[:, k, ts], start=(k == 0), stop=(k == KC - 1)),
                    reads=[wm_r[k], C.mlT_r], writes=[psr[b0 + 3]])
            P.op("dve", lambda e, q=q, b0=b0: e.tensor_tensor(out=ta[q], in0=bank(b0 + 2), in1=ga[q], op=ALU.mult),
                 reads=[psr[b0 + 2], ga_r[q]], writes=[ta_r[q]])
            P.op("dve", lambda e, q=q, b0=b0: e.tensor_tensor(out=gm[q], in0=bank(b0 + 3), in1=gm[q], op=ALU.mult),
                 reads=[psr[b0 + 3], gm_r[q]], writes=[gm_r[q]])
            P.op("dve", lambda e, q=q, mc=mc, ts=ts: e.tensor_tensor(out=mgT[:, mc, ts], in0=ta[q], in1=gm[q], op=ALU.add),
                 reads=[ta_r[q], gm_r[q]], writes=[C.mg_r])
    xc = [A.alloc((D,), F32) for _ in range(2)]
    tt = [A.alloc((D,), F32) for _ in range(2)]
    xc_r = [Reg() for _ in range(2)]
    tth_r = [[Reg(), Reg()] for _ in range(2)]
    ss2 = A.alloc((NT,), F32)
    r2 = A.alloc((NT,), F32)
    junk = ga[0].bitcast(BF16)
    junk_r = ga_r[0]
    r2_r = [Reg() for _ in range(NT)]
    for i in range(NT):
        s = i % 2
        P.dma("sp", xc[s], x_src[i * 128:(i + 1) * 128, :], f"xc{s}", writes=[xc_r[s]])
        pb = (i % 4) * 2
        psf = PS[:, 512 * pb:512 * (pb + 2)]
        for h in range(2):
            for k in range(KC):
                P.op("pe", lambda e, h=h, k=k, i=i, pb=pb: e.matmul(
                    bank(pb + h), lhsT=mgT[:, k, i * 128:(i + 1) * 128], rhs=wo[:, k, h * 512:(h + 1) * 512],
                    start=(k == 0), stop=(k == KC - 1)), reads=[wo_r[k], C.mg_r], writes=[psr[pb + h]])
        P.op("act", lambda e, i=i, psf=psf: e.activation(out=junk, in_=psf, func=AF.Square, accum_out=ss2[:, i:i + 1]),
             reads=[psr[pb], psr[pb + 1]], writes=[r2_r[i], junk_r])
        P.op("act", lambda e, i=i: e.activation(out=r2[:, i:i + 1], in_=ss2[:, i:i + 1], func=AF.Ln, scale=1.0 / D, bias=EPS),
             reads=[r2_r[i]], writes=[r2_r[i]])
        P.op("act", lambda e, i=i: e.activation(out=r2[:, i:i + 1], in_=r2[:, i:i + 1], func=AF.Exp, scale=-0.5),
             reads=[r2_r[i]], writes=[r2_r[i]])
        for h in range(2):
            P.op("dve", lambda e, s=s, h=h, pb=pb: e.tensor_tensor(
                out=tt[s][:, 512 * h:512 * (h + 1)], in0=bank(pb + h), in1=gb[:, 512 * h:512 * (h + 1)], op=ALU.mult),
                reads=[psr[pb + h], gb_r, r2_r[i]], writes=[tth_r[s][h]])
        P.op("dve", lambda e, s=s, i=i: e.scalar_tensor_tensor(out=xc[s], in0=tt[s], scalar=r2[:, i:i + 1], in1=xc[s],
                                                              op0=ALU.mult, op1=ALU.add),
             reads=[tth_r[s][0], tth_r[s][1], r2_r[i], xc_r[s]], writes=[xc_r[s]])
        P.dma("sp", x_dst[i * 128:(i + 1) * 128, :], xc[s], f"xo{s}", reads=[xc_r[s]])


def mixer_phase(C, x_src, x_dst, W):
    A = C.A.child()
    hT = A.alloc((KC, S), BF16)
    attT = A.alloc((4, S), BF16)
    mlT = A.alloc((8, S), BF16)
    C.attT_r = Reg()
    C.mlT_r = Reg()
    C.mg_r = Reg()
    base = A.off
    end = A.end
    norm_transpose(C, Arena(A.ap, base, end), x_src, W["mix_pre_g"][0], hT)
    attention_phase(C, Arena(A.ap, base, end), hT, attT, W["w_in"][0])
    C.P.barrier()
    mlstm_phase(C, Arena(A.ap, base, end), hT, mlT, W["w_in"][0], W["conv_w"][0], W["conv_b"][0],
                W["mlstm_i_bias"][0], W["mlstm_f_bias"][0], W["mlstm_head_g"][0])
    C.P.barrier()
    merge_phase(C, Arena(A.ap, base, end), hT, attT, mlT, W["w_in"][0], W["w_att_branch"][0], W["w_mlstm_branch"][0],
                W["w_out"][0], W["mix_post_g"][0], x_src, x_dst)
    C.P.barrier()

def host_consts():
    c = {}
    bf = ml_dtypes.bfloat16
    c["ident"] = np.eye(128, dtype=np.float32).astype(bf)
    half = 16
    inv_freq = np.power(np.float32(500000.0), -(np.arange(half, dtype=np.float32) * 2.0 / 32)).astype(np.float32)
    ang = np.arange(S, dtype=np.float32)[None, :] * inv_freq[:, None]
    c["cos"] = np.concatenate([np.cos(ang), np.cos(ang)], 0).astype(np.float32)
    c["sin"] = np.concatenate([np.sin(ang), np.sin(ang)], 0).astype(np.float32)
    rm = np.zeros((32, 32), np.float32)
    for j in range(16):
        rm[16 + j, j] = -1.0
        rm[j, 16 + j] = 1.0
    c["rm"] = rm.astype(bf)
    jj = np.arange(128)[:, None]
    ii = np.arange(128)[None, :]
    NEG = -30000.0
    c["maskc"] = np.where(jj <= ii, 0.0, NEG).astype(bf)
    c["maskp"] = np.where(jj >= ii, 0.0, NEG).astype(bf)
    c["maskn"] = np.full((128, 128), NEG, np.float32).astype(bf)
    c["tri"] = (jj <= ii).astype(np.float32)
    c["ones_f"] = np.ones((128, 128), np.float32)
    c["identf"] = np.eye(128, dtype=np.float32)
    c["ones_bf"] = np.ones((128, 128), np.float32).astype(bf)
    return c


def build(stage="full"):
    nc = bass.Bass("TRN2", target_bir_lowering=False)

    def din(name, shape, dt=F32):
        return nc.dram_tensor(name, list(shape), dt, kind="ExternalInput").ap()

    x = din("x", [S, D])
    W = {}
    for name, shape in [("ffn1_pre_g", [1, D]), ("ffn1_w_gate", [1, D, FF]), ("ffn1_w_up", [1, D, FF]),
                        ("ffn1_w_down", [1, FF, D]), ("ffn1_post_g", [1, D]), ("mix_pre_g", [1, D]),
                        ("w_in", [1, D, IN_W]), ("conv_w", [1, 4, 2048]), ("conv_b", [1, 2048]),
                        ("mlstm_i_bias", [1, 4]), ("mlstm_f_bias", [1, 4]), ("mlstm_head_g", [1, 1024]),
                        ("w_att_branch", [1, 512, D]), ("w_mlstm_branch", [1, D, D]), ("w_out", [1, D, D]),
                        ("mix_post_g", [1, D]), ("ffn2_pre_g", [1, D]), ("ffn2_w_gate", [1, D, FF]),
                        ("ffn2_w_up", [1, D, FF]), ("ffn2_w_down", [1, FF, D]), ("ffn2_post_g", [1, D])]:
        W[name] = din(name, shape)
    CD = {}
    for name, shape, dt in [("c_ident", [128, 128], BF16), ("c_cos", [32, S], F32), ("c_sin", [32, S], F32),
                            ("c_rm", [32, 32], BF16), ("c_maskc", [128, 128], BF16), ("c_maskp", [128, 128], BF16),
                            ("c_maskn", [128, 128], BF16), ("c_tri", [128, 128], F32), ("c_ones_f", [128, 128], F32), ("c_identf", [128, 128], F32),
                            ("c_ones_bf", [128, 128], BF16)]:
        CD[name] = din(name, shape, dt)
    c_ident = CD["c_ident"]
    out = nc.dram_tensor("out", [S, D], F32, kind="ExternalOutput").ap()
    x1 = nc.dram_tensor("x1", [S, D], F32, kind="Internal").ap()
    x2 = nc.dram_tensor("x2", [S, D], F32, kind="Internal").ap()

    with ExitStack() as st:
        ARENA_BYTES = 212480
        arena = st.enter_context(nc.sbuf_tensor("arena", [128, ARENA_BYTES // 2], BF16))
        PS = st.enter_context(nc.psum_tensor("ps", [128, 4096], F32))
        C = Ctx()
        C.nc = nc
        C.P = P = Prog(nc)
        C.PS = PS
        C.psr = [Reg(f"ps{i}", excl=True) for i in range(8)]
        top = Arena(arena, 0, ARENA_BYTES)
        C.ident = top.alloc((128,), BF16)
        C.ident_r = Reg()
        P.dma("sp", C.ident, c_ident, "cst", writes=[C.ident_r])
        C.dram = CD
        C.cst_r = C.ident_r
        for nm, shp, dt in [("rm", (32,), BF16), ("maskc", (128,), BF16), ("maskp", (128,), BF16), ("maskn", (128,), BF16),
                            ("tri", (128,), F32), ("ones_f", (128,), F32), ("identf", (128,), F32), ("ones_bf", (128,), BF16)]:
            v = top.alloc(shp, dt)
            if nm == "rm":
                v = v[:32]
            setattr(C, nm, v)
            P.dma("sp", v, CD["c_" + nm], "cst", writes=[C.cst_r])
        C.stage = [top.alloc((1024,), F32) for _ in range(NST)]
        C.stage_r = [Reg() for _ in range(NST)]
        C.stage_i = 0
        C.A = Arena(arena, top.off, ARENA_BYTES)

        if stage == "attn":
            A = C.A.child()
            hT = A.alloc((KC, S), BF16)
            attT = A.alloc((4, S), BF16)
            C.attT_r = Reg()
            mk = Arena(arena, A.off, ARENA_BYTES)
            norm_transpose(C, mk, x, W["mix_pre_g"][0], hT)
            attention_phase(C, Arena(arena, A.off, ARENA_BYTES), hT, attT, W["w_in"][0])
            P.barrier()
            ov = out.rearrange("(a b) d -> a (b d)", a=1024).rearrange("(k p) t -> p k t", p=128)
            for k in range(4):
                for hh in range(2):
                    P.dma("pool", ov[:, k, hh * 1024:(hh + 1) * 1024], attT[:, k, hh * 1024:(hh + 1) * 1024], "dbg")
        if stage == "full":
            ffn_phase(C, x, x1, W["ffn1_pre_g"][0], W["ffn1_w_gate"][0], W["ffn1_w_up"][0], W["ffn1_w_down"][0],
                      W["ffn1_post_g"][0])
            mixer_phase(C, x1, x2, W)
            ffn_phase(C, x2, out, W["ffn2_pre_g"][0], W["ffn2_w_gate"][0], W["ffn2_w_up"][0], W["ffn2_w_down"][0],
                      W["ffn2_post_g"][0])
        if stage == "mix":
            mixer_phase(C, x, out, W)
        if stage == "ml":
            A = C.A.child()
            hT = A.alloc((KC, S), BF16)
            mlT = A.alloc((8, S), BF16)
            C.mlT_r = Reg()
            mk = Arena(arena, A.off, ARENA_BYTES)
            norm_transpose(C, mk, x, W["mix_pre_g"][0], hT)
            mlstm_phase(C, Arena(arena, A.off, ARENA_BYTES), hT, mlT, W["w_in"][0], W["conv_w"][0], W["conv_b"][0],
                        W["mlstm_i_bias"][0], W["mlstm_f_bias"][0], W["mlstm_head_g"][0])
            P.barrier()
            ov = out.rearrange("(a b) d -> a (b d)", a=1024).rearrange("(k p) t -> p k t", p=128)
            for k in range(8):
                for hh in range(2):
                    P.dma("pool", ov[:, k, hh * 1024:(hh + 1) * 1024], mlT[:, k, hh * 1024:(hh + 1) * 1024], "dbg")
        if stage == "ffn1a":
            A = C.A.child()
            hT_ar = A.sub(KC * S * 2)
            actT_ar = A.sub(FC * S * 2)
            hT = hT_ar.child().alloc((KC, S), BF16)
            norm_transpose(C, actT_ar.child(), x, W["ffn1_pre_g"][0], hT)
            ov = out.rearrange("(a b) d -> a (b d)", a=1024).rearrange("(k p) t -> p k t", p=128)
            for k in range(KC):
                for hh in range(2):
                    P.dma("pool", ov[:, k, hh * 1024:(hh + 1) * 1024], hT[:, k, hh * 1024:(hh + 1) * 1024], "dbg")
        if stage == "ffn1b":
            ffn_phase(C, x, out, W["ffn1_pre_g"][0], W["ffn1_w_gate"][0], W["ffn1_w_up"][0], W["ffn1_w_down"][0],
                      W["ffn1_post_g"][0], stop_after="B")
        if stage == "ffn1":
            ffn_phase(C, x, out, W["ffn1_pre_g"][0], W["ffn1_w_gate"][0], W["ffn1_w_up"][0], W["ffn1_w_down"][0],
                      W["ffn1_post_g"][0])
        P.finish()
        P.emit()
        print("prog stats", P.stats, "sems", len(P.dma_tot) + 5)
        if os.environ.get("DUMP"):
            for e in ("sp", "dve", "act"):
                print("====", e)
                for r in P.dump[e][-int(os.environ["DUMP"]):]:
                    print(r)
    return nc


_NC_CACHE = {}


def kernel(**inputs):
    stage = inputs.pop("_stage", os.environ.get("KSTAGE", "full"))
    if stage not in _NC_CACHE:
        _NC_CACHE[stage] = build(stage)
    nc = _NC_CACHE[stage]
    consts = host_consts()
    xfull = np.ascontiguousarray(inputs["x"], dtype=np.float32)
    shared = {k: np.ascontiguousarray(v, dtype=np.float32) for k, v in inputs.items() if k != "x"}
    for k, v in consts.items():
        shared["c_" + k] = v
    in_maps = []
    ncores = int(os.environ.get("NCORES", 8))
    for b in range(ncores):
        m = dict(shared)
        m["x"] = xfull[b]
        in_maps.append(m)
    res = run_bass_kernel_spmd(nc, in_maps, core_ids=list(range(ncores)))
    return np.stack([r["out"] for r in res.results], axis=0)
```

```python
from contextlib import ExitStack
import math
import os

import numpy as np
import ml_dtypes
import concourse.bass as bass
import concourse.mybir as mybir
from concourse.bass_utils import run_bass_kernel_spmd

F32 = mybir.dt.float32
BF16 = mybir.dt.bfloat16
AF = mybir.ActivationFunctionType
ALU = mybir.AluOpType
AX = mybir.AxisListType

S = 2048
D = 1024
FF = 2816
NT = S // 128
KC = D // 128
FC = FF // 128
IN_W = 10760
EPS = 1e-6
ENGS = ("pe", "act", "dve", "pool", "sp")


class Reg:
    __slots__ = ("name", "w", "rs", "rd", "excl")

    def __init__(self, name="", excl=False):
        self.name = name
        self.excl = excl
        self.w = None
        self.rs = {}
        self.rd = []


class Ins:
    __slots__ = ("eng", "fn", "deps", "signal", "val", "dma", "key")

    def __init__(self, eng, fn, dma=False, key=None):
        self.eng = eng
        self.fn = fn
        self.deps = ()
        self.signal = dma
        self.val = 0
        self.dma = dma
        self.key = key


class Prog:
    def __init__(self, nc):
        self.nc = nc
        self.engs = {e: [] for e in ENGS}
        self.dma_tot = {}
        self.dma_last = {}

    def _add(self, ins, reads, writes):
        eng = ins.eng
        deps = {}
        for r in reads:
            d = r.w
            if d is not None:
                deps[id(d)] = d
            if r.excl:
                for e2, x in r.rs.items():
                    if e2 != eng:
                        deps[id(x)] = x
        for w in writes:
            d = w.w
            if d is not None:
                deps[id(d)] = d
            for e2, x in w.rs.items():
                if (not ins.dma) and e2 == eng:
                    continue
                deps[id(x)] = x
            for x in w.rd:
                deps[id(x)] = x
        out = []
        for d in deps.values():
            if d is ins:
                continue
            if (not d.dma) and (not ins.dma) and d.eng == "pe" and eng == "pe":
                continue
            d.signal = True
            out.append(d)
        ins.deps = out
        for r in reads:
            if ins.dma:
                r.rd.append(ins)
            else:
                r.rs[eng] = ins
        for w in writes:
            w.w = ins
            w.rs = {}
            w.rd = []
        self.engs[eng].append(ins)
        return ins

    def op(self, eng, fn, reads=(), writes=()):
        return self._add(Ins(eng, fn), reads, writes)

    def dma(self, queue, out, in_, key, reads=(), writes=(), **kw):
        ins = Ins(queue, lambda e: e.dma_start(out=out, in_=in_, **kw), dma=True, key=key)
        self.dma_tot[key] = self.dma_tot.get(key, 0) + 16
        ins.val = self.dma_tot[key]
        self.dma_last[key] = ins
        return self._add(ins, reads, writes)

    def barrier(self):
        lasts = []
        for e in ENGS:
            for ins in reversed(self.engs[e]):
                if ins.fn is not None and not ins.dma:
                    ins.signal = True
                    lasts.append(ins)
                    break
        lasts += list(self.dma_last.values())
        for e in ENGS:
            ins = Ins(e, None)
            ins.deps = [d for d in lasts if d.dma or d.eng != e]
            self.engs[e].append(ins)

    def finish(self):
        ins = Ins("sp", None)
        ins.deps = list(self.dma_last.values())
        self.engs["sp"].append(ins)

    def emit(self):
        nc = self.nc
        for e in ENGS:
            c = 0
            for ins in self.engs[e]:
                if ins.dma:
                    continue
                if ins.signal and ins.fn is not None:
                    c += 1
                    ins.val = c
        with ExitStack() as st:
            sems = {e: st.enter_context(nc.semaphore(f"s_{e}")) for e in ENGS}
            dsem = {k: st.enter_context(nc.semaphore(f"d_{k}")) for k in self.dma_tot}
            block = st.enter_context(nc.Block())
            bname = {"pe": "tensor", "act": "scalar", "dve": "vector", "pool": "gpsimd", "sp": "sync"}
            stats = {}
            self.dump = {}
            for e in ENGS:
                def body(engine, e=e):
                    seen = {}
                    nw = 0
                    for ins in self.engs[e]:
                        need = {}
                        for d in ins.deps:
                            s = ("d", d.key) if d.dma else ("c", d.eng)
                            if need.get(s, 0) < d.val:
                                need[s] = d.val
                        for s, v in need.items():
                            if seen.get(s, 0) < v:
                                seen[s] = v
                                sh = dsem[s[1]] if s[0] == "d" else sems[s[1]]
                                engine.wait_ge(sh, v)
                                nw += 1
                        if os.environ.get("DUMP"):
                            self.dump.setdefault(e, []).append((sorted((k, v) for k, v in need.items()), ins.fn is not None, ins.dma, ins.key, ins.signal, ins.val))
                        if ins.fn is not None:
                            bi = ins.fn(engine)
                            if ins.dma:
                                bi.then_inc(dsem[ins.key], 16)
                            elif ins.signal:
                                bi.then_inc(sems[e], 1)
                    stats[e] = (len(self.engs[e]), nw)
                getattr(block, bname[e])(body)
            self.stats = stats


class Arena:
    def __init__(self, ap, start, end):
        self.ap = ap
        self.start = start
        self.off = start
        self.end = end

    def alloc(self, free_shape, dt, parts=128):
        n = 1
        for v in free_shape:
            n *= v
        esz = 4 if dt == F32 else 2
        nbytes = n * esz
        st = (self.off + 63) // 64 * 64
        assert st + nbytes <= self.end, f"arena overflow: need {st + nbytes} > {self.end}"
        self.off = st + nbytes
        Arena.last = (st, tuple(free_shape), dt)
        v = self.ap[:parts, st // 2:(st + nbytes) // 2]
        if dt == F32:
            v = v.bitcast(F32)
        if len(free_shape) == 2:
            v = v.rearrange("p (a b) -> p a b", a=free_shape[0])
        elif len(free_shape) == 3:
            v = v.rearrange("p (a b c) -> p a b c", a=free_shape[0], b=free_shape[1])
        return v

    def sub(self, nbytes):
        st = (self.off + 63) // 64 * 64
        assert st + nbytes <= self.end, f"arena overflow(sub): need {st + nbytes} > {self.end}"
        self.off = st + nbytes
        return Arena(self.ap, st, st + nbytes)

    def child(self):
        return Arena(self.ap, self.start, self.end)


class Ctx:
    pass


DBG = {}


NST = 3


def load_cast(C, dst, src, dst_reg):
    P = C.P
    s = C.stage_i % NST
    C.stage_i += 1
    sh = src.shape
    n = 1
    for v in sh[1:]:
        n *= v
    assert n <= 1024
    stg = C.stage[s][:, :n]
    if len(sh) == 3:
        stg = stg.rearrange("p (a b) -> p a b", a=sh[1])
    P.dma("sp", stg, src, f"st{s}", writes=[C.stage_r[s]])
    P.op("pool", lambda e: e.tensor_copy(out=dst, in_=stg), reads=[C.stage_r[s]], writes=[dst_reg])


def norm_transpose(C, A_stage, x_src, g_pre_dram, hT):
    P, PS, psr = C.P, C.PS, C.psr
    gb = A_stage.alloc((D,), F32)
    gb_r = Reg()
    P.dma("sp", gb, g_pre_dram.partition_broadcast(128), "g", writes=[gb_r])
    ss = A_stage.alloc((NT,), F32)
    rstd = A_stage.alloc((NT,), F32)
    junk = A_stage.alloc((D,), BF16)
    junk_r = Reg()
    hb = [A_stage.alloc((D,), BF16) for _ in range(2)]
    hb_r = [Reg() for _ in range(2)]
    xs = [A_stage.alloc((D,), F32) for _ in range(NT)]
    xs_r = [Reg() for _ in range(NT)]
    ss_r = [Reg() for _ in range(NT)]
    rstd_r = Reg()
    for i in range(NT):
        P.dma("sp", xs[i], x_src[i * 128:(i + 1) * 128, :], f"xs{i}", writes=[xs_r[i]])
    for i in range(NT):
        P.op("act", lambda e, i=i: e.activation(out=junk, in_=xs[i], func=AF.Square, accum_out=ss[:, i:i + 1]),
             reads=[xs_r[i]], writes=[ss_r[i], junk_r])
    P.op("act", lambda e: e.activation(out=rstd, in_=ss, func=AF.Ln, scale=1.0 / D, bias=EPS),
         reads=ss_r, writes=[rstd_r])
    P.op("act", lambda e: e.activation(out=rstd, in_=rstd, func=AF.Exp, scale=-0.5),
         reads=[rstd_r], writes=[rstd_r])
    for i in range(NT):
        s = i % 2
        P.op("dve", lambda e, i=i, s=s: e.scalar_tensor_tensor(out=hb[s], in0=xs[i], scalar=rstd[:, i:i + 1], in1=gb,
                                                              op0=ALU.mult, op1=ALU.mult),
             reads=[xs_r[i], rstd_r, gb_r], writes=[hb_r[s]])
        b = i % 2
        psb = PS[:, 512 * b:512 * (b + 1)].bitcast(BF16)
        for k in range(KC):
            P.op("pe", lambda e, k=k, s=s, psb=psb: e.transpose(out=psb[:, k * 128:(k + 1) * 128],
                                                                in_=hb[s][:, k * 128:(k + 1) * 128], identity=C.ident),
                 reads=[hb_r[s], C.ident_r], writes=[psr[b]])
        P.op("act", lambda e, i=i, psb=psb: e.activation(out=hT[:, :, i * 128:(i + 1) * 128],
                                                         in_=psb.rearrange("p (k n) -> p k n", k=KC), func=AF.Copy),
             reads=[psr[b]], writes=[])
    P.barrier()


def ffn_phase(C, x_src, x_dst, g_pre, wg, wu, wd, g_post, stop_after=None):
    P, PS, psr = C.P, C.PS, C.psr
    A = C.A.child()
    hT_ar = A.sub(KC * S * 2)
    actT_ar = A.sub(FC * S * 2)
    hT = hT_ar.child().alloc((KC, S), BF16)
    actT = actT_ar.child().alloc((FC, S), BF16)
    norm_transpose(C, actT_ar.child(), x_src, g_pre, hT)

    NSL = 2
    wg_s = [A.alloc((KC, 256), BF16) for _ in range(NSL)]
    wu_s = [A.alloc((KC, 256), BF16) for _ in range(NSL)]
    wg_r = [[Reg(), Reg()] for _ in range(NSL)]
    wu_r = [[Reg(), Reg()] for _ in range(NSL)]
    wd_h = [A.alloc((FC, 512), BF16) for _ in range(2)]
    wd_r = [[Reg() for _ in range(FC // 2)] for _ in range(2)]
    sg = [A.alloc((512,), BF16) for _ in range(2)]
    sg_r = [Reg() for _ in range(2)]
    gb = A.alloc((D,), F32)
    gb_r = Reg()
    ss2 = A.alloc((NT,), F32)
    r2 = A.alloc((NT,), F32)
    junk = A.alloc((D,), BF16)
    junk_r = Reg()
    P.dma("sp", gb, g_post.partition_broadcast(128), "g", writes=[gb_r])

    wd_jobs = [(h, c2) for h in range(2) for c2 in range(FC // 2)]

    def load_wd_piece():
        if not wd_jobs:
            return
        h, c2 = wd_jobs.pop(0)
        load_cast(C, wd_h[h][:, 2 * c2:2 * c2 + 2, :],
                  wd[c2 * 256:(c2 + 1) * 256, h * 512:(h + 1) * 512].rearrange("(c p) n -> p c n", p=128),
                  wd_r[h][c2])

    j = 0
    for cb in range(FC // 2):
        s = cb % NSL
        for part in range(2):
            load_cast(C, wg_s[s][:, 4 * part:4 * part + 4, :],
                      wg[part * 512:(part + 1) * 512, cb * 256:(cb + 1) * 256].rearrange("(k p) n -> p k n", p=128),
                      wg_r[s][part])
            load_cast(C, wu_s[s][:, 4 * part:4 * part + 4, :],
                      wu[part * 512:(part + 1) * 512, cb * 256:(cb + 1) * 256].rearrange("(k p) n -> p k n", p=128),
                      wu_r[s][part])
        if cb >= 1:
            for _ in range(3):
                load_wd_piece()
        for sub in range(2):
            ffc = cb * 2 + sub
            for tg in range(4):
                bG = 2 * (j % 4)
                bU = bG + 1
                q = j % 2
                j += 1
                for k in range(KC):
                    P.op("pe", lambda e, k=k, s=s, sub=sub, tg=tg, bG=bG: e.matmul(
                        PS[:, 512 * bG:512 * (bG + 1)], lhsT=wg_s[s][:, k, sub * 128:(sub + 1) * 128],
                        rhs=hT[:, k, tg * 512:(tg + 1) * 512], start=(k == 0), stop=(k == KC - 1)),
                        reads=[wg_r[s][k // 4]], writes=[psr[bG]])
                for k in range(KC):
                    P.op("pe", lambda e, k=k, s=s, sub=sub, tg=tg, bU=bU: e.matmul(
                        PS[:, 512 * bU:512 * (bU + 1)], lhsT=wu_s[s][:, k, sub * 128:(sub + 1) * 128],
                        rhs=hT[:, k, tg * 512:(tg + 1) * 512], start=(k == 0), stop=(k == KC - 1)),
                        reads=[wu_r[s][k // 4]], writes=[psr[bU]])
                P.op("act", lambda e, q=q, bG=bG: e.activation(out=sg[q], in_=PS[:, 512 * bG:512 * (bG + 1)], func=AF.Silu),
                     reads=[psr[bG]], writes=[sg_r[q]])
                P.op("dve", lambda e, q=q, bU=bU, ffc=ffc, tg=tg: e.tensor_tensor(
                    out=actT[:, ffc, tg * 512:(tg + 1) * 512], in0=PS[:, 512 * bU:512 * (bU + 1)], in1=sg[q], op=ALU.mult),
                    reads=[psr[bU], sg_r[q]], writes=[])
    while wd_jobs:
        load_wd_piece()
    P.barrier()
    if stop_after == "B":
        ov = x_dst.rearrange("(a b) d -> a (b d)", a=1024).rearrange("(k p) t -> p k t", p=128)
        for k in range(KC):
            for hh in range(2):
                P.dma("pool", ov[:, k, hh * 1024:(hh + 1) * 1024], actT[:, k + 14, hh * 1024:(hh + 1) * 1024], "dbg")
        return

    Ah = hT_ar.child()
    NSC = int(os.environ.get('NSC', 2))
    NSX = int(os.environ.get('NSX', NSC))
    xc = [Ah.alloc((D,), F32) for _ in range(NSX)]
    tt = [Ah.alloc((D,), F32) for _ in range(NSC)]
    xc_r = [Reg() for _ in range(NSX)]
    tt_r = [Reg() for _ in range(NSC)]
    tth_r = [[Reg(), Reg()] for _ in range(NSC)]
    r2_r = [Reg() for _ in range(NT)]
    for i in range(int(os.environ.get("CT", NT))):
        s = i % NSC
        sx = i % NSX
        P.dma("sp", xc[sx], x_src[i * 128:(i + 1) * 128, :], f"xc{sx}", writes=[xc_r[sx]])
        pb = (i % 4) * 2
        psf = PS[:, 512 * pb:512 * (pb + 2)]
        for h in range(2):
            for ffc in range(FC):
                P.op("pe", lambda e, h=h, ffc=ffc, i=i, pb=pb: e.matmul(
                    PS[:, 512 * (pb + h):512 * (pb + h + 1)], lhsT=actT[:, ffc, i * 128:(i + 1) * 128],
                    rhs=wd_h[h][:, ffc, :], start=(ffc == 0), stop=(ffc == FC - 1)),
                    reads=[wd_r[h][ffc // 2]], writes=[psr[pb + h]])
        P.op("act", lambda e, i=i, psf=psf: e.activation(out=junk, in_=psf, func=AF.Square, accum_out=ss2[:, i:i + 1]),
             reads=[psr[pb], psr[pb + 1]], writes=[r2_r[i], junk_r])
        cstop = int(os.environ.get("CSTOP", 9))
        if cstop == 1:
            P.dma("sp", x_dst[i * 128:(i + 1) * 128, :], xc[sx], f"xo{s}", reads=[xc_r[sx], r2_r[i]])
            continue
        P.op("act", lambda e, i=i: e.activation(out=r2[:, i:i + 1], in_=ss2[:, i:i + 1], func=AF.Ln, scale=1.0 / D, bias=EPS),
             reads=[r2_r[i]], writes=[r2_r[i]])
        P.op("act", lambda e, i=i: e.activation(out=r2[:, i:i + 1], in_=r2[:, i:i + 1], func=AF.Exp, scale=-0.5,
                                                bias=math.log(0.5)),
             reads=[r2_r[i]], writes=[r2_r[i]])
        if cstop == 2:
            P.dma("sp", x_dst[i * 128:(i + 1) * 128, :], xc[sx], f"xo{s}", reads=[xc_r[sx], r2_r[i]])
            continue
        for h in range(2):
            if os.environ.get("DVEVAR") == "copy":
                P.op("dve", lambda e, s=s, h=h, pb=pb: e.tensor_copy(
                    out=tt[s][:, 512 * h:512 * (h + 1)], in_=PS[:, 512 * (pb + h):512 * (pb + h + 1)]),
                    reads=[psr[pb + h], gb_r], writes=[tth_r[s][h]])
                continue
            if os.environ.get("DVEVAR") == "sbuf":
                P.op("dve", lambda e, s=s, h=h, pb=pb: e.tensor_tensor(
                    out=tt[s][:, 512 * h:512 * (h + 1)], in0=xc[sx][:, 512 * h:512 * (h + 1)],
                    in1=gb[:, 512 * h:512 * (h + 1)], op=ALU.mult),
                    reads=[psr[pb + h], gb_r, xc_r[sx]], writes=[tth_r[s][h]])
                continue
            P.op("dve", lambda e, s=s, h=h, pb=pb: e.tensor_tensor(
                out=tt[s][:, 512 * h:512 * (h + 1)], in0=PS[:, 512 * (pb + h):512 * (pb + h + 1)],
                in1=gb[:, 512 * h:512 * (h + 1)], op=ALU.mult),
                reads=[psr[pb + h], gb_r, r2_r[i]], writes=[tth_r[s][h]])
        if cstop == 3:
            P.dma("sp", x_dst[i * 128:(i + 1) * 128, :], tt[s], f"xo{s}", reads=[xc_r[sx], r2_r[i], tth_r[s][0], tth_r[s][1]], writes=[tth_r[s][0], tth_r[s][1]])
            continue
        P.op("dve", lambda e, s=s, sx=sx, i=i: e.scalar_tensor_tensor(out=xc[sx], in0=tt[s], scalar=r2[:, i:i + 1], in1=xc[sx],
                                                              op0=ALU.mult, op1=ALU.add),
             reads=[tth_r[s][0], tth_r[s][1], r2_r[i], xc_r[sx]], writes=[xc_r[sx]])
        P.dma("sp", x_dst[i * 128:(i + 1) * 128, :], xc[sx], f"xo{sx}", reads=[xc_r[sx]])
    P.barrier()


def tok_slice(start, step):
    return slice(start, start + step * 127 + 1, step) if step > 1 else slice(start, start + 128)


def attention_phase(C, A, hT, attT, w_in):
    P, PS, psr = C.P, C.PS, C.psr
    cos = A.alloc((S,), F32)[:32]
    sin = A.alloc((S,), F32)[:32]
    cs_r = Reg()
    cs2_r = Reg()
    P.dma("sp", cos, C.dram["c_cos"], "cs", writes=[cs_r])
    P.dma("sp", sin, C.dram["c_sin"], "cs", writes=[cs2_r])
    wsl = [[A.alloc((KC, 128), BF16) for _ in range(3)] for _ in range(2)]
    wsl_r = [[Reg() for _ in range(3)] for _ in range(2)]
    DBG.clear()
    qk = [[None, None], [None, None]]
    for a_ in range(2):
        for b_ in range(2):
            qk[a_][b_] = A.alloc((S,), BF16)
            DBG[f"qk{a_}{b_}"] = Arena.last
    qk_r = [[[Reg() for _ in range(4)] for _ in range(2)] for _ in range(2)]
    vt = [A.alloc((NT, 128), BF16) for _ in range(2)]
    vt_r = [[Reg() for _ in range(4)] for _ in range(2)]
    acc_n = A.alloc((S,), F32)
    acc_d = A.alloc((S,), F32)
    accn_r = [Reg() for _ in range(4)]
    accd_r = [Reg() for _ in range(4)]
    acc_all = Reg()
    pT = [A.alloc((512,), BF16) for _ in range(4)]
    pT_r = [Reg() for _ in range(4)]
    t1 = [A.alloc((512,), F32)[:32] for _ in range(2)]
    t2 = [A.alloc((512,), F32)[:32] for _ in range(2)]
    t1_r = [Reg() for _ in range(2)]
    t2_r = [Reg() for _ in range(2)]
    rcp = [A.alloc((512,), F32) for _ in range(2)]
    rcp_r = [Reg() for _ in range(2)]
    SCALE = 128.0 ** -0.5

    def bank(b):
        return PS[:, 512 * b:512 * (b + 1)]

    cnt = {"proj": 0, "rot": 0, "s": 0, "p": 0, "nd": 0, "t": 0, "it": 0}
    for hs in range(4):
        for g in range(3):
            sl = cnt["it"] % 2
            cnt["it"] += 1
            head = g * 4 + hs
            dil = (1, 4, 16)[g]
            for m, off in enumerate((0, 1536, 3072)):
                c0 = off + head * 128
                load_cast(C, wsl[sl][m], w_in[:, c0:c0 + 128].rearrange("(k p) n -> p k n", p=128), wsl_r[sl][m])
            for m in range(2):
                dst = qk[sl][m]
                for tg in range(4):
                    pb = cnt["proj"] % 2
                    cnt["proj"] += 1
                    for k in range(KC):
                        P.op("pe", lambda e, k=k, m=m, tg=tg, pb=pb, sl=sl: e.matmul(
                            bank(pb), lhsT=wsl[sl][m][:, k, :], rhs=hT[:, k, tg * 512:(tg + 1) * 512],
                            start=(k == 0), stop=(k == KC - 1)), reads=[wsl_r[sl][m]], writes=[psr[pb]])
                    dcol = dst[:, tg * 512:(tg + 1) * 512]
                    dr = qk_r[sl][m][tg]
                    P.op("act", lambda e, dcol=dcol, pb=pb: e.activation(out=dcol, in_=bank(pb), func=AF.Copy),
                         reads=[psr[pb]], writes=[dr])
                    rb = 2 + cnt["rot"] % 2
                    ts = cnt["rot"] % 2
                    cnt["rot"] += 1
                    P.op("pe", lambda e, dcol=dcol, rb=rb: e.matmul(bank(rb)[:32, :], lhsT=C.rm, rhs=dcol[:32, :],
                                                                   start=True, stop=True),
                         reads=[dr, C.cst_r], writes=[psr[rb]])
                    P.op("dve", lambda e, rb=rb, ts=ts, tg=tg: e.tensor_tensor(
                        out=t1[ts], in0=bank(rb)[:32, :], in1=sin[:, tg * 512:(tg + 1) * 512], op=ALU.mult),
                        reads=[psr[rb], cs_r, cs2_r], writes=[t1_r[ts]])
                    P.op("dve", lambda e, dcol=dcol, ts=ts, tg=tg: e.tensor_tensor(
                        out=t2[ts], in0=dcol[:32, :], in1=cos[:, tg * 512:(tg + 1) * 512], op=ALU.mult),
                        reads=[dr, cs_r], writes=[t2_r[ts]])
                    P.op("dve", lambda e, dcol=dcol, ts=ts: e.tensor_tensor(out=dcol[:32, :], in0=t1[ts], in1=t2[ts], op=ALU.add),
                         reads=[t1_r[ts], t2_r[ts]], writes=[dr])
            blocks = []
            if g == 0:
                for b in range(16):
                    blocks.append((tok_slice(128 * b, 1), tok_slice(128 * (b - 1), 1) if b > 0 else None))
            elif g == 1:
                for r in range(4):
                    for b in range(4):
                        blocks.append((tok_slice(4 * 128 * b + r, 4), tok_slice(4 * 128 * (b - 1) + r, 4) if b > 0 else None))
            else:
                for r in range(16):
                    blocks.append((tok_slice(r, 16), None))
            for j in range(4):
                pb = cnt["proj"] % 2
                cnt["proj"] += 1
                for bi in range(4):
                    qs = blocks[4 * j + bi][0]
                    for k in range(KC):
                        P.op("pe", lambda e, k=k, qs=qs, pb=pb, bi=bi, sl=sl: e.matmul(
                            bank(pb)[:, bi * 128:(bi + 1) * 128], lhsT=hT[:, k, qs], rhs=wsl[sl][2][:, k, :],
                            start=(k == 0), stop=(k == KC - 1)), reads=[wsl_r[sl][2]], writes=[psr[pb]])
                P.op("act", lambda e, j=j, pb=pb, sl=sl: e.activation(
                    out=vt[sl][:, 4 * j:4 * j + 4, :], in_=bank(pb).rearrange("p (a b) -> p a b", a=4), func=AF.Copy),
                    reads=[psr[pb]], writes=[vt_r[sl][j]])
            qT, kT = qk[sl][0], qk[sl][1]
            qr = qk_r[sl][0] + qk_r[sl][1]
            pairs = [(2 * i, 2 * i + 1) for i in range(8)]

            def do_qk(pair):
                sb = 4 + cnt["s"] % 2
                cnt["s"] += 1
                for ti, blk in enumerate(pair):
                    qs, ps_ = blocks[blk]
                    for ci, ks in enumerate((ps_, qs)):
                        o = bank(sb)[:, (2 * ti + ci) * 128:(2 * ti + ci + 1) * 128]
                        if ks is None:
                            P.op("pe", lambda e, o=o: e.matmul(o, lhsT=C.ident, rhs=C.maskn, start=True, stop=True),
                                 reads=[C.cst_r], writes=[psr[sb]])
                            continue
                        P.op("pe", lambda e, o=o, ks=ks, qs=qs, kT=kT, qT=qT: e.matmul(o, lhsT=kT[:, ks], rhs=qT[:, qs], start=True, stop=False),
                             reads=qr, writes=[psr[sb]])
                        mk = C.maskc if ci == 1 else C.maskp
                        P.op("pe", lambda e, o=o, mk=mk: e.matmul(o, lhsT=C.ident, rhs=mk, start=False, stop=True),
                             reads=[C.cst_r], writes=[psr[sb]])
                pslot = cnt["p"] % 4
                cnt["p"] += 1
                P.op("act", lambda e, sb=sb, pslot=pslot: e.activation(out=pT[pslot], in_=bank(sb), func=AF.Exp, scale=SCALE),
                     reads=[psr[sb]], writes=[pT_r[pslot]])
                return pslot

            def do_pv(pair, pslot):
                nb = 6 + cnt["nd"] % 2
                cnt["nd"] += 1
                for ti, blk in enumerate(pair):
                    qs, ps_ = blocks[blk]
                    for is_num in (True, False):
                        col = ti * 128 + (0 if is_num else 256)
                        o = bank(nb)[:, col:col + 128]
                        for ci in range(2):
                            kblk = blk if (ci == 1 or ps_ is None) else blk - 1
                            lhs = vt[sl][:, kblk, :] if is_num else C.ones_bf
                            P.op("pe", lambda e, o=o, lhs=lhs, pslot=pslot, ti=ti, ci=ci: e.matmul(
                                o, lhsT=lhs, rhs=pT[pslot][:, (2 * ti + ci) * 128:(2 * ti + ci + 1) * 128],
                                start=(ci == 0), stop=(ci == 1)),
                                reads=[pT_r[pslot], vt_r[sl][kblk // 4], C.cst_r], writes=[psr[nb]])
                pi = pair[0] // 2
                if g == 0:
                    sel = slice(256 * pi, 256 * pi + 256)
                    dn, dd = acc_n[:, sel], acc_d[:, sel]
                    sn, sd = bank(nb)[:, 0:256], bank(nb)[:, 256:512]
                elif g == 1:
                    r_, b_ = pair[0] // 4, pair[0] % 4
                    st_ = r_ + 512 * b_
                    sel = slice(st_, st_ + 4 * 255 + 1, 4)
                    dn, dd = acc_n[:, sel], acc_d[:, sel]
                    sn, sd = bank(nb)[:, 0:256], bank(nb)[:, 256:512]
                else:
                    r_ = pair[0]
                    dn = acc_n.rearrange("p (n r) -> p r n", r=16)[:, r_:r_ + 2, :]
                    dd = acc_d.rearrange("p (n r) -> p r n", r=16)[:, r_:r_ + 2, :]
                    sn = bank(nb)[:, 0:256].rearrange("p (a b) -> p a b", a=2)
                    sd = bank(nb)[:, 256:512].rearrange("p (a b) -> p a b", a=2)
                if g == 0:
                    P.op("dve", lambda e, dn=dn, sn=sn: e.tensor_copy(out=dn, in_=sn), reads=[psr[nb]], writes=[acc_all])
                    P.op("dve", lambda e, dd=dd, sd=sd: e.tensor_copy(out=dd, in_=sd), reads=[psr[nb]], writes=[acc_all])
                else:
                    P.op("dve", lambda e, dn=dn, sn=sn: e.tensor_tensor(out=dn, in0=sn, in1=dn, op=ALU.add),
                         reads=[psr[nb], acc_all], writes=[acc_all])
                    P.op("dve", lambda e, dd=dd, sd=sd: e.tensor_tensor(out=dd, in0=sd, in1=dd, op=ALU.add),
                         reads=[psr[nb], acc_all], writes=[acc_all])

            prev = None
            for pair in pairs:
                pslot = do_qk(pair)
                if prev is not None:
                    do_pv(*prev)
                prev = (pair, pslot)
            do_pv(*prev)
        for tg in range(4):
            rs = tg % 2
            P.op("dve", lambda e, tg=tg, rs=rs: e.reciprocal(out=rcp[rs], in_=acc_d[:, tg * 512:(tg + 1) * 512]),
                 reads=[acc_all], writes=[rcp_r[rs]])
            P.op("dve", lambda e, tg=tg, rs=rs, hs=hs: e.tensor_tensor(
                out=attT[:, hs, tg * 512:(tg + 1) * 512], in0=acc_n[:, tg * 512:(tg + 1) * 512], in1=rcp[rs], op=ALU.mult),
                reads=[acc_all, rcp_r[rs]], writes=[C.attT_r])


def mlstm_phase(C, A, hT, mlT, w_in, conv_w, conv_b, i_bias, f_bias, head_g):
    P, PS, psr = C.P, C.PS, C.psr
    OQ, OK_, OV, OO, OI = 4608, 5632, 6656, 7680, 8704

    def bank(b):
        return PS[:, 512 * b:512 * (b + 1)]

    def R():
        return Reg()

    cwb = A.alloc((2048,), F32)[:5]
    cwb_r = R()
    cwb2_r = R()
    P.dma("sp", cwb[0:4, :], conv_w, "mca", writes=[cwb_r])
    P.dma("sp", cwb[4:5, :], conv_b.rearrange("(o n) -> o n", o=1), "mca", writes=[cwb2_r])
    cwT = A.alloc((16, 8), F32)
    ncb = A.alloc((16,), F32)
    cwT_r = R()
    for c in range(16):
        P.op("pe", lambda e, c=c: e.matmul(bank(6)[:, c * 8:c * 8 + 5], lhsT=cwb[:, c * 128:(c + 1) * 128],
                                           rhs=C.identf[:5, :5], start=True, stop=True),
             reads=[cwb_r, cwb2_r, C.cst_r], writes=[psr[6]])
    P.op("dve", lambda e: e.tensor_copy(out=cwT[:, :, 0:5], in_=bank(6)[:, 0:128].rearrange("p (c j) -> p c j", j=8)[:, :, 0:5]),
         reads=[psr[6]], writes=[cwT_r])
    P.op("dve", lambda e: e.tensor_scalar(out=ncb, in0=cwT[:, :, 4], scalar1=-1.0, scalar2=None, op0=ALU.mult),
         reads=[cwT_r], writes=[cwT_r])
    hg = A.alloc((1024,), F32)
    hg_r = R()
    P.dma("sp", hg, head_g.partition_broadcast(128), "mch", writes=[hg_r])
    bias8 = A.alloc((8,), F32)
    b8_r = R()
    b8b_r = R()
    P.dma("sp", bias8[:, 0:4], i_bias.partition_broadcast(128), "mcb", writes=[b8_r])
    P.dma("sp", bias8[:, 4:8], f_bias.partition_broadcast(128), "mcb", writes=[b8b_r])

    wif = A.alloc((KC, 8), BF16)
    wif_r = R()
    load_cast(C, wif, w_in[:, OI:OI + 8].rearrange("(k p) n -> p k n", p=128), wif_r)
    for c in range(NT):
        for k in range(KC):
            P.op("pe", lambda e, c=c, k=k: e.matmul(bank(7)[:, c * 8:(c + 1) * 8], lhsT=hT[:, k, c * 128:(c + 1) * 128],
                                                    rhs=wif[:, k, :], start=(k == 0), stop=(k == KC - 1)),
                 reads=[wif_r], writes=[psr[7]])
    gi = A.alloc((NT, 4), F32)
    lg = A.alloc((NT, 4), F32)
    g_r = R()
    pre3 = bank(7)[:, 0:128].rearrange("p (c j) -> p c j", j=8)
    P.op("dve", lambda e: e.tensor_tensor(out=gi, in0=pre3[:, :, 0:4],
                                          in1=bias8[:, 0:4].unsqueeze(1).to_broadcast([128, NT, 4]), op=ALU.add),
         reads=[psr[7], b8_r, b8b_r], writes=[g_r])
    P.op("dve", lambda e: e.tensor_tensor(out=lg, in0=pre3[:, :, 4:8],
                                          in1=bias8[:, 4:8].unsqueeze(1).to_broadcast([128, NT, 4]), op=ALU.add),
         reads=[psr[7], b8_r, b8b_r, g_r], writes=[g_r])
    P.op("act", lambda e: e.activation(out=lg, in_=lg, func=AF.Exp, scale=-1.0), reads=[g_r], writes=[g_r])
    P.op("act", lambda e: e.activation(out=lg, in_=lg, func=AF.Ln, bias=1.0), reads=[g_r], writes=[g_r])
    lg2 = lg.rearrange("p c h -> p (c h)")
    gi2 = gi.rearrange("p c h -> p (c h)")
    P.op("pe", lambda e: e.matmul(bank(6)[:, 0:64], lhsT=C.tri, rhs=lg2, start=True, stop=True),
         reads=[g_r, C.cst_r], writes=[psr[6]])
    P.op("pe", lambda e: e.matmul(bank(6)[:, 64:128], lhsT=C.ones_f, rhs=lg2, start=True, stop=True),
         reads=[g_r, C.cst_r], writes=[psr[6]])
    e_in = A.alloc((64,), F32)
    e_out = A.alloc((64,), F32)
    e_L = A.alloc((64,), F32)
    e_v = A.alloc((64,), F32)
    e_io = A.alloc((64,), F32)
    ee_r = R()
    P.op("dve", lambda e: e.tensor_tensor(out=e_in, in0=bank(6)[:, 0:64], in1=gi2, op=ALU.add),
         reads=[psr[6], g_r], writes=[ee_r])
    P.op("act", lambda e: e.activation(out=e_in, in_=e_in, func=AF.Exp), reads=[ee_r], writes=[ee_r])
    P.op("act", lambda e: e.activation(out=e_out, in_=bank(6)[:, 0:64], func=AF.Exp, scale=-1.0),
         reads=[psr[6], ee_r], writes=[ee_r])
    P.op("act", lambda e: e.activation(out=e_L, in_=bank(6)[:, 64:128], func=AF.Exp, scale=-1.0),
         reads=[psr[6], ee_r], writes=[ee_r])
    P.op("act", lambda e: e.activation(out=e_io, in_=bank(6)[:, 0:64], func=AF.Exp), reads=[psr[6], ee_r], writes=[ee_r])
    P.op("dve", lambda e: e.tensor_scalar(out=e_in, in0=e_in, scalar1=1.0 / 16.0, scalar2=None, op0=ALU.mult),
         reads=[ee_r], writes=[ee_r])
    P.op("dve", lambda e: e.tensor_tensor(out=e_v, in0=e_in, in1=e_L, op=ALU.mult), reads=[ee_r], writes=[ee_r])

    wq = [A.alloc((KC, 256), BF16) for _ in range(5)]
    wq_r = [[R(), R()] for _ in range(5)]
    raw = [A.alloc((S + 4,), BF16) for _ in range(2)]
    raw_r = [R(), R()]
    for i in range(2):
        P.op("dve", lambda e, i=i: e.memset(raw[i][:, 0:3], 0.0), writes=[raw_r[i]])
    neghalf = A.alloc((1,), F32)
    nh_r = R()
    P.op("dve", lambda e: e.memset(neghalf, -0.5), writes=[nh_r])
    dg = [[A.alloc((128,), BF16) for _ in range(4)] for _ in range(2)]
    dg_r = [R(), R()]
    qT2 = [A.alloc((2, S), BF16) for _ in range(2)]
    kT2 = [A.alloc((2, S), BF16) for _ in range(2)]
    qk2_r = [[[R(), R()], [R(), R()]] for _ in range(2)]
    ktok2 = [A.alloc((256,), BF16) for _ in range(2)]
    ktok2_r = [R(), R()]
    vaug2 = [A.alloc((258,), BF16) for _ in range(2)]
    vaug2_r = [R(), R()]
    for i in range(2):
        P.op("dve", lambda e, i=i: e.memset(vaug2[i][:, 256:257], 1.0), writes=[vaug2_r[i]])
        P.op("dve", lambda e, i=i: e.memset(vaug2[i][:, 257:258], 0.0), writes=[vaug2_r[i]])
    vp2 = [A.alloc((258,), BF16) for _ in range(2)]
    vp2_r = [R(), R()]
    sigo2 = [A.alloc((NT, 256), BF16) for _ in range(2)]
    sigo2_r = [[R() for _ in range(NT)] for _ in range(2)]
    wT2 = [A.alloc((128,), BF16) for _ in range(2)]
    wT2_r = [R(), R()]
    Cf = A.alloc((2, 258), F32)
    Cf_r = R()
    Cbf2 = [A.alloc((2, 258), BF16) for _ in range(2)]
    Cbf2_r = [R(), R()]
    hu3 = [A.alloc((256,), F32) for _ in range(3)]
    hu3_r = [R(), R(), R()]
    sm3 = [A.alloc((8,), F32) for _ in range(3)]
    sm3_r = [R(), R(), R()]
    junk3 = [A.alloc((256,), BF16) for _ in range(3)]
    mlb3 = [A.alloc((256,), BF16) for _ in range(3)]
    mlb3_r = [R(), R(), R()]

    def pre_units(hd):
        sl = hd % 2
        qT, kT, qk_r, sigo, sigo_r = qT2[sl], kT2[sl], qk2_r[sl], sigo2[sl], sigo2_r[sl]
        wsel = (0, 1, 3 + sl, 2)
        units = []

        def u_load():
            for m, off in enumerate((OQ, OK_, OV, OO)):
                c0 = off + hd * 256
                for part in range(2):
                    load_cast(C, wq[wsel[m]][:, 4 * part:4 * part + 4, :],
                              w_in[part * 512:(part + 1) * 512, c0:c0 + 256].rearrange("(k p) n -> p k n", p=128),
                              wq_r[wsel[m]][part])
        units.append(u_load)
        for m, dstT in ((0, qT), (1, kT)):
            for cc in range(2):
                ch = m * 8 + hd * 2 + cc
                bi = (m * 2 + cc) % 2

                def u_diag(ch=ch, bi=bi):
                    for j in range(4):
                        P.op("dve", lambda e, j=j: e.tensor_scalar(out=dg[bi][j], in0=C.ident, scalar1=cwT[:, ch, j:j + 1],
                                                                   scalar2=None, op0=ALU.mult),
                             reads=[C.cst_r, cwT_r], writes=[dg_r[bi]])
                units.append(u_diag)
                for tg in range(4):
                    def u_proj(m=m, cc=cc, tg=tg, ch=ch, bi=bi, dstT=dstT):
                        for k in range(KC):
                            P.op("pe", lambda e, k=k: e.matmul(
                                bank(6), lhsT=wq[m][:, k, cc * 128:(cc + 1) * 128], rhs=hT[:, k, tg * 512:(tg + 1) * 512],
                                start=(k == 0), stop=(k == KC - 1)), reads=[wq_r[m][k // 4]], writes=[psr[6]])
                        P.op("act", lambda e: e.activation(
                            out=raw[bi][:, 3 + tg * 512:3 + (tg + 1) * 512], in_=bank(6), func=AF.Copy),
                            reads=[psr[6]], writes=[raw_r[bi]])
                        for j in range(4):
                            P.op("pe", lambda e, j=j: e.matmul(bank(7), lhsT=dg[bi][j], rhs=raw[bi][:, tg * 512 + j:tg * 512 + j + 512],
                                                               start=(j == 0), stop=(j == 3)),
                                 reads=[dg_r[bi], raw_r[bi]], writes=[psr[7]])
                        P.op("act", lambda e: e.activation(out=dstT[:, cc, tg * 512:(tg + 1) * 512], in_=bank(7), func=AF.Silu,
                                                           bias=cwT[:, ch, 4:5]),
                             reads=[psr[7], cwT_r], writes=[qk_r[m][cc]])
                    units.append(u_proj)
        for c in range(NT):
            def u_gate(c=c):
                cs = slice(c * 128, (c + 1) * 128)
                ob = 6 + c % 2
                for k in range(KC):
                    P.op("pe", lambda e, k=k: e.matmul(bank(ob)[:, 0:256], lhsT=hT[:, k, cs], rhs=wq[2][:, k, :],
                                                       start=(k == 0), stop=(k == KC - 1)),
                         reads=[wq_r[2][k // 4]], writes=[psr[ob]])
                P.op("act", lambda e: e.activation(out=sigo[:, c, :], in_=bank(ob)[:, 0:256], func=AF.Sigmoid),
                     reads=[psr[ob]], writes=[sigo_r[c]])
            units.append(u_gate)
        return units

    def loop_steps(hd):
        sl = hd % 2
        qT, kT, qk_r, sigo, sigo_r = qT2[sl], kT2[sl], qk2_r[sl], sigo2[sl], sigo2_r[sl]
        wv, wv_r = wq[3 + sl], wq_r[3 + sl]
        qkr = [qk_r[0][0], qk_r[0][1], qk_r[1][0], qk_r[1][1]]
        def chunk_vars(c):
            p = c % 2
            q3 = c % 3
            return dict(cs=slice(c * 128, (c + 1) * 128), col=c * 4 + hd, ktok=ktok2[p], ktok_r=ktok2_r[p], vaug=vaug2[p],
                        vaug_r=vaug2_r[p], vp=vp2[p], vp_r=vp2_r[p], wT=wT2[p], wT_r=wT2_r[p], hu=hu3[q3], hu_r=hu3_r[q3],
                        sm=sm3[q3], sm_r=sm3_r[q3], junk=junk3[q3], mlb=mlb3[q3], mlb_r=mlb3_r[q3], ab=p, db=2 + p, tb=p)

        def stageA1(c):
            v_ = chunk_vars(c)
            cs, col, ktok, ktok_r, vaug, vaug_r, vp, vp_r, wT, wT_r, ab, db = (v_[k] for k in (
                "cs", "col", "ktok", "ktok_r", "vaug", "vaug_r", "vp", "vp_r", "wT", "wT_r", "ab", "db"))
            abf = bank(ab).bitcast(BF16)
            for k in range(KC):
                P.op("pe", lambda e, k=k: e.matmul(bank(ab)[:, 0:256], lhsT=hT[:, k, cs], rhs=wv[:, k, :],
                                                   start=(k == 0), stop=(k == KC - 1)),
                     reads=[wv_r[k // 4]], writes=[psr[ab]])
            for dk in range(2):
                P.op("pe", lambda e, dk=dk: e.transpose(out=abf[:, 512 + dk * 128:512 + (dk + 1) * 128],
                                                        in_=kT[:, dk, cs], identity=C.ident),
                     reads=[qk_r[1][dk], C.cst_r], writes=[psr[ab]])
            P.op("act", lambda e: e.activation(out=vaug[:, 0:256], in_=bank(ab)[:, 0:256], func=AF.Copy),
                 reads=[psr[ab]], writes=[vaug_r])
            P.op("act", lambda e: e.activation(out=ktok, in_=abf[:, 512:768], func=AF.Copy),
                 reads=[psr[ab]], writes=[ktok_r])
            for dk in range(2):
                P.op("pe", lambda e, dk=dk: e.matmul(bank(db)[:, 384:512], lhsT=kT[:, dk, cs], rhs=qT[:, dk, cs],
                                                     start=(dk == 0), stop=(dk == 1)),
                     reads=qkr, writes=[psr[db]])
            P.op("dve", lambda e: e.scalar_tensor_tensor(
                out=wT, in0=bank(db)[:, 384:512], scalar=e_in[:, col:col + 1], in1=C.tri, op0=ALU.mult, op1=ALU.mult),
                reads=[psr[db], ee_r, C.cst_r], writes=[wT_r])
            if c < NT - 1:
                P.op("dve", lambda e: e.tensor_scalar(out=vp, in0=vaug, scalar1=e_v[:, col:col + 1], scalar2=None, op0=ALU.mult),
                     reads=[vaug_r, ee_r], writes=[vp_r])

        def stageA2(c):
            v_ = chunk_vars(c)
            cs, col, ktok, ktok_r, vaug, vaug_r, vp, vp_r, wT, wT_r, db = (v_[k] for k in (
                "cs", "col", "ktok", "ktok_r", "vaug", "vaug_r", "vp", "vp_r", "wT", "wT_r", "db"))
            if c < NT - 1:
                for dk in range(2):
                    P.op("pe", lambda e, dk=dk: e.matmul(bank(4 + dk)[:, 0:258], lhsT=ktok[:, dk * 128:(dk + 1) * 128], rhs=vp,
                                                         start=True, stop=True),
                         reads=[ktok_r, vp_r], writes=[psr[4 + dk]])
            P.op("pe", lambda e: e.matmul(bank(db)[:, 0:258], lhsT=wT, rhs=vaug, start=True, stop=(c == 0)),
                 reads=[wT_r, vaug_r], writes=[psr[db]])
            if c > 0:
                cprev, cprev_r = Cbf2[(c - 1) % 2], Cbf2_r[(c - 1) % 2]
                for dk in range(2):
                    P.op("pe", lambda e, dk=dk: e.matmul(bank(db)[:, 0:258], lhsT=qT[:, dk, cs], rhs=cprev[:, dk, :],
                                                         start=False, stop=(dk == 1)),
                         reads=qkr + [cprev_r], writes=[psr[db]])
            if c < NT - 1:
                dC = PS[:, 2048:3072].rearrange("p (a b) -> p a b", a=2)[:, :, 0:258]
                if c == 0:
                    P.op("dve", lambda e: e.tensor_copy(out=Cf, in_=dC), reads=[psr[4], psr[5]], writes=[Cf_r])
                else:
                    P.op("dve", lambda e: e.scalar_tensor_tensor(out=Cf, in0=Cf, scalar=e_L[:, col:col + 1], in1=dC,
                                                                 op0=ALU.mult, op1=ALU.add),
                         reads=[psr[4], psr[5], Cf_r, ee_r], writes=[Cf_r])
                ccur, ccur_r = Cbf2[c % 2], Cbf2_r[c % 2]
                P.op("act", lambda e: e.activation(out=ccur, in_=Cf, func=AF.Copy), reads=[Cf_r], writes=[ccur_r])

        def stageB(c):
            v_ = chunk_vars(c)
            col, hu, hu_r, sm, sm_r, junk, db = (v_[k] for k in ("col", "hu", "hu_r", "sm", "sm_r", "junk", "db"))
            P.op("dve", lambda e, col=col, sm=sm, db=db: e.tensor_scalar(out=sm[:, 0:1], in0=bank(db)[:, 256:257],
                                                                       scalar1=e_io[:, col:col + 1], scalar2=None, op0=ALU.max),
                 reads=[psr[db], ee_r], writes=[sm_r])
            P.op("dve", lambda e, sm=sm, db=db: e.scalar_tensor_tensor(out=sm[:, 1:2], in0=bank(db)[:, 256:257], scalar=-1.0,
                                                                     in1=sm[:, 0:1], op0=ALU.mult, op1=ALU.max),
                 reads=[psr[db], sm_r], writes=[sm_r])
            P.op("dve", lambda e, sm=sm: e.reciprocal(out=sm[:, 3:4], in_=sm[:, 1:2]), reads=[sm_r], writes=[sm_r])
            P.op("dve", lambda e, sm=sm, hu=hu, db=db: e.tensor_scalar(out=hu, in0=bank(db)[:, 0:256], scalar1=sm[:, 3:4], scalar2=None,
                                                                     op0=ALU.mult),
                 reads=[psr[db], sm_r], writes=[hu_r])
            P.op("act", lambda e, sm=sm, hu=hu, junk=junk: e.activation(out=junk, in_=hu, func=AF.Square, accum_out=sm[:, 4:5]),
                 reads=[hu_r, sm_r], writes=[sm_r])
            P.op("pool", lambda e, sm=sm: e.tensor_scalar(out=sm[:, 5:6], in0=sm[:, 4:5], scalar1=1.0 / 256, scalar2=EPS,
                                                          op0=ALU.mult, op1=ALU.add), reads=[sm_r], writes=[sm_r])
            P.op("pool", lambda e, sm=sm: e.tensor_tensor(out=sm[:, 5:6], in0=sm[:, 5:6], in1=neghalf, op=ALU.pow),
                 reads=[sm_r, nh_r], writes=[sm_r])

        def stageC(c):
            v_ = chunk_vars(c)
            cs, hu, hu_r, sm, sm_r, mlb, mlb_r, tb = (v_[k] for k in ("cs", "hu", "hu_r", "sm", "sm_r", "mlb", "mlb_r", "tb"))
            abf = bank(tb).bitcast(BF16)
            ab = tb
            P.op("dve", lambda e, hd=hd, sm=sm, hu=hu: e.scalar_tensor_tensor(out=hu, in0=hu, scalar=sm[:, 5:6],
                                                                              in1=hg[:, hd * 256:(hd + 1) * 256],
                                                                              op0=ALU.mult, op1=ALU.mult),
                 reads=[hu_r, sm_r, hg_r], writes=[hu_r])
            P.op("dve", lambda e, c=c, hu=hu, mlb=mlb: e.tensor_tensor(out=mlb, in0=hu, in1=sigo[:, c, :], op=ALU.mult),
                 reads=[hu_r, sigo_r[c]], writes=[mlb_r])
            for j in range(2):
                P.op("pe", lambda e, j=j, abf=abf, mlb=mlb: e.transpose(out=abf[:, 768 + j * 128:768 + (j + 1) * 128],
                                                                       in_=mlb[:, j * 128:(j + 1) * 128], identity=C.ident),
                     reads=[mlb_r, C.cst_r], writes=[psr[ab]])
            P.op("act", lambda e, hd=hd, cs=cs, abf=abf: e.activation(
                out=mlT[:, 2 * hd:2 * hd + 2, cs], in_=abf[:, 768:1024].rearrange("p (j n) -> p j n", j=2), func=AF.Copy),
                reads=[psr[ab]], writes=[C.mlT_r])


        steps = []
        for step in range(NT + 2):
            def st(fill, step=step):
                if step == 0:
                    stageA1(0)
                if step + 1 < NT:
                    stageA1(step + 1)
                fill()
                if step < NT:
                    stageA2(step)
                fill()
                if 0 <= step - 1 < NT:
                    stageB(step - 1)
                fill()
                if 0 <= step - 2 < NT:
                    stageC(step - 2)
            steps.append(st)
        return steps

    for u in pre_units(0):
        u()
    NH = int(os.environ.get("NH", 4))
    for hd in range(NH):
        nxt = pre_units(hd + 1) if hd < NH - 1 else []
        steps = loop_steps(hd) if not os.environ.get("NOLOOP") else []
        if not steps:
            for u in nxt:
                u()
            continue
        nfill = 3 * len(steps)
        per = -(-len(nxt) // nfill) if nxt else 0
        pos = [0]

        def fill():
            for u in nxt[pos[0]:pos[0] + per]:
                u()
            pos[0] += per
        for st in steps:
            st(fill)
        for u in nxt[pos[0]:]:
            u()

def merge_phase(C, A, hT, attT, mlT, w_in, w_a, w_m, w_out, g_post, x_src, x_dst):
    P, PS, psr = C.P, C.PS, C.psr
    OGA, OGM = 8712, 9736

    def bank(b):
        return PS[:, 512 * b:512 * (b + 1)]

    mgT = A.alloc((KC, S), BF16)
    wa = A.alloc((4, 1024), BF16)
    wa_r = [Reg() for _ in range(4)]
    wm = A.alloc((KC, 1024), BF16)
    wm_r = [Reg() for _ in range(KC)]
    wo = A.alloc((KC, 1024), BF16)
    wo_r = [Reg() for _ in range(KC)]
    wg = [[A.alloc((KC, 128), BF16) for _ in range(2)] for _ in range(2)]
    wg_r = [[Reg() for _ in range(2)] for _ in range(2)]
    load_cast(C, wg[0][0], w_in[:, OGA:OGA + 128].rearrange("(k p) n -> p k n", p=128), wg_r[0][0])
    load_cast(C, wg[0][1], w_in[:, OGM:OGM + 128].rearrange("(k p) n -> p k n", p=128), wg_r[0][1])
    for k in range(4):
        load_cast(C, wa[:, k, :], w_a[k * 128:(k + 1) * 128, :], wa_r[k])
    for k in range(KC):
        load_cast(C, wm[:, k, :], w_m[k * 128:(k + 1) * 128, :], wm_r[k])
    ga = [A.alloc((512,), F32) for _ in range(2)]
    ga_r = [Reg() for _ in range(2)]
    gm = [A.alloc((512,), F32) for _ in range(2)]
    gm_r = [Reg() for _ in range(2)]
    ta = [A.alloc((512,), F32) for _ in range(2)]
    ta_r = [Reg() for _ in range(2)]
    gb = A.alloc((D,), F32)
    gb_r = Reg()
    P.dma("sp", gb, g_post.partition_broadcast(128), "g", writes=[gb_r])
    j = 0
    for mc in range(KC):
        sl = mc % 2
        if mc > 0:
            load_cast(C, wg[sl][0], w_in[:, OGA + mc * 128:OGA + (mc + 1) * 128].rearrange("(k p) n -> p k n", p=128), wg_r[sl][0])
            load_cast(C, wg[sl][1], w_in[:, OGM + mc * 128:OGM + (mc + 1) * 128].rearrange("(k p) n -> p k n", p=128), wg_r[sl][1])
        if mc == 0:
            for k in range(KC):
                load_cast(C, wo[:, k, :], w_out[k * 128:(k + 1) * 128, :], wo_r[k])
        for tg in range(4):
            q = j % 2
            j += 1
            ts = slice(tg * 512, (tg + 1) * 512)
            b0 = 4 * q
            for gi_, (gbuf, gr) in enumerate(((ga, ga_r), (gm, gm_r))):
                for k in range(KC):
                    P.op("pe", lambda e, k=k, gi_=gi_, sl=sl, ts=ts, b0=b0: e.matmul(
                        bank(b0 + gi_), lhsT=wg[sl][gi_][:, k, :], rhs=hT[:, k, ts], start=(k == 0), stop=(k == KC - 1)),
                        reads=[wg_r[sl][gi_]], writes=[psr[b0 + gi_]])
                P.op("act", lambda e, gbuf=gbuf, q=q, gi_=gi_, b0=b0: e.activation(out=gbuf[q], in_=bank(b0 + gi_), func=AF.Sigmoid),
                     reads=[psr[b0 + gi_]], writes=[gr[q]])
            for k in range(4):
                P.op("pe", lambda e, k=k, mc=mc, ts=ts, b0=b0: e.matmul(
                    bank(b0 + 2), lhsT=wa[:, k, mc * 128:(mc + 1) * 128], rhs=attT[:, k, ts], start=(k == 0), stop=(k == 3)),
                    reads=[wa_r[k], C.attT_r], writes=[psr[b0 + 2]])
            for k in range(KC):
                P.op("pe", lambda e, k=k, mc=mc, ts=ts, b0=b0: e.matmul(
                    bank(b0 + 3), lhsT=wm[:, k, mc * 128:(mc + 1) * 128], rhs=mlT[:, k, ts], start=(k == 0), stop=(k == KC - 1)),
                    reads=[wm_r[k], C.mlT_r], writes=[psr[b0 + 3]])
            P.op("dve", lambda e, q=q, b0=b0: e.tensor_tensor(out=ta[q], in0=bank(b0 + 2), in1=ga[q], op=ALU.mult),
                 reads=[psr[b0 + 2], ga_r[q]], writes=[ta_r[q]])
            P.op("dve", lambda e, q=q, b0=b0: e.tensor_tensor(out=gm[q], in0=bank(b0 + 3), in1=gm[q], op=ALU.mult),
                 reads=[psr[b0 + 3], gm_r[q]], writes=[gm_r[q]])
            P.op("dve", lambda e, q=q, mc=mc, ts=ts: e.tensor_tensor(out=mgT[:, mc, ts], in0=ta[q], in1=gm[q], op=ALU.add),
                 reads=[ta_r[q], gm_r[q]], writes=[C.mg_r])
    xc = [A.alloc((D,), F32) for _ in range(2)]
    tt = [A.alloc((D,), F32) for _ in range(2)]
    xc_r = [Reg() for _ in range(2)]
    tth_r = [[Reg(), Reg()] for _ in range(2)]
    ss2 = A.alloc((NT,), F32)
    r2 = A.alloc((NT,), F32)
    junk = ga[0].bitcast(BF16)
    junk_r = ga_r[0]
    r2_r = [Reg() for _ in range(NT)]
    for i in range(NT):
        s = i % 2
        P.dma("sp", xc[s], x_src[i * 128:(i + 1) * 128, :], f"xc{s}", writes=[xc_r[s]])
        pb = (i % 4) * 2
        psf = PS[:, 512 * pb:512 * (pb + 2)]
        for h in range(2):
            for k in range(KC):
                P.op("pe", lambda e, h=h, k=k, i=i, pb=pb: e.matmul(
                    bank(pb + h), lhsT=mgT[:, k, i * 128:(i + 1) * 128], rhs=wo[:, k, h * 512:(h + 1) * 512],
                    start=(k == 0), stop=(k == KC - 1)), reads=[wo_r[k], C.mg_r], writes=[psr[pb + h]])
        P.op("act", lambda e, i=i, psf=psf: e.activation(out=junk, in_=psf, func=AF.Square, accum_out=ss2[:, i:i + 1]),
             reads=[psr[pb], psr[pb + 1]], writes=[r2_r[i], junk_r])
        P.op("act", lambda e, i=i: e.activation(out=r2[:, i:i + 1], in_=ss2[:, i:i + 1], func=AF.Ln, scale=1.0 / D, bias=EPS),
             reads=[r2_r[i]], writes=[r2_r[i]])
        P.op("act", lambda e, i=i: e.activation(out=r2[:, i:i + 1], in_=r2[:, i:i + 1], func=AF.Exp, scale=-0.5),
             reads=[r2_r[i]], writes=[r2_r[i]])
        for h in range(2):
            P.op("dve", lambda e, s=s, h=h, pb=pb: e.tensor_tensor(
                out=tt[s][:, 512 * h:512 * (h + 1)], in0=bank(pb + h), in1=gb[:, 512 * h:512 * (h + 1)], op=ALU.mult),
                reads=[psr[pb + h], gb_r, r2_r[i]], writes=[tth_r[s][h]])
        P.op("dve", lambda e, s=s, i=i: e.scalar_tensor_tensor(out=xc[s], in0=tt[s], scalar=r2[:, i:i + 1], in1=xc[s],
                                                              op0=ALU.mult, op1=ALU.add),
             reads=[tth_r[s][0], tth_r[s][1], r2_r[i], xc_r[s]], writes=[xc_r[s]])
        P.dma("sp", x_dst[i * 128:(i + 1) * 128, :], xc[s], f"xo{s}", reads=[xc_r[s]])


def mixer_phase(C, x_src, x_dst, W):
    A = C.A.child()
    hT = A.alloc((KC, S), BF16)
    attT = A.alloc((4, S), BF16)
    mlT = A.alloc((8, S), BF16)
    C.attT_r = Reg()
    C.mlT_r = Reg()
    C.mg_r = Reg()
    base = A.off
    end = A.end
    norm_transpose(C, Arena(A.ap, base, end), x_src, W["mix_pre_g"][0], hT)
    attention_phase(C, Arena(A.ap, base, end), hT, attT, W["w_in"][0])
    C.P.barrier()
    mlstm_phase(C, Arena(A.ap, base, end), hT, mlT, W["w_in"][0], W["conv_w"][0], W["conv_b"][0],
                W["mlstm_i_bias"][0], W["mlstm_f_bias"][0], W["mlstm_head_g"][0])
    C.P.barrier()
    merge_phase(C, Arena(A.ap, base, end), hT, attT, mlT, W["w_in"][0], W["w_att_branch"][0], W["w_mlstm_branch"][0],
                W["w_out"][0], W["mix_post_g"][0], x_src, x_dst)
    C.P.barrier()

def host_consts():
    c = {}
    bf = ml_dtypes.bfloat16
    c["ident"] = np.eye(128, dtype=np.float32).astype(bf)
    half = 16
    inv_freq = np.power(np.float32(500000.0), -(np.arange(half, dtype=np.float32) * 2.0 / 32)).astype(np.float32)
    ang = np.arange(S, dtype=np.float32)[None, :] * inv_freq[:, None]
    c["cos"] = np.concatenate([np.cos(ang), np.cos(ang)], 0).astype(np.float32)
    c["sin"] = np.concatenate([np.sin(ang), np.sin(ang)], 0).astype(np.float32)
    rm = np.zeros((32, 32), np.float32)
    for j in range(16):
        rm[16 + j, j] = -1.0
        rm[j, 16 + j] = 1.0
    c["rm"] = rm.astype(bf)
    jj = np.arange(128)[:, None]
    ii = np.arange(128)[None, :]
    NEG = -30000.0
    c["maskc"] = np.where(jj <= ii, 0.0, NEG).astype(bf)
    c["maskp"] = np.where(jj >= ii, 0.0, NEG).astype(bf)
    c["maskn"] = np.full((128, 128), NEG, np.float32).astype(bf)
    c["tri"] = (jj <= ii).astype(np.float32)
    c["ones_f"] = np.ones((128, 128), np.float32)
    c["identf"] = np.eye(128, dtype=np.float32)
    c["ones_bf"] = np.ones((128, 128), np.float32).astype(bf)
    return c


def build(stage="full"):
    nc = bass.Bass("TRN2", target_bir_lowering=False)

    def din(name, shape, dt=F32):
        return nc.dram_tensor(name, list(shape), dt, kind="ExternalInput").ap()

    x = din("x", [S, D])
    W = {}
    for name, shape in [("ffn1_pre_g", [1, D]), ("ffn1_w_gate", [1, D, FF]), ("ffn1_w_up", [1, D, FF]),
                        ("ffn1_w_down", [1, FF, D]), ("ffn1_post_g", [1, D]), ("mix_pre_g", [1, D]),
                        ("w_in", [1, D, IN_W]), ("conv_w", [1, 4, 2048]), ("conv_b", [1, 2048]),
                        ("mlstm_i_bias", [1, 4]), ("mlstm_f_bias", [1, 4]), ("mlstm_head_g", [1, 1024]),
                        ("w_att_branch", [1, 512, D]), ("w_mlstm_branch", [1, D, D]), ("w_out", [1, D, D]),
                        ("mix_post_g", [1, D]), ("ffn2_pre_g", [1, D]), ("ffn2_w_gate", [1, D, FF]),
                        ("ffn2_w_up", [1, D, FF]), ("ffn2_w_down", [1, FF, D]), ("ffn2_post_g", [1, D])]:
        W[name] = din(name, shape)
    CD = {}
    for name, shape, dt in [("c_ident", [128, 128], BF16), ("c_cos", [32, S], F32), ("c_sin", [32, S], F32),
                            ("c_rm", [32, 32], BF16), ("c_maskc", [128, 128], BF16), ("c_maskp", [128, 128], BF16),
                            ("c_maskn", [128, 128], BF16), ("c_tri", [128, 128], F32), ("c_ones_f", [128, 128], F32), ("c_identf", [128, 128], F32),
                            ("c_ones_bf", [128, 128], BF16)]:
        CD[name] = din(name, shape, dt)
    c_ident = CD["c_ident"]
    out = nc.dram_tensor("out", [S, D], F32, kind="ExternalOutput").ap()
    x1 = nc.dram_tensor("x1", [S, D], F32, kind="Internal").ap()
    x2 = nc.dram_tensor("x2", [S, D], F32, kind="Internal").ap()

    with ExitStack() as st:
        ARENA_BYTES = 212480
        arena = st.enter_context(nc.sbuf_tensor("arena", [128, ARENA_BYTES // 2], BF16))
        PS = st.enter_context(nc.psum_tensor("ps", [128, 4096], F32))
        C = Ctx()
        C.nc = nc
        C.P = P = Prog(nc)
        C.PS = PS
        C.psr = [Reg(f"ps{i}", excl=True) for i in range(8)]
        top = Arena(arena, 0, ARENA_BYTES)
        C.ident = top.alloc((128,), BF16)
        C.ident_r = Reg()
        P.dma("sp", C.ident, c_ident, "cst", writes=[C.ident_r])
        C.dram = CD
        C.cst_r = C.ident_r
        for nm, shp, dt in [("rm", (32,), BF16), ("maskc", (128,), BF16), ("maskp", (128,), BF16), ("maskn", (128,), BF16),
                            ("tri", (128,), F32), ("ones_f", (128,), F32), ("identf", (128,), F32), ("ones_bf", (128,), BF16)]:
            v = top.alloc(shp, dt)
            if nm == "rm":
                v = v[:32]
            setattr(C, nm, v)
            P.dma("sp", v, CD["c_" + nm], "cst", writes=[C.cst_r])
        C.stage = [top.alloc((1024,), F32) for _ in range(NST)]
        C.stage_r = [Reg() for _ in range(NST)]
        C.stage_i = 0
        C.A = Arena(arena, top.off, ARENA_BYTES)

        if stage == "attn":
            A = C.A.child()
            hT = A.alloc((KC, S), BF16)
            attT = A.alloc((4, S), BF16)
            C.attT_r = Reg()
            mk = Arena(arena, A.off, ARENA_BYTES)
            norm_transpose(C, mk, x, W["mix_pre_g"][0], hT)
            attention_phase(C, Arena(arena, A.off, ARENA_BYTES), hT, attT, W["w_in"][0])
            P.barrier()
            ov = out.rearrange("(a b) d -> a (b d)", a=1024).rearrange("(k p) t -> p k t", p=128)
            for k in range(4):
                for hh in range(2):
                    P.dma("pool", ov[:, k, hh * 1024:(hh + 1) * 1024], attT[:, k, hh * 1024:(hh + 1) * 1024], "dbg")
        if stage == "full":
            ffn_phase(C, x, x1, W["ffn1_pre_g"][0], W["ffn1_w_gate"][0], W["ffn1_w_up"][0], W["ffn1_w_down"][0],
                      W["ffn1_post_g"][0])
            mixer_phase(C, x1, x2, W)
            ffn_phase(C, x2, out, W["ffn2_pre_g"][0], W["ffn2_w_gate"][0], W["ffn2_w_up"][0], W["ffn2_w_down"][0],
                      W["ffn2_post_g"][0])
        if stage == "mix":
            mixer_phase(C, x, out, W)
        if stage == "ml":
            A = C.A.child()
            hT = A.alloc((KC, S), BF16)
            mlT = A.alloc((8, S), BF16)
            C.mlT_r = Reg()
            mk = Arena(arena, A.off, ARENA_BYTES)
            norm_transpose(C, mk, x, W["mix_pre_g"][0], hT)
            mlstm_phase(C, Arena(arena, A.off, ARENA_BYTES), hT, mlT, W["w_in"][0], W["conv_w"][0], W["conv_b"][0],
                        W["mlstm_i_bias"][0], W["mlstm_f_bias"][0], W["mlstm_head_g"][0])
            P.barrier()
            ov = out.rearrange("(a b) d -> a (b d)", a=1024).rearrange("(k p) t -> p k t", p=128)
            for k in range(8):
                for hh in range(2):
                    P.dma("pool", ov[:, k, hh * 1024:(hh + 1) * 1024], mlT[:, k, hh * 1024:(hh + 1) * 1024], "dbg")
        if stage == "ffn1a":
            A = C.A.child()
            hT_ar = A.sub(KC * S * 2)
            actT_ar = A.sub(FC * S * 2)
            hT = hT_ar.child().alloc((KC, S), BF16)
            norm_transpose(C, actT_ar.child(), x, W["ffn1_pre_g"][0], hT)
            ov = out.rearrange("(a b) d -> a (b d)", a=1024).rearrange("(k p) t -> p k t", p=128)
            for k in range(KC):
                for hh in range(2):
                    P.dma("pool", ov[:, k, hh * 1024:(hh + 1) * 1024], hT[:, k, hh * 1024:(hh + 1) * 1024], "dbg")
        if stage == "ffn1b":
            ffn_phase(C, x, out, W["ffn1_pre_g"][0], W["ffn1_w_gate"][0], W["ffn1_w_up"][0], W["ffn1_w_down"][0],
                      W["ffn1_post_g"][0], stop_after="B")
        if stage == "ffn1":
            ffn_phase(C, x, out, W["ffn1_pre_g"][0], W["ffn1_w_gate"][0], W["ffn1_w_up"][0], W["ffn1_w_down"][0],
                      W["ffn1_post_g"][0])
        P.finish()
        P.emit()
        print("prog stats", P.stats, "sems", len(P.dma_tot) + 5)
        if os.environ.get("DUMP"):
            for e in ("sp", "dve", "act"):
                print("====", e)
                for r in P.dump[e][-int(os.environ["DUMP"]):]:
                    print(r)
    return nc


_NC_CACHE = {}


def kernel(**inputs):
    stage = inputs.pop("_stage", os.environ.get("KSTAGE", "full"))
    if stage not in _NC_CACHE:
        _NC_CACHE[stage] = build(stage)
    nc = _NC_CACHE[stage]
    consts = host_consts()
    xfull = np.ascontiguousarray(inputs["x"], dtype=np.float32)
    shared = {k: np.ascontiguousarray(v, dtype=np.float32) for k, v in inputs.items() if k != "x"}
    for k, v in consts.items():
        shared["c_" + k] = v
    in_maps = []
    ncores = int(os.environ.get("NCORES", 8))
    for b in range(ncores):
        m = dict(shared)
        m["x"] = xfull[b]
        in_maps.append(m)
    res = run_bass_kernel_spmd(nc, in_maps, core_ids=list(range(ncores)))
    return np.stack([r["out"] for r in res.results], axis=0)
```

```python
from contextlib import ExitStack
import math
import os

import numpy as np
import ml_dtypes
import concourse.bass as bass
import concourse.mybir as mybir
from concourse.bass_utils import run_bass_kernel_spmd

F32 = mybir.dt.float32
BF16 = mybir.dt.bfloat16
AF = mybir.ActivationFunctionType
ALU = mybir.AluOpType
AX = mybir.AxisListType

S = 2048
D = 1024
FF = 2816
NT = S // 128
KC = D // 128
FC = FF // 128
IN_W = 10760
EPS = 1e-6
ENGS = ("pe", "act", "dve", "pool", "sp")


class Reg:
    __slots__ = ("name", "w", "rs", "rd", "excl")

    def __init__(self, name="", excl=False):
        self.name = name
        self.excl = excl
        self.w = None
        self.rs = {}
        self.rd = []


class Ins:
    __slots__ = ("eng", "fn", "deps", "signal", "val", "dma", "key")

    def __init__(self, eng, fn, dma=False, key=None):
        self.eng = eng
        self.fn = fn
        self.deps = ()
        self.signal = dma
        self.val = 0
        self.dma = dma
        self.key = key


class Prog:
    def __init__(self, nc):
        self.nc = nc
        self.engs = {e: [] for e in ENGS}
        self.dma_tot = {}
        self.dma_last = {}

    def _add(self, ins, reads, writes):
        eng = ins.eng
        deps = {}
        for r in reads:
            d = r.w
            if d is not None:
                deps[id(d)] = d
            if r.excl:
                for e2, x in r.rs.items():
                    if e2 != eng:
                        deps[id(x)] = x
        for w in writes:
            d = w.w
            if d is not None:
                deps[id(d)] = d
            for e2, x in w.rs.items():
                if (not ins.dma) and e2 == eng:
                    continue
                deps[id(x)] = x
            for x in w.rd:
                deps[id(x)] = x
        out = []
        for d in deps.values():
            if d is ins:
                continue
            if (not d.dma) and (not ins.dma) and d.eng == "pe" and eng == "pe":
                continue
            d.signal = True
            out.append(d)
        ins.deps = out
        for r in reads:
            if ins.dma:
                r.rd.append(ins)
            else:
                r.rs[eng] = ins
        for w in writes:
            w.w = ins
            w.rs = {}
            w.rd = []
        self.engs[eng].append(ins)
        return ins

    def op(self, eng, fn, reads=(), writes=()):
        return self._add(Ins(eng, fn), reads, writes)

    def dma(self, queue, out, in_, key, reads=(), writes=(), **kw):
        ins = Ins(queue, lambda e: e.dma_start(out=out, in_=in_, **kw), dma=True, key=key)
        self.dma_tot[key] = self.dma_tot.get(key, 0) + 16
        ins.val = self.dma_tot[key]
        self.dma_last[key] = ins
        return self._add(ins, reads, writes)

    def barrier(self):
        lasts = []
        for e in ENGS:
            for ins in reversed(self.engs[e]):
                if ins.fn is not None and not ins.dma:
                    ins.signal = True
                    lasts.append(ins)
                    break
        lasts += list(self.dma_last.values())
        for e in ENGS:
            ins = Ins(e, None)
            ins.deps = [d for d in lasts if d.dma or d.eng != e]
            self.engs[e].append(ins)

    def finish(self):
        ins = Ins("sp", None)
        ins.deps = list(self.dma_last.values())
        self.engs["sp"].append(ins)

    def emit(self):
        nc = self.nc
        for e in ENGS:
            c = 0
            for ins in self.engs[e]:
                if ins.dma:
                    continue
                if ins.signal and ins.fn is not None:
                    c += 1
                    ins.val = c
        with ExitStack() as st:
            sems = {e: st.enter_context(nc.semaphore(f"s_{e}")) for e in ENGS}
            dsem = {k: st.enter_context(nc.semaphore(f"d_{k}")) for k in self.dma_tot}
            block = st.enter_context(nc.Block())
            bname = {"pe": "tensor", "act": "scalar", "dve": "vector", "pool": "gpsimd", "sp": "sync"}
            stats = {}
            self.dump = {}
            for e in ENGS:
                def body(engine, e=e):
                    seen = {}
                    nw = 0
                    for ins in self.engs[e]:
                        need = {}
                        for d in ins.deps:
                            s = ("d", d.key) if d.dma else ("c", d.eng)
                            if need.get(s, 0) < d.val:
                                need[s] = d.val
                        for s, v in need.items():
                            if seen.get(s, 0) < v:
                                seen[s] = v
                                sh = dsem[s[1]] if s[0] == "d" else sems[s[1]]
                                engine.wait_ge(sh, v)
                                nw += 1
                        if os.environ.get("DUMP"):
                            self.dump.setdefault(e, []).append((sorted((k, v) for k, v in need.items()), ins.fn is not None, ins.dma, ins.key, ins.signal, ins.val))
                        if ins.fn is not None:
                            bi = ins.fn(engine)
                            if ins.dma:
                                bi.then_inc(dsem[ins.key], 16)
                            elif ins.signal:
                                bi.then_inc(sems[e], 1)
                    stats[e] = (len(self.engs[e]), nw)
                getattr(block, bname[e])(body)
            self.stats = stats


class Arena:
    def __init__(self, ap, start, end):
        self.ap = ap
        self.start = start
        self.off = start
        self.end = end

    def alloc(self, free_shape, dt, parts=128):
        n = 1
        for v in free_shape:
            n *= v
        esz = 4 if dt == F32 else 2
        nbytes = n * esz
        st = (self.off + 63) // 64 * 64
        assert st + nbytes <= self.end, f"arena overflow: need {st + nbytes} > {self.end}"
        self.off = st + nbytes
        Arena.last = (st, tuple(free_shape), dt)
        v = self.ap[:parts, st // 2:(st + nbytes) // 2]
        if dt == F32:
            v = v.bitcast(F32)
        if len(free_shape) == 2:
            v = v.rearrange("p (a b) -> p a b", a=free_shape[0])
        elif len(free_shape) == 3:
            v = v.rearrange("p (a b c) -> p a b c", a=free_shape[0], b=free_shape[1])
        return v

    def sub(self, nbytes):
        st = (self.off + 63) // 64 * 64
        assert st + nbytes <= self.end, f"arena overflow(sub): need {st + nbytes} > {self.end}"
        self.off = st + nbytes
        return Arena(self.ap, st, st + nbytes)

    def child(self):
        return Arena(self.ap, self.start, self.end)


class Ctx:
    pass


DBG = {}


NST = 3


def load_cast(C, dst, src, dst_reg):
    P = C.P
    s = C.stage_i % NST
    C.stage_i += 1
    sh = src.shape
    n = 1
    for v in sh[1:]:
        n *= v
    assert n <= 1024
    stg = C.stage[s][:, :n]
    if len(sh) == 3:
        stg = stg.rearrange("p (a b) -> p a b", a=sh[1])
    P.dma("sp", stg, src, f"st{s}", writes=[C.stage_r[s]])
    P.op("pool", lambda e: e.tensor_copy(out=dst, in_=stg), reads=[C.stage_r[s]], writes=[dst_reg])


def norm_transpose(C, A_stage, x_src, g_pre_dram, hT):
    P, PS, psr = C.P, C.PS, C.psr
    gb = A_stage.alloc((D,), F32)
    gb_r = Reg()
    P.dma("sp", gb, g_pre_dram.partition_broadcast(128), "g", writes=[gb_r])
    ss = A_stage.alloc((NT,), F32)
    rstd = A_stage.alloc((NT,), F32)
    junk = A_stage.alloc((D,), BF16)
    junk_r = Reg()
    hb = [A_stage.alloc((D,), BF16) for _ in range(2)]
    hb_r = [Reg() for _ in range(2)]
    xs = [A_stage.alloc((D,), F32) for _ in range(NT)]
    xs_r = [Reg() for _ in range(NT)]
    ss_r = [Reg() for _ in range(NT)]
    rstd_r = Reg()
    for i in range(NT):
        P.dma("sp", xs[i], x_src[i * 128:(i + 1) * 128, :], f"xs{i}", writes=[xs_r[i]])
    for i in range(NT):
        P.op("act", lambda e, i=i: e.activation(out=junk, in_=xs[i], func=AF.Square, accum_out=ss[:, i:i + 1]),
             reads=[xs_r[i]], writes=[ss_r[i], junk_r])
    P.op("act", lambda e: e.activation(out=rstd, in_=ss, func=AF.Ln, scale=1.0 / D, bias=EPS),
         reads=ss_r, writes=[rstd_r])
    P.op("act", lambda e: e.activation(out=rstd, in_=rstd, func=AF.Exp, scale=-0.5),
         reads=[rstd_r], writes=[rstd_r])
    for i in range(NT):
        s = i % 2
        P.op("dve", lambda e, i=i, s=s: e.scalar_tensor_tensor(out=hb[s], in0=xs[i], scalar=rstd[:, i:i + 1], in1=gb,
                                                              op0=ALU.mult, op1=ALU.mult),
             reads=[xs_r[i], rstd_r, gb_r], writes=[hb_r[s]])
        b = i % 2
        psb = PS[:, 512 * b:512 * (b + 1)].bitcast(BF16)
        for k in range(KC):
            P.op("pe", lambda e, k=k, s=s, psb=psb: e.transpose(out=psb[:, k * 128:(k + 1) * 128],
                                                                in_=hb[s][:, k * 128:(k + 1) * 128], identity=C.ident),
                 reads=[hb_r[s], C.ident_r], writes=[psr[b]])
        P.op("act", lambda e, i=i, psb=psb: e.activation(out=hT[:, :, i * 128:(i + 1) * 128],
                                                         in_=psb.rearrange("p (k n) -> p k n", k=KC), func=AF.Copy),
             reads=[psr[b]], writes=[])
    P.barrier()


def ffn_phase(C, x_src, x_dst, g_pre, wg, wu, wd, g_post, stop_after=None):
    P, PS, psr = C.P, C.PS, C.psr
    A = C.A.child()
    hT_ar = A.sub(KC * S * 2)
    actT_ar = A.sub(FC * S * 2)
    hT = hT_ar.child().alloc((KC, S), BF16)
    actT = actT_ar.child().alloc((FC, S), BF16)
    norm_transpose(C, actT_ar.child(), x_src, g_pre, hT)

    NSL = 2
    wg_s = [A.alloc((KC, 256), BF16) for _ in range(NSL)]
    wu_s = [A.alloc((KC, 256), BF16) for _ in range(NSL)]
    wg_r = [[Reg(), Reg()] for _ in range(NSL)]
    wu_r = [[Reg(), Reg()] for _ in range(NSL)]
    wd_h = [A.alloc((FC, 512), BF16) for _ in range(2)]
    wd_r = [[Reg() for _ in range(FC // 2)] for _ in range(2)]
    sg = [A.alloc((512,), BF16) for _ in range(2)]
    sg_r = [Reg() for _ in range(2)]
    gb = A.alloc((D,), F32)
    gb_r = Reg()
    ss2 = A.alloc((NT,), F32)
    r2 = A.alloc((NT,), F32)
    junk = A.alloc((D,), BF16)
    junk_r = Reg()
    P.dma("sp", gb, g_post.partition_broadcast(128), "g", writes=[gb_r])

    wd_jobs = [(h, c2) for h in range(2) for c2 in range(FC // 2)]

    def load_wd_piece():
        if not wd_jobs:
            return
        h, c2 = wd_jobs.pop(0)
        load_cast(C, wd_h[h][:, 2 * c2:2 * c2 + 2, :],
                  wd[c2 * 256:(c2 + 1) * 256, h * 512:(h + 1) * 512].rearrange("(c p) n -> p c n", p=128),
                  wd_r[h][c2])

    j = 0
    for cb in range(FC // 2):
        s = cb % NSL
        for part in range(2):
            load_cast(C, wg_s[s][:, 4 * part:4 * part + 4, :],
                      wg[part * 512:(part + 1) * 512, cb * 256:(cb + 1) * 256].rearrange("(k p) n -> p k n", p=128),
                      wg_r[s][part])
            load_cast(C, wu_s[s][:, 4 * part:4 * part + 4, :],
                      wu[part * 512:(part + 1) * 512, cb * 256:(cb + 1) * 256].rearrange("(k p) n -> p k n", p=128),
                      wu_r[s][part])
        if cb >= 1:
            for _ in range(3):
                load_wd_piece()
        for sub in range(2):
            ffc = cb * 2 + sub
            for tg in range(4):
                bG = 2 * (j % 4)
                bU = bG + 1
                q = j % 2
                j += 1
                for k in range(KC):
                    P.op("pe", lambda e, k=k, s=s, sub=sub, tg=tg, bG=bG: e.matmul(
                        PS[:, 512 * bG:512 * (bG + 1)], lhsT=wg_s[s][:, k, sub * 128:(sub + 1) * 128],
                        rhs=hT[:, k, tg * 512:(tg + 1) * 512], start=(k == 0), stop=(k == KC - 1)),
                        reads=[wg_r[s][k // 4]], writes=[psr[bG]])
                for k in range(KC):
                    P.op("pe", lambda e, k=k, s=s, sub=sub, tg=tg, bU=bU: e.matmul(
                        PS[:, 512 * bU:512 * (bU + 1)], lhsT=wu_s[s][:, k, sub * 128:(sub + 1) * 128],
                        rhs=hT[:, k, tg * 512:(tg + 1) * 512], start=(k == 0), stop=(k == KC - 1)),
                        reads=[wu_r[s][k // 4]], writes=[psr[bU]])
                P.op("act", lambda e, q=q, bG=bG: e.activation(out=sg[q], in_=PS[:, 512 * bG:512 * (bG + 1)], func=AF.Silu),
                     reads=[psr[bG]], writes=[sg_r[q]])
                P.op("dve", lambda e, q=q, bU=bU, ffc=ffc, tg=tg: e.tensor_tensor(
                    out=actT[:, ffc, tg * 512:(tg + 1) * 512], in0=PS[:, 512 * bU:512 * (bU + 1)], in1=sg[q], op=ALU.mult),
                    reads=[psr[bU], sg_r[q]], writes=[])
    while wd_jobs:
        load_wd_piece()
    P.barrier()
    if stop_after == "B":
        ov = x_dst.rearrange("(a b) d -> a (b d)", a=1024).rearrange("(k p) t -> p k t", p=128)
        for k in range(KC):
            for hh in range(2):
                P.dma("pool", ov[:, k, hh * 1024:(hh + 1) * 1024], actT[:, k + 14, hh * 1024:(hh + 1) * 1024], "dbg")
        return

    Ah = hT_ar.child()
    NSC = int(os.environ.get('NSC', 2))
    NSX = int(os.environ.get('NSX', NSC))
    xc = [Ah.alloc((D,), F32) for _ in range(NSX)]
    tt = [Ah.alloc((D,), F32) for _ in range(NSC)]
    xc_r = [Reg() for _ in range(NSX)]
    tt_r = [Reg() for _ in range(NSC)]
    tth_r = [[Reg(), Reg()] for _ in range(NSC)]
    r2_r = [Reg() for _ in range(NT)]
    for i in range(int(os.environ.get("CT", NT))):
        s = i % NSC
        sx = i % NSX
        P.dma("sp", xc[sx], x_src[i * 128:(i + 1) * 128, :], f"xc{sx}", writes=[xc_r[sx]])
        pb = (i % 4) * 2
        psf = PS[:, 512 * pb:512 * (pb + 2)]
        for h in range(2):
            for ffc in range(FC):
                P.op("pe", lambda e, h=h, ffc=ffc, i=i, pb=pb: e.matmul(
                    PS[:, 512 * (pb + h):512 * (pb + h + 1)], lhsT=actT[:, ffc, i * 128:(i + 1) * 128],
                    rhs=wd_h[h][:, ffc, :], start=(ffc == 0), stop=(ffc == FC - 1)),
                    reads=[wd_r[h][ffc // 2]], writes=[psr[pb + h]])
        P.op("act", lambda e, i=i, psf=psf: e.activation(out=junk, in_=psf, func=AF.Square, accum_out=ss2[:, i:i + 1]),
             reads=[psr[pb], psr[pb + 1]], writes=[r2_r[i], junk_r])
        cstop = int(os.environ.get("CSTOP", 9))
        if cstop == 1:
            P.dma("sp", x_dst[i * 128:(i + 1) * 128, :], xc[sx], f"xo{s}", reads=[xc_r[sx], r2_r[i]])
            continue
        P.op("act", lambda e, i=i: e.activation(out=r2[:, i:i + 1], in_=ss2[:, i:i + 1], func=AF.Ln, scale=1.0 / D, bias=EPS),
             reads=[r2_r[i]], writes=[r2_r[i]])
        P.op("act", lambda e, i=i: e.activation(out=r2[:, i:i + 1], in_=r2[:, i:i + 1], func=AF.Exp, scale=-0.5,
                                                bias=math.log(0.5)),
             reads=[r2_r[i]], writes=[r2_r[i]])
        if cstop == 2:
            P.dma("sp", x_dst[i * 128:(i + 1) * 128, :], xc[sx], f"xo{s}", reads=[xc_r[sx], r2_r[i]])
            continue
        for h in range(2):
            if os.environ.get("DVEVAR") == "copy":
                P.op("dve", lambda e, s=s, h=h, pb=pb: e.tensor_copy(
                    out=tt[s][:, 512 * h:512 * (h + 1)], in_=PS[:, 512 * (pb + h):512 * (pb + h + 1)]),
                    reads=[psr[pb + h], gb_r], writes=[tth_r[s][h]])
                continue
            if os.environ.get("DVEVAR") == "sbuf":
                P.op("dve", lambda e, s=s, h=h, pb=pb: e.tensor_tensor(
                    out=tt[s][:, 512 * h:512 * (h + 1)], in0=xc[sx][:, 512 * h:512 * (h + 1)],
                    in1=gb[:, 512 * h:512 * (h + 1)], op=ALU.mult),
                    reads=[psr[pb + h], gb_r, xc_r[sx]], writes=[tth_r[s][h]])
                continue
            P.op("dve", lambda e, s=s, h=h, pb=pb: e.tensor_tensor(
                out=tt[s][:, 512 * h:512 * (h + 1)], in0=PS[:, 512 * (pb + h):512 * (pb + h + 1)],
                in1=gb[:, 512 * h:512 * (h + 1)], op=ALU.mult),
                reads=[psr[pb + h], gb_r, r2_r[i]], writes=[tth_r[s][h]])
        if cstop == 3:
            P.dma("sp", x_dst[i * 128:(i + 1) * 128, :], tt[s], f"xo{s}", reads=[xc_r[sx], r2_r[i], tth_r[s][0], tth_r[s][1]], writes=[tth_r[s][0], tth_r[s][1]])
            continue
        P.op("dve", lambda e, s=s, sx=sx, i=i: e.scalar_tensor_tensor(out=xc[sx], in0=tt[s], scalar=r2[:, i:i + 1], in1=xc[sx],
                                                              op0=ALU.mult, op1=ALU.add),
             reads=[tth_r[s][0], tth_r[s][1], r2_r[i], xc_r[sx]], writes=[xc_r[sx]])
        P.dma("sp", x_dst[i * 128:(i + 1) * 128, :], xc[sx], f"xo{sx}", reads=[xc_r[sx]])
    P.barrier()


def tok_slice(start, step):
    return slice(start, start + step * 127 + 1, step) if step > 1 else slice(start, start + 128)


def attention_phase(C, A, hT, attT, w_in):
    P, PS, psr = C.P, C.PS, C.psr
    cos = A.alloc((S,), F32)[:32]
    sin = A.alloc((S,), F32)[:32]
    cs_r = Reg()
    cs2_r = Reg()
    P.dma("sp", cos, C.dram["c_cos"], "cs", writes=[cs_r])
    P.dma("sp", sin, C.dram["c_sin"], "cs", writes=[cs2_r])
    wsl = [[A.alloc((KC, 128), BF16) for _ in range(3)] for _ in range(2)]
    wsl_r = [[Reg() for _ in range(3)] for _ in range(2)]
    DBG.clear()
    qk = [[None, None], [None, None]]
    for a_ in range(2):
        for b_ in range(2):
            qk[a_][b_] = A.alloc((S,), BF16)
            DBG[f"qk{a_}{b_}"] = Arena.last
    qk_r = [[[Reg() for _ in range(4)] for _ in range(2)] for _ in range(2)]
    vt = [A.alloc((NT, 128), BF16) for _ in range(2)]
    vt_r = [[Reg() for _ in range(4)] for _ in range(2)]
    acc_n = A.alloc((S,), F32)
    acc_d = A.alloc((S,), F32)
    accn_r = [Reg() for _ in range(4)]
    accd_r = [Reg() for _ in range(4)]
    acc_all = Reg()
    pT = [A.alloc((512,), BF16) for _ in range(4)]
    pT_r = [Reg() for _ in range(4)]
    t1 = [A.alloc((512,), F32)[:32] for _ in range(2)]
    t2 = [A.alloc((512,), F32)[:32] for _ in range(2)]
    t1_r = [Reg() for _ in range(2)]
    t2_r = [Reg() for _ in range(2)]
    rcp = [A.alloc((512,), F32) for _ in range(2)]
    rcp_r = [Reg() for _ in range(2)]
    SCALE = 128.0 ** -0.5

    def bank(b):
        return PS[:, 512 * b:512 * (b + 1)]

    cnt = {"proj": 0, "rot": 0, "s": 0, "p": 0, "nd": 0}
    iters = [(hs, g) for hs in range(4) for g in range(3)]

    def make_blocks(g):
        blocks = []
        if g == 0:
            for b in range(16):
                blocks.append((tok_slice(128 * b, 1), tok_slice(128 * (b - 1), 1) if b > 0 else None))
        elif g == 1:
            for r in range(4):
                for b in range(4):
                    blocks.append((tok_slice(4 * 128 * b + r, 4), tok_slice(4 * 128 * (b - 1) + r, 4) if b > 0 else None))
        else:
            for r in range(16):
                blocks.append((tok_slice(r, 16), None))
        return blocks

    def proj_units(idx):
        hs, g = iters[idx]
        sl = idx % 2
        head = g * 4 + hs
        blocks = make_blocks(g)
        units = []

        def u_load():
            for m, off in enumerate((0, 1536, 3072)):
                c0 = off + head * 128
                load_cast(C, wsl[sl][m], w_in[:, c0:c0 + 128].rearrange("(k p) n -> p k n", p=128), wsl_r[sl][m])
        units.append(u_load)
        for m in range(2):
            for tg in range(4):
                def u_qk(m=m, tg=tg):
                    dst = qk[sl][m]
                    pb = cnt["proj"] % 2
                    cnt["proj"] += 1
                    for k in range(KC):
                        P.op("pe", lambda e, k=k: e.matmul(
                            bank(pb), lhsT=wsl[sl][m][:, k, :], rhs=hT[:, k, tg * 512:(tg + 1) * 512],
                            start=(k == 0), stop=(k == KC - 1)), reads=[wsl_r[sl][m]], writes=[psr[pb]])
                    dcol = dst[:, tg * 512:(tg + 1) * 512]
                    dr = qk_r[sl][m][tg]
                    P.op("act", lambda e: e.activation(out=dcol, in_=bank(pb), func=AF.Copy), reads=[psr[pb]], writes=[dr])
                    rb = 2 + cnt["rot"] % 2
                    ts = cnt["rot"] % 2
                    cnt["rot"] += 1
                    P.op("pe", lambda e: e.matmul(bank(rb)[:32, :], lhsT=C.rm, rhs=dcol[:32, :], start=True, stop=True),
                         reads=[dr, C.cst_r], writes=[psr[rb]])
                    P.op("dve", lambda e: e.tensor_tensor(out=t1[ts], in0=bank(rb)[:32, :], in1=sin[:, tg * 512:(tg + 1) * 512],
                                                          op=ALU.mult), reads=[psr[rb], cs_r, cs2_r], writes=[t1_r[ts]])
                    P.op("dve", lambda e: e.tensor_tensor(out=t2[ts], in0=dcol[:32, :], in1=cos[:, tg * 512:(tg + 1) * 512],
                                                          op=ALU.mult), reads=[dr, cs_r], writes=[t2_r[ts]])
                    P.op("dve", lambda e: e.tensor_tensor(out=dcol[:32, :], in0=t1[ts], in1=t2[ts], op=ALU.add),
                         reads=[t1_r[ts], t2_r[ts]], writes=[dr])
                units.append(u_qk)
        for j in range(4):
            def u_v(j=j):
                pb = cnt["proj"] % 2
                cnt["proj"] += 1
                for bi in range(4):
                    qs = blocks[4 * j + bi][0]
                    for k in range(KC):
                        P.op("pe", lambda e, k=k, qs=qs, bi=bi: e.matmul(
                            bank(pb)[:, bi * 128:(bi + 1) * 128], lhsT=hT[:, k, qs], rhs=wsl[sl][2][:, k, :],
                            start=(k == 0), stop=(k == KC - 1)), reads=[wsl_r[sl][2]], writes=[psr[pb]])
                P.op("act", lambda e: e.activation(
                    out=vt[sl][:, 4 * j:4 * j + 4, :], in_=bank(pb).rearrange("p (a b) -> p a b", a=4), func=AF.Copy),
                    reads=[psr[pb]], writes=[vt_r[sl][j]])
            units.append(u_v)
        return units

    def core(idx, fill):
        hs, g = iters[idx]
        sl = idx % 2
        blocks = make_blocks(g)
        qT, kT = qk[sl][0], qk[sl][1]
        qr = qk_r[sl][0] + qk_r[sl][1]
        pairs = [(2 * i, 2 * i + 1) for i in range(8)]

        def do_qk(pair):
            sb = 4 + cnt["s"] % 2
            cnt["s"] += 1
            for ti, blk in enumerate(pair):
                qs, ps_ = blocks[blk]
                for ci, ks in enumerate((ps_, qs)):
                    o = bank(sb)[:, (2 * ti + ci) * 128:(2 * ti + ci + 1) * 128]
                    if ks is None:
                        P.op("pe", lambda e, o=o: e.matmul(o, lhsT=C.ident, rhs=C.maskn, start=True, stop=True),
                             reads=[C.cst_r], writes=[psr[sb]])
                        continue
                    P.op("pe", lambda e, o=o, ks=ks, qs=qs, kT=kT, qT=qT: e.matmul(o, lhsT=kT[:, ks], rhs=qT[:, qs], start=True, stop=False),
                         reads=qr, writes=[psr[sb]])
                    mk = C.maskc if ci == 1 else C.maskp
                    P.op("pe", lambda e, o=o, mk=mk: e.matmul(o, lhsT=C.ident, rhs=mk, start=False, stop=True),
                         reads=[C.cst_r], writes=[psr[sb]])
            pslot = cnt["p"] % 4
            cnt["p"] += 1
            P.op("act", lambda e, sb=sb, pslot=pslot: e.activation(out=pT[pslot], in_=bank(sb), func=AF.Exp, scale=SCALE),
                 reads=[psr[sb]], writes=[pT_r[pslot]])
            return pslot

        def do_pv(pair, pslot):
            nb = 6 + cnt["nd"] % 2
            cnt["nd"] += 1
            for ti, blk in enumerate(pair):
                qs, ps_ = blocks[blk]
                for is_num in (True, False):
                    col = ti * 128 + (0 if is_num else 256)
                    o = bank(nb)[:, col:col + 128]
                    for ci in range(2):
                        kblk = blk if (ci == 1 or ps_ is None) else blk - 1
                        lhs = vt[sl][:, kblk, :] if is_num else C.ones_bf
                        P.op("pe", lambda e, o=o, lhs=lhs, pslot=pslot, ti=ti, ci=ci: e.matmul(
                            o, lhsT=lhs, rhs=pT[pslot][:, (2 * ti + ci) * 128:(2 * ti + ci + 1) * 128],
                            start=(ci == 0), stop=(ci == 1)),
                            reads=[pT_r[pslot], vt_r[sl][kblk // 4], C.cst_r], writes=[psr[nb]])
            pi = pair[0] // 2
            if g == 0:
                sel = slice(256 * pi, 256 * pi + 256)
                dn, dd = acc_n[:, sel], acc_d[:, sel]
                sn, sd = bank(nb)[:, 0:256], bank(nb)[:, 256:512]
            elif g == 1:
                r_, b_ = pair[0] // 4, pair[0] % 4
                st_ = r_ + 512 * b_
                sel = slice(st_, st_ + 4 * 255 + 1, 4)
                dn, dd = acc_n[:, sel], acc_d[:, sel]
                sn, sd = bank(nb)[:, 0:256], bank(nb)[:, 256:512]
            else:
                r_ = pair[0]
                dn = acc_n.rearrange("p (n r) -> p r n", r=16)[:, r_:r_ + 2, :]
                dd = acc_d.rearrange("p (n r) -> p r n", r=16)[:, r_:r_ + 2, :]
                sn = bank(nb)[:, 0:256].rearrange("p (a b) -> p a b", a=2)
                sd = bank(nb)[:, 256:512].rearrange("p (a b) -> p a b", a=2)
            if g == 0:
                P.op("dve", lambda e, dn=dn, sn=sn: e.tensor_copy(out=dn, in_=sn), reads=[psr[nb]], writes=[acc_all])
                P.op("dve", lambda e, dd=dd, sd=sd: e.tensor_copy(out=dd, in_=sd), reads=[psr[nb]], writes=[acc_all])
            else:
                P.op("dve", lambda e, dn=dn, sn=sn: e.tensor_tensor(out=dn, in0=sn, in1=dn, op=ALU.add),
                     reads=[psr[nb], acc_all], writes=[acc_all])
                P.op("dve", lambda e, dd=dd, sd=sd: e.tensor_tensor(out=dd, in0=sd, in1=dd, op=ALU.add),
                     reads=[psr[nb], acc_all], writes=[acc_all])


        prev = None
        for pair in pairs:
            pslot = do_qk(pair)
            if prev is not None:
                do_pv(*prev)
                fill()
            prev = (pair, pslot)
        do_pv(*prev)
        fill()
        if g == 2:
            for tg in range(4):
                rs = tg % 2
                P.op("dve", lambda e, tg=tg, rs=rs: e.reciprocal(out=rcp[rs], in_=acc_d[:, tg * 512:(tg + 1) * 512]),
                     reads=[acc_all], writes=[rcp_r[rs]])
                P.op("dve", lambda e, tg=tg, rs=rs, hs=hs: e.tensor_tensor(
                    out=attT[:, hs, tg * 512:(tg + 1) * 512], in0=acc_n[:, tg * 512:(tg + 1) * 512], in1=rcp[rs], op=ALU.mult),
                    reads=[acc_all, rcp_r[rs]], writes=[C.attT_r])


    for u in proj_units(0):
        u()
    for idx in range(len(iters)):
        nxt = proj_units(idx + 1) if idx + 1 < len(iters) else []
        per = -(-len(nxt) // 8) if nxt else 0
        pos = [0]

        def fill():
            for u in nxt[pos[0]:pos[0] + per]:
                u()
            pos[0] += per
        core(idx, fill)
        for u in nxt[pos[0]:]:
            u()

def mlstm_phase(C, A, hT, mlT, w_in, conv_w, conv_b, i_bias, f_bias, head_g):
    P, PS, psr = C.P, C.PS, C.psr
    OQ, OK_, OV, OO, OI = 4608, 5632, 6656, 7680, 8704

    def bank(b):
        return PS[:, 512 * b:512 * (b + 1)]

    def R():
        return Reg()

    cwb = A.alloc((2048,), F32)[:5]
    cwb_r = R()
    cwb2_r = R()
    P.dma("sp", cwb[0:4, :], conv_w, "mca", writes=[cwb_r])
    P.dma("sp", cwb[4:5, :], conv_b.rearrange("(o n) -> o n", o=1), "mca", writes=[cwb2_r])
    cwT = A.alloc((16, 8), F32)
    ncb = A.alloc((16,), F32)
    cwT_r = R()
    for c in range(16):
        P.op("pe", lambda e, c=c: e.matmul(bank(6)[:, c * 8:c * 8 + 5], lhsT=cwb[:, c * 128:(c + 1) * 128],
                                           rhs=C.identf[:5, :5], start=True, stop=True),
             reads=[cwb_r, cwb2_r, C.cst_r], writes=[psr[6]])
    P.op("dve", lambda e: e.tensor_copy(out=cwT[:, :, 0:5], in_=bank(6)[:, 0:128].rearrange("p (c j) -> p c j", j=8)[:, :, 0:5]),
         reads=[psr[6]], writes=[cwT_r])
    P.op("dve", lambda e: e.tensor_scalar(out=ncb, in0=cwT[:, :, 4], scalar1=-1.0, scalar2=None, op0=ALU.mult),
         reads=[cwT_r], writes=[cwT_r])
    hg = A.alloc((1024,), F32)
    hg_r = R()
    P.dma("sp", hg, head_g.partition_broadcast(128), "mch", writes=[hg_r])
    bias8 = A.alloc((8,), F32)
    b8_r = R()
    b8b_r = R()
    P.dma("sp", bias8[:, 0:4], i_bias.partition_broadcast(128), "mcb", writes=[b8_r])
    P.dma("sp", bias8[:, 4:8], f_bias.partition_broadcast(128), "mcb", writes=[b8b_r])

    wif = A.alloc((KC, 8), BF16)
    wif_r = R()
    load_cast(C, wif, w_in[:, OI:OI + 8].rearrange("(k p) n -> p k n", p=128), wif_r)
    for c in range(NT):
        for k in range(KC):
            P.op("pe", lambda e, c=c, k=k: e.matmul(bank(7)[:, c * 8:(c + 1) * 8], lhsT=hT[:, k, c * 128:(c + 1) * 128],
                                                    rhs=wif[:, k, :], start=(k == 0), stop=(k == KC - 1)),
                 reads=[wif_r], writes=[psr[7]])
    gi = A.alloc((NT, 4), F32)
    lg = A.alloc((NT, 4), F32)
    g_r = R()
    pre3 = bank(7)[:, 0:128].rearrange("p (c j) -> p c j", j=8)
    P.op("dve", lambda e: e.tensor_tensor(out=gi, in0=pre3[:, :, 0:4],
                                          in1=bias8[:, 0:4].unsqueeze(1).to_broadcast([128, NT, 4]), op=ALU.add),
         reads=[psr[7], b8_r, b8b_r], writes=[g_r])
    P.op("dve", lambda e: e.tensor_tensor(out=lg, in0=pre3[:, :, 4:8],
                                          in1=bias8[:, 4:8].unsqueeze(1).to_broadcast([128, NT, 4]), op=ALU.add),
         reads=[psr[7], b8_r, b8b_r, g_r], writes=[g_r])
    P.op("act", lambda e: e.activation(out=lg, in_=lg, func=AF.Exp, scale=-1.0), reads=[g_r], writes=[g_r])
    P.op("act", lambda e: e.activation(out=lg, in_=lg, func=AF.Ln, bias=1.0), reads=[g_r], writes=[g_r])
    lg2 = lg.rearrange("p c h -> p (c h)")
    gi2 = gi.rearrange("p c h -> p (c h)")
    P.op("pe", lambda e: e.matmul(bank(6)[:, 0:64], lhsT=C.tri, rhs=lg2, start=True, stop=True),
         reads=[g_r, C.cst_r], writes=[psr[6]])
    P.op("pe", lambda e: e.matmul(bank(6)[:, 64:128], lhsT=C.ones_f, rhs=lg2, start=True, stop=True),
         reads=[g_r, C.cst_r], writes=[psr[6]])
    e_in = A.alloc((64,), F32)
    e_out = A.alloc((64,), F32)
    e_L = A.alloc((64,), F32)
    e_v = A.alloc((64,), F32)
    e_io = A.alloc((64,), F32)
    ee_r = R()
    P.op("dve", lambda e: e.tensor_tensor(out=e_in, in0=bank(6)[:, 0:64], in1=gi2, op=ALU.add),
         reads=[psr[6], g_r], writes=[ee_r])
    P.op("act", lambda e: e.activation(out=e_in, in_=e_in, func=AF.Exp), reads=[ee_r], writes=[ee_r])
    P.op("act", lambda e: e.activation(out=e_out, in_=bank(6)[:, 0:64], func=AF.Exp, scale=-1.0),
         reads=[psr[6], ee_r], writes=[ee_r])
    P.op("act", lambda e: e.activation(out=e_L, in_=bank(6)[:, 64:128], func=AF.Exp, scale=-1.0),
         reads=[psr[6], ee_r], writes=[ee_r])
    P.op("act", lambda e: e.activation(out=e_io, in_=bank(6)[:, 0:64], func=AF.Exp), reads=[psr[6], ee_r], writes=[ee_r])
    P.op("dve", lambda e: e.tensor_scalar(out=e_in, in0=e_in, scalar1=1.0 / 16.0, scalar2=None, op0=ALU.mult),
         reads=[ee_r], writes=[ee_r])
    P.op("dve", lambda e: e.tensor_tensor(out=e_v, in0=e_in, in1=e_L, op=ALU.mult), reads=[ee_r], writes=[ee_r])

    wq = [A.alloc((KC, 256), BF16) for _ in range(5)]
    wq_r = [[R(), R()] for _ in range(5)]
    raw = [A.alloc((S + 4,), BF16) for _ in range(2)]
    raw_r = [R(), R()]
    for i in range(2):
        P.op("dve", lambda e, i=i: e.memset(raw[i][:, 0:3], 0.0), writes=[raw_r[i]])
    neghalf = A.alloc((1,), F32)
    nh_r = R()
    P.op("dve", lambda e: e.memset(neghalf, -0.5), writes=[nh_r])
    dg = [[A.alloc((128,), BF16) for _ in range(4)] for _ in range(2)]
    dg_r = [R(), R()]
    qT2 = [A.alloc((2, S), BF16) for _ in range(2)]
    kT2 = [A.alloc((2, S), BF16) for _ in range(2)]
    qk2_r = [[[R(), R()], [R(), R()]] for _ in range(2)]
    ktok2 = [A.alloc((256,), BF16) for _ in range(2)]
    ktok2_r = [R(), R()]
    vaug2 = [A.alloc((258,), BF16) for _ in range(2)]
    vaug2_r = [R(), R()]
    for i in range(2):
        P.op("dve", lambda e, i=i: e.memset(vaug2[i][:, 256:257], 1.0), writes=[vaug2_r[i]])
        P.op("dve", lambda e, i=i: e.memset(vaug2[i][:, 257:258], 0.0), writes=[vaug2_r[i]])
    vp2 = [A.alloc((258,), BF16) for _ in range(2)]
    vp2_r = [R(), R()]
    sigo2 = [A.alloc((NT, 256), BF16) for _ in range(2)]
    sigo2_r = [[R() for _ in range(NT)] for _ in range(2)]
    wT2 = [A.alloc((128,), BF16) for _ in range(2)]
    wT2_r = [R(), R()]
    Cf = A.alloc((2, 258), F32)
    Cf_r = R()
    Cbf2 = [A.alloc((2, 258), BF16) for _ in range(2)]
    Cbf2_r = [R(), R()]
    hu3 = [A.alloc((256,), F32) for _ in range(3)]
    hu3_r = [R(), R(), R()]
    sm3 = [A.alloc((8,), F32) for _ in range(3)]
    sm3_r = [R(), R(), R()]
    junk3 = [A.alloc((256,), BF16) for _ in range(3)]
    mlb3 = [A.alloc((256,), BF16) for _ in range(3)]
    mlb3_r = [R(), R(), R()]

    def pre_units(hd):
        sl = hd % 2
        qT, kT, qk_r, sigo, sigo_r = qT2[sl], kT2[sl], qk2_r[sl], sigo2[sl], sigo2_r[sl]
        wsel = (0, 1, 3 + sl, 2)
        units = []

        def u_load():
            for m, off in enumerate((OQ, OK_, OV, OO)):
                c0 = off + hd * 256
                for part in range(2):
                    load_cast(C, wq[wsel[m]][:, 4 * part:4 * part + 4, :],
                              w_in[part * 512:(part + 1) * 512, c0:c0 + 256].rearrange("(k p) n -> p k n", p=128),
                              wq_r[wsel[m]][part])
        units.append(u_load)
        for m, dstT in ((0, qT), (1, kT)):
            for cc in range(2):
                ch = m * 8 + hd * 2 + cc
                bi = (m * 2 + cc) % 2

                def u_diag(ch=ch, bi=bi):
                    for j in range(4):
                        P.op("dve", lambda e, j=j: e.tensor_scalar(out=dg[bi][j], in0=C.ident, scalar1=cwT[:, ch, j:j + 1],
                                                                   scalar2=None, op0=ALU.mult),
                             reads=[C.cst_r, cwT_r], writes=[dg_r[bi]])
                units.append(u_diag)
                for tg in range(4):
                    def u_proj(m=m, cc=cc, tg=tg, ch=ch, bi=bi, dstT=dstT):
                        for k in range(KC):
                            P.op("pe", lambda e, k=k: e.matmul(
                                bank(6), lhsT=wq[m][:, k, cc * 128:(cc + 1) * 128], rhs=hT[:, k, tg * 512:(tg + 1) * 512],
                                start=(k == 0), stop=(k == KC - 1)), reads=[wq_r[m][k // 4]], writes=[psr[6]])
                        P.op("act", lambda e: e.activation(
                            out=raw[bi][:, 3 + tg * 512:3 + (tg + 1) * 512], in_=bank(6), func=AF.Copy),
                            reads=[psr[6]], writes=[raw_r[bi]])
                        for j in range(4):
                            P.op("pe", lambda e, j=j: e.matmul(bank(7), lhsT=dg[bi][j], rhs=raw[bi][:, tg * 512 + j:tg * 512 + j + 512],
                                                               start=(j == 0), stop=(j == 3)),
                                 reads=[dg_r[bi], raw_r[bi]], writes=[psr[7]])
                        P.op("act", lambda e: e.activation(out=dstT[:, cc, tg * 512:(tg + 1) * 512], in_=bank(7), func=AF.Silu,
                                                           bias=cwT[:, ch, 4:5]),
                             reads=[psr[7], cwT_r], writes=[qk_r[m][cc]])
                    units.append(u_proj)
        for c in range(NT):
            def u_gate(c=c):
                cs = slice(c * 128, (c + 1) * 128)
                ob = 6 + c % 2
                for k in range(KC):
                    P.op("pe", lambda e, k=k: e.matmul(bank(ob)[:, 0:256], lhsT=hT[:, k, cs], rhs=wq[2][:, k, :],
                                                       start=(k == 0), stop=(k == KC - 1)),
                         reads=[wq_r[2][k // 4]], writes=[psr[ob]])
                P.op("act", lambda e: e.activation(out=sigo[:, c, :], in_=bank(ob)[:, 0:256], func=AF.Sigmoid),
                     reads=[psr[ob]], writes=[sigo_r[c]])
            units.append(u_gate)
        return units

    def loop_steps(hd):
        sl = hd % 2
        qT, kT, qk_r, sigo, sigo_r = qT2[sl], kT2[sl], qk2_r[sl], sigo2[sl], sigo2_r[sl]
        wv, wv_r = wq[3 + sl], wq_r[3 + sl]
        qkr = [qk_r[0][0], qk_r[0][1], qk_r[1][0], qk_r[1][1]]
        def chunk_vars(c):
            p = c % 2
            q3 = c % 3
            return dict(cs=slice(c * 128, (c + 1) * 128), col=c * 4 + hd, ktok=ktok2[p], ktok_r=ktok2_r[p], vaug=vaug2[p],
                        vaug_r=vaug2_r[p], vp=vp2[p], vp_r=vp2_r[p], wT=wT2[p], wT_r=wT2_r[p], hu=hu3[q3], hu_r=hu3_r[q3],
                        sm=sm3[q3], sm_r=sm3_r[q3], junk=junk3[q3], mlb=mlb3[q3], mlb_r=mlb3_r[q3], ab=p, db=2 + p, tb=p)

        def stageA1(c):
            v_ = chunk_vars(c)
            cs, col, ktok, ktok_r, vaug, vaug_r, vp, vp_r, wT, wT_r, ab, db = (v_[k] for k in (
                "cs", "col", "ktok", "ktok_r", "vaug", "vaug_r", "vp", "vp_r", "wT", "wT_r", "ab", "db"))
            abf = bank(ab).bitcast(BF16)
            for k in range(KC):
                P.op("pe", lambda e, k=k: e.matmul(bank(ab)[:, 0:256], lhsT=hT[:, k, cs], rhs=wv[:, k, :],
                                                   start=(k == 0), stop=(k == KC - 1)),
                     reads=[wv_r[k // 4]], writes=[psr[ab]])
            for dk in range(2):
                P.op("pe", lambda e, dk=dk: e.transpose(out=abf[:, 512 + dk * 128:512 + (dk + 1) * 128],
                                                        in_=kT[:, dk, cs], identity=C.ident),
                     reads=[qk_r[1][dk], C.cst_r], writes=[psr[ab]])
            P.op("act", lambda e: e.activation(out=vaug[:, 0:256], in_=bank(ab)[:, 0:256], func=AF.Copy),
                 reads=[psr[ab]], writes=[vaug_r])
            P.op("act", lambda e: e.activation(out=ktok, in_=abf[:, 512:768], func=AF.Copy),
                 reads=[psr[ab]], writes=[ktok_r])
            for dk in range(2):
                P.op("pe", lambda e, dk=dk: e.matmul(bank(db)[:, 384:512], lhsT=kT[:, dk, cs], rhs=qT[:, dk, cs],
                                                     start=(dk == 0), stop=(dk == 1)),
                     reads=qkr, writes=[psr[db]])
            P.op("dve", lambda e: e.scalar_tensor_tensor(
                out=wT, in0=bank(db)[:, 384:512], scalar=e_in[:, col:col + 1], in1=C.tri, op0=ALU.mult, op1=ALU.mult),
                reads=[psr[db], ee_r, C.cst_r], writes=[wT_r])
            if c < NT - 1:
                P.op("dve", lambda e: e.tensor_scalar(out=vp, in0=vaug, scalar1=e_v[:, col:col + 1], scalar2=None, op0=ALU.mult),
                     reads=[vaug_r, ee_r], writes=[vp_r])

        def stageA2(c):
            v_ = chunk_vars(c)
            cs, col, ktok, ktok_r, vaug, vaug_r, vp, vp_r, wT, wT_r, db = (v_[k] for k in (
                "cs", "col", "ktok", "ktok_r", "vaug", "vaug_r", "vp", "vp_r", "wT", "wT_r", "db"))
            if c < NT - 1:
                for dk in range(2):
                    P.op("pe", lambda e, dk=dk: e.matmul(bank(4 + dk)[:, 0:258], lhsT=ktok[:, dk * 128:(dk + 1) * 128], rhs=vp,
                                                         start=True, stop=True),
                         reads=[ktok_r, vp_r], writes=[psr[4 + dk]])
            P.op("pe", lambda e: e.matmul(bank(db)[:, 0:258], lhsT=wT, rhs=vaug, start=True, stop=(c == 0)),
                 reads=[wT_r, vaug_r], writes=[psr[db]])
            if c > 0:
                cprev, cprev_r = Cbf2[(c - 1) % 2], Cbf2_r[(c - 1) % 2]
                for dk in range(2):
                    P.op("pe", lambda e, dk=dk: e.matmul(bank(db)[:, 0:258], lhsT=qT[:, dk, cs], rhs=cprev[:, dk, :],
                                                         start=False, stop=(dk == 1)),
                         reads=qkr + [cprev_r], writes=[psr[db]])
            if c < NT - 1:
                dC = PS[:, 2048:3072].rearrange("p (a b) -> p a b", a=2)[:, :, 0:258]
                if c == 0:
                    P.op("dve", lambda e: e.tensor_copy(out=Cf, in_=dC), reads=[psr[4], psr[5]], writes=[Cf_r])
                else:
                    P.op("dve", lambda e: e.scalar_tensor_tensor(out=Cf, in0=Cf, scalar=e_L[:, col:col + 1], in1=dC,
                                                                 op0=ALU.mult, op1=ALU.add),
                         reads=[psr[4], psr[5], Cf_r, ee_r], writes=[Cf_r])
                ccur, ccur_r = Cbf2[c % 2], Cbf2_r[c % 2]
                P.op("act", lambda e: e.activation(out=ccur, in_=Cf, func=AF.Copy), reads=[Cf_r], writes=[ccur_r])

        def stageB(c):
            v_ = chunk_vars(c)
            col, hu, hu_r, sm, sm_r, junk, db = (v_[k] for k in ("col", "hu", "hu_r", "sm", "sm_r", "junk", "db"))
            P.op("dve", lambda e, col=col, sm=sm, db=db: e.tensor_scalar(out=sm[:, 0:1], in0=bank(db)[:, 256:257],
                                                                       scalar1=e_io[:, col:col + 1], scalar2=None, op0=ALU.max),
                 reads=[psr[db], ee_r], writes=[sm_r])
            P.op("dve", lambda e, sm=sm, db=db: e.scalar_tensor_tensor(out=sm[:, 1:2], in0=bank(db)[:, 256:257], scalar=-1.0,
                                                                     in1=sm[:, 0:1], op0=ALU.mult, op1=ALU.max),
                 reads=[psr[db], sm_r], writes=[sm_r])
            P.op("dve", lambda e, sm=sm: e.reciprocal(out=sm[:, 3:4], in_=sm[:, 1:2]), reads=[sm_r], writes=[sm_r])
            P.op("dve", lambda e, sm=sm, hu=hu, db=db: e.tensor_scalar(out=hu, in0=bank(db)[:, 0:256], scalar1=sm[:, 3:4], scalar2=None,
                                                                     op0=ALU.mult),
                 reads=[psr[db], sm_r], writes=[hu_r])
            P.op("act", lambda e, sm=sm, hu=hu, junk=junk: e.activation(out=junk, in_=hu, func=AF.Square, accum_out=sm[:, 4:5]),
                 reads=[hu_r, sm_r], writes=[sm_r])
            P.op("pool", lambda e, sm=sm: e.tensor_scalar(out=sm[:, 5:6], in0=sm[:, 4:5], scalar1=1.0 / 256, scalar2=EPS,
                                                          op0=ALU.mult, op1=ALU.add), reads=[sm_r], writes=[sm_r])
            P.op("pool", lambda e, sm=sm: e.tensor_tensor(out=sm[:, 5:6], in0=sm[:, 5:6], in1=neghalf, op=ALU.pow),
                 reads=[sm_r, nh_r], writes=[sm_r])

        def stageC(c):
            v_ = chunk_vars(c)
            cs, hu, hu_r, sm, sm_r, mlb, mlb_r, tb = (v_[k] for k in ("cs", "hu", "hu_r", "sm", "sm_r", "mlb", "mlb_r", "tb"))
            abf = bank(tb).bitcast(BF16)
            ab = tb
            P.op("dve", lambda e, hd=hd, sm=sm, hu=hu: e.scalar_tensor_tensor(out=hu, in0=hu, scalar=sm[:, 5:6],
                                                                              in1=hg[:, hd * 256:(hd + 1) * 256],
                                                                              op0=ALU.mult, op1=ALU.mult),
                 reads=[hu_r, sm_r, hg_r], writes=[hu_r])
            P.op("dve", lambda e, c=c, hu=hu, mlb=mlb: e.tensor_tensor(out=mlb, in0=hu, in1=sigo[:, c, :], op=ALU.mult),
                 reads=[hu_r, sigo_r[c]], writes=[mlb_r])
            for j in range(2):
                P.op("pe", lambda e, j=j, abf=abf, mlb=mlb: e.transpose(out=abf[:, 768 + j * 128:768 + (j + 1) * 128],
                                                                       in_=mlb[:, j * 128:(j + 1) * 128], identity=C.ident),
                     reads=[mlb_r, C.cst_r], writes=[psr[ab]])
            P.op("act", lambda e, hd=hd, cs=cs, abf=abf: e.activation(
                out=mlT[:, 2 * hd:2 * hd + 2, cs], in_=abf[:, 768:1024].rearrange("p (j n) -> p j n", j=2), func=AF.Copy),
                reads=[psr[ab]], writes=[C.mlT_r])


        steps = []
        for step in range(NT + 2):
            def st(fill, step=step):
                if step == 0:
                    stageA1(0)
                if step + 1 < NT:
                    stageA1(step + 1)
                fill()
                if step < NT:
                    stageA2(step)
                fill()
                if 0 <= step - 1 < NT:
                    stageB(step - 1)
                fill()
                if 0 <= step - 2 < NT:
                    stageC(step - 2)
            steps.append(st)
        return steps

    for u in pre_units(0):
        u()
    NH = int(os.environ.get("NH", 4))
    for hd in range(NH):
        nxt = pre_units(hd + 1) if hd < NH - 1 else []
        steps = loop_steps(hd) if not os.environ.get("NOLOOP") else []
        if not steps:
            for u in nxt:
                u()
            continue
        nfill = 3 * len(steps)
        per = -(-len(nxt) // nfill) if nxt else 0
        pos = [0]

        def fill():
            for u in nxt[pos[0]:pos[0] + per]:
                u()
            pos[0] += per
        for st in steps:
            st(fill)
        for u in nxt[pos[0]:]:
            u()

def merge_phase(C, A, hT, attT, mlT, w_in, w_a, w_m, w_out, g_post, x_src, x_dst):
    P, PS, psr = C.P, C.PS, C.psr
    OGA, OGM = 8712, 9736

    def bank(b):
        return PS[:, 512 * b:512 * (b + 1)]

    mgT = A.alloc((KC, S), BF16)
    wa = A.alloc((4, 1024), BF16)
    wa_r = [Reg() for _ in range(4)]
    wm = A.alloc((KC, 1024), BF16)
    wm_r = [Reg() for _ in range(KC)]
    wo = A.alloc((KC, 1024), BF16)
    wo_r = [Reg() for _ in range(KC)]
    wg = [[A.alloc((KC, 128), BF16) for _ in range(2)] for _ in range(2)]
    wg_r = [[Reg() for _ in range(2)] for _ in range(2)]
    load_cast(C, wg[0][0], w_in[:, OGA:OGA + 128].rearrange("(k p) n -> p k n", p=128), wg_r[0][0])
    load_cast(C, wg[0][1], w_in[:, OGM:OGM + 128].rearrange("(k p) n -> p k n", p=128), wg_r[0][1])
    for k in range(4):
        load_cast(C, wa[:, k, :], w_a[k * 128:(k + 1) * 128, :], wa_r[k])
    for k in range(KC):
        load_cast(C, wm[:, k, :], w_m[k * 128:(k + 1) * 128, :], wm_r[k])
    ga = [A.alloc((512,), F32) for _ in range(2)]
    ga_r = [Reg() for _ in range(2)]
    gm = [A.alloc((512,), F32) for _ in range(2)]
    gm_r = [Reg() for _ in range(2)]
    ta = [A.alloc((512,), F32) for _ in range(2)]
    ta_r = [Reg() for _ in range(2)]
    gb = A.alloc((D,), F32)
    gb_r = Reg()
    P.dma("sp", gb, g_post.partition_broadcast(128), "g", writes=[gb_r])
    j = 0
    for mc in range(KC):
        sl = mc % 2
        if mc > 0:
            load_cast(C, wg[sl][0], w_in[:, OGA + mc * 128:OGA + (mc + 1) * 128].rearrange("(k p) n -> p k n", p=128), wg_r[sl][0])
            load_cast(C, wg[sl][1], w_in[:, OGM + mc * 128:OGM + (mc + 1) * 128].rearrange("(k p) n -> p k n", p=128), wg_r[sl][1])
        if mc == 0:
            for k in range(KC):
                load_cast(C, wo[:, k, :], w_out[k * 128:(k + 1) * 128, :], wo_r[k])
        for tg in range(4):
            q = j % 2
            j += 1
            ts = slice(tg * 512, (tg + 1) * 512)
            b0 = 4 * q
            for gi_, (gbuf, gr) in enumerate(((ga, ga_r), (gm, gm_r))):
                for k in range(KC):
                    P.op("pe", lambda e, k=k, gi_=gi_, sl=sl, ts=ts, b0=b0: e.matmul(
                        bank(b0 + gi_), lhsT=wg[sl][gi_][:, k, :], rhs=hT[:, k, ts], start=(k == 0), stop=(k == KC - 1)),
                        reads=[wg_r[sl][gi_]], writes=[psr[b0 + gi_]])
                P.op("act", lambda e, gbuf=gbuf, q=q, gi_=gi_, b0=b0: e.activation(out=gbuf[q], in_=bank(b0 + gi_), func=AF.Sigmoid),
                     reads=[psr[b0 + gi_]], writes=[gr[q]])
            for k in range(4):
                P.op("pe", lambda e, k=k, mc=mc, ts=ts, b0=b0: e.matmul(
                    bank(b0 + 2), lhsT=wa[:, k, mc * 128:(mc + 1) * 128], rhs=attT[:, k, ts], start=(k == 0), stop=(k == 3)),
                    reads=[wa_r[k], C.attT_r], writes=[psr[b0 + 2]])
            for k in range(KC):
                P.op("pe", lambda e, k=k, mc=mc, ts=ts, b0=b0: e.matmul(
                    bank(b0 + 3), lhsT=wm[:, k, mc * 128:(mc + 1) * 128], rhs=mlT[:, k, ts], start=(k == 0), stop=(k == KC - 1)),
                    reads=[wm_r[k], C.mlT_r], writes=[psr[b0 + 3]])
            P.op("dve", lambda e, q=q, b0=b0: e.tensor_tensor(out=ta[q], in0=bank(b0 + 2), in1=ga[q], op=ALU.mult),
                 reads=[psr[b0 + 2], ga_r[q]], writes=[ta_r[q]])
            P.op("dve", lambda e, q=q, b0=b0: e.tensor_tensor(out=gm[q], in0=bank(b0 + 3), in1=gm[q], op=ALU.mult),
                 reads=[psr[b0 + 3], gm_r[q]], writes=[gm_r[q]])
            P.op("dve", lambda e, q=q, mc=mc, ts=ts: e.tensor_tensor(out=mgT[:, mc, ts], in0=ta[q], in1=gm[q], op=ALU.add),
                 reads=[ta_r[q], gm_r[q]], writes=[C.mg_r])
    xc = [A.alloc((D,), F32) for _ in range(2)]
    tt = [A.alloc((D,), F32) for _ in range(2)]
    xc_r = [Reg() for _ in range(2)]
    tth_r = [[Reg(), Reg()] for _ in range(2)]
    ss2 = A.alloc((NT,), F32)
    r2 = A.alloc((NT,), F32)
    junk = ga[0].bitcast(BF16)
    junk_r = ga_r[0]
    r2_r = [Reg() for _ in range(NT)]
    for i in range(NT):
        s = i % 2
        P.dma("sp", xc[s], x_src[i * 128:(i + 1) * 128, :], f"xc{s}", writes=[xc_r[s]])
        pb = (i % 4) * 2
        psf = PS[:, 512 * pb:512 * (pb + 2)]
        for h in range(2):
            for k in range(KC):
                P.op("pe", lambda e, h=h, k=k, i=i, pb=pb: e.matmul(
                    bank(pb + h), lhsT=mgT[:, k, i * 128:(i + 1) * 128], rhs=wo[:, k, h * 512:(h + 1) * 512],
                    start=(k == 0), stop=(k == KC - 1)), reads=[wo_r[k], C.mg_r], writes=[psr[pb + h]])
        P.op("act", lambda e, i=i, psf=psf: e.activation(out=junk, in_=psf, func=AF.Square, accum_out=ss2[:, i:i + 1]),
             reads=[psr[pb], psr[pb + 1]], writes=[r2_r[i], junk_r])
        P.op("act", lambda e, i=i: e.activation(out=r2[:, i:i + 1], in_=ss2[:, i:i + 1], func=AF.Ln, scale=1.0 / D, bias=EPS),
             reads=[r2_r[i]], writes=[r2_r[i]])
        P.op("act", lambda e, i=i: e.activation(out=r2[:, i:i + 1], in_=r2[:, i:i + 1], func=AF.Exp, scale=-0.5),
             reads=[r2_r[i]], writes=[r2_r[i]])
        for h in range(2):
            P.op("dve", lambda e, s=s, h=h, pb=pb: e.tensor_tensor(
                out=tt[s][:, 512 * h:512 * (h + 1)], in0=bank(pb + h), in1=gb[:, 512 * h:512 * (h + 1)], op=ALU.mult),
                reads=[psr[pb + h], gb_r, r2_r[i]], writes=[tth_r[s][h]])
        P.op("dve", lambda e, s=s, i=i: e.scalar_tensor_tensor(out=xc[s], in0=tt[s], scalar=r2[:, i:i + 1], in1=xc[s],
                                                              op0=ALU.mult, op1=ALU.add),
             reads=[tth_r[s][0], tth_r[s][1], r2_r[i], xc_r[s]], writes=[xc_r[s]])
        P.dma("sp", x_dst[i * 128:(i + 1) * 128, :], xc[s], f"xo{s}", reads=[xc_r[s]])


def mixer_phase(C, x_src, x_dst, W):
    A = C.A.child()
    hT = A.alloc((KC, S), BF16)
    attT = A.alloc((4, S), BF16)
    mlT = A.alloc((8, S), BF16)
    C.attT_r = Reg()
    C.mlT_r = Reg()
    C.mg_r = Reg()
    base = A.off
    end = A.end
    norm_transpose(C, Arena(A.ap, base, end), x_src, W["mix_pre_g"][0], hT)
    attention_phase(C, Arena(A.ap, base, end), hT, attT, W["w_in"][0])
    C.P.barrier()
    mlstm_phase(C, Arena(A.ap, base, end), hT, mlT, W["w_in"][0], W["conv_w"][0], W["conv_b"][0],
                W["mlstm_i_bias"][0], W["mlstm_f_bias"][0], W["mlstm_head_g"][0])
    C.P.barrier()
    merge_phase(C, Arena(A.ap, base, end), hT, attT, mlT, W["w_in"][0], W["w_att_branch"][0], W["w_mlstm_branch"][0],
                W["w_out"][0], W["mix_post_g"][0], x_src, x_dst)
    C.P.barrier()

def host_consts():
    c = {}
    bf = ml_dtypes.bfloat16
    c["ident"] = np.eye(128, dtype=np.float32).astype(bf)
    half = 16
    inv_freq = np.power(np.float32(500000.0), -(np.arange(half, dtype=np.float32) * 2.0 / 32)).astype(np.float32)
    ang = np.arange(S, dtype=np.float32)[None, :] * inv_freq[:, None]
    c["cos"] = np.concatenate([np.cos(ang), np.cos(ang)], 0).astype(np.float32)
    c["sin"] = np.concatenate([np.sin(ang), np.sin(ang)], 0).astype(np.float32)
    rm = np.zeros((32, 32), np.float32)
    for j in range(16):
        rm[16 + j, j] = -1.0
        rm[j, 16 + j] = 1.0
    c["rm"] = rm.astype(bf)
    jj = np.arange(128)[:, None]
    ii = np.arange(128)[None, :]
    NEG = -30000.0
    c["maskc"] = np.where(jj <= ii, 0.0, NEG).astype(bf)
    c["maskp"] = np.where(jj >= ii, 0.0, NEG).astype(bf)
    c["maskn"] = np.full((128, 128), NEG, np.float32).astype(bf)
    c["tri"] = (jj <= ii).astype(np.float32)
    c["ones_f"] = np.ones((128, 128), np.float32)
    c["identf"] = np.eye(128, dtype=np.float32)
    c["ones_bf"] = np.ones((128, 128), np.float32).astype(bf)
    return c


def build(stage="full"):
    nc = bass.Bass("TRN2", target_bir_lowering=False)

    def din(name, shape, dt=F32):
        return nc.dram_tensor(name, list(shape), dt, kind="ExternalInput").ap()

    x = din("x", [S, D])
    W = {}
    for name, shape in [("ffn1_pre_g", [1, D]), ("ffn1_w_gate", [1, D, FF]), ("ffn1_w_up", [1, D, FF]),
                        ("ffn1_w_down", [1, FF, D]), ("ffn1_post_g", [1, D]), ("mix_pre_g", [1, D]),
                        ("w_in", [1, D, IN_W]), ("conv_w", [1, 4, 2048]), ("conv_b", [1, 2048]),
                        ("mlstm_i_bias", [1, 4]), ("mlstm_f_bias", [1, 4]), ("mlstm_head_g", [1, 1024]),
                        ("w_att_branch", [1, 512, D]), ("w_mlstm_branch", [1, D, D]), ("w_out", [1, D, D]),
                        ("mix_post_g", [1, D]), ("ffn2_pre_g", [1, D]), ("ffn2_w_gate", [1, D, FF]),
                        ("ffn2_w_up", [1, D, FF]), ("ffn2_w_down", [1, FF, D]), ("ffn2_post_g", [1, D])]:
        W[name] = din(name, shape)
    CD = {}
    for name, shape, dt in [("c_ident", [128, 128], BF16), ("c_cos", [32, S], F32), ("c_sin", [32, S], F32),
                            ("c_rm", [32, 32], BF16), ("c_maskc", [128, 128], BF16), ("c_maskp", [128, 128], BF16),
                            ("c_maskn", [128, 128], BF16), ("c_tri", [128, 128], F32), ("c_ones_f", [128, 128], F32), ("c_identf", [128, 128], F32),
                            ("c_ones_bf", [128, 128], BF16)]:
        CD[name] = din(name, shape, dt)
    c_ident = CD["c_ident"]
    out = nc.dram_tensor("out", [S, D], F32, kind="ExternalOutput").ap()
    x1 = nc.dram_tensor("x1", [S, D], F32, kind="Internal").ap()
    x2 = nc.dram_tensor("x2", [S, D], F32, kind="Internal").ap()

    with ExitStack() as st:
        ARENA_BYTES = 212480
        arena = st.enter_context(nc.sbuf_tensor("arena", [128, ARENA_BYTES // 2], BF16))
        PS = st.enter_context(nc.psum_tensor("ps", [128, 4096], F32))
        C = Ctx()
        C.nc = nc
        C.P = P = Prog(nc)
        C.PS = PS
        C.psr = [Reg(f"ps{i}", excl=True) for i in range(8)]
        top = Arena(arena, 0, ARENA_BYTES)
        C.ident = top.alloc((128,), BF16)
        C.ident_r = Reg()
        P.dma("sp", C.ident, c_ident, "cst", writes=[C.ident_r])
        C.dram = CD
        C.cst_r = C.ident_r
        for nm, shp, dt in [("rm", (32,), BF16), ("maskc", (128,), BF16), ("maskp", (128,), BF16), ("maskn", (128,), BF16),
                            ("tri", (128,), F32), ("ones_f", (128,), F32), ("identf", (128,), F32), ("ones_bf", (128,), BF16)]:
            v = top.alloc(shp, dt)
            if nm == "rm":
                v = v[:32]
            setattr(C, nm, v)
            P.dma("sp", v, CD["c_" + nm], "cst", writes=[C.cst_r])
        C.stage = [top.alloc((1024,), F32) for _ in range(NST)]
        C.stage_r = [Reg() for _ in range(NST)]
        C.stage_i = 0
        C.A = Arena(arena, top.off, ARENA_BYTES)

        if stage == "attn":
            A = C.A.child()
            hT = A.alloc((KC, S), BF16)
            attT = A.alloc((4, S), BF16)
            C.attT_r = Reg()
            mk = Arena(arena, A.off, ARENA_BYTES)
            norm_transpose(C, mk, x, W["mix_pre_g"][0], hT)
            attention_phase(C, Arena(arena, A.off, ARENA_BYTES), hT, attT, W["w_in"][0])
            P.barrier()
            ov = out.rearrange("(a b) d -> a (b d)", a=1024).rearrange("(k p) t -> p k t", p=128)
            for k in range(4):
                for hh in range(2):
                    P.dma("pool", ov[:, k, hh * 1024:(hh + 1) * 1024], attT[:, k, hh * 1024:(hh + 1) * 1024], "dbg")
        if stage == "full":
            ffn_phase(C, x, x1, W["ffn1_pre_g"][0], W["ffn1_w_gate"][0], W["ffn1_w_up"][0], W["ffn1_w_down"][0],
                      W["ffn1_post_g"][0])
            mixer_phase(C, x1, x2, W)
            ffn_phase(C, x2, out, W["ffn2_pre_g"][0], W["ffn2_w_gate"][0], W["ffn2_w_up"][0], W["ffn2_w_down"][0],
                      W["ffn2_post_g"][0])
        if stage == "mix":
            mixer_phase(C, x, out, W)
        if stage == "ml":
            A = C.A.child()
            hT = A.alloc((KC, S), BF16)
            mlT = A.alloc((8, S), BF16)
            C.mlT_r = Reg()
            mk = Arena(arena, A.off, ARENA_BYTES)
            norm_transpose(C, mk, x, W["mix_pre_g"][0], hT)
            mlstm_phase(C, Arena(arena, A.off, ARENA_BYTES), hT, mlT, W["w_in"][0], W["conv_w"][0], W["conv_b"][0],
                        W["mlstm_i_bias"][0], W["mlstm_f_bias"][0], W["mlstm_head_g"][0])
            P.barrier()
            ov = out.rearrange("(a b) d -> a (b d)", a=1024).rearrange("(k p) t -> p k t", p=128)
            for k in range(8):
                for hh in range(2):
                    P.dma("pool", ov[:, k, hh * 1024:(hh + 1) * 1024], mlT[:, k, hh * 1024:(hh + 1) * 1024], "dbg")
        if stage == "ffn1a":
            A = C.A.child()
            hT_ar = A.sub(KC * S * 2)
            actT_ar = A.sub(FC * S * 2)
            hT = hT_ar.child().alloc((KC, S), BF16)
            norm_transpose(C, actT_ar.child(), x, W["ffn1_pre_g"][0], hT)
            ov = out.rearrange("(a b) d -> a (b d)", a=1024).rearrange("(k p) t -> p k t", p=128)
            for k in range(KC):
                for hh in range(2):
                    P.dma("pool", ov[:, k, hh * 1024:(hh + 1) * 1024], hT[:, k, hh * 1024:(hh + 1) * 1024], "dbg")
        if stage == "ffn1b":
            ffn_phase(C, x, out, W["ffn1_pre_g"][0], W["ffn1_w_gate"][0], W["ffn1_w_up"][0], W["ffn1_w_down"][0],
                      W["ffn1_post_g"][0], stop_after="B")
        if stage == "ffn1":
            ffn_phase(C, x, out, W["ffn1_pre_g"][0], W["ffn1_w_gate"][0], W["ffn1_w_up"][0], W["ffn1_w_down"][0],
                      W["ffn1_post_g"][0])
        P.finish()
        P.emit()
        print("prog stats", P.stats, "sems", len(P.dma_tot) + 5)
        if os.environ.get("DUMP"):
            for e in ("sp", "dve", "act"):
                print("====", e)
                for r in P.dump[e][-int(os.environ["DUMP"]):]:
                    print(r)
    return nc


_NC_CACHE = {}


def kernel(**inputs):
    stage = inputs.pop("_stage", os.environ.get("KSTAGE", "full"))
    if stage not in _NC_CACHE:
        _NC_CACHE[stage] = build(stage)
    nc = _NC_CACHE[stage]
    consts = host_consts()
    xfull = np.ascontiguousarray(inputs["x"], dtype=np.float32)
    shared = {k: np.ascontiguousarray(v, dtype=np.float32) for k, v in inputs.items() if k != "x"}
    for k, v in consts.items():
        shared["c_" + k] = v
    in_maps = []
    ncores = int(os.environ.get("NCORES", 8))
    for b in range(ncores):
        m = dict(shared)
        m["x"] = xfull[b]
        in_maps.append(m)
    res = run_bass_kernel_spmd(nc, in_maps, core_ids=list(range(ncores)))
    return np.stack([r["out"] for r in res.results], axis=0)
```

```python
from contextlib import ExitStack
import math
import os

import numpy as np
import ml_dtypes
import concourse.bass as bass
import concourse.mybir as mybir
from concourse.bass_utils import run_bass_kernel_spmd

F32 = mybir.dt.float32
BF16 = mybir.dt.bfloat16
AF = mybir.ActivationFunctionType
ALU = mybir.AluOpType
AX = mybir.AxisListType

S = 2048
D = 1024
FF = 2816
NT = S // 128
KC = D // 128
FC = FF // 128
IN_W = 10760
EPS = 1e-6
ENGS = ("pe", "act", "dve", "pool", "sp")


class Reg:
    __slots__ = ("name", "w", "rs", "rd", "excl")

    def __init__(self, name="", excl=False):
        self.name = name
        self.excl = excl
        self.w = None
        self.rs = {}
        self.rd = []


class Ins:
    __slots__ = ("eng", "fn", "deps", "signal", "val", "dma", "key")

    def __init__(self, eng, fn, dma=False, key=None):
        self.eng = eng
        self.fn = fn
        self.deps = ()
        self.signal = dma
        self.val = 0
        self.dma = dma
        self.key = key


class Prog:
    def __init__(self, nc):
        self.nc = nc
        self.engs = {e: [] for e in ENGS}
        self.dma_tot = {}
        self.dma_last = {}

    def _add(self, ins, reads, writes):
        eng = ins.eng
        deps = {}
        for r in reads:
            d = r.w
            if d is not None:
                deps[id(d)] = d
            if r.excl:
                for e2, x in r.rs.items():
                    if e2 != eng:
                        deps[id(x)] = x
        for w in writes:
            d = w.w
            if d is not None:
                deps[id(d)] = d
            for e2, x in w.rs.items():
                if (not ins.dma) and e2 == eng:
                    continue
                deps[id(x)] = x
            for x in w.rd:
                deps[id(x)] = x
        out = []
        for d in deps.values():
            if d is ins:
                continue
            if (not d.dma) and (not ins.dma) and d.eng == "pe" and eng == "pe":
                continue
            d.signal = True
            out.append(d)
        ins.deps = out
        for r in reads:
            if ins.dma:
                r.rd.append(ins)
            else:
                r.rs[eng] = ins
        for w in writes:
            w.w = ins
            w.rs = {}
            w.rd = []
        self.engs[eng].append(ins)
        return ins

    def op(self, eng, fn, reads=(), writes=()):
        return self._add(Ins(eng, fn), reads, writes)

    def dma(self, queue, out, in_, key, reads=(), writes=(), **kw):
        ins = Ins(queue, lambda e: e.dma_start(out=out, in_=in_, **kw), dma=True, key=key)
        self.dma_tot[key] = self.dma_tot.get(key, 0) + 16
        ins.val = self.dma_tot[key]
        self.dma_last[key] = ins
        return self._add(ins, reads, writes)

    def barrier(self):
        lasts = []
        for e in ENGS:
            for ins in reversed(self.engs[e]):
                if ins.fn is not None and not ins.dma:
                    ins.signal = True
                    lasts.append(ins)
                    break
        lasts += list(self.dma_last.values())
        for e in ENGS:
            ins = Ins(e, None)
            ins.deps = [d for d in lasts if d.dma or d.eng != e]
            self.engs[e].append(ins)

    def finish(self):
        ins = Ins("sp", None)
        ins.deps = list(self.dma_last.values())
        self.engs["sp"].append(ins)

    def emit(self):
        nc = self.nc
        for e in ENGS:
            c = 0
            for ins in self.engs[e]:
                if ins.dma:
                    continue
                if ins.signal and ins.fn is not None:
                    c += 1
                    ins.val = c
        with ExitStack() as st:
            sems = {e: st.enter_context(nc.semaphore(f"s_{e}")) for e in ENGS}
            dsem = {k: st.enter_context(nc.semaphore(f"d_{k}")) for k in self.dma_tot}
            block = st.enter_context(nc.Block())
            bname = {"pe": "tensor", "act": "scalar", "dve": "vector", "pool": "gpsimd", "sp": "sync"}
            stats = {}
            self.dump = {}
            for e in ENGS:
                def body(engine, e=e):
                    seen = {}
                    nw = 0
                    for ins in self.engs[e]:
                        need = {}
                        for d in ins.deps:
                            s = ("d", d.key) if d.dma else ("c", d.eng)
                            if need.get(s, 0) < d.val:
                                need[s] = d.val
                        for s, v in need.items():
                            if seen.get(s, 0) < v:
                                seen[s] = v
                                sh = dsem[s[1]] if s[0] == "d" else sems[s[1]]
                                engine.wait_ge(sh, v)
                                nw += 1
                        if os.environ.get("DUMP"):
                            self.dump.setdefault(e, []).append((sorted((k, v) for k, v in need.items()), ins.fn is not None, ins.dma, ins.key, ins.signal, ins.val))
                        if ins.fn is not None:
                            bi = ins.fn(engine)
                            if ins.dma:
                                bi.then_inc(dsem[ins.key], 16)
                            elif ins.signal:
                                bi.then_inc(sems[e], 1)
                    stats[e] = (len(self.engs[e]), nw)
                getattr(block, bname[e])(body)
            self.stats = stats


class Arena:
    def __init__(self, ap, start, end):
        self.ap = ap
        self.start = start
        self.off = start
        self.end = end

    def alloc(self, free_shape, dt, parts=128):
        n = 1
        for v in free_shape:
            n *= v
        esz = 4 if dt == F32 else 2
        nbytes = n * esz
        st = (self.off + 63) // 64 * 64
        assert st + nbytes <= self.end, f"arena overflow: need {st + nbytes} > {self.end}"
        self.off = st + nbytes
        Arena.last = (st, tuple(free_shape), dt)
        v = self.ap[:parts, st // 2:(st + nbytes) // 2]
        if dt == F32:
            v = v.bitcast(F32)
        if len(free_shape) == 2:
            v = v.rearrange("p (a b) -> p a b", a=free_shape[0])
        elif len(free_shape) == 3:
            v = v.rearrange("p (a b c) -> p a b c", a=free_shape[0], b=free_shape[1])
        return v

    def sub(self, nbytes):
        st = (self.off + 63) // 64 * 64
        assert st + nbytes <= self.end, f"arena overflow(sub): need {st + nbytes} > {self.end}"
        self.off = st + nbytes
        return Arena(self.ap, st, st + nbytes)

    def child(self):
        return Arena(self.ap, self.start, self.end)


class Ctx:
    pass


DBG = {}


NST = 3


def load_cast(C, dst, src, dst_reg):
    P = C.P
    s = C.stage_i % NST
    C.stage_i += 1
    sh = src.shape
    n = 1
    for v in sh[1:]:
        n *= v
    assert n <= 1024
    stg = C.stage[s][:, :n]
    if len(sh) == 3:
        stg = stg.rearrange("p (a b) -> p a b", a=sh[1])
    P.dma("sp", stg, src, f"st{s}", writes=[C.stage_r[s]])
    P.op("pool", lambda e: e.tensor_copy(out=dst, in_=stg), reads=[C.stage_r[s]], writes=[dst_reg])


def norm_transpose(C, A_stage, x_src, g_pre_dram, hT):
    P, PS, psr = C.P, C.PS, C.psr
    gb = A_stage.alloc((D,), F32)
    gb_r = Reg()
    P.dma("sp", gb, g_pre_dram.partition_broadcast(128), "g", writes=[gb_r])
    ss = A_stage.alloc((NT,), F32)
    rstd = A_stage.alloc((NT,), F32)
    junk = A_stage.alloc((D,), BF16)
    junk_r = Reg()
    hb = [A_stage.alloc((D,), BF16) for _ in range(2)]
    hb_r = [Reg() for _ in range(2)]
    xs = [A_stage.alloc((D,), F32) for _ in range(NT)]
    xs_r = [Reg() for _ in range(NT)]
    ss_r = [Reg() for _ in range(NT)]
    rstd_r = Reg()
    for i in range(NT):
        P.dma("sp", xs[i], x_src[i * 128:(i + 1) * 128, :], f"xs{i}", writes=[xs_r[i]])
    for i in range(NT):
        P.op("act", lambda e, i=i: e.activation(out=junk, in_=xs[i], func=AF.Square, accum_out=ss[:, i:i + 1]),
             reads=[xs_r[i]], writes=[ss_r[i], junk_r])
    P.op("act", lambda e: e.activation(out=rstd, in_=ss, func=AF.Ln, scale=1.0 / D, bias=EPS),
         reads=ss_r, writes=[rstd_r])
    P.op("act", lambda e: e.activation(out=rstd, in_=rstd, func=AF.Exp, scale=-0.5),
         reads=[rstd_r], writes=[rstd_r])
    for i in range(NT):
        s = i % 2
        P.op("dve", lambda e, i=i, s=s: e.scalar_tensor_tensor(out=hb[s], in0=xs[i], scalar=rstd[:, i:i + 1], in1=gb,
                                                              op0=ALU.mult, op1=ALU.mult),
             reads=[xs_r[i], rstd_r, gb_r], writes=[hb_r[s]])
        b = i % 2
        psb = PS[:, 512 * b:512 * (b + 1)].bitcast(BF16)
        for k in range(KC):
            P.op("pe", lambda e, k=k, s=s, psb=psb: e.transpose(out=psb[:, k * 128:(k + 1) * 128],
                                                                in_=hb[s][:, k * 128:(k + 1) * 128], identity=C.ident),
                 reads=[hb_r[s], C.ident_r], writes=[psr[b]])
        P.op("act", lambda e, i=i, psb=psb: e.activation(out=hT[:, :, i * 128:(i + 1) * 128],
                                                         in_=psb.rearrange("p (k n) -> p k n", k=KC), func=AF.Copy),
             reads=[psr[b]], writes=[])
    P.barrier()


def ffn_phase(C, x_src, x_dst, g_pre, wg, wu, wd, g_post, stop_after=None):
    P, PS, psr = C.P, C.PS, C.psr
    A = C.A.child()
    hT_ar = A.sub(KC * S * 2)
    actT_ar = A.sub(FC * S * 2)
    hT = hT_ar.child().alloc((KC, S), BF16)
    actT = actT_ar.child().alloc((FC, S), BF16)
    norm_transpose(C, actT_ar.child(), x_src, g_pre, hT)

    NSL = 2
    wg_s = [A.alloc((KC, 256), BF16) for _ in range(NSL)]
    wu_s = [A.alloc((KC, 256), BF16) for _ in range(NSL)]
    wg_r = [[Reg(), Reg()] for _ in range(NSL)]
    wu_r = [[Reg(), Reg()] for _ in range(NSL)]
    wd_h = [A.alloc((FC, 512), BF16) for _ in range(2)]
    wd_r = [[Reg() for _ in range(FC // 2)] for _ in range(2)]
    sg = [A.alloc((512,), BF16) for _ in range(2)]
    sg_r = [Reg() for _ in range(2)]
    gb = A.alloc((D,), F32)
    gb_r = Reg()
    ss2 = A.alloc((NT,), F32)
    r2 = A.alloc((NT,), F32)
    junk = A.alloc((D,), BF16)
    junk_r = Reg()
    P.dma("sp", gb, g_post.partition_broadcast(128), "g", writes=[gb_r])

    wd_jobs = [(h, c2) for h in range(2) for c2 in range(FC // 2)]

    def load_wd_piece():
        if not wd_jobs:
            return
        h, c2 = wd_jobs.pop(0)
        load_cast(C, wd_h[h][:, 2 * c2:2 * c2 + 2, :],
                  wd[c2 * 256:(c2 + 1) * 256, h * 512:(h + 1) * 512].rearrange("(c p) n -> p c n", p=128),
                  wd_r[h][c2])

    j = 0
    for cb in range(FC // 2):
        s = cb % NSL
        for part in range(2):
            load_cast(C, wg_s[s][:, 4 * part:4 * part + 4, :],
                      wg[part * 512:(part + 1) * 512, cb * 256:(cb + 1) * 256].rearrange("(k p) n -> p k n", p=128),
                      wg_r[s][part])
            load_cast(C, wu_s[s][:, 4 * part:4 * part + 4, :],
                      wu[part * 512:(part + 1) * 512, cb * 256:(cb + 1) * 256].rearrange("(k p) n -> p k n", p=128),
                      wu_r[s][part])
        if cb >= 1:
            for _ in range(3):
                load_wd_piece()
        for sub in range(2):
            ffc = cb * 2 + sub
            for tg in range(4):
                bG = 2 * (j % 4)
                bU = bG + 1
                q = j % 2
                j += 1
                for k in range(KC):
                    P.op("pe", lambda e, k=k, s=s, sub=sub, tg=tg, bG=bG: e.matmul(
                        PS[:, 512 * bG:512 * (bG + 1)], lhsT=wg_s[s][:, k, sub * 128:(sub + 1) * 128],
                        rhs=hT[:, k, tg * 512:(tg + 1) * 512], start=(k == 0), stop=(k == KC - 1)),
                        reads=[wg_r[s][k // 4]], writes=[psr[bG]])
                for k in range(KC):
                    P.op("pe", lambda e, k=k, s=s, sub=sub, tg=tg, bU=bU: e.matmul(
                        PS[:, 512 * bU:512 * (bU + 1)], lhsT=wu_s[s][:, k, sub * 128:(sub + 1) * 128],
                        rhs=hT[:, k, tg * 512:(tg + 1) * 512], start=(k == 0), stop=(k == KC - 1)),
                        reads=[wu_r[s][k // 4]], writes=[psr[bU]])
                P.op("act", lambda e, q=q, bG=bG: e.activation(out=sg[q], in_=PS[:, 512 * bG:512 * (bG + 1)], func=AF.Silu),
                     reads=[psr[bG]], writes=[sg_r[q]])
                P.op("dve", lambda e, q=q, bU=bU, ffc=ffc, tg=tg: e.tensor_tensor(
                    out=actT[:, ffc, tg * 512:(tg + 1) * 512], in0=PS[:, 512 * bU:512 * (bU + 1)], in1=sg[q], op=ALU.mult),
                    reads=[psr[bU], sg_r[q]], writes=[])
    while wd_jobs:
        load_wd_piece()
    P.barrier()
    if stop_after == "B":
        ov = x_dst.rearrange("(a b) d -> a (b d)", a=1024).rearrange("(k p) t -> p k t", p=128)
        for k in range(KC):
            for hh in range(2):
                P.dma("pool", ov[:, k, hh * 1024:(hh + 1) * 1024], actT[:, k + 14, hh * 1024:(hh + 1) * 1024], "dbg")
        return

    Ah = hT_ar.child()
    NSC = int(os.environ.get('NSC', 2))
    NSX = int(os.environ.get('NSX', NSC))
    xc = [Ah.alloc((D,), F32) for _ in range(NSX)]
    tt = [Ah.alloc((D,), F32) for _ in range(NSC)]
    xc_r = [Reg() for _ in range(NSX)]
    tt_r = [Reg() for _ in range(NSC)]
    tth_r = [[Reg(), Reg()] for _ in range(NSC)]
    r2_r = [Reg() for _ in range(NT)]
    for i in range(int(os.environ.get("CT", NT))):
        s = i % NSC
        sx = i % NSX
        P.dma("sp", xc[sx], x_src[i * 128:(i + 1) * 128, :], f"xc{sx}", writes=[xc_r[sx]])
        pb = (i % 4) * 2
        psf = PS[:, 512 * pb:512 * (pb + 2)]
        for h in range(2):
            for ffc in range(FC):
                P.op("pe", lambda e, h=h, ffc=ffc, i=i, pb=pb: e.matmul(
                    PS[:, 512 * (pb + h):512 * (pb + h + 1)], lhsT=actT[:, ffc, i * 128:(i + 1) * 128],
                    rhs=wd_h[h][:, ffc, :], start=(ffc == 0), stop=(ffc == FC - 1)),
                    reads=[wd_r[h][ffc // 2]], writes=[psr[pb + h]])
        P.op("act", lambda e, i=i, psf=psf: e.activation(out=junk, in_=psf, func=AF.Square, accum_out=ss2[:, i:i + 1]),
             reads=[psr[pb], psr[pb + 1]], writes=[r2_r[i], junk_r])
        cstop = int(os.environ.get("CSTOP", 9))
        if cstop == 1:
            P.dma("sp", x_dst[i * 128:(i + 1) * 128, :], xc[sx], f"xo{s}", reads=[xc_r[sx], r2_r[i]])
            continue
        P.op("act", lambda e, i=i: e.activation(out=r2[:, i:i + 1], in_=ss2[:, i:i + 1], func=AF.Ln, scale=1.0 / D, bias=EPS),
             reads=[r2_r[i]], writes=[r2_r[i]])
        P.op("act", lambda e, i=i: e.activation(out=r2[:, i:i + 1], in_=r2[:, i:i + 1], func=AF.Exp, scale=-0.5,
                                                bias=math.log(0.5)),
             reads=[r2_r[i]], writes=[r2_r[i]])
        if cstop == 2:
            P.dma("sp", x_dst[i * 128:(i + 1) * 128, :], xc[sx], f"xo{s}", reads=[xc_r[sx], r2_r[i]])
            continue
        for h in range(2):
            if os.environ.get("DVEVAR") == "copy":
                P.op("dve", lambda e, s=s, h=h, pb=pb: e.tensor_copy(
                    out=tt[s][:, 512 * h:512 * (h + 1)], in_=PS[:, 512 * (pb + h):512 * (pb + h + 1)]),
                    reads=[psr[pb + h], gb_r], writes=[tth_r[s][h]])
                continue
            if os.environ.get("DVEVAR") == "sbuf":
                P.op("dve", lambda e, s=s, h=h, pb=pb: e.tensor_tensor(
                    out=tt[s][:, 512 * h:512 * (h + 1)], in0=xc[sx][:, 512 * h:512 * (h + 1)],
                    in1=gb[:, 512 * h:512 * (h + 1)], op=ALU.mult),
                    reads=[psr[pb + h], gb_r, xc_r[sx]], writes=[tth_r[s][h]])
                continue
            P.op("dve", lambda e, s=s, h=h, pb=pb: e.tensor_tensor(
                out=tt[s][:, 512 * h:512 * (h + 1)], in0=PS[:, 512 * (pb + h):512 * (pb + h + 1)],
                in1=gb[:, 512 * h:512 * (h + 1)], op=ALU.mult),
                reads=[psr[pb + h], gb_r, r2_r[i]], writes=[tth_r[s][h]])
        if cstop == 3:
            P.dma("sp", x_dst[i * 128:(i + 1) * 128, :], tt[s], f"xo{s}", reads=[xc_r[sx], r2_r[i], tth_r[s][0], tth_r[s][1]], writes=[tth_r[s][0], tth_r[s][1]])
            continue
        P.op("dve", lambda e, s=s, sx=sx, i=i: e.scalar_tensor_tensor(out=xc[sx], in0=tt[s], scalar=r2[:, i:i + 1], in1=xc[sx],
                                                              op0=ALU.mult, op1=ALU.add),
             reads=[tth_r[s][0], tth_r[s][1], r2_r[i], xc_r[sx]], writes=[xc_r[sx]])
        P.dma("sp", x_dst[i * 128:(i + 1) * 128, :], xc[sx], f"xo{sx}", reads=[xc_r[sx]])
    P.barrier()


def tok_slice(start, step):
    return slice(start, start + step * 127 + 1, step) if step > 1 else slice(start, start + 128)


def attention_phase(C, A, hT, attT, w_in):
    P, PS, psr = C.P, C.PS, C.psr
    cos = A.alloc((S,), F32)[:32]
    sin = A.alloc((S,), F32)[:32]
    cs_r = Reg()
    cs2_r = Reg()
    P.dma("sp", cos, C.dram["c_cos"], "cs", writes=[cs_r])
    P.dma("sp", sin, C.dram["c_sin"], "cs", writes=[cs2_r])
    wsl = [[A.alloc((KC, 128), BF16) for _ in range(3)] for _ in range(2)]
    wsl_r = [[Reg() for _ in range(3)] for _ in range(2)]
    DBG.clear()
    qk = [[None, None], [None, None]]
    for a_ in range(2):
        for b_ in range(2):
            qk[a_][b_] = A.alloc((S,), BF16)
            DBG[f"qk{a_}{b_}"] = Arena.last
    qk_r = [[[Reg() for _ in range(4)] for _ in range(2)] for _ in range(2)]
    vt = [A.alloc((NT, 128), BF16) for _ in range(2)]
    vt_r = [[Reg() for _ in range(4)] for _ in range(2)]
    acc_n = A.alloc((S,), F32)
    acc_d = A.alloc((S,), F32)
    accn_r = [Reg() for _ in range(4)]
    accd_r = [Reg() for _ in range(4)]
    acc_all = Reg()
    pT = [A.alloc((512,), BF16) for _ in range(4)]
    pT_r = [Reg() for _ in range(4)]
    t1 = [A.alloc((512,), F32)[:32] for _ in range(2)]
    t2 = [A.alloc((512,), F32)[:32] for _ in range(2)]
    t1_r = [Reg() for _ in range(2)]
    t2_r = [Reg() for _ in range(2)]
    rcp = [A.alloc((512,), F32) for _ in range(2)]
    rcp_r = [Reg() for _ in range(2)]
    SCALE = 128.0 ** -0.5

    def bank(b):
        return PS[:, 512 * b:512 * (b + 1)]

    cnt = {"proj": 0, "rot": 0, "s": 0, "p": 0, "nd": 0}
    iters = [(hs, g) for hs in range(4) for g in range(3)]

    def make_blocks(g):
        blocks = []
        if g == 0:
            for b in range(16):
                blocks.append((tok_slice(128 * b, 1), tok_slice(128 * (b - 1), 1) if b > 0 else None))
        elif g == 1:
            for r in range(4):
                for b in range(4):
                    blocks.append((tok_slice(4 * 128 * b + r, 4), tok_slice(4 * 128 * (b - 1) + r, 4) if b > 0 else None))
        else:
            for r in range(16):
                blocks.append((tok_slice(r, 16), None))
        return blocks

    def proj_units(idx):
        hs, g = iters[idx]
        sl = idx % 2
        head = g * 4 + hs
        blocks = make_blocks(g)
        units = []

        def u_load():
            for m, off in enumerate((0, 1536, 3072)):
                c0 = off + head * 128
                load_cast(C, wsl[sl][m], w_in[:, c0:c0 + 128].rearrange("(k p) n -> p k n", p=128), wsl_r[sl][m])
        units.append(u_load)
        for m in range(2):
            for tg in range(4):
                def u_qk(m=m, tg=tg):
                    dst = qk[sl][m]
                    pb = cnt["proj"] % 2
                    cnt["proj"] += 1
                    for k in range(KC):
                        P.op("pe", lambda e, k=k: e.matmul(
                            bank(pb), lhsT=wsl[sl][m][:, k, :], rhs=hT[:, k, tg * 512:(tg + 1) * 512],
                            start=(k == 0), stop=(k == KC - 1)), reads=[wsl_r[sl][m]], writes=[psr[pb]])
                    dcol = dst[:, tg * 512:(tg + 1) * 512]
                    dr = qk_r[sl][m][tg]
                    P.op("act", lambda e: e.activation(out=dcol, in_=bank(pb), func=AF.Copy), reads=[psr[pb]], writes=[dr])
                    rb = 2 + cnt["rot"] % 2
                    ts = cnt["rot"] % 2
                    cnt["rot"] += 1
                    P.op("pe", lambda e: e.matmul(bank(rb)[:32, :], lhsT=C.rm, rhs=dcol[:32, :], start=True, stop=True),
                         reads=[dr, C.cst_r], writes=[psr[rb]])
                    P.op("dve", lambda e: e.tensor_tensor(out=t1[ts], in0=bank(rb)[:32, :], in1=sin[:, tg * 512:(tg + 1) * 512],
                                                          op=ALU.mult), reads=[psr[rb], cs_r, cs2_r], writes=[t1_r[ts]])
                    P.op("dve", lambda e: e.tensor_tensor(out=t2[ts], in0=dcol[:32, :], in1=cos[:, tg * 512:(tg + 1) * 512],
                                                          op=ALU.mult), reads=[dr, cs_r], writes=[t2_r[ts]])
                    P.op("dve", lambda e: e.tensor_tensor(out=dcol[:32, :], in0=t1[ts], in1=t2[ts], op=ALU.add),
                         reads=[t1_r[ts], t2_r[ts]], writes=[dr])
                units.append(u_qk)
        for j in range(4):
            def u_v(j=j):
                pb = cnt["proj"] % 2
                cnt["proj"] += 1
                for bi in range(4):
                    qs = blocks[4 * j + bi][0]
                    for k in range(KC):
                        P.op("pe", lambda e, k=k, qs=qs, bi=bi: e.matmul(
                            bank(pb)[:, bi * 128:(bi + 1) * 128], lhsT=hT[:, k, qs], rhs=wsl[sl][2][:, k, :],
                            start=(k == 0), stop=(k == KC - 1)), reads=[wsl_r[sl][2]], writes=[psr[pb]])
                P.op("act", lambda e: e.activation(
                    out=vt[sl][:, 4 * j:4 * j + 4, :], in_=bank(pb).rearrange("p (a b) -> p a b", a=4), func=AF.Copy),
                    reads=[psr[pb]], writes=[vt_r[sl][j]])
            units.append(u_v)
        return units

    def core(idx, fill):
        hs, g = iters[idx]
        sl = idx % 2
        blocks = make_blocks(g)
        qT, kT = qk[sl][0], qk[sl][1]
        qr = qk_r[sl][0] + qk_r[sl][1]
        pairs = [(2 * i, 2 * i + 1) for i in range(8)]

        def do_qk(pair):
            sb = 4 + cnt["s"] % 2
            cnt["s"] += 1
            for ti, blk in enumerate(pair):
                qs, ps_ = blocks[blk]
                for ci, ks in enumerate((ps_, qs)):
                    o = bank(sb)[:, (2 * ti + ci) * 128:(2 * ti + ci + 1) * 128]
                    if ks is None:
                        P.op("pe", lambda e, o=o: e.matmul(o, lhsT=C.ident, rhs=C.maskn, start=True, stop=True),
                             reads=[C.cst_r], writes=[psr[sb]])
                        continue
                    P.op("pe", lambda e, o=o, ks=ks, qs=qs, kT=kT, qT=qT: e.matmul(o, lhsT=kT[:, ks], rhs=qT[:, qs], start=True, stop=False),
                         reads=qr, writes=[psr[sb]])
                    mk = C.maskc if ci == 1 else C.maskp
                    P.op("pe", lambda e, o=o, mk=mk: e.matmul(o, lhsT=C.ident, rhs=mk, start=False, stop=True),
                         reads=[C.cst_r], writes=[psr[sb]])
            pslot = cnt["p"] % 4
            cnt["p"] += 1
            P.op("act", lambda e, sb=sb, pslot=pslot: e.activation(out=pT[pslot], in_=bank(sb), func=AF.Exp, scale=SCALE),
                 reads=[psr[sb]], writes=[pT_r[pslot]])
            return pslot

        def do_pv(pair, pslot):
            nb = 6 + cnt["nd"] % 2
            cnt["nd"] += 1
            for ti, blk in enumerate(pair):
                qs, ps_ = blocks[blk]
                for is_num in (True, False):
                    col = ti * 128 + (0 if is_num else 256)
                    o = bank(nb)[:, col:col + 128]
                    for ci in range(2):
                        kblk = blk if (ci == 1 or ps_ is None) else blk - 1
                        lhs = vt[sl][:, kblk, :] if is_num else C.ones_bf
                        P.op("pe", lambda e, o=o, lhs=lhs, pslot=pslot, ti=ti, ci=ci: e.matmul(
                            o, lhsT=lhs, rhs=pT[pslot][:, (2 * ti + ci) * 128:(2 * ti + ci + 1) * 128],
                            start=(ci == 0), stop=(ci == 1)),
                            reads=[pT_r[pslot], vt_r[sl][kblk // 4], C.cst_r], writes=[psr[nb]])
            pi = pair[0] // 2
            if g == 0:
                sel = slice(256 * pi, 256 * pi + 256)
                dn, dd = acc_n[:, sel], acc_d[:, sel]
                sn, sd = bank(nb)[:, 0:256], bank(nb)[:, 256:512]
            elif g == 1:
                r_, b_ = pair[0] // 4, pair[0] % 4
                st_ = r_ + 512 * b_
                sel = slice(st_, st_ + 4 * 255 + 1, 4)
                dn, dd = acc_n[:, sel], acc_d[:, sel]
                sn, sd = bank(nb)[:, 0:256], bank(nb)[:, 256:512]
            else:
                r_ = pair[0]
                dn = acc_n.rearrange("p (n r) -> p r n", r=16)[:, r_:r_ + 2, :]
                dd = acc_d.rearrange("p (n r) -> p r n", r=16)[:, r_:r_ + 2, :]
                sn = bank(nb)[:, 0:256].rearrange("p (a b) -> p a b", a=2)
                sd = bank(nb)[:, 256:512].rearrange("p (a b) -> p a b", a=2)
            if g == 0:
                P.op("dve", lambda e, dn=dn, sn=sn: e.tensor_copy(out=dn, in_=sn), reads=[psr[nb]], writes=[acc_all])
                P.op("dve", lambda e, dd=dd, sd=sd: e.tensor_copy(out=dd, in_=sd), reads=[psr[nb]], writes=[acc_all])
            else:
                P.op("dve", lambda e, dn=dn, sn=sn: e.tensor_tensor(out=dn, in0=sn, in1=dn, op=ALU.add),
                     reads=[psr[nb], acc_all], writes=[acc_all])
                P.op("dve", lambda e, dd=dd, sd=sd: e.tensor_tensor(out=dd, in0=sd, in1=dd, op=ALU.add),
                     reads=[psr[nb], acc_all], writes=[acc_all])


        prev = None
        for pair in pairs:
            pslot = do_qk(pair)
            if prev is not None:
                do_pv(*prev)
                fill()
            prev = (pair, pslot)
        do_pv(*prev)
        fill()
        if g == 2:
            for tg in range(4):
                rs = tg % 2
                P.op("dve", lambda e, tg=tg, rs=rs: e.reciprocal(out=rcp[rs], in_=acc_d[:, tg * 512:(tg + 1) * 512]),
                     reads=[acc_all], writes=[rcp_r[rs]])
                P.op("dve", lambda e, tg=tg, rs=rs, hs=hs: e.tensor_tensor(
                    out=attT[:, hs, tg * 512:(tg + 1) * 512], in0=acc_n[:, tg * 512:(tg + 1) * 512], in1=rcp[rs], op=ALU.mult),
                    reads=[acc_all, rcp_r[rs]], writes=[C.attT_r])


    for u in proj_units(0):
        u()
    for idx in range(len(iters)):
        nxt = proj_units(idx + 1) if idx + 1 < len(iters) else []
        per = -(-len(nxt) // 8) if nxt else 0
        pos = [0]

        def fill():
            for u in nxt[pos[0]:pos[0] + per]:
                u()
            pos[0] += per
        core(idx, fill)
        for u in nxt[pos[0]:]:
            u()

def mlstm_phase(C, A, hT, mlT, w_in, conv_w, conv_b, i_bias, f_bias, head_g):
    P, PS, psr = C.P, C.PS, C.psr
    OQ, OK_, OV, OO, OI = 4608, 5632, 6656, 7680, 8704

    def bank(b):
        return PS[:, 512 * b:512 * (b + 1)]

    def R():
        return Reg()

    cwb = A.alloc((2048,), F32)[:5]
    cwb_r = R()
    cwb2_r = R()
    P.dma("sp", cwb[0:4, :], conv_w, "mca", writes=[cwb_r])
    P.dma("sp", cwb[4:5, :], conv_b.rearrange("(o n) -> o n", o=1), "mca", writes=[cwb2_r])
    cwT = A.alloc((16, 8), F32)
    ncb = A.alloc((16,), F32)
    cwT_r = R()
    for c in range(16):
        P.op("pe", lambda e, c=c: e.matmul(bank(6)[:, c * 8:c * 8 + 5], lhsT=cwb[:, c * 128:(c + 1) * 128],
                                           rhs=C.identf[:5, :5], start=True, stop=True),
             reads=[cwb_r, cwb2_r, C.cst_r], writes=[psr[6]])
    P.op("dve", lambda e: e.tensor_copy(out=cwT[:, :, 0:5], in_=bank(6)[:, 0:128].rearrange("p (c j) -> p c j", j=8)[:, :, 0:5]),
         reads=[psr[6]], writes=[cwT_r])
    P.op("dve", lambda e: e.tensor_scalar(out=ncb, in0=cwT[:, :, 4], scalar1=-1.0, scalar2=None, op0=ALU.mult),
         reads=[cwT_r], writes=[cwT_r])
    hg = A.alloc((1024,), F32)
    hg_r = R()
    P.dma("sp", hg, head_g.partition_broadcast(128), "mch", writes=[hg_r])
    bias8 = A.alloc((8,), F32)
    b8_r = R()
    b8b_r = R()
    P.dma("sp", bias8[:, 0:4], i_bias.partition_broadcast(128), "mcb", writes=[b8_r])
    P.dma("sp", bias8[:, 4:8], f_bias.partition_broadcast(128), "mcb", writes=[b8b_r])

    wif = A.alloc((KC, 8), BF16)
    wif_r = R()
    load_cast(C, wif, w_in[:, OI:OI + 8].rearrange("(k p) n -> p k n", p=128), wif_r)
    for c in range(NT):
        for k in range(KC):
            P.op("pe", lambda e, c=c, k=k: e.matmul(bank(7)[:, c * 8:(c + 1) * 8], lhsT=hT[:, k, c * 128:(c + 1) * 128],
                                                    rhs=wif[:, k, :], start=(k == 0), stop=(k == KC - 1)),
                 reads=[wif_r], writes=[psr[7]])
    gi = A.alloc((NT, 4), F32)
    lg = A.alloc((NT, 4), F32)
    g_r = R()
    pre3 = bank(7)[:, 0:128].rearrange("p (c j) -> p c j", j=8)
    P.op("dve", lambda e: e.tensor_tensor(out=gi, in0=pre3[:, :, 0:4],
                                          in1=bias8[:, 0:4].unsqueeze(1).to_broadcast([128, NT, 4]), op=ALU.add),
         reads=[psr[7], b8_r, b8b_r], writes=[g_r])
    P.op("dve", lambda e: e.tensor_tensor(out=lg, in0=pre3[:, :, 4:8],
                                          in1=bias8[:, 4:8].unsqueeze(1).to_broadcast([128, NT, 4]), op=ALU.add),
         reads=[psr[7], b8_r, b8b_r, g_r], writes=[g_r])
    P.op("act", lambda e: e.activation(out=lg, in_=lg, func=AF.Exp, scale=-1.0), reads=[g_r], writes=[g_r])
    P.op("act", lambda e: e.activation(out=lg, in_=lg, func=AF.Ln, bias=1.0), reads=[g_r], writes=[g_r])
    lg2 = lg.rearrange("p c h -> p (c h)")
    gi2 = gi.rearrange("p c h -> p (c h)")
    P.op("pe", lambda e: e.matmul(bank(6)[:, 0:64], lhsT=C.tri, rhs=lg2, start=True, stop=True),
         reads=[g_r, C.cst_r], writes=[psr[6]])
    P.op("pe", lambda e: e.matmul(bank(6)[:, 64:128], lhsT=C.ones_f, rhs=lg2, start=True, stop=True),
         reads=[g_r, C.cst_r], writes=[psr[6]])
    e_in = A.alloc((64,), F32)
    e_out = A.alloc((64,), F32)
    e_L = A.alloc((64,), F32)
    e_v = A.alloc((64,), F32)
    e_io = A.alloc((64,), F32)
    ee_r = R()
    P.op("dve", lambda e: e.tensor_tensor(out=e_in, in0=bank(6)[:, 0:64], in1=gi2, op=ALU.add),
         reads=[psr[6], g_r], writes=[ee_r])
    P.op("act", lambda e: e.activation(out=e_in, in_=e_in, func=AF.Exp), reads=[ee_r], writes=[ee_r])
    P.op("act", lambda e: e.activation(out=e_out, in_=bank(6)[:, 0:64], func=AF.Exp, scale=-1.0),
         reads=[psr[6], ee_r], writes=[ee_r])
    P.op("act", lambda e: e.activation(out=e_L, in_=bank(6)[:, 64:128], func=AF.Exp, scale=-1.0),
         reads=[psr[6], ee_r], writes=[ee_r])
    P.op("act", lambda e: e.activation(out=e_io, in_=bank(6)[:, 0:64], func=AF.Exp), reads=[psr[6], ee_r], writes=[ee_r])
    P.op("dve", lambda e: e.tensor_scalar(out=e_in, in0=e_in, scalar1=1.0 / 16.0, scalar2=None, op0=ALU.mult),
         reads=[ee_r], writes=[ee_r])
    P.op("dve", lambda e: e.tensor_tensor(out=e_v, in0=e_in, in1=e_L, op=ALU.mult), reads=[ee_r], writes=[ee_r])

    wq = [A.alloc((KC, 256), BF16) for _ in range(5)]
    wq_r = [[R(), R()] for _ in range(5)]
    raw = [A.alloc((S + 4,), BF16) for _ in range(2)]
    raw_r = [R(), R()]
    for i in range(2):
        P.op("dve", lambda e, i=i: e.memset(raw[i][:, 0:3], 0.0), writes=[raw_r[i]])
    neghalf = A.alloc((1,), F32)
    nh_r = R()
    P.op("dve", lambda e: e.memset(neghalf, -0.5), writes=[nh_r])
    dg = [[A.alloc((128,), BF16) for _ in range(4)] for _ in range(2)]
    dg_r = [R(), R()]
    qT2 = [A.alloc((2, S), BF16) for _ in range(2)]
    kT2 = [A.alloc((2, S), BF16) for _ in range(2)]
    qk2_r = [[[R(), R()], [R(), R()]] for _ in range(2)]
    ktok2 = [A.alloc((256,), BF16) for _ in range(2)]
    ktok2_r = [R(), R()]
    vaug2 = [A.alloc((258,), BF16) for _ in range(2)]
    vaug2_r = [R(), R()]
    for i in range(2):
        P.op("dve", lambda e, i=i: e.memset(vaug2[i][:, 256:257], 1.0), writes=[vaug2_r[i]])
        P.op("dve", lambda e, i=i: e.memset(vaug2[i][:, 257:258], 0.0), writes=[vaug2_r[i]])
    vp2 = [A.alloc((258,), BF16) for _ in range(2)]
    vp2_r = [R(), R()]
    sigo2 = [A.alloc((NT, 256), BF16) for _ in range(2)]
    sigo2_r = [[R() for _ in range(NT)] for _ in range(2)]
    wT2 = [A.alloc((128,), BF16) for _ in range(2)]
    wT2_r = [R(), R()]
    Cf = A.alloc((2, 258), F32)
    Cf_r = R()
    Cbf2 = [A.alloc((2, 258), BF16) for _ in range(2)]
    Cbf2_r = [R(), R()]
    hu3 = [A.alloc((256,), F32) for _ in range(3)]
    hu3_r = [R(), R(), R()]
    sm3 = [A.alloc((8,), F32) for _ in range(3)]
    sm3_r = [R(), R(), R()]
    junk3 = [A.alloc((256,), BF16) for _ in range(3)]
    mlb3 = [A.alloc((256,), BF16) for _ in range(3)]
    mlb3_r = [R(), R(), R()]

    def pre_units(hd):
        sl = hd % 2
        qT, kT, qk_r, sigo, sigo_r = qT2[sl], kT2[sl], qk2_r[sl], sigo2[sl], sigo2_r[sl]
        wsel = (0, 1, 3 + sl, 2)
        units = []
        pending = []

        def u_load():
            for m, off in enumerate((OQ, OK_, OV, OO)):
                c0 = off + hd * 256
                for part in range(2):
                    load_cast(C, wq[wsel[m]][:, 4 * part:4 * part + 4, :],
                              w_in[part * 512:(part + 1) * 512, c0:c0 + 256].rearrange("(k p) n -> p k n", p=128),
                              wq_r[wsel[m]][part])
        units.append(u_load)
        for m, dstT in ((0, qT), (1, kT)):
            for cc in range(2):
                ch = m * 8 + hd * 2 + cc
                bi = (m * 2 + cc) % 2

                def u_diag(ch=ch, bi=bi):
                    for j in range(4):
                        P.op("dve", lambda e, j=j: e.tensor_scalar(out=dg[bi][j], in0=C.ident, scalar1=cwT[:, ch, j:j + 1],
                                                                   scalar2=None, op0=ALU.mult),
                             reads=[C.cst_r, cwT_r], writes=[dg_r[bi]])
                units.append(u_diag)
                for tg in range(4):
                    def u_proj(m=m, cc=cc, tg=tg, ch=ch, bi=bi, dstT=dstT):
                        for k in range(KC):
                            P.op("pe", lambda e, k=k: e.matmul(
                                bank(6), lhsT=wq[m][:, k, cc * 128:(cc + 1) * 128], rhs=hT[:, k, tg * 512:(tg + 1) * 512],
                                start=(k == 0), stop=(k == KC - 1)), reads=[wq_r[m][k // 4]], writes=[psr[6]])
                        P.op("act", lambda e: e.activation(
                            out=raw[bi][:, 3 + tg * 512:3 + (tg + 1) * 512], in_=bank(6), func=AF.Copy),
                            reads=[psr[6]], writes=[raw_r[bi]])
                        def conv_part():
                            for j in range(4):
                                P.op("pe", lambda e, j=j: e.matmul(bank(7), lhsT=dg[bi][j],
                                                                   rhs=raw[bi][:, tg * 512 + j:tg * 512 + j + 512],
                                                                   start=(j == 0), stop=(j == 3)),
                                     reads=[dg_r[bi], raw_r[bi]], writes=[psr[7]])
                            P.op("act", lambda e: e.activation(out=dstT[:, cc, tg * 512:(tg + 1) * 512], in_=bank(7), func=AF.Silu,
                                                               bias=cwT[:, ch, 4:5]),
                                 reads=[psr[7], cwT_r], writes=[qk_r[m][cc]])
                        if pending:
                            pending.pop(0)()
                        pending.append(conv_part)
                    units.append(u_proj)
        def u_flush():
            while pending:
                pending.pop(0)()
        units.append(u_flush)
        for c in range(NT):
            def u_gate(c=c):
                cs = slice(c * 128, (c + 1) * 128)
                ob = 6 + c % 2
                for k in range(KC):
                    P.op("pe", lambda e, k=k: e.matmul(bank(ob)[:, 0:256], lhsT=hT[:, k, cs], rhs=wq[2][:, k, :],
                                                       start=(k == 0), stop=(k == KC - 1)),
                         reads=[wq_r[2][k // 4]], writes=[psr[ob]])
                P.op("act", lambda e: e.activation(out=sigo[:, c, :], in_=bank(ob)[:, 0:256], func=AF.Sigmoid),
                     reads=[psr[ob]], writes=[sigo_r[c]])
            units.append(u_gate)
        return units

    def loop_steps(hd):
        sl = hd % 2
        qT, kT, qk_r, sigo, sigo_r = qT2[sl], kT2[sl], qk2_r[sl], sigo2[sl], sigo2_r[sl]
        wv, wv_r = wq[3 + sl], wq_r[3 + sl]
        qkr = [qk_r[0][0], qk_r[0][1], qk_r[1][0], qk_r[1][1]]
        def chunk_vars(c):
            p = c % 2
            q3 = c % 3
            return dict(cs=slice(c * 128, (c + 1) * 128), col=c * 4 + hd, ktok=ktok2[p], ktok_r=ktok2_r[p], vaug=vaug2[p],
                        vaug_r=vaug2_r[p], vp=vp2[p], vp_r=vp2_r[p], wT=wT2[p], wT_r=wT2_r[p], hu=hu3[q3], hu_r=hu3_r[q3],
                        sm=sm3[q3], sm_r=sm3_r[q3], junk=junk3[q3], mlb=mlb3[q3], mlb_r=mlb3_r[q3], ab=p, db=2 + p, tb=p)

        def stageA1(c):
            v_ = chunk_vars(c)
            cs, col, ktok, ktok_r, vaug, vaug_r, vp, vp_r, wT, wT_r, ab, db = (v_[k] for k in (
                "cs", "col", "ktok", "ktok_r", "vaug", "vaug_r", "vp", "vp_r", "wT", "wT_r", "ab", "db"))
            abf = bank(ab).bitcast(BF16)
            for k in range(KC):
                P.op("pe", lambda e, k=k: e.matmul(bank(ab)[:, 0:256], lhsT=hT[:, k, cs], rhs=wv[:, k, :],
                                                   start=(k == 0), stop=(k == KC - 1)),
                     reads=[wv_r[k // 4]], writes=[psr[ab]])
            for dk in range(2):
                P.op("pe", lambda e, dk=dk: e.transpose(out=abf[:, 512 + dk * 128:512 + (dk + 1) * 128],
                                                        in_=kT[:, dk, cs], identity=C.ident),
                     reads=[qk_r[1][dk], C.cst_r], writes=[psr[ab]])
            P.op("act", lambda e: e.activation(out=vaug[:, 0:256], in_=bank(ab)[:, 0:256], func=AF.Copy),
                 reads=[psr[ab]], writes=[vaug_r])
            P.op("act", lambda e: e.activation(out=ktok, in_=abf[:, 512:768], func=AF.Copy),
                 reads=[psr[ab]], writes=[ktok_r])
            for dk in range(2):
                P.op("pe", lambda e, dk=dk: e.matmul(bank(db)[:, 384:512], lhsT=kT[:, dk, cs], rhs=qT[:, dk, cs],
                                                     start=(dk == 0), stop=(dk == 1)),
                     reads=qkr, writes=[psr[db]])
            P.op("dve", lambda e: e.scalar_tensor_tensor(
                out=wT, in0=bank(db)[:, 384:512], scalar=e_in[:, col:col + 1], in1=C.tri, op0=ALU.mult, op1=ALU.mult),
                reads=[psr[db], ee_r, C.cst_r], writes=[wT_r])
            if c < NT - 1:
                P.op("dve", lambda e: e.tensor_scalar(out=vp, in0=vaug, scalar1=e_v[:, col:col + 1], scalar2=None, op0=ALU.mult),
                     reads=[vaug_r, ee_r], writes=[vp_r])

        def stageA2(c):
            v_ = chunk_vars(c)
            cs, col, ktok, ktok_r, vaug, vaug_r, vp, vp_r, wT, wT_r, db = (v_[k] for k in (
                "cs", "col", "ktok", "ktok_r", "vaug", "vaug_r", "vp", "vp_r", "wT", "wT_r", "db"))
            if c < NT - 1:
                for dk in range(2):
                    P.op("pe", lambda e, dk=dk: e.matmul(bank(4 + dk)[:, 0:258], lhsT=ktok[:, dk * 128:(dk + 1) * 128], rhs=vp,
                                                         start=True, stop=True),
                         reads=[ktok_r, vp_r], writes=[psr[4 + dk]])
            P.op("pe", lambda e: e.matmul(bank(db)[:, 0:258], lhsT=wT, rhs=vaug, start=True, stop=(c == 0)),
                 reads=[wT_r, vaug_r], writes=[psr[db]])
            if c > 0:
                cprev, cprev_r = Cbf2[(c - 1) % 2], Cbf2_r[(c - 1) % 2]
                for dk in range(2):
                    P.op("pe", lambda e, dk=dk: e.matmul(bank(db)[:, 0:258], lhsT=qT[:, dk, cs], rhs=cprev[:, dk, :],
                                                         start=False, stop=(dk == 1)),
                         reads=qkr + [cprev_r], writes=[psr[db]])
            if c < NT - 1:
                dC = PS[:, 2048:3072].rearrange("p (a b) -> p a b", a=2)[:, :, 0:258]
                if c == 0:
                    P.op("dve", lambda e: e.tensor_copy(out=Cf, in_=dC), reads=[psr[4], psr[5]], writes=[Cf_r])
                else:
                    P.op("dve", lambda e: e.scalar_tensor_tensor(out=Cf, in0=Cf, scalar=e_L[:, col:col + 1], in1=dC,
                                                                 op0=ALU.mult, op1=ALU.add),
                         reads=[psr[4], psr[5], Cf_r, ee_r], writes=[Cf_r])
                ccur, ccur_r = Cbf2[c % 2], Cbf2_r[c % 2]
                P.op("act", lambda e: e.activation(out=ccur, in_=Cf, func=AF.Copy), reads=[Cf_r], writes=[ccur_r])

        def stageB(c):
            v_ = chunk_vars(c)
            col, hu, hu_r, sm, sm_r, junk, db = (v_[k] for k in ("col", "hu", "hu_r", "sm", "sm_r", "junk", "db"))
            P.op("dve", lambda e, col=col, sm=sm, db=db: e.tensor_scalar(out=sm[:, 0:1], in0=bank(db)[:, 256:257],
                                                                       scalar1=e_io[:, col:col + 1], scalar2=None, op0=ALU.max),
                 reads=[psr[db], ee_r], writes=[sm_r])
            P.op("dve", lambda e, sm=sm, db=db: e.scalar_tensor_tensor(out=sm[:, 1:2], in0=bank(db)[:, 256:257], scalar=-1.0,
                                                                     in1=sm[:, 0:1], op0=ALU.mult, op1=ALU.max),
                 reads=[psr[db], sm_r], writes=[sm_r])
            P.op("dve", lambda e, sm=sm: e.reciprocal(out=sm[:, 3:4], in_=sm[:, 1:2]), reads=[sm_r], writes=[sm_r])
            P.op("dve", lambda e, sm=sm, hu=hu, db=db: e.tensor_scalar(out=hu, in0=bank(db)[:, 0:256], scalar1=sm[:, 3:4], scalar2=None,
                                                                     op0=ALU.mult),
                 reads=[psr[db], sm_r], writes=[hu_r])
            P.op("act", lambda e, sm=sm, hu=hu, junk=junk: e.activation(out=junk, in_=hu, func=AF.Square, accum_out=sm[:, 4:5]),
                 reads=[hu_r, sm_r], writes=[sm_r])
            P.op("pool", lambda e, sm=sm: e.tensor_scalar(out=sm[:, 5:6], in0=sm[:, 4:5], scalar1=1.0 / 256, scalar2=EPS,
                                                          op0=ALU.mult, op1=ALU.add), reads=[sm_r], writes=[sm_r])
            P.op("pool", lambda e, sm=sm: e.tensor_tensor(out=sm[:, 5:6], in0=sm[:, 5:6], in1=neghalf, op=ALU.pow),
                 reads=[sm_r, nh_r], writes=[sm_r])

        def stageC(c):
            v_ = chunk_vars(c)
            cs, hu, hu_r, sm, sm_r, mlb, mlb_r, tb = (v_[k] for k in ("cs", "hu", "hu_r", "sm", "sm_r", "mlb", "mlb_r", "tb"))
            abf = bank(tb).bitcast(BF16)
            ab = tb
            P.op("dve", lambda e, hd=hd, sm=sm, hu=hu: e.scalar_tensor_tensor(out=hu, in0=hu, scalar=sm[:, 5:6],
                                                                              in1=hg[:, hd * 256:(hd + 1) * 256],
                                                                              op0=ALU.mult, op1=ALU.mult),
                 reads=[hu_r, sm_r, hg_r], writes=[hu_r])
            P.op("dve", lambda e, c=c, hu=hu, mlb=mlb: e.tensor_tensor(out=mlb, in0=hu, in1=sigo[:, c, :], op=ALU.mult),
                 reads=[hu_r, sigo_r[c]], writes=[mlb_r])
            for j in range(2):
                P.op("pe", lambda e, j=j, abf=abf, mlb=mlb: e.transpose(out=abf[:, 768 + j * 128:768 + (j + 1) * 128],
                                                                       in_=mlb[:, j * 128:(j + 1) * 128], identity=C.ident),
                     reads=[mlb_r, C.cst_r], writes=[psr[ab]])
            P.op("act", lambda e, hd=hd, cs=cs, abf=abf: e.activation(
                out=mlT[:, 2 * hd:2 * hd + 2, cs], in_=abf[:, 768:1024].rearrange("p (j n) -> p j n", j=2), func=AF.Copy),
                reads=[psr[ab]], writes=[C.mlT_r])


        steps = []
        for step in range(NT + 2):
            def st(fill, step=step):
                if step == 0:
                    stageA1(0)
                if step + 1 < NT:
                    stageA1(step + 1)
                fill()
                if step < NT:
                    stageA2(step)
                fill()
                if 0 <= step - 1 < NT:
                    stageB(step - 1)
                fill()
                if 0 <= step - 2 < NT:
                    stageC(step - 2)
            steps.append(st)
        return steps

    for u in pre_units(0):
        u()
    NH = int(os.environ.get("NH", 4))
    for hd in range(NH):
        nxt = pre_units(hd + 1) if hd < NH - 1 else []
        steps = loop_steps(hd) if not os.environ.get("NOLOOP") else []
        if not steps:
            for u in nxt:
                u()
            continue
        nfill = 3 * len(steps)
        per = -(-len(nxt) // nfill) if nxt else 0
        pos = [0]

        def fill():
            for u in nxt[pos[0]:pos[0] + per]:
                u()
            pos[0] += per
        for st in steps:
            st(fill)
        for u in nxt[pos[0]:]:
            u()

def merge_phase(C, A, hT, attT, mlT, w_in, w_a, w_m, w_out, g_post, x_src, x_dst):
    P, PS, psr = C.P, C.PS, C.psr
    OGA, OGM = 8712, 9736

    def bank(b):
        return PS[:, 512 * b:512 * (b + 1)]

    mgT = A.alloc((KC, S), BF16)
    wa = A.alloc((4, 1024), BF16)
    wa_r = [Reg() for _ in range(4)]
    wm = A.alloc((KC, 1024), BF16)
    wm_r = [Reg() for _ in range(KC)]
    wo = A.alloc((KC, 1024), BF16)
    wo_r = [Reg() for _ in range(KC)]
    wg = [[A.alloc((KC, 128), BF16) for _ in range(2)] for _ in range(2)]
    wg_r = [[Reg() for _ in range(2)] for _ in range(2)]
    load_cast(C, wg[0][0], w_in[:, OGA:OGA + 128].rearrange("(k p) n -> p k n", p=128), wg_r[0][0])
    load_cast(C, wg[0][1], w_in[:, OGM:OGM + 128].rearrange("(k p) n -> p k n", p=128), wg_r[0][1])
    for k in range(4):
        load_cast(C, wa[:, k, :], w_a[k * 128:(k + 1) * 128, :], wa_r[k])
    for k in range(KC):
        load_cast(C, wm[:, k, :], w_m[k * 128:(k + 1) * 128, :], wm_r[k])
    ga = [A.alloc((512,), F32) for _ in range(2)]
    ga_r = [Reg() for _ in range(2)]
    gm = [A.alloc((512,), F32) for _ in range(2)]
    gm_r = [Reg() for _ in range(2)]
    ta = [A.alloc((512,), F32) for _ in range(2)]
    ta_r = [Reg() for _ in range(2)]
    gb = A.alloc((D,), F32)
    gb_r = Reg()
    P.dma("sp", gb, g_post.partition_broadcast(128), "g", writes=[gb_r])
    j = 0
    for mc in range(KC):
        sl = mc % 2
        if mc > 0:
            load_cast(C, wg[sl][0], w_in[:, OGA + mc * 128:OGA + (mc + 1) * 128].rearrange("(k p) n -> p k n", p=128), wg_r[sl][0])
            load_cast(C, wg[sl][1], w_in[:, OGM + mc * 128:OGM + (mc + 1) * 128].rearrange("(k p) n -> p k n", p=128), wg_r[sl][1])
        if mc == 0:
            for k in range(KC):
                load_cast(C, wo[:, k, :], w_out[k * 128:(k + 1) * 128, :], wo_r[k])
        for tg in range(4):
            q = j % 2
            j += 1
            ts = slice(tg * 512, (tg + 1) * 512)
            b0 = 4 * q
            for gi_, (gbuf, gr) in enumerate(((ga, ga_r), (gm, gm_r))):
                for k in range(KC):
                    P.op("pe", lambda e, k=k, gi_=gi_, sl=sl, ts=ts, b0=b0: e.matmul(
                        bank(b0 + gi_), lhsT=wg[sl][gi_][:, k, :], rhs=hT[:, k, ts], start=(k == 0), stop=(k == KC - 1)),
                        reads=[wg_r[sl][gi_]], writes=[psr[b0 + gi_]])
                P.op("act", lambda e, gbuf=gbuf, q=q, gi_=gi_, b0=b0: e.activation(out=gbuf[q], in_=bank(b0 + gi_), func=AF.Sigmoid),
                     reads=[psr[b0 + gi_]], writes=[gr[q]])
            for k in range(4):
                P.op("pe", lambda e, k=k, mc=mc, ts=ts, b0=b0: e.matmul(
                    bank(b0 + 2), lhsT=wa[:, k, mc * 128:(mc + 1) * 128], rhs=attT[:, k, ts], start=(k == 0), stop=(k == 3)),
                    reads=[wa_r[k], C.attT_r], writes=[psr[b0 + 2]])
            for k in range(KC):
                P.op("pe", lambda e, k=k, mc=mc, ts=ts, b0=b0: e.matmul(
                    bank(b0 + 3), lhsT=wm[:, k, mc * 128:(mc + 1) * 128], rhs=mlT[:, k, ts], start=(k == 0), stop=(k == KC - 1)),
                    reads=[wm_r[k], C.mlT_r], writes=[psr[b0 + 3]])
            P.op("dve", lambda e, q=q, b0=b0: e.tensor_tensor(out=ta[q], in0=bank(b0 + 2), in1=ga[q], op=ALU.mult),
                 reads=[psr[b0 + 2], ga_r[q]], writes=[ta_r[q]])
            P.op("dve", lambda e, q=q, b0=b0: e.tensor_tensor(out=gm[q], in0=bank(b0 + 3), in1=gm[q], op=ALU.mult),
                 reads=[psr[b0 + 3], gm_r[q]], writes=[gm_r[q]])
            P.op("dve", lambda e, q=q, mc=mc, ts=ts: e.tensor_tensor(out=mgT[:, mc, ts], in0=ta[q], in1=gm[q], op=ALU.add),
                 reads=[ta_r[q], gm_r[q]], writes=[C.mg_r])
    xc = [A.alloc((D,), F32) for _ in range(2)]
    tt = [A.alloc((D,), F32) for _ in range(2)]
    xc_r = [Reg() for _ in range(2)]
    tth_r = [[Reg(), Reg()] for _ in range(2)]
    ss2 = A.alloc((NT,), F32)
    r2 = A.alloc((NT,), F32)
    junk = ga[0].bitcast(BF16)
    junk_r = ga_r[0]
    r2_r = [Reg() for _ in range(NT)]
    for i in range(NT):
        s = i % 2
        P.dma("sp", xc[s], x_src[i * 128:(i + 1) * 128, :], f"xc{s}", writes=[xc_r[s]])
        pb = (i % 4) * 2
        psf = PS[:, 512 * pb:512 * (pb + 2)]
        for h in range(2):
            for k in range(KC):
                P.op("pe", lambda e, h=h, k=k, i=i, pb=pb: e.matmul(
                    bank(pb + h), lhsT=mgT[:, k, i * 128:(i + 1) * 128], rhs=wo[:, k, h * 512:(h + 1) * 512],
                    start=(k == 0), stop=(k == KC - 1)), reads=[wo_r[k], C.mg_r], writes=[psr[pb + h]])
        P.op("act", lambda e, i=i, psf=psf: e.activation(out=junk, in_=psf, func=AF.Square, accum_out=ss2[:, i:i + 1]),
             reads=[psr[pb], psr[pb + 1]], writes=[r2_r[i], junk_r])
        P.op("act", lambda e, i=i: e.activation(out=r2[:, i:i + 1], in_=ss2[:, i:i + 1], func=AF.Ln, scale=1.0 / D, bias=EPS),
             reads=[r2_r[i]], writes=[r2_r[i]])
        P.op("act", lambda e, i=i: e.activation(out=r2[:, i:i + 1], in_=r2[:, i:i + 1], func=AF.Exp, scale=-0.5),
             reads=[r2_r[i]], writes=[r2_r[i]])
        for h in range(2):
            P.op("dve", lambda e, s=s, h=h, pb=pb: e.tensor_tensor(
                out=tt[s][:, 512 * h:512 * (h + 1)], in0=bank(pb + h), in1=gb[:, 512 * h:512 * (h + 1)], op=ALU.mult),
                reads=[psr[pb + h], gb_r, r2_r[i]], writes=[tth_r[s][h]])
        P.op("dve", lambda e, s=s, i=i: e.scalar_tensor_tensor(out=xc[s], in0=tt[s], scalar=r2[:, i:i + 1], in1=xc[s],
                                                              op0=ALU.mult, op1=ALU.add),
             reads=[tth_r[s][0], tth_r[s][1], r2_r[i], xc_r[s]], writes=[xc_r[s]])
        P.dma("sp", x_dst[i * 128:(i + 1) * 128, :], xc[s], f"xo{s}", reads=[xc_r[s]])


def mixer_phase(C, x_src, x_dst, W):
    A = C.A.child()
    hT = A.alloc((KC, S), BF16)
    attT = A.alloc((4, S), BF16)
    mlT = A.alloc((8, S), BF16)
    C.attT_r = Reg()
    C.mlT_r = Reg()
    C.mg_r = Reg()
    base = A.off
    end = A.end
    norm_transpose(C, Arena(A.ap, base, end), x_src, W["mix_pre_g"][0], hT)
    attention_phase(C, Arena(A.ap, base, end), hT, attT, W["w_in"][0])
    C.P.barrier()
    mlstm_phase(C, Arena(A.ap, base, end), hT, mlT, W["w_in"][0], W["conv_w"][0], W["conv_b"][0],
                W["mlstm_i_bias"][0], W["mlstm_f_bias"][0], W["mlstm_head_g"][0])
    C.P.barrier()
    merge_phase(C, Arena(A.ap, base, end), hT, attT, mlT, W["w_in"][0], W["w_att_branch"][0], W["w_mlstm_branch"][0],
                W["w_out"][0], W["mix_post_g"][0], x_src, x_dst)
    C.P.barrier()

def host_consts():
    c = {}
    bf = ml_dtypes.bfloat16
    c["ident"] = np.eye(128, dtype=np.float32).astype(bf)
    half = 16
    inv_freq = np.power(np.float32(500000.0), -(np.arange(half, dtype=np.float32) * 2.0 / 32)).astype(np.float32)
    ang = np.arange(S, dtype=np.float32)[None, :] * inv_freq[:, None]
    c["cos"] = np.concatenate([np.cos(ang), np.cos(ang)], 0).astype(np.float32)
    c["sin"] = np.concatenate([np.sin(ang), np.sin(ang)], 0).astype(np.float32)
    rm = np.zeros((32, 32), np.float32)
    for j in range(16):
        rm[16 + j, j] = -1.0
        rm[j, 16 + j] = 1.0
    c["rm"] = rm.astype(bf)
    jj = np.arange(128)[:, None]
    ii = np.arange(128)[None, :]
    NEG = -30000.0
    c["maskc"] = np.where(jj <= ii, 0.0, NEG).astype(bf)
    c["maskp"] = np.where(jj >= ii, 0.0, NEG).astype(bf)
    c["maskn"] = np.full((128, 128), NEG, np.float32).astype(bf)
    c["tri"] = (jj <= ii).astype(np.float32)
    c["ones_f"] = np.ones((128, 128), np.float32)
    c["identf"] = np.eye(128, dtype=np.float32)
    c["ones_bf"] = np.ones((128, 128), np.float32).astype(bf)
    return c


def build(stage="full"):
    nc = bass.Bass("TRN2", target_bir_lowering=False)

    def din(name, shape, dt=F32):
        return nc.dram_tensor(name, list(shape), dt, kind="ExternalInput").ap()

    x = din("x", [S, D])
    W = {}
    for name, shape in [("ffn1_pre_g", [1, D]), ("ffn1_w_gate", [1, D, FF]), ("ffn1_w_up", [1, D, FF]),
                        ("ffn1_w_down", [1, FF, D]), ("ffn1_post_g", [1, D]), ("mix_pre_g", [1, D]),
                        ("w_in", [1, D, IN_W]), ("conv_w", [1, 4, 2048]), ("conv_b", [1, 2048]),
                        ("mlstm_i_bias", [1, 4]), ("mlstm_f_bias", [1, 4]), ("mlstm_head_g", [1, 1024]),
                        ("w_att_branch", [1, 512, D]), ("w_mlstm_branch", [1, D, D]), ("w_out", [1, D, D]),
                        ("mix_post_g", [1, D]), ("ffn2_pre_g", [1, D]), ("ffn2_w_gate", [1, D, FF]),
                        ("ffn2_w_up", [1, D, FF]), ("ffn2_w_down", [1, FF, D]), ("ffn2_post_g", [1, D])]:
        W[name] = din(name, shape)
    CD = {}
    for name, shape, dt in [("c_ident", [128, 128], BF16), ("c_cos", [32, S], F32), ("c_sin", [32, S], F32),
                            ("c_rm", [32, 32], BF16), ("c_maskc", [128, 128], BF16), ("c_maskp", [128, 128], BF16),
                            ("c_maskn", [128, 128], BF16), ("c_tri", [128, 128], F32), ("c_ones_f", [128, 128], F32), ("c_identf", [128, 128], F32),
                            ("c_ones_bf", [128, 128], BF16)]:
        CD[name] = din(name, shape, dt)
    c_ident = CD["c_ident"]
    out = nc.dram_tensor("out", [S, D], F32, kind="ExternalOutput").ap()
    x1 = nc.dram_tensor("x1", [S, D], F32, kind="Internal").ap()
    x2 = nc.dram_tensor("x2", [S, D], F32, kind="Internal").ap()

    with ExitStack() as st:
        ARENA_BYTES = 212480
        arena = st.enter_context(nc.sbuf_tensor("arena", [128, ARENA_BYTES // 2], BF16))
        PS = st.enter_context(nc.psum_tensor("ps", [128, 4096], F32))
        C = Ctx()
        C.nc = nc
        C.P = P = Prog(nc)
        C.PS = PS
        C.psr = [Reg(f"ps{i}", excl=True) for i in range(8)]
        top = Arena(arena, 0, ARENA_BYTES)
        C.ident = top.alloc((128,), BF16)
        C.ident_r = Reg()
        P.dma("sp", C.ident, c_ident, "cst", writes=[C.ident_r])
        C.dram = CD
        C.cst_r = C.ident_r
        for nm, shp, dt in [("rm", (32,), BF16), ("maskc", (128,), BF16), ("maskp", (128,), BF16), ("maskn", (128,), BF16),
                            ("tri", (128,), F32), ("ones_f", (128,), F32), ("identf", (128,), F32), ("ones_bf", (128,), BF16)]:
            v = top.alloc(shp, dt)
            if nm == "rm":
                v = v[:32]
            setattr(C, nm, v)
            P.dma("sp", v, CD["c_" + nm], "cst", writes=[C.cst_r])
        C.stage = [top.alloc((1024,), F32) for _ in range(NST)]
        C.stage_r = [Reg() for _ in range(NST)]
        C.stage_i = 0
        C.A = Arena(arena, top.off, ARENA_BYTES)

        if stage == "attn":
            A = C.A.child()
            hT = A.alloc((KC, S), BF16)
            attT = A.alloc((4, S), BF16)
            C.attT_r = Reg()
            mk = Arena(arena, A.off, ARENA_BYTES)
            norm_transpose(C, mk, x, W["mix_pre_g"][0], hT)
            attention_phase(C, Arena(arena, A.off, ARENA_BYTES), hT, attT, W["w_in"][0])
            P.barrier()
            ov = out.rearrange("(a b) d -> a (b d)", a=1024).rearrange("(k p) t -> p k t", p=128)
            for k in range(4):
                for hh in range(2):
                    P.dma("pool", ov[:, k, hh * 1024:(hh + 1) * 1024], attT[:, k, hh * 1024:(hh + 1) * 1024], "dbg")
        if stage == "full":
            ffn_phase(C, x, x1, W["ffn1_pre_g"][0], W["ffn1_w_gate"][0], W["ffn1_w_up"][0], W["ffn1_w_down"][0],
                      W["ffn1_post_g"][0])
            mixer_phase(C, x1, x2, W)
            ffn_phase(C, x2, out, W["ffn2_pre_g"][0], W["ffn2_w_gate"][0], W["ffn2_w_up"][0], W["ffn2_w_down"][0],
                      W["ffn2_post_g"][0])
        if stage == "mix":
            mixer_phase(C, x, out, W)
        if stage == "ml":
            A = C.A.child()
            hT = A.alloc((KC, S), BF16)
            mlT = A.alloc((8, S), BF16)
            C.mlT_r = Reg()
            mk = Arena(arena, A.off, ARENA_BYTES)
            norm_transpose(C, mk, x, W["mix_pre_g"][0], hT)
            mlstm_phase(C, Arena(arena, A.off, ARENA_BYTES), hT, mlT, W["w_in"][0], W["conv_w"][0], W["conv_b"][0],
                        W["mlstm_i_bias"][0], W["mlstm_f_bias"][0], W["mlstm_head_g"][0])
            P.barrier()
            ov = out.rearrange("(a b) d -> a (b d)", a=1024).rearrange("(k p) t -> p k t", p=128)
            for k in range(8):
                for hh in range(2):
                    P.dma("pool", ov[:, k, hh * 1024:(hh + 1) * 1024], mlT[:, k, hh * 1024:(hh + 1) * 1024], "dbg")
        if stage == "ffn1a":
            A = C.A.child()
            hT_ar = A.sub(KC * S * 2)
            actT_ar = A.sub(FC * S * 2)
            hT = hT_ar.child().alloc((KC, S), BF16)
            norm_transpose(C, actT_ar.child(), x, W["ffn1_pre_g"][0], hT)
            ov = out.rearrange("(a b) d -> a (b d)", a=1024).rearrange("(k p) t -> p k t", p=128)
            for k in range(KC):
                for hh in range(2):
                    P.dma("pool", ov[:, k, hh * 1024:(hh + 1) * 1024], hT[:, k, hh * 1024:(hh + 1) * 1024], "dbg")
        if stage == "ffn1b":
            ffn_phase(C, x, out, W["ffn1_pre_g"][0], W["ffn1_w_gate"][0], W["ffn1_w_up"][0], W["ffn1_w_down"][0],
                      W["ffn1_post_g"][0], stop_after="B")
        if stage == "ffn1":
            ffn_phase(C, x, out, W["ffn1_pre_g"][0], W["ffn1_w_gate"][0], W["ffn1_w_up"][0], W["ffn1_w_down"][0],
                      W["ffn1_post_g"][0])
        P.finish()
        P.emit()
        print("prog stats", P.stats, "sems", len(P.dma_tot) + 5)
        if os.environ.get("DUMP"):
            for e in ("sp", "dve", "act"):
                print("====", e)
                for r in P.dump[e][-int(os.environ["DUMP"]):]:
                    print(r)
    return nc


_NC_CACHE = {}


def kernel(**inputs):
    stage = inputs.pop("_stage", os.environ.get("KSTAGE", "full"))
    if stage not in _NC_CACHE:
        _NC_CACHE[stage] = build(stage)
    nc = _NC_CACHE[stage]
    consts = host_consts()
    xfull = np.ascontiguousarray(inputs["x"], dtype=np.float32)
    shared = {k: np.ascontiguousarray(v, dtype=np.float32) for k, v in inputs.items() if k != "x"}
    for k, v in consts.items():
        shared["c_" + k] = v
    in_maps = []
    ncores = int(os.environ.get("NCORES", 8))
    for b in range(ncores):
        m = dict(shared)
        m["x"] = xfull[b]
        in_maps.append(m)
    res = run_bass_kernel_spmd(nc, in_maps, core_ids=list(range(ncores)))
    return np.stack([r["out"] for r in res.results], axis=0)
```

```python
from contextlib import ExitStack
import math
import os

import numpy as np
import ml_dtypes
import concourse.bass as bass
import concourse.mybir as mybir
from concourse.bass_utils import run_bass_kernel_spmd

F32 = mybir.dt.float32
BF16 = mybir.dt.bfloat16
AF = mybir.ActivationFunctionType
ALU = mybir.AluOpType
AX = mybir.AxisListType

S = 2048
D = 1024
FF = 2816
NT = S // 128
KC = D // 128
FC = FF // 128
IN_W = 10760
EPS = 1e-6
ENGS = ("pe", "act", "dve", "pool", "sp")


class Reg:
    __slots__ = ("name", "w", "rs", "rd", "excl")

    def __init__(self, name="", excl=False):
        self.name = name
        self.excl = excl
        self.w = None
        self.rs = {}
        self.rd = []


class Ins:
    __slots__ = ("eng", "fn", "deps", "signal", "val", "dma", "key")

    def __init__(self, eng, fn, dma=False, key=None):
        self.eng = eng
        self.fn = fn
        self.deps = ()
        self.signal = dma
        self.val = 0
        self.dma = dma
        self.key = key


class Prog:
    def __init__(self, nc):
        self.nc = nc
        self.engs = {e: [] for e in ENGS}
        self.dma_tot = {}
        self.dma_last = {}

    def _add(self, ins, reads, writes):
        eng = ins.eng
        deps = {}
        for r in reads:
            d = r.w
            if d is not None:
                deps[id(d)] = d
            if r.excl:
                for e2, x in r.rs.items():
                    if e2 != eng:
                        deps[id(x)] = x
        for w in writes:
            d = w.w
            if d is not None:
                deps[id(d)] = d
            for e2, x in w.rs.items():
                if (not ins.dma) and e2 == eng:
                    continue
                deps[id(x)] = x
            for x in w.rd:
                deps[id(x)] = x
        out = []
        for d in deps.values():
            if d is ins:
                continue
            if (not d.dma) and (not ins.dma) and d.eng == "pe" and eng == "pe":
                continue
            d.signal = True
            out.append(d)
        ins.deps = out
        for r in reads:
            if ins.dma:
                r.rd.append(ins)
            else:
                r.rs[eng] = ins
        for w in writes:
            w.w = ins
            w.rs = {}
            w.rd = []
        self.engs[eng].append(ins)
        return ins

    def op(self, eng, fn, reads=(), writes=()):
        return self._add(Ins(eng, fn), reads, writes)

    def dma(self, queue, out, in_, key, reads=(), writes=(), **kw):
        ins = Ins(queue, lambda e: e.dma_start(out=out, in_=in_, **kw), dma=True, key=key)
        self.dma_tot[key] = self.dma_tot.get(key, 0) + 16
        ins.val = self.dma_tot[key]
        self.dma_last[key] = ins
        return self._add(ins, reads, writes)

    def barrier(self):
        lasts = []
        for e in ENGS:
            for ins in reversed(self.engs[e]):
                if ins.fn is not None and not ins.dma:
                    ins.signal = True
                    lasts.append(ins)
                    break
        lasts += list(self.dma_last.values())
        for e in ENGS:
            ins = Ins(e, None)
            ins.deps = [d for d in lasts if d.dma or d.eng != e]
            self.engs[e].append(ins)

    def finish(self):
        ins = Ins("sp", None)
        ins.deps = list(self.dma_last.values())
        self.engs["sp"].append(ins)

    def emit(self):
        nc = self.nc
        for e in ENGS:
            c = 0
            for ins in self.engs[e]:
                if ins.dma:
                    continue
                if ins.signal and ins.fn is not None:
                    c += 1
                    ins.val = c
        with ExitStack() as st:
            sems = {e: st.enter_context(nc.semaphore(f"s_{e}")) for e in ENGS}
            dsem = {k: st.enter_context(nc.semaphore(f"d_{k}")) for k in self.dma_tot}
            block = st.enter_context(nc.Block())
            bname = {"pe": "tensor", "act": "scalar", "dve": "vector", "pool": "gpsimd", "sp": "sync"}
            stats = {}
            self.dump = {}
            for e in ENGS:
                def body(engine, e=e):
                    seen = {}
                    nw = 0
                    for ins in self.engs[e]:
                        need = {}
                        for d in ins.deps:
                            s = ("d", d.key) if d.dma else ("c", d.eng)
                            if need.get(s, 0) < d.val:
                                need[s] = d.val
                        for s, v in need.items():
                            if seen.get(s, 0) < v:
                                seen[s] = v
                                sh = dsem[s[1]] if s[0] == "d" else sems[s[1]]
                                engine.wait_ge(sh, v)
                                nw += 1
                        if os.environ.get("DUMP"):
                            self.dump.setdefault(e, []).append((sorted((k, v) for k, v in need.items()), ins.fn is not None, ins.dma, ins.key, ins.signal, ins.val))
                        if ins.fn is not None:
                            bi = ins.fn(engine)
                            if ins.dma:
                                bi.then_inc(dsem[ins.key], 16)
                            elif ins.signal:
                                bi.then_inc(sems[e], 1)
                    stats[e] = (len(self.engs[e]), nw)
                getattr(block, bname[e])(body)
            self.stats = stats


class Arena:
    def __init__(self, ap, start, end):
        self.ap = ap
        self.start = start
        self.off = start
        self.end = end

    def alloc(self, free_shape, dt, parts=128):
        n = 1
        for v in free_shape:
            n *= v
        esz = 4 if dt == F32 else 2
        nbytes = n * esz
        st = (self.off + 63) // 64 * 64
        assert st + nbytes <= self.end, f"arena overflow: need {st + nbytes} > {self.end}"
        self.off = st + nbytes
        Arena.last = (st, tuple(free_shape), dt)
        v = self.ap[:parts, st // 2:(st + nbytes) // 2]
        if dt == F32:
            v = v.bitcast(F32)
        if len(free_shape) == 2:
            v = v.rearrange("p (a b) -> p a b", a=free_shape[0])
        elif len(free_shape) == 3:
            v = v.rearrange("p (a b c) -> p a b c", a=free_shape[0], b=free_shape[1])
        return v

    def sub(self, nbytes):
        st = (self.off + 63) // 64 * 64
        assert st + nbytes <= self.end, f"arena overflow(sub): need {st + nbytes} > {self.end}"
        self.off = st + nbytes
        return Arena(self.ap, st, st + nbytes)

    def child(self):
        return Arena(self.ap, self.start, self.end)


class Ctx:
    pass


DBG = {}


NST = 3


def load_cast(C, dst, src, dst_reg):
    P = C.P
    s = C.stage_i % NST
    C.stage_i += 1
    sh = src.shape
    n = 1
    for v in sh[1:]:
        n *= v
    assert n <= 1024
    stg = C.stage[s][:, :n]
    if len(sh) == 3:
        stg = stg.rearrange("p (a b) -> p a b", a=sh[1])
    P.dma("sp", stg, src, f"st{s}", writes=[C.stage_r[s]])
    P.op("pool", lambda e: e.tensor_copy(out=dst, in_=stg), reads=[C.stage_r[s]], writes=[dst_reg])


def norm_transpose(C, A_stage, x_src, g_pre_dram, hT):
    P, PS, psr = C.P, C.PS, C.psr
    gb = A_stage.alloc((D,), F32)
    gb_r = Reg()
    P.dma("sp", gb, g_pre_dram.partition_broadcast(128), "g", writes=[gb_r])
    ss = A_stage.alloc((NT,), F32)
    rstd = A_stage.alloc((NT,), F32)
    junk = A_stage.alloc((D,), BF16)
    junk_r = Reg()
    hb = [A_stage.alloc((D,), BF16) for _ in range(2)]
    hb_r = [Reg() for _ in range(2)]
    xs = [A_stage.alloc((D,), F32) for _ in range(NT)]
    xs_r = [Reg() for _ in range(NT)]
    ss_r = [Reg() for _ in range(NT)]
    rstd_r = Reg()
    for i in range(NT):
        P.dma("sp", xs[i], x_src[i * 128:(i + 1) * 128, :], f"xs{i}", writes=[xs_r[i]])
    for i in range(NT):
        P.op("act", lambda e, i=i: e.activation(out=junk, in_=xs[i], func=AF.Square, accum_out=ss[:, i:i + 1]),
             reads=[xs_r[i]], writes=[ss_r[i], junk_r])
    P.op("act", lambda e: e.activation(out=rstd, in_=ss, func=AF.Ln, scale=1.0 / D, bias=EPS),
         reads=ss_r, writes=[rstd_r])
    P.op("act", lambda e: e.activation(out=rstd, in_=rstd, func=AF.Exp, scale=-0.5),
         reads=[rstd_r], writes=[rstd_r])
    for i in range(NT):
        s = i % 2
        P.op("dve", lambda e, i=i, s=s: e.scalar_tensor_tensor(out=hb[s], in0=xs[i], scalar=rstd[:, i:i + 1], in1=gb,
                                                              op0=ALU.mult, op1=ALU.mult),
             reads=[xs_r[i], rstd_r, gb_r], writes=[hb_r[s]])
        b = i % 2
        psb = PS[:, 512 * b:512 * (b + 1)].bitcast(BF16)
        for k in range(KC):
            P.op("pe", lambda e, k=k, s=s, psb=psb: e.transpose(out=psb[:, k * 128:(k + 1) * 128],
                                                                in_=hb[s][:, k * 128:(k + 1) * 128], identity=C.ident),
                 reads=[hb_r[s], C.ident_r], writes=[psr[b]])
        P.op("act", lambda e, i=i, psb=psb: e.activation(out=hT[:, :, i * 128:(i + 1) * 128],
                                                         in_=psb.rearrange("p (k n) -> p k n", k=KC), func=AF.Copy),
             reads=[psr[b]], writes=[])
    P.barrier()


def ffn_phase(C, x_src, x_dst, g_pre, wg, wu, wd, g_post, stop_after=None):
    P, PS, psr = C.P, C.PS, C.psr
    A = C.A.child()
    hT_ar = A.sub(KC * S * 2)
    actT_ar = A.sub(FC * S * 2)
    hT = hT_ar.child().alloc((KC, S), BF16)
    actT = actT_ar.child().alloc((FC, S), BF16)
    norm_transpose(C, actT_ar.child(), x_src, g_pre, hT)

    NSL = 2
    wg_s = [A.alloc((KC, 256), BF16) for _ in range(NSL)]
    wu_s = [A.alloc((KC, 256), BF16) for _ in range(NSL)]
    wg_r = [[Reg(), Reg()] for _ in range(NSL)]
    wu_r = [[Reg(), Reg()] for _ in range(NSL)]
    wd_h = [A.alloc((FC, 512), BF16) for _ in range(2)]
    wd_r = [[Reg() for _ in range(FC // 2)] for _ in range(2)]
    sg = [A.alloc((512,), BF16) for _ in range(2)]
    sg_r = [Reg() for _ in range(2)]
    gb = A.alloc((D,), F32)
    gb_r = Reg()
    ss2 = A.alloc((NT,), F32)
    r2 = A.alloc((NT,), F32)
    junk = A.alloc((D,), BF16)
    junk_r = Reg()
    P.dma("sp", gb, g_post.partition_broadcast(128), "g", writes=[gb_r])

    wd_jobs = [(h, c2) for h in range(2) for c2 in range(FC // 2)]

    def load_wd_piece():
        if not wd_jobs:
            return
        h, c2 = wd_jobs.pop(0)
        load_cast(C, wd_h[h][:, 2 * c2:2 * c2 + 2, :],
                  wd[c2 * 256:(c2 + 1) * 256, h * 512:(h + 1) * 512].rearrange("(c p) n -> p c n", p=128),
                  wd_r[h][c2])

    j = 0
    for cb in range(FC // 2):
        s = cb % NSL
        for part in range(2):
            load_cast(C, wg_s[s][:, 4 * part:4 * part + 4, :],
                      wg[part * 512:(part + 1) * 512, cb * 256:(cb + 1) * 256].rearrange("(k p) n -> p k n", p=128),
                      wg_r[s][part])
            load_cast(C, wu_s[s][:, 4 * part:4 * part + 4, :],
                      wu[part * 512:(part + 1) * 512, cb * 256:(cb + 1) * 256].rearrange("(k p) n -> p k n", p=128),
                      wu_r[s][part])
        if cb >= 1:
            for _ in range(3):
                load_wd_piece()
        for sub in range(2):
            ffc = cb * 2 + sub
            for tg in range(4):
                bG = 2 * (j % 4)
                bU = bG + 1
                q = j % 2
                j += 1
                for k in range(KC):
                    P.op("pe", lambda e, k=k, s=s, sub=sub, tg=tg, bG=bG: e.matmul(
                        PS[:, 512 * bG:512 * (bG + 1)], lhsT=wg_s[s][:, k, sub * 128:(sub + 1) * 128],
                        rhs=hT[:, k, tg * 512:(tg + 1) * 512], start=(k == 0), stop=(k == KC - 1)),
                        reads=[wg_r[s][k // 4]], writes=[psr[bG]])
                for k in range(KC):
                    P.op("pe", lambda e, k=k, s=s, sub=sub, tg=tg, bU=bU: e.matmul(
                        PS[:, 512 * bU:512 * (bU + 1)], lhsT=wu_s[s][:, k, sub * 128:(sub + 1) * 128],
                        rhs=hT[:, k, tg * 512:(tg + 1) * 512], start=(k == 0), stop=(k == KC - 1)),
                        reads=[wu_r[s][k // 4]], writes=[psr[bU]])
                P.op("act", lambda e, q=q, bG=bG: e.activation(out=sg[q], in_=PS[:, 512 * bG:512 * (bG + 1)], func=AF.Silu),
                     reads=[psr[bG]], writes=[sg_r[q]])
                P.op("dve", lambda e, q=q, bU=bU, ffc=ffc, tg=tg: e.tensor_tensor(
                    out=actT[:, ffc, tg * 512:(tg + 1) * 512], in0=PS[:, 512 * bU:512 * (bU + 1)], in1=sg[q], op=ALU.mult),
                    reads=[psr[bU], sg_r[q]], writes=[])
    while wd_jobs:
        load_wd_piece()
    P.barrier()
    if stop_after == "B":
        ov = x_dst.rearrange("(a b) d -> a (b d)", a=1024).rearrange("(k p) t -> p k t", p=128)
        for k in range(KC):
            for hh in range(2):
                P.dma("pool", ov[:, k, hh * 1024:(hh + 1) * 1024], actT[:, k + 14, hh * 1024:(hh + 1) * 1024], "dbg")
        return

    Ah = hT_ar.child()
    NSC = int(os.environ.get('NSC', 2))
    NSX = int(os.environ.get('NSX', NSC))
    xc = [Ah.alloc((D,), F32) for _ in range(NSX)]
    tt = [Ah.alloc((D,), F32) for _ in range(NSC)]
    xc_r = [Reg() for _ in range(NSX)]
    tt_r = [Reg() for _ in range(NSC)]
    tth_r = [[Reg(), Reg()] for _ in range(NSC)]
    r2_r = [Reg() for _ in range(NT)]
    for i in range(int(os.environ.get("CT", NT))):
        s = i % NSC
        sx = i % NSX
        P.dma("sp", xc[sx], x_src[i * 128:(i + 1) * 128, :], f"xc{sx}", writes=[xc_r[sx]])
        pb = (i % 4) * 2
        psf = PS[:, 512 * pb:512 * (pb + 2)]
        for h in range(2):
            for ffc in range(FC):
                P.op("pe", lambda e, h=h, ffc=ffc, i=i, pb=pb: e.matmul(
                    PS[:, 512 * (pb + h):512 * (pb + h + 1)], lhsT=actT[:, ffc, i * 128:(i + 1) * 128],
                    rhs=wd_h[h][:, ffc, :], start=(ffc == 0), stop=(ffc == FC - 1)),
                    reads=[wd_r[h][ffc // 2]], writes=[psr[pb + h]])
        P.op("act", lambda e, i=i, psf=psf: e.activation(out=junk, in_=psf, func=AF.Square, accum_out=ss2[:, i:i + 1]),
             reads=[psr[pb], psr[pb + 1]], writes=[r2_r[i], junk_r])
        cstop = int(os.environ.get("CSTOP", 9))
        if cstop == 1:
            P.dma("sp", x_dst[i * 128:(i + 1) * 128, :], xc[sx], f"xo{s}", reads=[xc_r[sx], r2_r[i]])
            continue
        P.op("act", lambda e, i=i: e.activation(out=r2[:, i:i + 1], in_=ss2[:, i:i + 1], func=AF.Ln, scale=1.0 / D, bias=EPS),
             reads=[r2_r[i]], writes=[r2_r[i]])
        P.op("act", lambda e, i=i: e.activation(out=r2[:, i:i + 1], in_=r2[:, i:i + 1], func=AF.Exp, scale=-0.5,
                                                bias=math.log(0.5)),
             reads=[r2_r[i]], writes=[r2_r[i]])
        if cstop == 2:
            P.dma("sp", x_dst[i * 128:(i + 1) * 128, :], xc[sx], f"xo{s}", reads=[xc_r[sx], r2_r[i]])
            continue
        for h in range(2):
            if os.environ.get("DVEVAR") == "copy":
                P.op("dve", lambda e, s=s, h=h, pb=pb: e.tensor_copy(
                    out=tt[s][:, 512 * h:512 * (h + 1)], in_=PS[:, 512 * (pb + h):512 * (pb + h + 1)]),
                    reads=[psr[pb + h], gb_r], writes=[tth_r[s][h]])
                continue
            if os.environ.get("DVEVAR") == "sbuf":
                P.op("dve", lambda e, s=s, h=h, pb=pb: e.tensor_tensor(
                    out=tt[s][:, 512 * h:512 * (h + 1)], in0=xc[sx][:, 512 * h:512 * (h + 1)],
                    in1=gb[:, 512 * h:512 * (h + 1)], op=ALU.mult),
                    reads=[psr[pb + h], gb_r, xc_r[sx]], writes=[tth_r[s][h]])
                continue
            P.op("dve", lambda e, s=s, h=h, pb=pb: e.tensor_tensor(
                out=tt[s][:, 512 * h:512 * (h + 1)], in0=PS[:, 512 * (pb + h):512 * (pb + h + 1)],
                in1=gb[:, 512 * h:512 * (h + 1)], op=ALU.mult),
                reads=[psr[pb + h], gb_r, r2_r[i]], writes=[tth_r[s][h]])
        if cstop == 3:
            P.dma("sp", x_dst[i * 128:(i + 1) * 128, :], tt[s], f"xo{s}", reads=[xc_r[sx], r2_r[i], tth_r[s][0], tth_r[s][1]], writes=[tth_r[s][0], tth_r[s][1]])
            continue
        P.op("dve", lambda e, s=s, sx=sx, i=i: e.scalar_tensor_tensor(out=xc[sx], in0=tt[s], scalar=r2[:, i:i + 1], in1=xc[sx],
                                                              op0=ALU.mult, op1=ALU.add),
             reads=[tth_r[s][0], tth_r[s][1], r2_r[i], xc_r[sx]], writes=[xc_r[sx]])
        P.dma("sp", x_dst[i * 128:(i + 1) * 128, :], xc[sx], f"xo{sx}", reads=[xc_r[sx]])
    P.barrier()


def tok_slice(start, step):
    return slice(start, start + step * 127 + 1, step) if step > 1 else slice(start, start + 128)


def attention_phase(C, A, hT, attT, w_in):
    P, PS, psr = C.P, C.PS, C.psr
    cos = A.alloc((S,), F32)[:32]
    sin = A.alloc((S,), F32)[:32]
    cs_r = Reg()
    cs2_r = Reg()
    P.dma("sp", cos, C.dram["c_cos"], "cs", writes=[cs_r])
    P.dma("sp", sin, C.dram["c_sin"], "cs", writes=[cs2_r])
    wsl = [[A.alloc((KC, 128), BF16) for _ in range(3)] for _ in range(2)]
    wsl_r = [[Reg() for _ in range(3)] for _ in range(2)]
    DBG.clear()
    qk = [[None, None], [None, None]]
    for a_ in range(2):
        for b_ in range(2):
            qk[a_][b_] = A.alloc((S,), BF16)
            DBG[f"qk{a_}{b_}"] = Arena.last
    qk_r = [[[Reg() for _ in range(4)] for _ in range(2)] for _ in range(2)]
    vt = [A.alloc((NT, 128), BF16) for _ in range(2)]
    vt_r = [[Reg() for _ in range(4)] for _ in range(2)]
    acc_n = A.alloc((S,), F32)
    acc_d = A.alloc((S,), F32)
    accn_r = [Reg() for _ in range(4)]
    accd_r = [Reg() for _ in range(4)]
    acc_all = Reg()
    pT = [A.alloc((512,), BF16) for _ in range(4)]
    pT_r = [Reg() for _ in range(4)]
    t1 = [A.alloc((512,), F32)[:32] for _ in range(2)]
    t2 = [A.alloc((512,), F32)[:32] for _ in range(2)]
    t1_r = [Reg() for _ in range(2)]
    t2_r = [Reg() for _ in range(2)]
    rcp = [A.alloc((512,), F32) for _ in range(2)]
    rcp_r = [Reg() for _ in range(2)]
    SCALE = 128.0 ** -0.5

    def bank(b):
        return PS[:, 512 * b:512 * (b + 1)]

    cnt = {"proj": 0, "rot": 0, "s": 0, "p": 0, "nd": 0}
    iters = [(hs, g) for hs in range(4) for g in range(3)]

    def make_blocks(g):
        blocks = []
        if g == 0:
            for b in range(16):
                blocks.append((tok_slice(128 * b, 1), tok_slice(128 * (b - 1), 1) if b > 0 else None))
        elif g == 1:
            for r in range(4):
                for b in range(4):
                    blocks.append((tok_slice(4 * 128 * b + r, 4), tok_slice(4 * 128 * (b - 1) + r, 4) if b > 0 else None))
        else:
            for r in range(16):
                blocks.append((tok_slice(r, 16), None))
        return blocks

    def proj_units(idx):
        hs, g = iters[idx]
        sl = idx % 2
        head = g * 4 + hs
        blocks = make_blocks(g)
        units = []
        pend = []

        def u_load():
            for m, off in enumerate((0, 1536, 3072)):
                c0 = off + head * 128
                load_cast(C, wsl[sl][m], w_in[:, c0:c0 + 128].rearrange("(k p) n -> p k n", p=128), wsl_r[sl][m])
        units.append(u_load)
        for m in range(2):
            for tg in range(4):
                def u_qk(m=m, tg=tg):
                    dst = qk[sl][m]
                    pb = cnt["proj"] % 2
                    cnt["proj"] += 1
                    for k in range(KC):
                        P.op("pe", lambda e, k=k: e.matmul(
                            bank(pb), lhsT=wsl[sl][m][:, k, :], rhs=hT[:, k, tg * 512:(tg + 1) * 512],
                            start=(k == 0), stop=(k == KC - 1)), reads=[wsl_r[sl][m]], writes=[psr[pb]])
                    dcol = dst[:, tg * 512:(tg + 1) * 512]
                    dr = qk_r[sl][m][tg]
                    P.op("act", lambda e: e.activation(out=dcol, in_=bank(pb), func=AF.Copy), reads=[psr[pb]], writes=[dr])
                    def rope_part():
                        rb = 2 + cnt["rot"] % 2
                        ts = cnt["rot"] % 2
                        cnt["rot"] += 1
                        P.op("pe", lambda e: e.matmul(bank(rb)[:32, :], lhsT=C.rm, rhs=dcol[:32, :], start=True, stop=True),
                             reads=[dr, C.cst_r], writes=[psr[rb]])
                        P.op("dve", lambda e: e.tensor_tensor(out=t1[ts], in0=bank(rb)[:32, :], in1=sin[:, tg * 512:(tg + 1) * 512],
                                                              op=ALU.mult), reads=[psr[rb], cs_r, cs2_r], writes=[t1_r[ts]])
                        P.op("dve", lambda e: e.tensor_tensor(out=t2[ts], in0=dcol[:32, :], in1=cos[:, tg * 512:(tg + 1) * 512],
                                                              op=ALU.mult), reads=[dr, cs_r], writes=[t2_r[ts]])
                        P.op("dve", lambda e: e.tensor_tensor(out=dcol[:32, :], in0=t1[ts], in1=t2[ts], op=ALU.add),
                             reads=[t1_r[ts], t2_r[ts]], writes=[dr])
                    if pend:
                        pend.pop(0)()
                    pend.append(rope_part)
                units.append(u_qk)

        def u_flush():
            while pend:
                pend.pop(0)()
        units.append(u_flush)
        for j in range(4):
            def u_v(j=j):
                pb = cnt["proj"] % 2
                cnt["proj"] += 1
                for bi in range(4):
                    qs = blocks[4 * j + bi][0]
                    for k in range(KC):
                        P.op("pe", lambda e, k=k, qs=qs, bi=bi: e.matmul(
                            bank(pb)[:, bi * 128:(bi + 1) * 128], lhsT=hT[:, k, qs], rhs=wsl[sl][2][:, k, :],
                            start=(k == 0), stop=(k == KC - 1)), reads=[wsl_r[sl][2]], writes=[psr[pb]])
                P.op("act", lambda e: e.activation(
                    out=vt[sl][:, 4 * j:4 * j + 4, :], in_=bank(pb).rearrange("p (a b) -> p a b", a=4), func=AF.Copy),
                    reads=[psr[pb]], writes=[vt_r[sl][j]])
            units.append(u_v)
        return units

    def core(idx, fill):
        hs, g = iters[idx]
        sl = idx % 2
        blocks = make_blocks(g)
        qT, kT = qk[sl][0], qk[sl][1]
        qr = qk_r[sl][0] + qk_r[sl][1]
        pairs = [(2 * i, 2 * i + 1) for i in range(8)]

        def do_qk(pair):
            sb = 4 + cnt["s"] % 2
            cnt["s"] += 1
            for ti, blk in enumerate(pair):
                qs, ps_ = blocks[blk]
                for ci, ks in enumerate((ps_, qs)):
                    o = bank(sb)[:, (2 * ti + ci) * 128:(2 * ti + ci + 1) * 128]
                    if ks is None:
                        P.op("pe", lambda e, o=o: e.matmul(o, lhsT=C.ident, rhs=C.maskn, start=True, stop=True),
                             reads=[C.cst_r], writes=[psr[sb]])
                        continue
                    P.op("pe", lambda e, o=o, ks=ks, qs=qs, kT=kT, qT=qT: e.matmul(o, lhsT=kT[:, ks], rhs=qT[:, qs], start=True, stop=False),
                         reads=qr, writes=[psr[sb]])
                    mk = C.maskc if ci == 1 else C.maskp
                    P.op("pe", lambda e, o=o, mk=mk: e.matmul(o, lhsT=C.ident, rhs=mk, start=False, stop=True),
                         reads=[C.cst_r], writes=[psr[sb]])
            pslot = cnt["p"] % 4
            cnt["p"] += 1
            P.op("act", lambda e, sb=sb, pslot=pslot: e.activation(out=pT[pslot], in_=bank(sb), func=AF.Exp, scale=SCALE),
                 reads=[psr[sb]], writes=[pT_r[pslot]])
            return pslot

        def do_pv(pair, pslot):
            nb = 6 + cnt["nd"] % 2
            cnt["nd"] += 1
            for ti, blk in enumerate(pair):
                qs, ps_ = blocks[blk]
                for is_num in (True, False):
                    col = ti * 128 + (0 if is_num else 256)
                    o = bank(nb)[:, col:col + 128]
                    for ci in range(2):
                        kblk = blk if (ci == 1 or ps_ is None) else blk - 1
                        lhs = vt[sl][:, kblk, :] if is_num else C.ones_bf
                        P.op("pe", lambda e, o=o, lhs=lhs, pslot=pslot, ti=ti, ci=ci: e.matmul(
                            o, lhsT=lhs, rhs=pT[pslot][:, (2 * ti + ci) * 128:(2 * ti + ci + 1) * 128],
                            start=(ci == 0), stop=(ci == 1)),
                            reads=[pT_r[pslot], vt_r[sl][kblk // 4], C.cst_r], writes=[psr[nb]])
            pi = pair[0] // 2
            if g == 0:
                sel = slice(256 * pi, 256 * pi + 256)
                dn, dd = acc_n[:, sel], acc_d[:, sel]
                sn, sd = bank(nb)[:, 0:256], bank(nb)[:, 256:512]
            elif g == 1:
                r_, b_ = pair[0] // 4, pair[0] % 4
                st_ = r_ + 512 * b_
                sel = slice(st_, st_ + 4 * 255 + 1, 4)
                dn, dd = acc_n[:, sel], acc_d[:, sel]
                sn, sd = bank(nb)[:, 0:256], bank(nb)[:, 256:512]
            else:
                r_ = pair[0]
                dn = acc_n.rearrange("p (n r) -> p r n", r=16)[:, r_:r_ + 2, :]
                dd = acc_d.rearrange("p (n r) -> p r n", r=16)[:, r_:r_ + 2, :]
                sn = bank(nb)[:, 0:256].rearrange("p (a b) -> p a b", a=2)
                sd = bank(nb)[:, 256:512].rearrange("p (a b) -> p a b", a=2)
            if g == 0:
                P.op("dve", lambda e, dn=dn, sn=sn: e.tensor_copy(out=dn, in_=sn), reads=[psr[nb]], writes=[acc_all])
                P.op("dve", lambda e, dd=dd, sd=sd: e.tensor_copy(out=dd, in_=sd), reads=[psr[nb]], writes=[acc_all])
            else:
                P.op("dve", lambda e, dn=dn, sn=sn: e.tensor_tensor(out=dn, in0=sn, in1=dn, op=ALU.add),
                     reads=[psr[nb], acc_all], writes=[acc_all])
                P.op("dve", lambda e, dd=dd, sd=sd: e.tensor_tensor(out=dd, in0=sd, in1=dd, op=ALU.add),
                     reads=[psr[nb], acc_all], writes=[acc_all])


        prev = None
        for pair in pairs:
            pslot = do_qk(pair)
            if prev is not None:
                do_pv(*prev)
                fill()
            prev = (pair, pslot)
        do_pv(*prev)
        fill()
        if g == 2:
            for tg in range(4):
                rs = tg % 2
                P.op("dve", lambda e, tg=tg, rs=rs: e.reciprocal(out=rcp[rs], in_=acc_d[:, tg * 512:(tg + 1) * 512]),
                     reads=[acc_all], writes=[rcp_r[rs]])
                P.op("dve", lambda e, tg=tg, rs=rs, hs=hs: e.tensor_tensor(
                    out=attT[:, hs, tg * 512:(tg + 1) * 512], in0=acc_n[:, tg * 512:(tg + 1) * 512], in1=rcp[rs], op=ALU.mult),
                    reads=[acc_all, rcp_r[rs]], writes=[C.attT_r])


    for u in proj_units(0):
        u()
    for idx in range(len(iters)):
        nxt = proj_units(idx + 1) if idx + 1 < len(iters) else []
        per = -(-len(nxt) // 8) if nxt else 0
        pos = [0]

        def fill():
            for u in nxt[pos[0]:pos[0] + per]:
                u()
            pos[0] += per
        core(idx, fill)
        for u in nxt[pos[0]:]:
            u()

def mlstm_phase(C, A, hT, mlT, w_in, conv_w, conv_b, i_bias, f_bias, head_g):
    P, PS, psr = C.P, C.PS, C.psr
    OQ, OK_, OV, OO, OI = 4608, 5632, 6656, 7680, 8704

    def bank(b):
        return PS[:, 512 * b:512 * (b + 1)]

    def R():
        return Reg()

    cwb = A.alloc((2048,), F32)[:5]
    cwb_r = R()
    cwb2_r = R()
    P.dma("sp", cwb[0:4, :], conv_w, "mca", writes=[cwb_r])
    P.dma("sp", cwb[4:5, :], conv_b.rearrange("(o n) -> o n", o=1), "mca", writes=[cwb2_r])
    cwT = A.alloc((16, 8), F32)
    ncb = A.alloc((16,), F32)
    cwT_r = R()
    for c in range(16):
        P.op("pe", lambda e, c=c: e.matmul(bank(6)[:, c * 8:c * 8 + 5], lhsT=cwb[:, c * 128:(c + 1) * 128],
                                           rhs=C.identf[:5, :5], start=True, stop=True),
             reads=[cwb_r, cwb2_r, C.cst_r], writes=[psr[6]])
    P.op("dve", lambda e: e.tensor_copy(out=cwT[:, :, 0:5], in_=bank(6)[:, 0:128].rearrange("p (c j) -> p c j", j=8)[:, :, 0:5]),
         reads=[psr[6]], writes=[cwT_r])
    P.op("dve", lambda e: e.tensor_scalar(out=ncb, in0=cwT[:, :, 4], scalar1=-1.0, scalar2=None, op0=ALU.mult),
         reads=[cwT_r], writes=[cwT_r])
    hg = A.alloc((1024,), F32)
    hg_r = R()
    P.dma("sp", hg, head_g.partition_broadcast(128), "mch", writes=[hg_r])
    bias8 = A.alloc((8,), F32)
    b8_r = R()
    b8b_r = R()
    P.dma("sp", bias8[:, 0:4], i_bias.partition_broadcast(128), "mcb", writes=[b8_r])
    P.dma("sp", bias8[:, 4:8], f_bias.partition_broadcast(128), "mcb", writes=[b8b_r])

    wif = A.alloc((KC, 8), BF16)
    wif_r = R()
    load_cast(C, wif, w_in[:, OI:OI + 8].rearrange("(k p) n -> p k n", p=128), wif_r)
    for c in range(NT):
        for k in range(KC):
            P.op("pe", lambda e, c=c, k=k: e.matmul(bank(7)[:, c * 8:(c + 1) * 8], lhsT=hT[:, k, c * 128:(c + 1) * 128],
                                                    rhs=wif[:, k, :], start=(k == 0), stop=(k == KC - 1)),
                 reads=[wif_r], writes=[psr[7]])
    gi = A.alloc((NT, 4), F32)
    lg = A.alloc((NT, 4), F32)
    g_r = R()
    pre3 = bank(7)[:, 0:128].rearrange("p (c j) -> p c j", j=8)
    P.op("dve", lambda e: e.tensor_tensor(out=gi, in0=pre3[:, :, 0:4],
                                          in1=bias8[:, 0:4].unsqueeze(1).to_broadcast([128, NT, 4]), op=ALU.add),
         reads=[psr[7], b8_r, b8b_r], writes=[g_r])
    P.op("dve", lambda e: e.tensor_tensor(out=lg, in0=pre3[:, :, 4:8],
                                          in1=bias8[:, 4:8].unsqueeze(1).to_broadcast([128, NT, 4]), op=ALU.add),
         reads=[psr[7], b8_r, b8b_r, g_r], writes=[g_r])
    P.op("act", lambda e: e.activation(out=lg, in_=lg, func=AF.Exp, scale=-1.0), reads=[g_r], writes=[g_r])
    P.op("act", lambda e: e.activation(out=lg, in_=lg, func=AF.Ln, bias=1.0), reads=[g_r], writes=[g_r])
    lg2 = lg.rearrange("p c h -> p (c h)")
    gi2 = gi.rearrange("p c h -> p (c h)")
    P.op("pe", lambda e: e.matmul(bank(6)[:, 0:64], lhsT=C.tri, rhs=lg2, start=True, stop=True),
         reads=[g_r, C.cst_r], writes=[psr[6]])
    P.op("pe", lambda e: e.matmul(bank(6)[:, 64:128], lhsT=C.ones_f, rhs=lg2, start=True, stop=True),
         reads=[g_r, C.cst_r], writes=[psr[6]])
    e_in = A.alloc((64,), F32)
    e_out = A.alloc((64,), F32)
    e_L = A.alloc((64,), F32)
    e_v = A.alloc((64,), F32)
    e_io = A.alloc((64,), F32)
    ee_r = R()
    P.op("dve", lambda e: e.tensor_tensor(out=e_in, in0=bank(6)[:, 0:64], in1=gi2, op=ALU.add),
         reads=[psr[6], g_r], writes=[ee_r])
    P.op("act", lambda e: e.activation(out=e_in, in_=e_in, func=AF.Exp), reads=[ee_r], writes=[ee_r])
    P.op("act", lambda e: e.activation(out=e_out, in_=bank(6)[:, 0:64], func=AF.Exp, scale=-1.0),
         reads=[psr[6], ee_r], writes=[ee_r])
    P.op("act", lambda e: e.activation(out=e_L, in_=bank(6)[:, 64:128], func=AF.Exp, scale=-1.0),
         reads=[psr[6], ee_r], writes=[ee_r])
    P.op("act", lambda e: e.activation(out=e_io, in_=bank(6)[:, 0:64], func=AF.Exp), reads=[psr[6], ee_r], writes=[ee_r])
    P.op("dve", lambda e: e.tensor_scalar(out=e_in, in0=e_in, scalar1=1.0 / 16.0, scalar2=None, op0=ALU.mult),
         reads=[ee_r], writes=[ee_r])
    P.op("dve", lambda e: e.tensor_tensor(out=e_v, in0=e_in, in1=e_L, op=ALU.mult), reads=[ee_r], writes=[ee_r])

    wq = [A.alloc((KC, 256), BF16) for _ in range(5)]
    wq_r = [[R(), R()] for _ in range(5)]
    raw = [A.alloc((S + 4,), BF16) for _ in range(2)]
    raw_r = [R(), R()]
    for i in range(2):
        P.op("dve", lambda e, i=i: e.memset(raw[i][:, 0:3], 0.0), writes=[raw_r[i]])
    neghalf = A.alloc((1,), F32)
    nh_r = R()
    P.op("dve", lambda e: e.memset(neghalf, -0.5), writes=[nh_r])
    dg = [[A.alloc((128,), BF16) for _ in range(4)] for _ in range(2)]
    dg_r = [R(), R()]
    qT2 = [A.alloc((2, S), BF16) for _ in range(2)]
    kT2 = [A.alloc((2, S), BF16) for _ in range(2)]
    qk2_r = [[[R(), R()], [R(), R()]] for _ in range(2)]
    ktok2 = [A.alloc((256,), BF16) for _ in range(2)]
    ktok2_r = [R(), R()]
    vaug2 = [A.alloc((258,), BF16) for _ in range(2)]
    vaug2_r = [R(), R()]
    for i in range(2):
        P.op("dve", lambda e, i=i: e.memset(vaug2[i][:, 256:257], 1.0), writes=[vaug2_r[i]])
        P.op("dve", lambda e, i=i: e.memset(vaug2[i][:, 257:258], 0.0), writes=[vaug2_r[i]])
    vp2 = [A.alloc((258,), BF16) for _ in range(2)]
    vp2_r = [R(), R()]
    sigo2 = [A.alloc((NT, 256), BF16) for _ in range(2)]
    sigo2_r = [[R() for _ in range(NT)] for _ in range(2)]
    wT2 = [A.alloc((128,), BF16) for _ in range(2)]
    wT2_r = [R(), R()]
    Cf = A.alloc((2, 258), F32)
    Cf_r = R()
    Cbf2 = [A.alloc((2, 258), BF16) for _ in range(2)]
    Cbf2_r = [R(), R()]
    hu3 = [A.alloc((256,), F32) for _ in range(3)]
    hu3_r = [R(), R(), R()]
    sm3 = [A.alloc((8,), F32) for _ in range(3)]
    sm3_r = [R(), R(), R()]
    junk3 = [A.alloc((256,), BF16) for _ in range(3)]
    mlb3 = [A.alloc((256,), BF16) for _ in range(3)]
    mlb3_r = [R(), R(), R()]

    def pre_units(hd):
        sl = hd % 2
        qT, kT, qk_r, sigo, sigo_r = qT2[sl], kT2[sl], qk2_r[sl], sigo2[sl], sigo2_r[sl]
        wsel = (0, 1, 3 + sl, 2)
        units = []
        pending = []

        def u_load():
            for m, off in enumerate((OQ, OK_, OV, OO)):
                c0 = off + hd * 256
                for part in range(2):
                    load_cast(C, wq[wsel[m]][:, 4 * part:4 * part + 4, :],
                              w_in[part * 512:(part + 1) * 512, c0:c0 + 256].rearrange("(k p) n -> p k n", p=128),
                              wq_r[wsel[m]][part])
        units.append(u_load)
        for m, dstT in ((0, qT), (1, kT)):
            for cc in range(2):
                ch = m * 8 + hd * 2 + cc
                bi = (m * 2 + cc) % 2

                def u_diag(ch=ch, bi=bi):
                    for j in range(4):
                        P.op("dve", lambda e, j=j: e.tensor_scalar(out=dg[bi][j], in0=C.ident, scalar1=cwT[:, ch, j:j + 1],
                                                                   scalar2=None, op0=ALU.mult),
                             reads=[C.cst_r, cwT_r], writes=[dg_r[bi]])
                units.append(u_diag)
                for tg in range(4):
                    def u_proj(m=m, cc=cc, tg=tg, ch=ch, bi=bi, dstT=dstT):
                        for k in range(KC):
                            P.op("pe", lambda e, k=k: e.matmul(
                                bank(6), lhsT=wq[m][:, k, cc * 128:(cc + 1) * 128], rhs=hT[:, k, tg * 512:(tg + 1) * 512],
                                start=(k == 0), stop=(k == KC - 1)), reads=[wq_r[m][k // 4]], writes=[psr[6]])
                        P.op("act", lambda e: e.activation(
                            out=raw[bi][:, 3 + tg * 512:3 + (tg + 1) * 512], in_=bank(6), func=AF.Copy),
                            reads=[psr[6]], writes=[raw_r[bi]])
                        def conv_part():
                            for j in range(4):
                                P.op("pe", lambda e, j=j: e.matmul(bank(7), lhsT=dg[bi][j],
                                                                   rhs=raw[bi][:, tg * 512 + j:tg * 512 + j + 512],
                                                                   start=(j == 0), stop=(j == 3)),
                                     reads=[dg_r[bi], raw_r[bi]], writes=[psr[7]])
                            P.op("act", lambda e: e.activation(out=dstT[:, cc, tg * 512:(tg + 1) * 512], in_=bank(7), func=AF.Silu,
                                                               bias=cwT[:, ch, 4:5]),
                                 reads=[psr[7], cwT_r], writes=[qk_r[m][cc]])
                        if pending:
                            pending.pop(0)()
                        pending.append(conv_part)
                    units.append(u_proj)
        def u_flush():
            while pending:
                pending.pop(0)()
        units.append(u_flush)
        for c in range(NT):
            def u_gate(c=c):
                cs = slice(c * 128, (c + 1) * 128)
                ob = 6 + c % 2
                for k in range(KC):
                    P.op("pe", lambda e, k=k: e.matmul(bank(ob)[:, 0:256], lhsT=hT[:, k, cs], rhs=wq[2][:, k, :],
                                                       start=(k == 0), stop=(k == KC - 1)),
                         reads=[wq_r[2][k // 4]], writes=[psr[ob]])
                P.op("act", lambda e: e.activation(out=sigo[:, c, :], in_=bank(ob)[:, 0:256], func=AF.Sigmoid),
                     reads=[psr[ob]], writes=[sigo_r[c]])
            units.append(u_gate)
        return units

    def loop_steps(hd):
        sl = hd % 2
        qT, kT, qk_r, sigo, sigo_r = qT2[sl], kT2[sl], qk2_r[sl], sigo2[sl], sigo2_r[sl]
        wv, wv_r = wq[3 + sl], wq_r[3 + sl]
        qkr = [qk_r[0][0], qk_r[0][1], qk_r[1][0], qk_r[1][1]]
        def chunk_vars(c):
            p = c % 2
            q3 = c % 3
            return dict(cs=slice(c * 128, (c + 1) * 128), col=c * 4 + hd, ktok=ktok2[p], ktok_r=ktok2_r[p], vaug=vaug2[p],
                        vaug_r=vaug2_r[p], vp=vp2[p], vp_r=vp2_r[p], wT=wT2[p], wT_r=wT2_r[p], hu=hu3[q3], hu_r=hu3_r[q3],
                        sm=sm3[q3], sm_r=sm3_r[q3], junk=junk3[q3], mlb=mlb3[q3], mlb_r=mlb3_r[q3], ab=p, db=2 + p, tb=p)

        def stageA1(c):
            v_ = chunk_vars(c)
            cs, col, ktok, ktok_r, vaug, vaug_r, vp, vp_r, wT, wT_r, ab, db = (v_[k] for k in (
                "cs", "col", "ktok", "ktok_r", "vaug", "vaug_r", "vp", "vp_r", "wT", "wT_r", "ab", "db"))
            abf = bank(ab).bitcast(BF16)
            for k in range(KC):
                P.op("pe", lambda e, k=k: e.matmul(bank(ab)[:, 0:256], lhsT=hT[:, k, cs], rhs=wv[:, k, :],
                                                   start=(k == 0), stop=(k == KC - 1)),
                     reads=[wv_r[k // 4]], writes=[psr[ab]])
            for dk in range(2):
                P.op("pe", lambda e, dk=dk: e.transpose(out=abf[:, 512 + dk * 128:512 + (dk + 1) * 128],
                                                        in_=kT[:, dk, cs], identity=C.ident),
                     reads=[qk_r[1][dk], C.cst_r], writes=[psr[ab]])
            P.op("act", lambda e: e.activation(out=vaug[:, 0:256], in_=bank(ab)[:, 0:256], func=AF.Copy),
                 reads=[psr[ab]], writes=[vaug_r])
            P.op("act", lambda e: e.activation(out=ktok, in_=abf[:, 512:768], func=AF.Copy),
                 reads=[psr[ab]], writes=[ktok_r])
            for dk in range(2):
                P.op("pe", lambda e, dk=dk: e.matmul(bank(db)[:, 384:512], lhsT=kT[:, dk, cs], rhs=qT[:, dk, cs],
                                                     start=(dk == 0), stop=(dk == 1)),
                     reads=qkr, writes=[psr[db]])
            P.op("dve", lambda e: e.scalar_tensor_tensor(
                out=wT, in0=bank(db)[:, 384:512], scalar=e_in[:, col:col + 1], in1=C.tri, op0=ALU.mult, op1=ALU.mult),
                reads=[psr[db], ee_r, C.cst_r], writes=[wT_r])
            if c < NT - 1:
                P.op("dve", lambda e: e.tensor_scalar(out=vp, in0=vaug, scalar1=e_v[:, col:col + 1], scalar2=None, op0=ALU.mult),
                     reads=[vaug_r, ee_r], writes=[vp_r])

        def stageA2(c):
            v_ = chunk_vars(c)
            cs, col, ktok, ktok_r, vaug, vaug_r, vp, vp_r, wT, wT_r, db = (v_[k] for k in (
                "cs", "col", "ktok", "ktok_r", "vaug", "vaug_r", "vp", "vp_r", "wT", "wT_r", "db"))
            if c < NT - 1:
                for dk in range(2):
                    P.op("pe", lambda e, dk=dk: e.matmul(bank(4 + dk)[:, 0:258], lhsT=ktok[:, dk * 128:(dk + 1) * 128], rhs=vp,
                                                         start=True, stop=True),
                         reads=[ktok_r, vp_r], writes=[psr[4 + dk]])
            P.op("pe", lambda e: e.matmul(bank(db)[:, 0:258], lhsT=wT, rhs=vaug, start=True, stop=(c == 0)),
                 reads=[wT_r, vaug_r], writes=[psr[db]])
            if c > 0:
                cprev, cprev_r = Cbf2[(c - 1) % 2], Cbf2_r[(c - 1) % 2]
                for dk in range(2):
                    P.op("pe", lambda e, dk=dk: e.matmul(bank(db)[:, 0:258], lhsT=qT[:, dk, cs], rhs=cprev[:, dk, :],
                                                         start=False, stop=(dk == 1)),
                         reads=qkr + [cprev_r], writes=[psr[db]])
            if c < NT - 1:
                dC = PS[:, 2048:3072].rearrange("p (a b) -> p a b", a=2)[:, :, 0:258]
                if c == 0:
                    P.op("dve", lambda e: e.tensor_copy(out=Cf, in_=dC), reads=[psr[4], psr[5]], writes=[Cf_r])
                else:
                    P.op("dve", lambda e: e.scalar_tensor_tensor(out=Cf, in0=Cf, scalar=e_L[:, col:col + 1], in1=dC,
                                                                 op0=ALU.mult, op1=ALU.add),
                         reads=[psr[4], psr[5], Cf_r, ee_r], writes=[Cf_r])
                ccur, ccur_r = Cbf2[c % 2], Cbf2_r[c % 2]
                P.op("act", lambda e: e.activation(out=ccur, in_=Cf, func=AF.Copy), reads=[Cf_r], writes=[ccur_r])

        def stageB(c):
            v_ = chunk_vars(c)
            col, hu, hu_r, sm, sm_r, junk, db = (v_[k] for k in ("col", "hu", "hu_r", "sm", "sm_r", "junk", "db"))
            P.op("dve", lambda e, col=col, sm=sm, db=db: e.tensor_scalar(out=sm[:, 0:1], in0=bank(db)[:, 256:257],
                                                                       scalar1=e_io[:, col:col + 1], scalar2=None, op0=ALU.max),
                 reads=[psr[db], ee_r], writes=[sm_r])
            P.op("dve", lambda e, sm=sm, db=db: e.scalar_tensor_tensor(out=sm[:, 1:2], in0=bank(db)[:, 256:257], scalar=-1.0,
                                                                     in1=sm[:, 0:1], op0=ALU.mult, op1=ALU.max),
                 reads=[psr[db], sm_r], writes=[sm_r])
            P.op("dve", lambda e, sm=sm: e.reciprocal(out=sm[:, 3:4], in_=sm[:, 1:2]), reads=[sm_r], writes=[sm_r])
            P.op("dve", lambda e, sm=sm, hu=hu, db=db: e.tensor_scalar(out=hu, in0=bank(db)[:, 0:256], scalar1=sm[:, 3:4], scalar2=None,
                                                                     op0=ALU.mult),
                 reads=[psr[db], sm_r], writes=[hu_r])
            P.op("act", lambda e, sm=sm, hu=hu, junk=junk: e.activation(out=junk, in_=hu, func=AF.Square, accum_out=sm[:, 4:5]),
                 reads=[hu_r, sm_r], writes=[sm_r])
            P.op("pool", lambda e, sm=sm: e.tensor_scalar(out=sm[:, 5:6], in0=sm[:, 4:5], scalar1=1.0 / 256, scalar2=EPS,
                                                          op0=ALU.mult, op1=ALU.add), reads=[sm_r], writes=[sm_r])
            P.op("pool", lambda e, sm=sm: e.tensor_tensor(out=sm[:, 5:6], in0=sm[:, 5:6], in1=neghalf, op=ALU.pow),
                 reads=[sm_r, nh_r], writes=[sm_r])

        def stageC(c):
            v_ = chunk_vars(c)
            cs, hu, hu_r, sm, sm_r, mlb, mlb_r, tb = (v_[k] for k in ("cs", "hu", "hu_r", "sm", "sm_r", "mlb", "mlb_r", "tb"))
            abf = bank(tb).bitcast(BF16)
            ab = tb
            P.op("dve", lambda e, hd=hd, sm=sm, hu=hu: e.scalar_tensor_tensor(out=hu, in0=hu, scalar=sm[:, 5:6],
                                                                              in1=hg[:, hd * 256:(hd + 1) * 256],
                                                                              op0=ALU.mult, op1=ALU.mult),
                 reads=[hu_r, sm_r, hg_r], writes=[hu_r])
            P.op("dve", lambda e, c=c, hu=hu, mlb=mlb: e.tensor_tensor(out=mlb, in0=hu, in1=sigo[:, c, :], op=ALU.mult),
                 reads=[hu_r, sigo_r[c]], writes=[mlb_r])
            for j in range(2):
                P.op("pe", lambda e, j=j, abf=abf, mlb=mlb: e.transpose(out=abf[:, 768 + j * 128:768 + (j + 1) * 128],
                                                                       in_=mlb[:, j * 128:(j + 1) * 128], identity=C.ident),
                     reads=[mlb_r, C.cst_r], writes=[psr[ab]])
            P.op("act", lambda e, hd=hd, cs=cs, abf=abf: e.activation(
                out=mlT[:, 2 * hd:2 * hd + 2, cs], in_=abf[:, 768:1024].rearrange("p (j n) -> p j n", j=2), func=AF.Copy),
                reads=[psr[ab]], writes=[C.mlT_r])


        steps = []
        for step in range(NT + 2):
            def st(fill, step=step):
                if step == 0:
                    stageA1(0)
                if step + 1 < NT:
                    stageA1(step + 1)
                fill()
                if step < NT:
                    stageA2(step)
                fill()
                if 0 <= step - 1 < NT:
                    stageB(step - 1)
                fill()
                if 0 <= step - 2 < NT:
                    stageC(step - 2)
            steps.append(st)
        return steps

    for u in pre_units(0):
        u()
    NH = int(os.environ.get("NH", 4))
    for hd in range(NH):
        nxt = pre_units(hd + 1) if hd < NH - 1 else []
        steps = loop_steps(hd) if not os.environ.get("NOLOOP") else []
        if not steps:
            for u in nxt:
                u()
            continue
        nfill = 3 * len(steps)
        per = -(-len(nxt) // nfill) if nxt else 0
        pos = [0]

        def fill():
            for u in nxt[pos[0]:pos[0] + per]:
                u()
            pos[0] += per
        for st in steps:
            st(fill)
        for u in nxt[pos[0]:]:
            u()

def merge_phase(C, A, hT, attT, mlT, w_in, w_a, w_m, w_out, g_post, x_src, x_dst):
    P, PS, psr = C.P, C.PS, C.psr
    OGA, OGM = 8712, 9736

    def bank(b):
        return PS[:, 512 * b:512 * (b + 1)]

    mgT = A.alloc((KC, S), BF16)
    wa = A.alloc((4, 1024), BF16)
    wa_r = [Reg() for _ in range(4)]
    wm = A.alloc((KC, 1024), BF16)
    wm_r = [Reg() for _ in range(KC)]
    wo = A.alloc((KC, 1024), BF16)
    wo_r = [Reg() for _ in range(KC)]
    wg = [[A.alloc((KC, 128), BF16) for _ in range(2)] for _ in range(2)]
    wg_r = [[Reg() for _ in range(2)] for _ in range(2)]
    load_cast(C, wg[0][0], w_in[:, OGA:OGA + 128].rearrange("(k p) n -> p k n", p=128), wg_r[0][0])
    load_cast(C, wg[0][1], w_in[:, OGM:OGM + 128].rearrange("(k p) n -> p k n", p=128), wg_r[0][1])
    for k in range(4):
        load_cast(C, wa[:, k, :], w_a[k * 128:(k + 1) * 128, :], wa_r[k])
    for k in range(KC):
        load_cast(C, wm[:, k, :], w_m[k * 128:(k + 1) * 128, :], wm_r[k])
    ga = [A.alloc((512,), F32) for _ in range(2)]
    ga_r = [Reg() for _ in range(2)]
    gm = [A.alloc((512,), F32) for _ in range(2)]
    gm_r = [Reg() for _ in range(2)]
    ta = [A.alloc((512,), F32) for _ in range(2)]
    ta_r = [Reg() for _ in range(2)]
    gb = A.alloc((D,), F32)
    gb_r = Reg()
    P.dma("sp", gb, g_post.partition_broadcast(128), "g", writes=[gb_r])
    j = 0
    for mc in range(KC):
        sl = mc % 2
        if mc > 0:
            load_cast(C, wg[sl][0], w_in[:, OGA + mc * 128:OGA + (mc + 1) * 128].rearrange("(k p) n -> p k n", p=128), wg_r[sl][0])
            load_cast(C, wg[sl][1], w_in[:, OGM + mc * 128:OGM + (mc + 1) * 128].rearrange("(k p) n -> p k n", p=128), wg_r[sl][1])
        if mc == 0:
            for k in range(KC):
                load_cast(C, wo[:, k, :], w_out[k * 128:(k + 1) * 128, :], wo_r[k])
        for tg in range(4):
            q = j % 2
            j += 1
            ts = slice(tg * 512, (tg + 1) * 512)
            b0 = 4 * q
            for gi_, (gbuf, gr) in enumerate(((ga, ga_r), (gm, gm_r))):
                for k in range(KC):
                    P.op("pe", lambda e, k=k, gi_=gi_, sl=sl, ts=ts, b0=b0: e.matmul(
                        bank(b0 + gi_), lhsT=wg[sl][gi_][:, k, :], rhs=hT[:, k, ts], start=(k == 0), stop=(k == KC - 1)),
                        reads=[wg_r[sl][gi_]], writes=[psr[b0 + gi_]])
                P.op("act", lambda e, gbuf=gbuf, q=q, gi_=gi_, b0=b0: e.activation(out=gbuf[q], in_=bank(b0 + gi_), func=AF.Sigmoid),
                     reads=[psr[b0 + gi_]], writes=[gr[q]])
            for k in range(4):
                P.op("pe", lambda e, k=k, mc=mc, ts=ts, b0=b0: e.matmul(
                    bank(b0 + 2), lhsT=wa[:, k, mc * 128:(mc + 1) * 128], rhs=attT[:, k, ts], start=(k == 0), stop=(k == 3)),
                    reads=[wa_r[k], C.attT_r], writes=[psr[b0 + 2]])
            for k in range(KC):
                P.op("pe", lambda e, k=k, mc=mc, ts=ts, b0=b0: e.matmul(
                    bank(b0 + 3), lhsT=wm[:, k, mc * 128:(mc + 1) * 128], rhs=mlT[:, k, ts], start=(k == 0), stop=(k == KC - 1)),
                    reads=[wm_r[k], C.mlT_r], writes=[psr[b0 + 3]])
            P.op("dve", lambda e, q=q, b0=b0: e.tensor_tensor(out=ta[q], in0=bank(b0 + 2), in1=ga[q], op=ALU.mult),
                 reads=[psr[b0 + 2], ga_r[q]], writes=[ta_r[q]])
            P.op("dve", lambda e, q=q, b0=b0: e.tensor_tensor(out=gm[q], in0=bank(b0 + 3), in1=gm[q], op=ALU.mult),
                 reads=[psr[b0 + 3], gm_r[q]], writes=[gm_r[q]])
            P.op("dve", lambda e, q=q, mc=mc, ts=ts: e.tensor_tensor(out=mgT[:, mc, ts], in0=ta[q], in1=gm[q], op=ALU.add),
                 reads=[ta_r[q], gm_r[q]], writes=[C.mg_r])
    xc = [A.alloc((D,), F32) for _ in range(2)]
    tt = [A.alloc((D,), F32) for _ in range(2)]
    xc_r = [Reg() for _ in range(2)]
    tth_r = [[Reg(), Reg()] for _ in range(2)]
    ss2 = A.alloc((NT,), F32)
    r2 = A.alloc((NT,), F32)
    junk = ga[0].bitcast(BF16)
    junk_r = ga_r[0]
    r2_r = [Reg() for _ in range(NT)]
    for i in range(NT):
        s = i % 2
        P.dma("sp", xc[s], x_src[i * 128:(i + 1) * 128, :], f"xc{s}", writes=[xc_r[s]])
        pb = (i % 4) * 2
        psf = PS[:, 512 * pb:512 * (pb + 2)]
        for h in range(2):
            for k in range(KC):
                P.op("pe", lambda e, h=h, k=k, i=i, pb=pb: e.matmul(
                    bank(pb + h), lhsT=mgT[:, k, i * 128:(i + 1) * 128], rhs=wo[:, k, h * 512:(h + 1) * 512],
                    start=(k == 0), stop=(k == KC - 1)), reads=[wo_r[k], C.mg_r], writes=[psr[pb + h]])
        P.op("act", lambda e, i=i, psf=psf: e.activation(out=junk, in_=psf, func=AF.Square, accum_out=ss2[:, i:i + 1]),
             reads=[psr[pb], psr[pb + 1]], writes=[r2_r[i], junk_r])
        P.op("act", lambda e, i=i: e.activation(out=r2[:, i:i + 1], in_=ss2[:, i:i + 1], func=AF.Ln, scale=1.0 / D, bias=EPS),
             reads=[r2_r[i]], writes=[r2_r[i]])
        P.op("act", lambda e, i=i: e.activation(out=r2[:, i:i + 1], in_=r2[:, i:i + 1], func=AF.Exp, scale=-0.5),
             reads=[r2_r[i]], writes=[r2_r[i]])
        for h in range(2):
            P.op("dve", lambda e, s=s, h=h, pb=pb: e.tensor_tensor(
                out=tt[s][:, 512 * h:512 * (h + 1)], in0=bank(pb + h), in1=gb[:, 512 * h:512 * (h + 1)], op=ALU.mult),
                reads=[psr[pb + h], gb_r, r2_r[i]], writes=[tth_r[s][h]])
        P.op("dve", lambda e, s=s, i=i: e.scalar_tensor_tensor(out=xc[s], in0=tt[s], scalar=r2[:, i:i + 1], in1=xc[s],
                                                              op0=ALU.mult, op1=ALU.add),
             reads=[tth_r[s][0], tth_r[s][1], r2_r[i], xc_r[s]], writes=[xc_r[s]])
        P.dma("sp", x_dst[i * 128:(i + 1) * 128, :], xc[s], f"xo{s}", reads=[xc_r[s]])


def mixer_phase(C, x_src, x_dst, W):
    A = C.A.child()
    hT = A.alloc((KC, S), BF16)
    attT = A.alloc((4, S), BF16)
    mlT = A.alloc((8, S), BF16)
    C.attT_r = Reg()
    C.mlT_r = Reg()
    C.mg_r = Reg()
    base = A.off
    end = A.end
    norm_transpose(C, Arena(A.ap, base, end), x_src, W["mix_pre_g"][0], hT)
    attention_phase(C, Arena(A.ap, base, end), hT, attT, W["w_in"][0])
    C.P.barrier()
    mlstm_phase(C, Arena(A.ap, base, end), hT, mlT, W["w_in"][0], W["conv_w"][0], W["conv_b"][0],
                W["mlstm_i_bias"][0], W["mlstm_f_bias"][0], W["mlstm_head_g"][0])
    C.P.barrier()
    merge_phase(C, Arena(A.ap, base, end), hT, attT, mlT, W["w_in"][0], W["w_att_branch"][0], W["w_mlstm_branch"][0],
                W["w_out"][0], W["mix_post_g"][0], x_src, x_dst)
    C.P.barrier()

def host_consts():
    c = {}
    bf = ml_dtypes.bfloat16
    c["ident"] = np.eye(128, dtype=np.float32).astype(bf)
    half = 16
    inv_freq = np.power(np.float32(500000.0), -(np.arange(half, dtype=np.float32) * 2.0 / 32)).astype(np.float32)
    ang = np.arange(S, dtype=np.float32)[None, :] * inv_freq[:, None]
    c["cos"] = np.concatenate([np.cos(ang), np.cos(ang)], 0).astype(np.float32)
    c["sin"] = np.concatenate([np.sin(ang), np.sin(ang)], 0).astype(np.float32)
    rm = np.zeros((32, 32), np.float32)
    for j in range(16):
        rm[16 + j, j] = -1.0
        rm[j, 16 + j] = 1.0
    c["rm"] = rm.astype(bf)
    jj = np.arange(128)[:, None]
    ii = np.arange(128)[None, :]
    NEG = -30000.0
    c["maskc"] = np.where(jj <= ii, 0.0, NEG).astype(bf)
    c["maskp"] = np.where(jj >= ii, 0.0, NEG).astype(bf)
    c["maskn"] = np.full((128, 128), NEG, np.float32).astype(bf)
    c["tri"] = (jj <= ii).astype(np.float32)
    c["ones_f"] = np.ones((128, 128), np.float32)
    c["identf"] = np.eye(128, dtype=np.float32)
    c["ones_bf"] = np.ones((128, 128), np.float32).astype(bf)
    return c


def build(stage="full"):
    nc = bass.Bass("TRN2", target_bir_lowering=False)

    def din(name, shape, dt=F32):
        return nc.dram_tensor(name, list(shape), dt, kind="ExternalInput").ap()

    x = din("x", [S, D])
    W = {}
    for name, shape in [("ffn1_pre_g", [1, D]), ("ffn1_w_gate", [1, D, FF]), ("ffn1_w_up", [1, D, FF]),
                        ("ffn1_w_down", [1, FF, D]), ("ffn1_post_g", [1, D]), ("mix_pre_g", [1, D]),
                        ("w_in", [1, D, IN_W]), ("conv_w", [1, 4, 2048]), ("conv_b", [1, 2048]),
                        ("mlstm_i_bias", [1, 4]), ("mlstm_f_bias", [1, 4]), ("mlstm_head_g", [1, 1024]),
                        ("w_att_branch", [1, 512, D]), ("w_mlstm_branch", [1, D, D]), ("w_out", [1, D, D]),
                        ("mix_post_g", [1, D]), ("ffn2_pre_g", [1, D]), ("ffn2_w_gate", [1, D, FF]),
                        ("ffn2_w_up", [1, D, FF]), ("ffn2_w_down", [1, FF, D]), ("ffn2_post_g", [1, D])]:
        W[name] = din(name, shape)
    CD = {}
    for name, shape, dt in [("c_ident", [128, 128], BF16), ("c_cos", [32, S], F32), ("c_sin", [32, S], F32),
                            ("c_rm", [32, 32], BF16), ("c_maskc", [128, 128], BF16), ("c_maskp", [128, 128], BF16),
                            ("c_maskn", [128, 128], BF16), ("c_tri", [128, 128], F32), ("c_ones_f", [128, 128], F32), ("c_identf", [128, 128], F32),
                            ("c_ones_bf", [128, 128], BF16)]:
        CD[name] = din(name, shape, dt)
    c_ident = CD["c_ident"]
    out = nc.dram_tensor("out", [S, D], F32, kind="ExternalOutput").ap()
    x1 = nc.dram_tensor("x1", [S, D], F32, kind="Internal").ap()
    x2 = nc.dram_tensor("x2", [S, D], F32, kind="Internal").ap()

    with ExitStack() as st:
        ARENA_BYTES = 212480
        arena = st.enter_context(nc.sbuf_tensor("arena", [128, ARENA_BYTES // 2], BF16))
        PS = st.enter_context(nc.psum_tensor("ps", [128, 4096], F32))
        C = Ctx()
        C.nc = nc
        C.P = P = Prog(nc)
        C.PS = PS
        C.psr = [Reg(f"ps{i}", excl=True) for i in range(8)]
        top = Arena(arena, 0, ARENA_BYTES)
        C.ident = top.alloc((128,), BF16)
        C.ident_r = Reg()
        P.dma("sp", C.ident, c_ident, "cst", writes=[C.ident_r])
        C.dram = CD
        C.cst_r = C.ident_r
        for nm, shp, dt in [("rm", (32,), BF16), ("maskc", (128,), BF16), ("maskp", (128,), BF16), ("maskn", (128,), BF16),
                            ("tri", (128,), F32), ("ones_f", (128,), F32), ("identf", (128,), F32), ("ones_bf", (128,), BF16)]:
            v = top.alloc(shp, dt)
            if nm == "rm":
                v = v[:32]
            setattr(C, nm, v)
            P.dma("sp", v, CD["c_" + nm], "cst", writes=[C.cst_r])
        C.stage = [top.alloc((1024,), F32) for _ in range(NST)]
        C.stage_r = [Reg() for _ in range(NST)]
        C.stage_i = 0
        C.A = Arena(arena, top.off, ARENA_BYTES)

        if stage == "attn":
            A = C.A.child()
            hT = A.alloc((KC, S), BF16)
            attT = A.alloc((4, S), BF16)
            C.attT_r = Reg()
            mk = Arena(arena, A.off, ARENA_BYTES)
            norm_transpose(C, mk, x, W["mix_pre_g"][0], hT)
            attention_phase(C, Arena(arena, A.off, ARENA_BYTES), hT, attT, W["w_in"][0])
            P.barrier()
            ov = out.rearrange("(a b) d -> a (b d)", a=1024).rearrange("(k p) t -> p k t", p=128)
            for k in range(4):
                for hh in range(2):
                    P.dma("pool", ov[:, k, hh * 1024:(hh + 1) * 1024], attT[:, k, hh * 1024:(hh + 1) * 1024], "dbg")
        if stage == "full":
            ffn_phase(C, x, x1, W["ffn1_pre_g"][0], W["ffn1_w_gate"][0], W["ffn1_w_up"][0], W["ffn1_w_down"][0],
                      W["ffn1_post_g"][0])
            mixer_phase(C, x1, x2, W)
            ffn_phase(C, x2, out, W["ffn2_pre_g"][0], W["ffn2_w_gate"][0], W["ffn2_w_up"][0], W["ffn2_w_down"][0],
                      W["ffn2_post_g"][0])
        if stage == "mix":
            mixer_phase(C, x, out, W)
        if stage == "ml":
            A = C.A.child()
            hT = A.alloc((KC, S), BF16)
            mlT = A.alloc((8, S), BF16)
            C.mlT_r = Reg()
            mk = Arena(arena, A.off, ARENA_BYTES)
            norm_transpose(C, mk, x, W["mix_pre_g"][0], hT)
            mlstm_phase(C, Arena(arena, A.off, ARENA_BYTES), hT, mlT, W["w_in"][0], W["conv_w"][0], W["conv_b"][0],
                        W["mlstm_i_bias"][0], W["mlstm_f_bias"][0], W["mlstm_head_g"][0])
            P.barrier()
            ov = out.rearrange("(a b) d -> a (b d)", a=1024).rearrange("(k p) t -> p k t", p=128)
            for k in range(8):
                for hh in range(2):
                    P.dma("pool", ov[:, k, hh * 1024:(hh + 1) * 1024], mlT[:, k, hh * 1024:(hh + 1) * 1024], "dbg")
        if stage == "ffn1a":
            A = C.A.child()
            hT_ar = A.sub(KC * S * 2)
            actT_ar = A.sub(FC * S * 2)
            hT = hT_ar.child().alloc((KC, S), BF16)
            norm_transpose(C, actT_ar.child(), x, W["ffn1_pre_g"][0], hT)
            ov = out.rearrange("(a b) d -> a (b d)", a=1024).rearrange("(k p) t -> p k t", p=128)
            for k in range(KC):
                for hh in range(2):
                    P.dma("pool", ov[:, k, hh * 1024:(hh + 1) * 1024], hT[:, k, hh * 1024:(hh + 1) * 1024], "dbg")
        if stage == "ffn1b":
            ffn_phase(C, x, out, W["ffn1_pre_g"][0], W["ffn1_w_gate"][0], W["ffn1_w_up"][0], W["ffn1_w_down"][0],
                      W["ffn1_post_g"][0], stop_after="B")
        if stage == "ffn1":
            ffn_phase(C, x, out, W["ffn1_pre_g"][0], W["ffn1_w_gate"][0], W["ffn1_w_up"][0], W["ffn1_w_down"][0],
                      W["ffn1_post_g"][0])
        P.finish()
        P.emit()
        print("prog stats", P.stats, "sems", len(P.dma_tot) + 5)
        if os.environ.get("DUMP"):
            for e in ("sp", "dve", "act"):
                print("====", e)
                for r in P.dump[e][-int(os.environ["DUMP"]):]:
                    print(r)
    return nc


_NC_CACHE = {}


def kernel(**inputs):
    stage = inputs.pop("_stage", os.environ.get("KSTAGE", "full"))
    if stage not in _NC_CACHE:
        _NC_CACHE[stage] = build(stage)
    nc = _NC_CACHE[stage]
    consts = host_consts()
    xfull = np.ascontiguousarray(inputs["x"], dtype=np.float32)
    shared = {k: np.ascontiguousarray(v, dtype=np.float32) for k, v in inputs.items() if k != "x"}
    for k, v in consts.items():
        shared["c_" + k] = v
    in_maps = []
    ncores = int(os.environ.get("NCORES", 8))
    for b in range(ncores):
        m = dict(shared)
        m["x"] = xfull[b]
        in_maps.append(m)
    res = run_bass_kernel_spmd(nc, in_maps, core_ids=list(range(ncores)))
    return np.stack([r["out"] for r in res.results], axis=0)
```

```python
from contextlib import ExitStack
import math
import os

import numpy as np
import ml_dtypes
import concourse.bass as bass
import concourse.mybir as mybir
from concourse.bass_utils import run_bass_kernel_spmd

F32 = mybir.dt.float32
BF16 = mybir.dt.bfloat16
AF = mybir.ActivationFunctionType
ALU = mybir.AluOpType
AX = mybir.AxisListType

S = 2048
D = 1024
FF = 2816
NT = S // 128
KC = D // 128
FC = FF // 128
IN_W = 10760
EPS = 1e-6
ENGS = ("pe", "act", "dve", "pool", "sp")


class Reg:
    __slots__ = ("name", "w", "rs", "rd", "excl")

    def __init__(self, name="", excl=False):
        self.name = name
        self.excl = excl
        self.w = None
        self.rs = {}
        self.rd = []


class Ins:
    __slots__ = ("eng", "fn", "deps", "signal", "val", "dma", "key")

    def __init__(self, eng, fn, dma=False, key=None):
        self.eng = eng
        self.fn = fn
        self.deps = ()
        self.signal = dma
        self.val = 0
        self.dma = dma
        self.key = key


class Prog:
    def __init__(self, nc):
        self.nc = nc
        self.engs = {e: [] for e in ENGS}
        self.dma_tot = {}
        self.dma_last = {}

    def _add(self, ins, reads, writes):
        eng = ins.eng
        deps = {}
        for r in reads:
            d = r.w
            if d is not None:
                deps[id(d)] = d
            if r.excl:
                for e2, x in r.rs.items():
                    if e2 != eng:
                        deps[id(x)] = x
        for w in writes:
            d = w.w
            if d is not None:
                deps[id(d)] = d
            for e2, x in w.rs.items():
                if (not ins.dma) and e2 == eng:
                    continue
                deps[id(x)] = x
            for x in w.rd:
                deps[id(x)] = x
        out = []
        for d in deps.values():
            if d is ins:
                continue
            if (not d.dma) and (not ins.dma) and d.eng == "pe" and eng == "pe":
                continue
            d.signal = True
            out.append(d)
        ins.deps = out
        for r in reads:
            if ins.dma:
                r.rd.append(ins)
            else:
                r.rs[eng] = ins
        for w in writes:
            w.w = ins
            w.rs = {}
            w.rd = []
        self.engs[eng].append(ins)
        return ins

    def op(self, eng, fn, reads=(), writes=()):
        return self._add(Ins(eng, fn), reads, writes)

    def dma(self, queue, out, in_, key, reads=(), writes=(), **kw):
        ins = Ins(queue, lambda e: e.dma_start(out=out, in_=in_, **kw), dma=True, key=key)
        self.dma_tot[key] = self.dma_tot.get(key, 0) + 16
        ins.val = self.dma_tot[key]
        self.dma_last[key] = ins
        return self._add(ins, reads, writes)

    def barrier(self):
        lasts = []
        for e in ENGS:
            for ins in reversed(self.engs[e]):
                if ins.fn is not None and not ins.dma:
                    ins.signal = True
                    lasts.append(ins)
                    break
        lasts += list(self.dma_last.values())
        for e in ENGS:
            ins = Ins(e, None)
            ins.deps = [d for d in lasts if d.dma or d.eng != e]
            self.engs[e].append(ins)

    def finish(self):
        ins = Ins("sp", None)
        ins.deps = list(self.dma_last.values())
        self.engs["sp"].append(ins)

    def emit(self):
        nc = self.nc
        for e in ENGS:
            c = 0
            for ins in self.engs[e]:
                if ins.dma:
                    continue
                if ins.signal and ins.fn is not None:
                    c += 1
                    ins.val = c
        with ExitStack() as st:
            sems = {e: st.enter_context(nc.semaphore(f"s_{e}")) for e in ENGS}
            dsem = {k: st.enter_context(nc.semaphore(f"d_{k}")) for k in self.dma_tot}
            block = st.enter_context(nc.Block())
            bname = {"pe": "tensor", "act": "scalar", "dve": "vector", "pool": "gpsimd", "sp": "sync"}
            stats = {}
            self.dump = {}
            for e in ENGS:
                def body(engine, e=e):
                    seen = {}
                    nw = 0
                    for ins in self.engs[e]:
                        need = {}
                        for d in ins.deps:
                            s = ("d", d.key) if d.dma else ("c", d.eng)
                            if need.get(s, 0) < d.val:
                                need[s] = d.val
                        for s, v in need.items():
                            if seen.get(s, 0) < v:
                                seen[s] = v
                                sh = dsem[s[1]] if s[0] == "d" else sems[s[1]]
                                engine.wait_ge(sh, v)
                                nw += 1
                        if os.environ.get("DUMP"):
                            self.dump.setdefault(e, []).append((sorted((k, v) for k, v in need.items()), ins.fn is not None, ins.dma, ins.key, ins.signal, ins.val))
                        if ins.fn is not None:
                            bi = ins.fn(engine)
                            if ins.dma:
                                bi.then_inc(dsem[ins.key], 16)
                            elif ins.signal:
                                bi.then_inc(sems[e], 1)
                    stats[e] = (len(self.engs[e]), nw)
                getattr(block, bname[e])(body)
            self.stats = stats


class Arena:
    def __init__(self, ap, start, end):
        self.ap = ap
        self.start = start
        self.off = start
        self.end = end

    def alloc(self, free_shape, dt, parts=128):
        n = 1
        for v in free_shape:
            n *= v
        esz = 4 if dt == F32 else 2
        nbytes = n * esz
        st = (self.off + 63) // 64 * 64
        assert st + nbytes <= self.end, f"arena overflow: need {st + nbytes} > {self.end}"
        self.off = st + nbytes
        Arena.last = (st, tuple(free_shape), dt)
        v = self.ap[:parts, st // 2:(st + nbytes) // 2]
        if dt == F32:
            v = v.bitcast(F32)
        if len(free_shape) == 2:
            v = v.rearrange("p (a b) -> p a b", a=free_shape[0])
        elif len(free_shape) == 3:
            v = v.rearrange("p (a b c) -> p a b c", a=free_shape[0], b=free_shape[1])
        return v

    def sub(self, nbytes):
        st = (self.off + 63) // 64 * 64
        assert st + nbytes <= self.end, f"arena overflow(sub): need {st + nbytes} > {self.end}"
        self.off = st + nbytes
        return Arena(self.ap, st, st + nbytes)

    def child(self):
        return Arena(self.ap, self.start, self.end)


class Ctx:
    pass


DBG = {}


NST = 3


def load_cast(C, dst, src, dst_reg):
    P = C.P
    s = C.stage_i % NST
    C.stage_i += 1
    sh = src.shape
    n = 1
    for v in sh[1:]:
        n *= v
    assert n <= 1024
    stg = C.stage[s][:, :n]
    if len(sh) == 3:
        stg = stg.rearrange("p (a b) -> p a b", a=sh[1])
    P.dma("sp", stg, src, f"st{s}", writes=[C.stage_r[s]])
    P.op("pool", lambda e: e.tensor_copy(out=dst, in_=stg), reads=[C.stage_r[s]], writes=[dst_reg])


def norm_transpose(C, A_stage, x_src, g_pre_dram, hT):
    P, PS, psr = C.P, C.PS, C.psr
    gb = A_stage.alloc((D,), F32)
    gb_r = Reg()
    P.dma("sp", gb, g_pre_dram.partition_broadcast(128), "g", writes=[gb_r])
    ss = A_stage.alloc((NT,), F32)
    rstd = A_stage.alloc((NT,), F32)
    junk = A_stage.alloc((D,), BF16)
    junk_r = Reg()
    hb = [A_stage.alloc((D,), BF16) for _ in range(2)]
    hb_r = [Reg() for _ in range(2)]
    xs = [A_stage.alloc((D,), F32) for _ in range(NT)]
    xs_r = [Reg() for _ in range(NT)]
    ss_r = [Reg() for _ in range(NT)]
    rstd_r = Reg()
    for i in range(NT):
        P.dma("sp", xs[i], x_src[i * 128:(i + 1) * 128, :], f"xs{i}", writes=[xs_r[i]])
    for i in range(NT):
        P.op("act", lambda e, i=i: e.activation(out=junk, in_=xs[i], func=AF.Square, accum_out=ss[:, i:i + 1]),
             reads=[xs_r[i]], writes=[ss_r[i], junk_r])
    P.op("act", lambda e: e.activation(out=rstd, in_=ss, func=AF.Ln, scale=1.0 / D, bias=EPS),
         reads=ss_r, writes=[rstd_r])
    P.op("act", lambda e: e.activation(out=rstd, in_=rstd, func=AF.Exp, scale=-0.5),
         reads=[rstd_r], writes=[rstd_r])
    for i in range(NT):
        s = i % 2
        P.op("dve", lambda e, i=i, s=s: e.scalar_tensor_tensor(out=hb[s], in0=xs[i], scalar=rstd[:, i:i + 1], in1=gb,
                                                              op0=ALU.mult, op1=ALU.mult),
             reads=[xs_r[i], rstd_r, gb_r], writes=[hb_r[s]])
        b = i % 2
        psb = PS[:, 512 * b:512 * (b + 1)].bitcast(BF16)
        for k in range(KC):
            P.op("pe", lambda e, k=k, s=s, psb=psb: e.transpose(out=psb[:, k * 128:(k + 1) * 128],
                                                                in_=hb[s][:, k * 128:(k + 1) * 128], identity=C.ident),
                 reads=[hb_r[s], C.ident_r], writes=[psr[b]])
        P.op("act", lambda e, i=i, psb=psb: e.activation(out=hT[:, :, i * 128:(i + 1) * 128],
                                                         in_=psb.rearrange("p (k n) -> p k n", k=KC), func=AF.Copy),
             reads=[psr[b]], writes=[])
    P.barrier()


def ffn_phase(C, x_src, x_dst, g_pre, wg, wu, wd, g_post, stop_after=None):
    P, PS, psr = C.P, C.PS, C.psr
    A = C.A.child()
    hT_ar = A.sub(KC * S * 2)
    actT_ar = A.sub(FC * S * 2)
    hT = hT_ar.child().alloc((KC, S), BF16)
    actT = actT_ar.child().alloc((FC, S), BF16)
    norm_transpose(C, actT_ar.child(), x_src, g_pre, hT)

    NSL = 2
    wg_s = [A.alloc((KC, 256), BF16) for _ in range(NSL)]
    wu_s = [A.alloc((KC, 256), BF16) for _ in range(NSL)]
    wg_r = [[Reg(), Reg()] for _ in range(NSL)]
    wu_r = [[Reg(), Reg()] for _ in range(NSL)]
    wd_h = [A.alloc((FC, 512), BF16) for _ in range(2)]
    wd_r = [[Reg() for _ in range(FC // 2)] for _ in range(2)]
    sg = [A.alloc((512,), BF16) for _ in range(2)]
    sg_r = [Reg() for _ in range(2)]
    gb = A.alloc((D,), F32)
    gb_r = Reg()
    ss2 = A.alloc((NT,), F32)
    r2 = A.alloc((NT,), F32)
    junk = A.alloc((D,), BF16)
    junk_r = Reg()
    P.dma("sp", gb, g_post.partition_broadcast(128), "g", writes=[gb_r])

    wd_jobs = [(h, c2) for h in range(2) for c2 in range(FC // 2)]

    def load_wd_piece():
        if not wd_jobs:
            return
        h, c2 = wd_jobs.pop(0)
        load_cast(C, wd_h[h][:, 2 * c2:2 * c2 + 2, :],
                  wd[c2 * 256:(c2 + 1) * 256, h * 512:(h + 1) * 512].rearrange("(c p) n -> p c n", p=128),
                  wd_r[h][c2])

    j = 0
    for cb in range(FC // 2):
        s = cb % NSL
        for part in range(2):
            load_cast(C, wg_s[s][:, 4 * part:4 * part + 4, :],
                      wg[part * 512:(part + 1) * 512, cb * 256:(cb + 1) * 256].rearrange("(k p) n -> p k n", p=128),
                      wg_r[s][part])
            load_cast(C, wu_s[s][:, 4 * part:4 * part + 4, :],
                      wu[part * 512:(part + 1) * 512, cb * 256:(cb + 1) * 256].rearrange("(k p) n -> p k n", p=128),
                      wu_r[s][part])
        if cb >= 1:
            for _ in range(3):
                load_wd_piece()
        for sub in range(2):
            ffc = cb * 2 + sub
            for tg in range(4):
                bG = 2 * (j % 4)
                bU = bG + 1
                q = j % 2
                j += 1
                for k in range(KC):
                    P.op("pe", lambda e, k=k, s=s, sub=sub, tg=tg, bG=bG: e.matmul(
                        PS[:, 512 * bG:512 * (bG + 1)], lhsT=wg_s[s][:, k, sub * 128:(sub + 1) * 128],
                        rhs=hT[:, k, tg * 512:(tg + 1) * 512], start=(k == 0), stop=(k == KC - 1)),
                        reads=[wg_r[s][k // 4]], writes=[psr[bG]])
                for k in range(KC):
                    P.op("pe", lambda e, k=k, s=s, sub=sub, tg=tg, bU=bU: e.matmul(
                        PS[:, 512 * bU:512 * (bU + 1)], lhsT=wu_s[s][:, k, sub * 128:(sub + 1) * 128],
                        rhs=hT[:, k, tg * 512:(tg + 1) * 512], start=(k == 0), stop=(k == KC - 1)),
                        reads=[wu_r[s][k // 4]], writes=[psr[bU]])
                P.op("act", lambda e, q=q, bG=bG: e.activation(out=sg[q], in_=PS[:, 512 * bG:512 * (bG + 1)], func=AF.Silu),
                     reads=[psr[bG]], writes=[sg_r[q]])
                P.op("dve", lambda e, q=q, bU=bU, ffc=ffc, tg=tg: e.tensor_tensor(
                    out=actT[:, ffc, tg * 512:(tg + 1) * 512], in0=PS[:, 512 * bU:512 * (bU + 1)], in1=sg[q], op=ALU.mult),
                    reads=[psr[bU], sg_r[q]], writes=[])
    while wd_jobs:
        load_wd_piece()
    P.barrier()
    if stop_after == "B":
        ov = x_dst.rearrange("(a b) d -> a (b d)", a=1024).rearrange("(k p) t -> p k t", p=128)
        for k in range(KC):
            for hh in range(2):
                P.dma("pool", ov[:, k, hh * 1024:(hh + 1) * 1024], actT[:, k + 14, hh * 1024:(hh + 1) * 1024], "dbg")
        return

    Ah = hT_ar.child()
    NSC = int(os.environ.get('NSC', 2))
    NSX = int(os.environ.get('NSX', NSC))
    xc = [Ah.alloc((D,), F32) for _ in range(NSX)]
    tt = [Ah.alloc((D,), F32) for _ in range(NSC)]
    xc_r = [Reg() for _ in range(NSX)]
    tt_r = [Reg() for _ in range(NSC)]
    tth_r = [[Reg(), Reg()] for _ in range(NSC)]
    r2_r = [Reg() for _ in range(NT)]
    for i in range(int(os.environ.get("CT", NT))):
        s = i % NSC
        sx = i % NSX
        P.dma("sp", xc[sx], x_src[i * 128:(i + 1) * 128, :], f"xc{sx}", writes=[xc_r[sx]])
        pb = (i % 4) * 2
        psf = PS[:, 512 * pb:512 * (pb + 2)]
        for h in range(2):
            for ffc in range(FC):
                P.op("pe", lambda e, h=h, ffc=ffc, i=i, pb=pb: e.matmul(
                    PS[:, 512 * (pb + h):512 * (pb + h + 1)], lhsT=actT[:, ffc, i * 128:(i + 1) * 128],
                    rhs=wd_h[h][:, ffc, :], start=(ffc == 0), stop=(ffc == FC - 1)),
                    reads=[wd_r[h][ffc // 2]], writes=[psr[pb + h]])
        P.op("act", lambda e, i=i, psf=psf: e.activation(out=junk, in_=psf, func=AF.Square, accum_out=ss2[:, i:i + 1]),
             reads=[psr[pb], psr[pb + 1]], writes=[r2_r[i], junk_r])
        cstop = int(os.environ.get("CSTOP", 9))
        if cstop == 1:
            P.dma("sp", x_dst[i * 128:(i + 1) * 128, :], xc[sx], f"xo{s}", reads=[xc_r[sx], r2_r[i]])
            continue
        P.op("act", lambda e, i=i: e.activation(out=r2[:, i:i + 1], in_=ss2[:, i:i + 1], func=AF.Ln, scale=1.0 / D, bias=EPS),
             reads=[r2_r[i]], writes=[r2_r[i]])
        P.op("act", lambda e, i=i: e.activation(out=r2[:, i:i + 1], in_=r2[:, i:i + 1], func=AF.Exp, scale=-0.5,
                                                bias=math.log(0.5)),
             reads=[r2_r[i]], writes=[r2_r[i]])
        if cstop == 2:
            P.dma("sp", x_dst[i * 128:(i + 1) * 128, :], xc[sx], f"xo{s}", reads=[xc_r[sx], r2_r[i]])
            continue
        for h in range(2):
            if os.environ.get("DVEVAR") == "copy":
                P.op("dve", lambda e, s=s, h=h, pb=pb: e.tensor_copy(
                    out=tt[s][:, 512 * h:512 * (h + 1)], in_=PS[:, 512 * (pb + h):512 * (pb + h + 1)]),
                    reads=[psr[pb + h], gb_r], writes=[tth_r[s][h]])
                continue
            if os.environ.get("DVEVAR") == "sbuf":
                P.op("dve", lambda e, s=s, h=h, pb=pb: e.tensor_tensor(
                    out=tt[s][:, 512 * h:512 * (h + 1)], in0=xc[sx][:, 512 * h:512 * (h + 1)],
                    in1=gb[:, 512 * h:512 * (h + 1)], op=ALU.mult),
                    reads=[psr[pb + h], gb_r, xc_r[sx]], writes=[tth_r[s][h]])
                continue
            P.op("dve", lambda e, s=s, h=h, pb=pb: e.tensor_tensor(
                out=tt[s][:, 512 * h:512 * (h + 1)], in0=PS[:, 512 * (pb + h):512 * (pb + h + 1)],
                in1=gb[:, 512 * h:512 * (h + 1)], op=ALU.mult),
                reads=[psr[pb + h], gb_r, r2_r[i]], writes=[tth_r[s][h]])
        if cstop == 3:
            P.dma("sp", x_dst[i * 128:(i + 1) * 128, :], tt[s], f"xo{s}", reads=[xc_r[sx], r2_r[i], tth_r[s][0], tth_r[s][1]], writes=[tth_r[s][0], tth_r[s][1]])
            continue
        P.op("dve", lambda e, s=s, sx=sx, i=i: e.scalar_tensor_tensor(out=xc[sx], in0=tt[s], scalar=r2[:, i:i + 1], in1=xc[sx],
                                                              op0=ALU.mult, op1=ALU.add),
             reads=[tth_r[s][0], tth_r[s][1], r2_r[i], xc_r[sx]], writes=[xc_r[sx]])
        P.dma("sp", x_dst[i * 128:(i + 1) * 128, :], xc[sx], f"xo{sx}", reads=[xc_r[sx]])
    P.barrier()


def tok_slice(start, step):
    return slice(start, start + step * 127 + 1, step) if step > 1 else slice(start, start + 128)


def attention_phase(C, A, hT, attT, w_in):
    P, PS, psr = C.P, C.PS, C.psr
    cos = A.alloc((S,), F32)[:32]
    sin = A.alloc((S,), F32)[:32]
    cs_r = Reg()
    cs2_r = Reg()
    P.dma("sp", cos, C.dram["c_cos"], "cs", writes=[cs_r])
    P.dma("sp", sin, C.dram["c_sin"], "cs", writes=[cs2_r])
    wsl = [[A.alloc((KC, 128), BF16) for _ in range(3)] for _ in range(2)]
    wsl_r = [[Reg() for _ in range(3)] for _ in range(2)]
    DBG.clear()
    qk = [[None, None], [None, None]]
    for a_ in range(2):
        for b_ in range(2):
            qk[a_][b_] = A.alloc((S,), BF16)
            DBG[f"qk{a_}{b_}"] = Arena.last
    qk_r = [[[Reg() for _ in range(4)] for _ in range(2)] for _ in range(2)]
    vt = [A.alloc((NT, 128), BF16) for _ in range(2)]
    vt_r = [[Reg() for _ in range(4)] for _ in range(2)]
    acc_n = A.alloc((S,), F32)
    acc_d = A.alloc((S,), F32)
    accn_r = [Reg() for _ in range(4)]
    accd_r = [Reg() for _ in range(4)]
    acc_all = Reg()
    pT = [A.alloc((512,), BF16) for _ in range(4)]
    pT_r = [Reg() for _ in range(4)]
    t1 = [A.alloc((512,), F32)[:32] for _ in range(2)]
    t2 = [A.alloc((512,), F32)[:32] for _ in range(2)]
    t1_r = [Reg() for _ in range(2)]
    t2_r = [Reg() for _ in range(2)]
    rcp = [A.alloc((512,), F32) for _ in range(2)]
    rcp_r = [Reg() for _ in range(2)]
    SCALE = 128.0 ** -0.5

    def bank(b):
        return PS[:, 512 * b:512 * (b + 1)]

    cnt = {"proj": 0, "rot": 0, "s": 0, "p": 0, "nd": 0}
    iters = [(hs, g) for hs in range(4) for g in range(3)]

    def make_blocks(g):
        blocks = []
        if g == 0:
            for b in range(16):
                blocks.append((tok_slice(128 * b, 1), tok_slice(128 * (b - 1), 1) if b > 0 else None))
        elif g == 1:
            for r in range(4):
                for b in range(4):
                    blocks.append((tok_slice(4 * 128 * b + r, 4), tok_slice(4 * 128 * (b - 1) + r, 4) if b > 0 else None))
        else:
            for r in range(16):
                blocks.append((tok_slice(r, 16), None))
        return blocks

    def proj_units(idx):
        hs, g = iters[idx]
        sl = idx % 2
        head = g * 4 + hs
        blocks = make_blocks(g)
        units = []
        pend = []

        def u_load():
            for m, off in enumerate((0, 1536, 3072)):
                c0 = off + head * 128
                load_cast(C, wsl[sl][m], w_in[:, c0:c0 + 128].rearrange("(k p) n -> p k n", p=128), wsl_r[sl][m])
        units.append(u_load)
        for m in range(2):
            for tg in range(4):
                def u_qk(m=m, tg=tg):
                    dst = qk[sl][m]
                    pb = cnt["proj"] % 2
                    cnt["proj"] += 1
                    for k in range(KC):
                        P.op("pe", lambda e, k=k: e.matmul(
                            bank(pb), lhsT=wsl[sl][m][:, k, :], rhs=hT[:, k, tg * 512:(tg + 1) * 512],
                            start=(k == 0), stop=(k == KC - 1)), reads=[wsl_r[sl][m]], writes=[psr[pb]])
                    dcol = dst[:, tg * 512:(tg + 1) * 512]
                    dr = qk_r[sl][m][tg]
                    P.op("act", lambda e: e.activation(out=dcol, in_=bank(pb), func=AF.Copy), reads=[psr[pb]], writes=[dr])
                    def rope_part():
                        rb = 2 + cnt["rot"] % 2
                        ts = cnt["rot"] % 2
                        cnt["rot"] += 1
                        P.op("pe", lambda e: e.matmul(bank(rb)[:32, :], lhsT=C.rm, rhs=dcol[:32, :], start=True, stop=True),
                             reads=[dr, C.cst_r], writes=[psr[rb]])
                        P.op("dve", lambda e: e.tensor_tensor(out=t1[ts], in0=bank(rb)[:32, :], in1=sin[:, tg * 512:(tg + 1) * 512],
                                                              op=ALU.mult), reads=[psr[rb], cs_r, cs2_r], writes=[t1_r[ts]])
                        P.op("dve", lambda e: e.tensor_tensor(out=t2[ts], in0=dcol[:32, :], in1=cos[:, tg * 512:(tg + 1) * 512],
                                                              op=ALU.mult), reads=[dr, cs_r], writes=[t2_r[ts]])
                        P.op("dve", lambda e: e.tensor_tensor(out=dcol[:32, :], in0=t1[ts], in1=t2[ts], op=ALU.add),
                             reads=[t1_r[ts], t2_r[ts]], writes=[dr])
                    if pend:
                        pend.pop(0)()
                    pend.append(rope_part)
                units.append(u_qk)

        def u_flush():
            while pend:
                pend.pop(0)()
        units.append(u_flush)
        for j in range(4):
            def u_v(j=j):
                pb = cnt["proj"] % 2
                cnt["proj"] += 1
                for bi in range(4):
                    qs = blocks[4 * j + bi][0]
                    for k in range(KC):
                        P.op("pe", lambda e, k=k, qs=qs, bi=bi: e.matmul(
                            bank(pb)[:, bi * 128:(bi + 1) * 128], lhsT=hT[:, k, qs], rhs=wsl[sl][2][:, k, :],
                            start=(k == 0), stop=(k == KC - 1)), reads=[wsl_r[sl][2]], writes=[psr[pb]])
                P.op("act", lambda e: e.activation(
                    out=vt[sl][:, 4 * j:4 * j + 4, :], in_=bank(pb).rearrange("p (a b) -> p a b", a=4), func=AF.Copy),
                    reads=[psr[pb]], writes=[vt_r[sl][j]])
            units.append(u_v)
        return units

    def core(idx, fill):
        hs, g = iters[idx]
        sl = idx % 2
        blocks = make_blocks(g)
        qT, kT = qk[sl][0], qk[sl][1]
        qr = qk_r[sl][0] + qk_r[sl][1]
        pairs = [(2 * i, 2 * i + 1) for i in range(8)]

        def do_qk(pair):
            sb = 4 + cnt["s"] % 2
            cnt["s"] += 1
            for ti, blk in enumerate(pair):
                qs, ps_ = blocks[blk]
                for ci, ks in enumerate((ps_, qs)):
                    o = bank(sb)[:, (2 * ti + ci) * 128:(2 * ti + ci + 1) * 128]
                    if ks is None:
                        P.op("pe", lambda e, o=o: e.matmul(o, lhsT=C.ident, rhs=C.maskn, start=True, stop=True),
                             reads=[C.cst_r], writes=[psr[sb]])
                        continue
                    P.op("pe", lambda e, o=o, ks=ks, qs=qs, kT=kT, qT=qT: e.matmul(o, lhsT=kT[:, ks], rhs=qT[:, qs], start=True, stop=False),
                         reads=qr, writes=[psr[sb]])
                    mk = C.maskc if ci == 1 else C.maskp
                    P.op("pe", lambda e, o=o, mk=mk: e.matmul(o, lhsT=C.ident, rhs=mk, start=False, stop=True),
                         reads=[C.cst_r], writes=[psr[sb]])
            pslot = cnt["p"] % 4
            cnt["p"] += 1
            P.op("act", lambda e, sb=sb, pslot=pslot: e.activation(out=pT[pslot], in_=bank(sb), func=AF.Exp, scale=SCALE),
                 reads=[psr[sb]], writes=[pT_r[pslot]])
            return pslot

        def do_pv(pair, pslot):
            nb = 6 + cnt["nd"] % 2
            cnt["nd"] += 1
            for ti, blk in enumerate(pair):
                qs, ps_ = blocks[blk]
                for is_num in (True, False):
                    col = ti * 128 + (0 if is_num else 256)
                    o = bank(nb)[:, col:col + 128]
                    for ci in range(2):
                        kblk = blk if (ci == 1 or ps_ is None) else blk - 1
                        lhs = vt[sl][:, kblk, :] if is_num else C.ones_bf
                        P.op("pe", lambda e, o=o, lhs=lhs, pslot=pslot, ti=ti, ci=ci: e.matmul(
                            o, lhsT=lhs, rhs=pT[pslot][:, (2 * ti + ci) * 128:(2 * ti + ci + 1) * 128],
                            start=(ci == 0), stop=(ci == 1)),
                            reads=[pT_r[pslot], vt_r[sl][kblk // 4], C.cst_r], writes=[psr[nb]])
            pi = pair[0] // 2
            if g == 0:
                sel = slice(256 * pi, 256 * pi + 256)
                dn, dd = acc_n[:, sel], acc_d[:, sel]
                sn, sd = bank(nb)[:, 0:256], bank(nb)[:, 256:512]
            elif g == 1:
                r_, b_ = pair[0] // 4, pair[0] % 4
                st_ = r_ + 512 * b_
                sel = slice(st_, st_ + 4 * 255 + 1, 4)
                dn, dd = acc_n[:, sel], acc_d[:, sel]
                sn, sd = bank(nb)[:, 0:256], bank(nb)[:, 256:512]
            else:
                r_ = pair[0]
                dn = acc_n.rearrange("p (n r) -> p r n", r=16)[:, r_:r_ + 2, :]
                dd = acc_d.rearrange("p (n r) -> p r n", r=16)[:, r_:r_ + 2, :]
                sn = bank(nb)[:, 0:256].rearrange("p (a b) -> p a b", a=2)
                sd = bank(nb)[:, 256:512].rearrange("p (a b) -> p a b", a=2)
            if g == 0:
                P.op("dve", lambda e, dn=dn, sn=sn: e.tensor_copy(out=dn, in_=sn), reads=[psr[nb]], writes=[acc_all])
                P.op("dve", lambda e, dd=dd, sd=sd: e.tensor_copy(out=dd, in_=sd), reads=[psr[nb]], writes=[acc_all])
            else:
                P.op("dve", lambda e, dn=dn, sn=sn: e.tensor_tensor(out=dn, in0=sn, in1=dn, op=ALU.add),
                     reads=[psr[nb], acc_all], writes=[acc_all])
                P.op("dve", lambda e, dd=dd, sd=sd: e.tensor_tensor(out=dd, in0=sd, in1=dd, op=ALU.add),
                     reads=[psr[nb], acc_all], writes=[acc_all])


        prev = None
        for pair in pairs:
            pslot = do_qk(pair)
            if prev is not None:
                do_pv(*prev)
                fill()
            prev = (pair, pslot)
        do_pv(*prev)
        fill()
        if g == 2:
            for tg in range(4):
                rs = tg % 2
                P.op("dve", lambda e, tg=tg, rs=rs: e.reciprocal(out=rcp[rs], in_=acc_d[:, tg * 512:(tg + 1) * 512]),
                     reads=[acc_all], writes=[rcp_r[rs]])
                P.op("dve", lambda e, tg=tg, rs=rs, hs=hs: e.tensor_tensor(
                    out=attT[:, hs, tg * 512:(tg + 1) * 512], in0=acc_n[:, tg * 512:(tg + 1) * 512], in1=rcp[rs], op=ALU.mult),
                    reads=[acc_all, rcp_r[rs]], writes=[C.attT_r])


    for u in proj_units(0):
        u()
    for idx in range(len(iters)):
        nxt = proj_units(idx + 1) if idx + 1 < len(iters) else []
        per = -(-len(nxt) // 8) if nxt else 0
        pos = [0]

        def fill():
            for u in nxt[pos[0]:pos[0] + per]:
                u()
            pos[0] += per
        core(idx, fill)
        for u in nxt[pos[0]:]:
            u()

def mlstm_phase(C, A, hT, mlT, w_in, conv_w, conv_b, i_bias, f_bias, head_g):
    P, PS, psr = C.P, C.PS, C.psr
    OQ, OK_, OV, OO, OI = 4608, 5632, 6656, 7680, 8704

    def bank(b):
        return PS[:, 512 * b:512 * (b + 1)]

    def R():
        return Reg()

    cwb = A.alloc((2048,), F32)[:5]
    cwb_r = R()
    cwb2_r = R()
    P.dma("sp", cwb[0:4, :], conv_w, "mca", writes=[cwb_r])
    P.dma("sp", cwb[4:5, :], conv_b.rearrange("(o n) -> o n", o=1), "mca", writes=[cwb2_r])
    cwT = A.alloc((16, 8), F32)
    ncb = A.alloc((16,), F32)
    cwT_r = R()
    for c in range(16):
        P.op("pe", lambda e, c=c: e.matmul(bank(6)[:, c * 8:c * 8 + 5], lhsT=cwb[:, c * 128:(c + 1) * 128],
                                           rhs=C.identf[:5, :5], start=True, stop=True),
             reads=[cwb_r, cwb2_r, C.cst_r], writes=[psr[6]])
    P.op("dve", lambda e: e.tensor_copy(out=cwT[:, :, 0:5], in_=bank(6)[:, 0:128].rearrange("p (c j) -> p c j", j=8)[:, :, 0:5]),
         reads=[psr[6]], writes=[cwT_r])
    P.op("dve", lambda e: e.tensor_scalar(out=ncb, in0=cwT[:, :, 4], scalar1=-1.0, scalar2=None, op0=ALU.mult),
         reads=[cwT_r], writes=[cwT_r])
    hg = A.alloc((1024,), F32)
    hg_r = R()
    P.dma("sp", hg, head_g.partition_broadcast(128), "mch", writes=[hg_r])
    bias8 = A.alloc((8,), F32)
    b8_r = R()
    b8b_r = R()
    P.dma("sp", bias8[:, 0:4], i_bias.partition_broadcast(128), "mcb", writes=[b8_r])
    P.dma("sp", bias8[:, 4:8], f_bias.partition_broadcast(128), "mcb", writes=[b8b_r])

    wif = A.alloc((KC, 8), BF16)
    wif_r = R()
    load_cast(C, wif, w_in[:, OI:OI + 8].rearrange("(k p) n -> p k n", p=128), wif_r)
    for c in range(NT):
        for k in range(KC):
            P.op("pe", lambda e, c=c, k=k: e.matmul(bank(7)[:, c * 8:(c + 1) * 8], lhsT=hT[:, k, c * 128:(c + 1) * 128],
                                                    rhs=wif[:, k, :], start=(k == 0), stop=(k == KC - 1)),
                 reads=[wif_r], writes=[psr[7]])
    gi = A.alloc((NT, 4), F32)
    lg = A.alloc((NT, 4), F32)
    g_r = R()
    pre3 = bank(7)[:, 0:128].rearrange("p (c j) -> p c j", j=8)
    P.op("dve", lambda e: e.tensor_tensor(out=gi, in0=pre3[:, :, 0:4],
                                          in1=bias8[:, 0:4].unsqueeze(1).to_broadcast([128, NT, 4]), op=ALU.add),
         reads=[psr[7], b8_r, b8b_r], writes=[g_r])
    P.op("dve", lambda e: e.tensor_tensor(out=lg, in0=pre3[:, :, 4:8],
                                          in1=bias8[:, 4:8].unsqueeze(1).to_broadcast([128, NT, 4]), op=ALU.add),
         reads=[psr[7], b8_r, b8b_r, g_r], writes=[g_r])
    P.op("act", lambda e: e.activation(out=lg, in_=lg, func=AF.Exp, scale=-1.0), reads=[g_r], writes=[g_r])
    P.op("act", lambda e: e.activation(out=lg, in_=lg, func=AF.Ln, bias=1.0), reads=[g_r], writes=[g_r])
    lg2 = lg.rearrange("p c h -> p (c h)")
    gi2 = gi.rearrange("p c h -> p (c h)")
    P.op("pe", lambda e: e.matmul(bank(6)[:, 0:64], lhsT=C.tri, rhs=lg2, start=True, stop=True),
         reads=[g_r, C.cst_r], writes=[psr[6]])
    P.op("pe", lambda e: e.matmul(bank(6)[:, 64:128], lhsT=C.ones_f, rhs=lg2, start=True, stop=True),
         reads=[g_r, C.cst_r], writes=[psr[6]])
    e_in = A.alloc((64,), F32)
    e_out = A.alloc((64,), F32)
    e_L = A.alloc((64,), F32)
    e_v = A.alloc((64,), F32)
    e_io = A.alloc((64,), F32)
    ee_r = R()
    P.op("dve", lambda e: e.tensor_tensor(out=e_in, in0=bank(6)[:, 0:64], in1=gi2, op=ALU.add),
         reads=[psr[6], g_r], writes=[ee_r])
    P.op("act", lambda e: e.activation(out=e_in, in_=e_in, func=AF.Exp), reads=[ee_r], writes=[ee_r])
    P.op("act", lambda e: e.activation(out=e_out, in_=bank(6)[:, 0:64], func=AF.Exp, scale=-1.0),
         reads=[psr[6], ee_r], writes=[ee_r])
    P.op("act", lambda e: e.activation(out=e_L, in_=bank(6)[:, 64:128], func=AF.Exp, scale=-1.0),
         reads=[psr[6], ee_r], writes=[ee_r])
    P.op("act", lambda e: e.activation(out=e_io, in_=bank(6)[:, 0:64], func=AF.Exp), reads=[psr[6], ee_r], writes=[ee_r])
    P.op("dve", lambda e: e.tensor_scalar(out=e_in, in0=e_in, scalar1=1.0 / 16.0, scalar2=None, op0=ALU.mult),
         reads=[ee_r], writes=[ee_r])
    P.op("dve", lambda e: e.tensor_tensor(out=e_v, in0=e_in, in1=e_L, op=ALU.mult), reads=[ee_r], writes=[ee_r])

    wq = [A.alloc((KC, 256), BF16) for _ in range(5)]
    wq_r = [[R(), R()] for _ in range(5)]
    raw = [A.alloc((S + 4,), BF16) for _ in range(2)]
    raw_r = [R(), R()]
    for i in range(2):
        P.op("dve", lambda e, i=i: e.memset(raw[i][:, 0:3], 0.0), writes=[raw_r[i]])
    neghalf = A.alloc((1,), F32)
    nh_r = R()
    P.op("dve", lambda e: e.memset(neghalf, -0.5), writes=[nh_r])
    dg = [[A.alloc((128,), BF16) for _ in range(4)] for _ in range(2)]
    dg_r = [R(), R()]
    qT2 = [A.alloc((2, S), BF16) for _ in range(2)]
    kT2 = [A.alloc((2, S), BF16) for _ in range(2)]
    qk2_r = [[[R(), R()], [R(), R()]] for _ in range(2)]
    ktok2 = [A.alloc((256,), BF16) for _ in range(2)]
    ktok2_r = [R(), R()]
    vaug2 = [A.alloc((258,), BF16) for _ in range(2)]
    vaug2_r = [R(), R()]
    for i in range(2):
        P.op("dve", lambda e, i=i: e.memset(vaug2[i][:, 256:257], 1.0), writes=[vaug2_r[i]])
        P.op("dve", lambda e, i=i: e.memset(vaug2[i][:, 257:258], 0.0), writes=[vaug2_r[i]])
    vp2 = [A.alloc((258,), BF16) for _ in range(2)]
    vp2_r = [R(), R()]
    sigo2 = [A.alloc((NT, 256), BF16) for _ in range(2)]
    sigo2_r = [[R() for _ in range(NT)] for _ in range(2)]
    wT2 = [A.alloc((128,), BF16) for _ in range(2)]
    wT2_r = [R(), R()]
    Cf = A.alloc((2, 258), F32)
    Cf_r = R()
    Cbf2 = [A.alloc((2, 258), BF16) for _ in range(2)]
    Cbf2_r = [R(), R()]
    hu3 = [A.alloc((256,), F32) for _ in range(3)]
    hu3_r = [R(), R(), R()]
    sm3 = [A.alloc((8,), F32) for _ in range(3)]
    sm3_r = [R(), R(), R()]
    junk3 = [A.alloc((256,), BF16) for _ in range(3)]
    mlb3 = [A.alloc((256,), BF16) for _ in range(3)]
    mlb3_r = [R(), R(), R()]

    def pre_units(hd):
        sl = hd % 2
        qT, kT, qk_r, sigo, sigo_r = qT2[sl], kT2[sl], qk2_r[sl], sigo2[sl], sigo2_r[sl]
        wsel = (0, 1, 3 + sl, 2)
        units = []
        pending = []

        def u_load():
            for m, off in enumerate((OQ, OK_, OV, OO)):
                c0 = off + hd * 256
                for part in range(2):
                    load_cast(C, wq[wsel[m]][:, 4 * part:4 * part + 4, :],
                              w_in[part * 512:(part + 1) * 512, c0:c0 + 256].rearrange("(k p) n -> p k n", p=128),
                              wq_r[wsel[m]][part])
        units.append(u_load)
        for m, dstT in ((0, qT), (1, kT)):
            for cc in range(2):
                ch = m * 8 + hd * 2 + cc
                bi = (m * 2 + cc) % 2

                def u_diag(ch=ch, bi=bi):
                    for j in range(4):
                        P.op("dve", lambda e, j=j: e.tensor_scalar(out=dg[bi][j], in0=C.ident, scalar1=cwT[:, ch, j:j + 1],
                                                                   scalar2=None, op0=ALU.mult),
                             reads=[C.cst_r, cwT_r], writes=[dg_r[bi]])
                units.append(u_diag)
                for tg in range(4):
                    def u_proj(m=m, cc=cc, tg=tg, ch=ch, bi=bi, dstT=dstT):
                        for k in range(KC):
                            P.op("pe", lambda e, k=k: e.matmul(
                                bank(6), lhsT=wq[m][:, k, cc * 128:(cc + 1) * 128], rhs=hT[:, k, tg * 512:(tg + 1) * 512],
                                start=(k == 0), stop=(k == KC - 1)), reads=[wq_r[m][k // 4]], writes=[psr[6]])
                        P.op("act", lambda e: e.activation(
                            out=raw[bi][:, 3 + tg * 512:3 + (tg + 1) * 512], in_=bank(6), func=AF.Copy),
                            reads=[psr[6]], writes=[raw_r[bi]])
                        def conv_part():
                            for j in range(4):
                                P.op("pe", lambda e, j=j: e.matmul(bank(7), lhsT=dg[bi][j],
                                                                   rhs=raw[bi][:, tg * 512 + j:tg * 512 + j + 512],
                                                                   start=(j == 0), stop=(j == 3)),
                                     reads=[dg_r[bi], raw_r[bi]], writes=[psr[7]])
                            P.op("act", lambda e: e.activation(out=dstT[:, cc, tg * 512:(tg + 1) * 512], in_=bank(7), func=AF.Silu,
                                                               bias=cwT[:, ch, 4:5]),
                                 reads=[psr[7], cwT_r], writes=[qk_r[m][cc]])
                        if pending:
                            pending.pop(0)()
                        pending.append(conv_part)
                    units.append(u_proj)
        def u_flush():
            while pending:
                pending.pop(0)()
        units.append(u_flush)
        for c in range(NT):
            def u_gate(c=c):
                cs = slice(c * 128, (c + 1) * 128)
                ob = 6 + c % 2
                for k in range(KC):
                    P.op("pe", lambda e, k=k: e.matmul(bank(ob)[:, 0:256], lhsT=hT[:, k, cs], rhs=wq[2][:, k, :],
                                                       start=(k == 0), stop=(k == KC - 1)),
                         reads=[wq_r[2][k // 4]], writes=[psr[ob]])
                P.op("act", lambda e: e.activation(out=sigo[:, c, :], in_=bank(ob)[:, 0:256], func=AF.Sigmoid),
                     reads=[psr[ob]], writes=[sigo_r[c]])
            units.append(u_gate)
        return units

    def loop_steps(hd):
        sl = hd % 2
        qT, kT, qk_r, sigo, sigo_r = qT2[sl], kT2[sl], qk2_r[sl], sigo2[sl], sigo2_r[sl]
        wv, wv_r = wq[3 + sl], wq_r[3 + sl]
        qkr = [qk_r[0][0], qk_r[0][1], qk_r[1][0], qk_r[1][1]]
        def chunk_vars(c):
            p = c % 2
            q3 = c % 3
            return dict(cs=slice(c * 128, (c + 1) * 128), col=c * 4 + hd, ktok=ktok2[p], ktok_r=ktok2_r[p], vaug=vaug2[p],
                        vaug_r=vaug2_r[p], vp=vp2[p], vp_r=vp2_r[p], wT=wT2[p], wT_r=wT2_r[p], hu=hu3[q3], hu_r=hu3_r[q3],
                        sm=sm3[q3], sm_r=sm3_r[q3], junk=junk3[q3], mlb=mlb3[q3], mlb_r=mlb3_r[q3], ab=p, db=2 + p, tb=p)

        def stageA1(c):
            v_ = chunk_vars(c)
            cs, col, ktok, ktok_r, vaug, vaug_r, vp, vp_r, wT, wT_r, ab, db = (v_[k] for k in (
                "cs", "col", "ktok", "ktok_r", "vaug", "vaug_r", "vp", "vp_r", "wT", "wT_r", "ab", "db"))
            abf = bank(ab).bitcast(BF16)
            for k in range(KC):
                P.op("pe", lambda e, k=k: e.matmul(bank(ab)[:, 0:256], lhsT=hT[:, k, cs], rhs=wv[:, k, :],
                                                   start=(k == 0), stop=(k == KC - 1)),
                     reads=[wv_r[k // 4]], writes=[psr[ab]])
            for dk in range(2):
                P.op("pe", lambda e, dk=dk: e.transpose(out=abf[:, 512 + dk * 128:512 + (dk + 1) * 128],
                                                        in_=kT[:, dk, cs], identity=C.ident),
                     reads=[qk_r[1][dk], C.cst_r], writes=[psr[ab]])
            P.op("act", lambda e: e.activation(out=vaug[:, 0:256], in_=bank(ab)[:, 0:256], func=AF.Copy),
                 reads=[psr[ab]], writes=[vaug_r])
            P.op("act", lambda e: e.activation(out=ktok, in_=abf[:, 512:768], func=AF.Copy),
                 reads=[psr[ab]], writes=[ktok_r])
            for dk in range(2):
                P.op("pe", lambda e, dk=dk: e.matmul(bank(db)[:, 384:512], lhsT=kT[:, dk, cs], rhs=qT[:, dk, cs],
                                                     start=(dk == 0), stop=(dk == 1)),
                     reads=qkr, writes=[psr[db]])
            P.op("dve", lambda e: e.scalar_tensor_tensor(
                out=wT, in0=bank(db)[:, 384:512], scalar=e_in[:, col:col + 1], in1=C.tri, op0=ALU.mult, op1=ALU.mult),
                reads=[psr[db], ee_r, C.cst_r], writes=[wT_r])
            if c < NT - 1:
                P.op("dve", lambda e: e.tensor_scalar(out=vp, in0=vaug, scalar1=e_v[:, col:col + 1], scalar2=None, op0=ALU.mult),
                     reads=[vaug_r, ee_r], writes=[vp_r])

        def stageA2(c):
            v_ = chunk_vars(c)
            cs, col, ktok, ktok_r, vaug, vaug_r, vp, vp_r, wT, wT_r, db = (v_[k] for k in (
                "cs", "col", "ktok", "ktok_r", "vaug", "vaug_r", "vp", "vp_r", "wT", "wT_r", "db"))
            if c < NT - 1:
                for dk in range(2):
                    P.op("pe", lambda e, dk=dk: e.matmul(bank(4 + dk)[:, 0:258], lhsT=ktok[:, dk * 128:(dk + 1) * 128], rhs=vp,
                                                         start=True, stop=True),
                         reads=[ktok_r, vp_r], writes=[psr[4 + dk]])
            P.op("pe", lambda e: e.matmul(bank(db)[:, 0:258], lhsT=wT, rhs=vaug, start=True, stop=(c == 0)),
                 reads=[wT_r, vaug_r], writes=[psr[db]])
            if c > 0:
                cprev, cprev_r = Cbf2[(c - 1) % 2], Cbf2_r[(c - 1) % 2]
                for dk in range(2):
                    P.op("pe", lambda e, dk=dk: e.matmul(bank(db)[:, 0:258], lhsT=qT[:, dk, cs], rhs=cprev[:, dk, :],
                                                         start=False, stop=(dk == 1)),
                         reads=qkr + [cprev_r], writes=[psr[db]])
            if c < NT - 1:
                dC = PS[:, 2048:3072].rearrange("p (a b) -> p a b", a=2)[:, :, 0:258]
                if c == 0:
                    P.op("dve", lambda e: e.tensor_copy(out=Cf, in_=dC), reads=[psr[4], psr[5]], writes=[Cf_r])
                else:
                    P.op("dve", lambda e: e.scalar_tensor_tensor(out=Cf, in0=Cf, scalar=e_L[:, col:col + 1], in1=dC,
                                                                 op0=ALU.mult, op1=ALU.add),
                         reads=[psr[4], psr[5], Cf_r, ee_r], writes=[Cf_r])
                ccur, ccur_r = Cbf2[c % 2], Cbf2_r[c % 2]
                P.op("act", lambda e: e.activation(out=ccur, in_=Cf, func=AF.Copy), reads=[Cf_r], writes=[ccur_r])

        def stageB(c):
            v_ = chunk_vars(c)
            col, hu, hu_r, sm, sm_r, junk, db = (v_[k] for k in ("col", "hu", "hu_r", "sm", "sm_r", "junk", "db"))
            P.op("dve", lambda e, col=col, sm=sm, db=db: e.tensor_scalar(out=sm[:, 0:1], in0=bank(db)[:, 256:257],
                                                                       scalar1=e_io[:, col:col + 1], scalar2=None, op0=ALU.max),
                 reads=[psr[db], ee_r], writes=[sm_r])
            P.op("dve", lambda e, sm=sm, db=db: e.scalar_tensor_tensor(out=sm[:, 1:2], in0=bank(db)[:, 256:257], scalar=-1.0,
                                                                     in1=sm[:, 0:1], op0=ALU.mult, op1=ALU.max),
                 reads=[psr[db], sm_r], writes=[sm_r])
            P.op("dve", lambda e, sm=sm: e.reciprocal(out=sm[:, 3:4], in_=sm[:, 1:2]), reads=[sm_r], writes=[sm_r])
            P.op("dve", lambda e, sm=sm, hu=hu, db=db: e.tensor_scalar(out=hu, in0=bank(db)[:, 0:256], scalar1=sm[:, 3:4], scalar2=None,
                                                                     op0=ALU.mult),
                 reads=[psr[db], sm_r], writes=[hu_r])
            P.op("act", lambda e, sm=sm, hu=hu, junk=junk: e.activation(out=junk, in_=hu, func=AF.Square, accum_out=sm[:, 4:5]),
                 reads=[hu_r, sm_r], writes=[sm_r])
            P.op("pool", lambda e, sm=sm: e.tensor_scalar(out=sm[:, 5:6], in0=sm[:, 4:5], scalar1=1.0 / 256, scalar2=EPS,
                                                          op0=ALU.mult, op1=ALU.add), reads=[sm_r], writes=[sm_r])
            P.op("pool", lambda e, sm=sm: e.tensor_tensor(out=sm[:, 5:6], in0=sm[:, 5:6], in1=neghalf, op=ALU.pow),
                 reads=[sm_r, nh_r], writes=[sm_r])

        def stageC(c):
            v_ = chunk_vars(c)
            cs, hu, hu_r, sm, sm_r, mlb, mlb_r, tb = (v_[k] for k in ("cs", "hu", "hu_r", "sm", "sm_r", "mlb", "mlb_r", "tb"))
            abf = bank(tb).bitcast(BF16)
            ab = tb
            P.op("dve", lambda e, hd=hd, sm=sm, hu=hu: e.scalar_tensor_tensor(out=hu, in0=hu, scalar=sm[:, 5:6],
                                                                              in1=hg[:, hd * 256:(hd + 1) * 256],
                                                                              op0=ALU.mult, op1=ALU.mult),
                 reads=[hu_r, sm_r, hg_r], writes=[hu_r])
            P.op("dve", lambda e, c=c, hu=hu, mlb=mlb: e.tensor_tensor(out=mlb, in0=hu, in1=sigo[:, c, :], op=ALU.mult),
                 reads=[hu_r, sigo_r[c]], writes=[mlb_r])

        def stageC2(c):
            v_ = chunk_vars(c)
            cs, hu, hu_r, sm, sm_r, mlb, mlb_r, tb = (v_[k] for k in ("cs", "hu", "hu_r", "sm", "sm_r", "mlb", "mlb_r", "tb"))
            abf = bank(tb).bitcast(BF16)
            ab = tb
            for j in range(2):
                P.op("pe", lambda e, j=j, abf=abf, mlb=mlb: e.transpose(out=abf[:, 768 + j * 128:768 + (j + 1) * 128],
                                                                       in_=mlb[:, j * 128:(j + 1) * 128], identity=C.ident),
                     reads=[mlb_r, C.cst_r], writes=[psr[ab]])
            P.op("act", lambda e, hd=hd, cs=cs, abf=abf: e.activation(
                out=mlT[:, 2 * hd:2 * hd + 2, cs], in_=abf[:, 768:1024].rearrange("p (j n) -> p j n", j=2), func=AF.Copy),
                reads=[psr[ab]], writes=[C.mlT_r])


        steps = []
        for step in range(NT + 3):
            def st(fill, step=step):
                if step == 0:
                    stageA1(0)
                if step + 1 < NT:
                    stageA1(step + 1)
                fill()
                if step < NT:
                    stageA2(step)
                fill()
                if 0 <= step - 1 < NT:
                    stageB(step - 1)
                fill()
                if 0 <= step - 3 < NT:
                    stageC2(step - 3)
                if 0 <= step - 2 < NT:
                    stageC(step - 2)
            steps.append(st)
        return steps

    for u in pre_units(0):
        u()
    NH = int(os.environ.get("NH", 4))
    for hd in range(NH):
        nxt = pre_units(hd + 1) if hd < NH - 1 else []
        steps = loop_steps(hd) if not os.environ.get("NOLOOP") else []
        if not steps:
            for u in nxt:
                u()
            continue
        nfill = 3 * len(steps)
        per = -(-len(nxt) // nfill) if nxt else 0
        pos = [0]

        def fill():
            for u in nxt[pos[0]:pos[0] + per]:
                u()
            pos[0] += per
        for st in steps:
            st(fill)
        for u in nxt[pos[0]:]:
            u()

def merge_phase(C, A, hT, attT, mlT, w_in, w_a, w_m, w_out, g_post, x_src, x_dst):
    P, PS, psr = C.P, C.PS, C.psr
    OGA, OGM = 8712, 9736

    def bank(b):
        return PS[:, 512 * b:512 * (b + 1)]

    mgT = A.alloc((KC, S), BF16)
    wa = A.alloc((4, 1024), BF16)
    wa_r = [Reg() for _ in range(4)]
    wm = A.alloc((KC, 1024), BF16)
    wm_r = [Reg() for _ in range(KC)]
    wo = A.alloc((KC, 1024), BF16)
    wo_r = [Reg() for _ in range(KC)]
    wg = [[A.alloc((KC, 128), BF16) for _ in range(2)] for _ in range(2)]
    wg_r = [[Reg() for _ in range(2)] for _ in range(2)]
    load_cast(C, wg[0][0], w_in[:, OGA:OGA + 128].rearrange("(k p) n -> p k n", p=128), wg_r[0][0])
    load_cast(C, wg[0][1], w_in[:, OGM:OGM + 128].rearrange("(k p) n -> p k n", p=128), wg_r[0][1])
    for k in range(4):
        load_cast(C, wa[:, k, :], w_a[k * 128:(k + 1) * 128, :], wa_r[k])
    for k in range(KC):
        load_cast(C, wm[:, k, :], w_m[k * 128:(k + 1) * 128, :], wm_r[k])
    ga = [A.alloc((512,), F32) for _ in range(2)]
    ga_r = [Reg() for _ in range(2)]
    gm = [A.alloc((512,), F32) for _ in range(2)]
    gm_r = [Reg() for _ in range(2)]
    ta = [A.alloc((512,), F32) for _ in range(2)]
    ta_r = [Reg() for _ in range(2)]
    gb = A.alloc((D,), F32)
    gb_r = Reg()
    P.dma("sp", gb, g_post.partition_broadcast(128), "g", writes=[gb_r])
    j = 0
    for mc in range(KC):
        sl = mc % 2
        if mc > 0:
            load_cast(C, wg[sl][0], w_in[:, OGA + mc * 128:OGA + (mc + 1) * 128].rearrange("(k p) n -> p k n", p=128), wg_r[sl][0])
            load_cast(C, wg[sl][1], w_in[:, OGM + mc * 128:OGM + (mc + 1) * 128].rearrange("(k p) n -> p k n", p=128), wg_r[sl][1])
        if mc == 0:
            for k in range(KC):
                load_cast(C, wo[:, k, :], w_out[k * 128:(k + 1) * 128, :], wo_r[k])
        for tg in range(4):
            q = j % 2
            j += 1
            ts = slice(tg * 512, (tg + 1) * 512)
            b0 = 4 * q
            for gi_, (gbuf, gr) in enumerate(((ga, ga_r), (gm, gm_r))):
                for k in range(KC):
                    P.op("pe", lambda e, k=k, gi_=gi_, sl=sl, ts=ts, b0=b0: e.matmul(
                        bank(b0 + gi_), lhsT=wg[sl][gi_][:, k, :], rhs=hT[:, k, ts], start=(k == 0), stop=(k == KC - 1)),
                        reads=[wg_r[sl][gi_]], writes=[psr[b0 + gi_]])
                P.op("act", lambda e, gbuf=gbuf, q=q, gi_=gi_, b0=b0: e.activation(out=gbuf[q], in_=bank(b0 + gi_), func=AF.Sigmoid),
                     reads=[psr[b0 + gi_]], writes=[gr[q]])
            for k in range(4):
                P.op("pe", lambda e, k=k, mc=mc, ts=ts, b0=b0: e.matmul(
                    bank(b0 + 2), lhsT=wa[:, k, mc * 128:(mc + 1) * 128], rhs=attT[:, k, ts], start=(k == 0), stop=(k == 3)),
                    reads=[wa_r[k], C.attT_r], writes=[psr[b0 + 2]])
            for k in range(KC):
                P.op("pe", lambda e, k=k, mc=mc, ts=ts, b0=b0: e.matmul(
                    bank(b0 + 3), lhsT=wm[:, k, mc * 128:(mc + 1) * 128], rhs=mlT[:, k, ts], start=(k == 0), stop=(k == KC - 1)),
                    reads=[wm_r[k], C.mlT_r], writes=[psr[b0 + 3]])
            P.op("dve", lambda e, q=q, b0=b0: e.tensor_tensor(out=ta[q], in0=bank(b0 + 2), in1=ga[q], op=ALU.mult),
                 reads=[psr[b0 + 2], ga_r[q]], writes=[ta_r[q]])
            P.op("dve", lambda e, q=q, b0=b0: e.tensor_tensor(out=gm[q], in0=bank(b0 + 3), in1=gm[q], op=ALU.mult),
                 reads=[psr[b0 + 3], gm_r[q]], writes=[gm_r[q]])
            P.op("dve", lambda e, q=q, mc=mc, ts=ts: e.tensor_tensor(out=mgT[:, mc, ts], in0=ta[q], in1=gm[q], op=ALU.add),
                 reads=[ta_r[q], gm_r[q]], writes=[C.mg_r])
    xc = [A.alloc((D,), F32) for _ in range(2)]
    tt = [A.alloc((D,), F32) for _ in range(2)]
    xc_r = [Reg() for _ in range(2)]
    tth_r = [[Reg(), Reg()] for _ in range(2)]
    ss2 = A.alloc((NT,), F32)
    r2 = A.alloc((NT,), F32)
    junk = ga[0].bitcast(BF16)
    junk_r = ga_r[0]
    r2_r = [Reg() for _ in range(NT)]
    for i in range(NT):
        s = i % 2
        P.dma("sp", xc[s], x_src[i * 128:(i + 1) * 128, :], f"xc{s}", writes=[xc_r[s]])
        pb = (i % 4) * 2
        psf = PS[:, 512 * pb:512 * (pb + 2)]
        for h in range(2):
            for k in range(KC):
                P.op("pe", lambda e, h=h, k=k, i=i, pb=pb: e.matmul(
                    bank(pb + h), lhsT=mgT[:, k, i * 128:(i + 1) * 128], rhs=wo[:, k, h * 512:(h + 1) * 512],
                    start=(k == 0), stop=(k == KC - 1)), reads=[wo_r[k], C.mg_r], writes=[psr[pb + h]])
        P.op("act", lambda e, i=i, psf=psf: e.activation(out=junk, in_=psf, func=AF.Square, accum_out=ss2[:, i:i + 1]),
             reads=[psr[pb], psr[pb + 1]], writes=[r2_r[i], junk_r])
        P.op("act", lambda e, i=i: e.activation(out=r2[:, i:i + 1], in_=ss2[:, i:i + 1], func=AF.Ln, scale=1.0 / D, bias=EPS),
             reads=[r2_r[i]], writes=[r2_r[i]])
        P.op("act", lambda e, i=i: e.activation(out=r2[:, i:i + 1], in_=r2[:, i:i + 1], func=AF.Exp, scale=-0.5),
             reads=[r2_r[i]], writes=[r2_r[i]])
        for h in range(2):
            P.op("dve", lambda e, s=s, h=h, pb=pb: e.tensor_tensor(
                out=tt[s][:, 512 * h:512 * (h + 1)], in0=bank(pb + h), in1=gb[:, 512 * h:512 * (h + 1)], op=ALU.mult),
                reads=[psr[pb + h], gb_r, r2_r[i]], writes=[tth_r[s][h]])
        P.op("dve", lambda e, s=s, i=i: e.scalar_tensor_tensor(out=xc[s], in0=tt[s], scalar=r2[:, i:i + 1], in1=xc[s],
                                                              op0=ALU.mult, op1=ALU.add),
             reads=[tth_r[s][0], tth_r[s][1], r2_r[i], xc_r[s]], writes=[xc_r[s]])
        P.dma("sp", x_dst[i * 128:(i + 1) * 128, :], xc[s], f"xo{s}", reads=[xc_r[s]])


def mixer_phase(C, x_src, x_dst, W):
    A = C.A.child()
    hT = A.alloc((KC, S), BF16)
    attT = A.alloc((4, S), BF16)
    mlT = A.alloc((8, S), BF16)
    C.attT_r = Reg()
    C.mlT_r = Reg()
    C.mg_r = Reg()
    base = A.off
    end = A.end
    norm_transpose(C, Arena(A.ap, base, end), x_src, W["mix_pre_g"][0], hT)
    attention_phase(C, Arena(A.ap, base, end), hT, attT, W["w_in"][0])
    C.P.barrier()
    mlstm_phase(C, Arena(A.ap, base, end), hT, mlT, W["w_in"][0], W["conv_w"][0], W["conv_b"][0],
                W["mlstm_i_bias"][0], W["mlstm_f_bias"][0], W["mlstm_head_g"][0])
    C.P.barrier()
    merge_phase(C, Arena(A.ap, base, end), hT, attT, mlT, W["w_in"][0], W["w_att_branch"][0], W["w_mlstm_branch"][0],
                W["w_out"][0], W["mix_post_g"][0], x_src, x_dst)
    C.P.barrier()

def host_consts():
    c = {}
    bf = ml_dtypes.bfloat16
    c["ident"] = np.eye(128, dtype=np.float32).astype(bf)
    half = 16
    inv_freq = np.power(np.float32(500000.0), -(np.arange(half, dtype=np.float32) * 2.0 / 32)).astype(np.float32)
    ang = np.arange(S, dtype=np.float32)[None, :] * inv_freq[:, None]
    c["cos"] = np.concatenate([np.cos(ang), np.cos(ang)], 0).astype(np.float32)
    c["sin"] = np.concatenate([np.sin(ang), np.sin(ang)], 0).astype(np.float32)
    rm = np.zeros((32, 32), np.float32)
    for j in range(16):
        rm[16 + j, j] = -1.0
        rm[j, 16 + j] = 1.0
    c["rm"] = rm.astype(bf)
    jj = np.arange(128)[:, None]
    ii = np.arange(128)[None, :]
    NEG = -30000.0
    c["maskc"] = np.where(jj <= ii, 0.0, NEG).astype(bf)
    c["maskp"] = np.where(jj >= ii, 0.0, NEG).astype(bf)
    c["maskn"] = np.full((128, 128), NEG, np.float32).astype(bf)
    c["tri"] = (jj <= ii).astype(np.float32)
    c["ones_f"] = np.ones((128, 128), np.float32)
    c["identf"] = np.eye(128, dtype=np.float32)
    c["ones_bf"] = np.ones((128, 128), np.float32).astype(bf)
    return c


def build(stage="full"):
    nc = bass.Bass("TRN2", target_bir_lowering=False)

    def din(name, shape, dt=F32):
        return nc.dram_tensor(name, list(shape), dt, kind="ExternalInput").ap()

    x = din("x", [S, D])
    W = {}
    for name, shape in [("ffn1_pre_g", [1, D]), ("ffn1_w_gate", [1, D, FF]), ("ffn1_w_up", [1, D, FF]),
                        ("ffn1_w_down", [1, FF, D]), ("ffn1_post_g", [1, D]), ("mix_pre_g", [1, D]),
                        ("w_in", [1, D, IN_W]), ("conv_w", [1, 4, 2048]), ("conv_b", [1, 2048]),
                        ("mlstm_i_bias", [1, 4]), ("mlstm_f_bias", [1, 4]), ("mlstm_head_g", [1, 1024]),
                        ("w_att_branch", [1, 512, D]), ("w_mlstm_branch", [1, D, D]), ("w_out", [1, D, D]),
                        ("mix_post_g", [1, D]), ("ffn2_pre_g", [1, D]), ("ffn2_w_gate", [1, D, FF]),
                        ("ffn2_w_up", [1, D, FF]), ("ffn2_w_down", [1, FF, D]), ("ffn2_post_g", [1, D])]:
        W[name] = din(name, shape)
    CD = {}
    for name, shape, dt in [("c_ident", [128, 128], BF16), ("c_cos", [32, S], F32), ("c_sin", [32, S], F32),
                            ("c_rm", [32, 32], BF16), ("c_maskc", [128, 128], BF16), ("c_maskp", [128, 128], BF16),
                            ("c_maskn", [128, 128], BF16), ("c_tri", [128, 128], F32), ("c_ones_f", [128, 128], F32), ("c_identf", [128, 128], F32),
                            ("c_ones_bf", [128, 128], BF16)]:
        CD[name] = din(name, shape, dt)
    c_ident = CD["c_ident"]
    out = nc.dram_tensor("out", [S, D], F32, kind="ExternalOutput").ap()
    x1 = nc.dram_tensor("x1", [S, D], F32, kind="Internal").ap()
    x2 = nc.dram_tensor("x2", [S, D], F32, kind="Internal").ap()

    with ExitStack() as st:
        ARENA_BYTES = 212480
        arena = st.enter_context(nc.sbuf_tensor("arena", [128, ARENA_BYTES // 2], BF16))
        PS = st.enter_context(nc.psum_tensor("ps", [128, 4096], F32))
        C = Ctx()
        C.nc = nc
        C.P = P = Prog(nc)
        C.PS = PS
        C.psr = [Reg(f"ps{i}", excl=True) for i in range(8)]
        top = Arena(arena, 0, ARENA_BYTES)
        C.ident = top.alloc((128,), BF16)
        C.ident_r = Reg()
        P.dma("sp", C.ident, c_ident, "cst", writes=[C.ident_r])
        C.dram = CD
        C.cst_r = C.ident_r
        for nm, shp, dt in [("rm", (32,), BF16), ("maskc", (128,), BF16), ("maskp", (128,), BF16), ("maskn", (128,), BF16),
                            ("tri", (128,), F32), ("ones_f", (128,), F32), ("identf", (128,), F32), ("ones_bf", (128,), BF16)]:
            v = top.alloc(shp, dt)
            if nm == "rm":
                v = v[:32]
            setattr(C, nm, v)
            P.dma("sp", v, CD["c_" + nm], "cst", writes=[C.cst_r])
        C.stage = [top.alloc((1024,), F32) for _ in range(NST)]
        C.stage_r = [Reg() for _ in range(NST)]
        C.stage_i = 0
        C.A = Arena(arena, top.off, ARENA_BYTES)

        if stage == "attn":
            A = C.A.child()
            hT = A.alloc((KC, S), BF16)
            attT = A.alloc((4, S), BF16)
            C.attT_r = Reg()
            mk = Arena(arena, A.off, ARENA_BYTES)
            norm_transpose(C, mk, x, W["mix_pre_g"][0], hT)
            attention_phase(C, Arena(arena, A.off, ARENA_BYTES), hT, attT, W["w_in"][0])
            P.barrier()
            ov = out.rearrange("(a b) d -> a (b d)", a=1024).rearrange("(k p) t -> p k t", p=128)
            for k in range(4):
                for hh in range(2):
                    P.dma("pool", ov[:, k, hh * 1024:(hh + 1) * 1024], attT[:, k, hh * 1024:(hh + 1) * 1024], "dbg")
        if stage == "full":
            ffn_phase(C, x, x1, W["ffn1_pre_g"][0], W["ffn1_w_gate"][0], W["ffn1_w_up"][0], W["ffn1_w_down"][0],
                      W["ffn1_post_g"][0])
            mixer_phase(C, x1, x2, W)
            ffn_phase(C, x2, out, W["ffn2_pre_g"][0], W["ffn2_w_gate"][0], W["ffn2_w_up"][0], W["ffn2_w_down"][0],
                      W["ffn2_post_g"][0])
        if stage == "mix":
            mixer_phase(C, x, out, W)
        if stage == "ml":
            A = C.A.child()
            hT = A.alloc((KC, S), BF16)
            mlT = A.alloc((8, S), BF16)
            C.mlT_r = Reg()
            mk = Arena(arena, A.off, ARENA_BYTES)
            norm_transpose(C, mk, x, W["mix_pre_g"][0], hT)
            mlstm_phase(C, Arena(arena, A.off, ARENA_BYTES), hT, mlT, W["w_in"][0], W["conv_w"][0], W["conv_b"][0],
                        W["mlstm_i_bias"][0], W["mlstm_f_bias"][0], W["mlstm_head_g"][0])
            P.barrier()
            ov = out.rearrange("(a b) d -> a (b d)", a=1024).rearrange("(k p) t -> p k t", p=128)
            for k in range(8):
                for hh in range(2):
                    P.dma("pool", ov[:, k, hh * 1024:(hh + 1) * 1024], mlT[:, k, hh * 1024:(hh + 1) * 1024], "dbg")
        if stage == "ffn1a":
            A = C.A.child()
            hT_ar = A.sub(KC * S * 2)
            actT_ar = A.sub(FC * S * 2)
            hT = hT_ar.child().alloc((KC, S), BF16)
            norm_transpose(C, actT_ar.child(), x, W["ffn1_pre_g"][0], hT)
            ov = out.rearrange("(a b) d -> a (b d)", a=1024).rearrange("(k p) t -> p k t", p=128)
            for k in range(KC):
                for hh in range(2):
                    P.dma("pool", ov[:, k, hh * 1024:(hh + 1) * 1024], hT[:, k, hh * 1024:(hh + 1) * 1024], "dbg")
        if stage == "ffn1b":
            ffn_phase(C, x, out, W["ffn1_pre_g"][0], W["ffn1_w_gate"][0], W["ffn1_w_up"][0], W["ffn1_w_down"][0],
                      W["ffn1_post_g"][0], stop_after="B")
        if stage == "ffn1":
            ffn_phase(C, x, out, W["ffn1_pre_g"][0], W["ffn1_w_gate"][0], W["ffn1_w_up"][0], W["ffn1_w_down"][0],
                      W["ffn1_post_g"][0])
        P.finish()
        P.emit()
        print("prog stats", P.stats, "sems", len(P.dma_tot) + 5)
        if os.environ.get("DUMP"):
            for e in ("sp", "dve", "act"):
                print("====", e)
                for r in P.dump[e][-int(os.environ["DUMP"]):]:
                    print(r)
    return nc


_NC_CACHE = {}


def kernel(**inputs):
    stage = inputs.pop("_stage", os.environ.get("KSTAGE", "full"))
    if stage not in _NC_CACHE:
        _NC_CACHE[stage] = build(stage)
    nc = _NC_CACHE[stage]
    consts = host_consts()
    xfull = np.ascontiguousarray(inputs["x"], dtype=np.float32)
    shared = {k: np.ascontiguousarray(v, dtype=np.float32) for k, v in inputs.items() if k != "x"}
    for k, v in consts.items():
        shared["c_" + k] = v
    in_maps = []
    ncores = int(os.environ.get("NCORES", 8))
    for b in range(ncores):
        m = dict(shared)
        m["x"] = xfull[b]
        in_maps.append(m)
    res = run_bass_kernel_spmd(nc, in_maps, core_ids=list(range(ncores)))
    return np.stack([r["out"] for r in res.results], axis=0)
```
